# Optimizing a Trainium2 kernel written in Bass

```python
import math
import jax, jax.numpy as jnp
from jax import lax
import numpy as np

D_MODEL = 1024
BATCH = 8
SEQ = 4096
DEPTH = 1

CHUNK = 64
N_MEM = 256
Q_BLOCK = 128
SSM_WIDTH = D_MODEL // 2
SSM_GROUP = 16
SSM_GROUPS = SSM_WIDTH // SSM_GROUP
SSM_STATE = 64
DT_MIN = 1e-3
DT_MAX = 1e-1
DIFF_HEADS = 4
DIFF_HEAD_DIM = D_MODEL // 16
DIFF_QK = 2 * DIFF_HEADS * DIFF_HEAD_DIM
DIFF_V = DIFF_HEADS * 2 * DIFF_HEAD_DIM
MEM_HEADS = 4
MEM_HEAD_DIM = D_MODEL // 8
MEM_WIDTH = MEM_HEADS * MEM_HEAD_DIM
N_BRANCH = 3
REL_BUCKETS = 32
REL_MAX_DIST = 128
LN_EPS = 1e-5
RMS_EPS = 1e-5
NEG_INF = -1e30
DEEPNORM_ALPHA = (2.0 * DEPTH) ** 0.25
DEEPNORM_BETA = (8.0 * DEPTH) ** -0.25
SPLITS = [SSM_WIDTH, SSM_WIDTH, DIFF_QK, DIFF_QK, DIFF_V, DIFF_V, MEM_WIDTH, MEM_WIDTH, N_BRANCH * D_MODEL]
D_IN = SSM_WIDTH * 2 + DIFF_QK * 2 + DIFF_V * 2 + MEM_WIDTH * 2 + N_BRANCH * D_MODEL

kernel_name = "hybrid_s5_diffattn_memxattn_deepnorm"


def _split_points():
    pts, acc = [], 0
    for s in SPLITS[:-1]:
        acc += s
        pts.append(acc)
    return pts


def _layer_norm(h, g, b):
    hf = h.astype(jnp.float32)
    mu = jnp.mean(hf, axis=-1, keepdims=True)
    var = jnp.mean(jnp.square(hf - mu), axis=-1, keepdims=True)
    return ((hf - mu) * lax.rsqrt(var + LN_EPS) * g + b).astype(h.dtype)


def _t5_bucket(rel):
    half = REL_BUCKETS // 2
    max_exact = half // 2
    ret = jnp.where(rel > 0, half, 0)
    n = jnp.abs(rel)
    large = max_exact + (jnp.log(jnp.maximum(n, 1).astype(jnp.float32) / max_exact)
                         / math.log(REL_MAX_DIST / max_exact) * (half - max_exact)).astype(jnp.int32)
    large = jnp.minimum(large, half - 1)
    return ret + jnp.where(n < max_exact, n, large)


def _scan_op(e1, e2):
    a1, b1 = e1
    a2, b2 = e2
    return a2 * a1, a2 * b1 + b2


def _s5_branch(u, lam_re, lam_im, log_dt, b_re, b_im, c_re, c_im, d_skip, w_glu):
    bsz, seq, _ = u.shape
    uf = u.astype(jnp.float32)
    lam = lax.complex(lam_re.astype(jnp.float32), lam_im.astype(jnp.float32))
    dt = jnp.exp(log_dt.astype(jnp.float32))[:, None]
    lam_bar = jnp.exp(lam * dt)
    b = lax.complex(b_re.astype(jnp.float32), b_im.astype(jnp.float32))
    b_bar = ((lam_bar - 1.0) / lam)[..., None] * b
    ug = uf.reshape(bsz, seq, SSM_GROUPS, SSM_GROUP).astype(jnp.complex64)
    bu = jnp.einsum('bsgh,gph->bsgp', ug, b_bar)
    a = jnp.broadcast_to(lam_bar, (1, seq) + lam_bar.shape)
    _, states = lax.associative_scan(_scan_op, (a, bu), axis=1)
    c = lax.complex(c_re.astype(jnp.float32), c_im.astype(jnp.float32))
    y = jnp.real(jnp.einsum('bsgp,ghp->bsgh', states, c)).reshape(bsz, seq, SSM_WIDTH)
    y = jax.nn.gelu(y + d_skip.astype(jnp.float32) * uf)
    val, gate = jnp.split(y @ w_glu.astype(jnp.float32), 2, axis=-1)
    return (val * jax.nn.sigmoid(gate)).astype(u.dtype)


def _diff_attention(q, k, v, lambda_q1, lambda_k1, lambda_q2, lambda_k2, subln_w, rel_bias, lambda_init):
    bsz, seq, _ = q.shape
    scale = DIFF_HEAD_DIM ** -0.5
    q = q.reshape(bsz, seq, DIFF_HEADS, 2, DIFF_HEAD_DIM) * scale
    k = k.reshape(bsz, seq, DIFF_HEADS, 2, DIFF_HEAD_DIM)
    v = v.reshape(bsz, seq, DIFF_HEADS, 2 * DIFF_HEAD_DIM)
    lam = (jnp.exp(jnp.sum(lambda_q1.astype(jnp.float32) * lambda_k1.astype(jnp.float32)))
           - jnp.exp(jnp.sum(lambda_q2.astype(jnp.float32) * lambda_k2.astype(jnp.float32)))
           + lambda_init)
    dist = jnp.arange(-(seq - 1), seq)
    bias_d = rel_bias[_t5_bucket(dist)].astype(jnp.float32)
    n_blk = seq // Q_BLOCK
    qb = q.reshape(bsz, n_blk, Q_BLOCK, DIFF_HEADS, 2, DIFF_HEAD_DIM).transpose(1, 0, 2, 3, 4, 5)
    kpos = jnp.arange(seq)

    def attend(args):
        q_blk, blk = args
        qpos = blk * Q_BLOCK + jnp.arange(Q_BLOCK)
        s = jnp.einsum('bqhcd,bkhcd->bhcqk', q_blk, k).astype(jnp.float32)
        bias = bias_d[kpos[None, :] - qpos[:, None] + seq - 1]
        s = s + bias.transpose(2, 0, 1)[None, :, None]
        allowed = (kpos[None, :] // CHUNK) <= (qpos[:, None] // CHUNK)
        p = jax.nn.softmax(jnp.where(allowed, s, NEG_INF), axis=-1)
        a = p[:, :, 0] - lam * p[:, :, 1]
        return jnp.einsum('bhqk,bkhe->bqhe', a.astype(v.dtype), v)

    o = lax.map(attend, (qb, jnp.arange(n_blk)))
    o = o.transpose(1, 0, 2, 3, 4).reshape(bsz, seq, DIFF_HEADS, 2 * DIFF_HEAD_DIM).astype(jnp.float32)
    o = o * lax.rsqrt(jnp.mean(jnp.square(o), axis=-1, keepdims=True) + RMS_EPS) * subln_w
    o = o * (1.0 - lambda_init)
    return o.reshape(bsz, seq, DIFF_V)


def _memory_attention(q, mk, mv):
    bsz, seq, _ = q.shape
    n_mem = mk.shape[1]
    q = q.reshape(bsz, seq, MEM_HEADS, MEM_HEAD_DIM) * (MEM_HEAD_DIM ** -0.5)
    mk = mk.reshape(bsz, n_mem, MEM_HEADS, MEM_HEAD_DIM)
    mv = mv.reshape(bsz, n_mem, MEM_HEADS, MEM_HEAD_DIM)
    s = jnp.einsum('bshd,bmhd->bhsm', q, mk).astype(jnp.float32)
    p = jax.nn.softmax(s, axis=-1)
    o = jnp.einsum('bhsm,bmhd->bshd', p.astype(mv.dtype), mv)
    return o.reshape(bsz, seq, MEM_WIDTH)


def setup_inputs(seed: int = 0) -> dict:
    key = jax.random.key(seed)
    ks = jax.random.split(key, 32)
    nrm = jax.random.normal
    f32 = jnp.float32
    pts = _split_points()
    x = nrm(ks[0], (BATCH, SEQ, D_MODEL), f32)
    mem = nrm(ks[1], (BATCH, N_MEM, D_MODEL), f32)
    w_in = nrm(ks[2], (DEPTH, D_MODEL, D_IN), f32) * D_MODEL ** -0.5
    w_in = w_in.at[:, :, pts[3]:pts[4]].multiply(DEEPNORM_BETA)
    lam_re = -0.5 + 0.01 * nrm(ks[3], (DEPTH, SSM_GROUPS, SSM_STATE), f32)
    lam_im = (jnp.pi * jnp.arange(SSM_STATE, dtype=f32))[None, None, :] + 0.01 * nrm(ks[4], (DEPTH, SSM_GROUPS, SSM_STATE), f32)
    log_dt = jax.random.uniform(ks[5], (DEPTH, SSM_GROUPS), f32, math.log(DT_MIN), math.log(DT_MAX))
    b_re = nrm(ks[6], (DEPTH, SSM_GROUPS, SSM_STATE, SSM_GROUP), f32) * (2.0 * SSM_GROUP) ** -0.5
    b_im = nrm(ks[7], (DEPTH, SSM_GROUPS, SSM_STATE, SSM_GROUP), f32) * (2.0 * SSM_GROUP) ** -0.5
    c_re = nrm(ks[8], (DEPTH, SSM_GROUPS, SSM_GROUP, SSM_STATE), f32) * SSM_STATE ** -0.5
    c_im = nrm(ks[9], (DEPTH, SSM_GROUPS, SSM_GROUP, SSM_STATE), f32) * SSM_STATE ** -0.5
    d_skip = nrm(ks[10], (DEPTH, SSM_WIDTH), f32)
    w_glu = nrm(ks[11], (DEPTH, SSM_WIDTH, 2 * SSM_WIDTH), f32) * SSM_WIDTH ** -0.5
    lambda_q1 = 0.1 * nrm(ks[12], (DEPTH, DIFF_HEAD_DIM), f32)
    lambda_k1 = 0.1 * nrm(ks[13], (DEPTH, DIFF_HEAD_DIM), f32)
    lambda_q2 = 0.1 * nrm(ks[14], (DEPTH, DIFF_HEAD_DIM), f32)
    lambda_k2 = 0.1 * nrm(ks[15], (DEPTH, DIFF_HEAD_DIM), f32)
    subln_w = 1.0 + 0.01 * nrm(ks[16], (DEPTH, 2 * DIFF_HEAD_DIM), f32)
    rel_bias = 0.5 * nrm(ks[17], (REL_BUCKETS, DIFF_HEADS), f32)
    w_mem_kv = nrm(ks[18], (DEPTH, D_MODEL, 2 * MEM_WIDTH), f32) * D_MODEL ** -0.5
    w_mem_kv = w_mem_kv.at[:, :, MEM_WIDTH:].multiply(DEEPNORM_BETA)
    w_br_ssm = nrm(ks[19], (DEPTH, SSM_WIDTH, D_MODEL), f32) * SSM_WIDTH ** -0.5 * DEEPNORM_BETA
    w_br_diff = nrm(ks[20], (DEPTH, DIFF_V, D_MODEL), f32) * DIFF_V ** -0.5 * DEEPNORM_BETA
    w_br_mem = nrm(ks[21], (DEPTH, MEM_WIDTH, D_MODEL), f32) * MEM_WIDTH ** -0.5 * DEEPNORM_BETA
    w_out = nrm(ks[22], (DEPTH, D_MODEL, D_MODEL), f32) * D_MODEL ** -0.5 * DEEPNORM_BETA
    ln_g = 1.0 + 0.01 * nrm(ks[23], (DEPTH, D_MODEL), f32)
    ln_b = 0.01 * nrm(ks[24], (DEPTH, D_MODEL), f32)
    return {"x": x, "mem": mem, "w_in": w_in, "lam_re": lam_re, "lam_im": lam_im, "log_dt": log_dt,
            "b_re": b_re, "b_im": b_im, "c_re": c_re, "c_im": c_im, "d_skip": d_skip, "w_glu": w_glu,
            "lambda_q1": lambda_q1, "lambda_k1": lambda_k1, "lambda_q2": lambda_q2, "lambda_k2": lambda_k2,
            "subln_w": subln_w, "rel_bias": rel_bias, "w_mem_kv": w_mem_kv, "w_br_ssm": w_br_ssm,
            "w_br_diff": w_br_diff, "w_br_mem": w_br_mem, "w_out": w_out, "ln_g": ln_g, "ln_b": ln_b}


def reference(x, mem, w_in, lam_re, lam_im, log_dt, b_re, b_im, c_re, c_im, d_skip, w_glu,
              lambda_q1, lambda_k1, lambda_q2, lambda_k2, subln_w, rel_bias, w_mem_kv,
              w_br_ssm, w_br_diff, w_br_mem, w_out, ln_g, ln_b):
    bsz, seq, _ = x.shape
    pts = _split_points()
    h = x
    for layer in range(DEPTH):
        lambda_init = 0.8 - 0.6 * math.exp(-0.3 * layer)
        proj = h @ w_in[layer]
        u, z_ssm, dq, dk, dv, z_diff, mq, z_mem, gates = jnp.split(proj, pts, axis=-1)
        y_ssm = _s5_branch(u, lam_re[layer], lam_im[layer], log_dt[layer], b_re[layer], b_im[layer],
                           c_re[layer], c_im[layer], d_skip[layer], w_glu[layer])
        y_ssm = (y_ssm * jax.nn.silu(z_ssm)).astype(h.dtype)
        y_diff = _diff_attention(dq, dk, dv, lambda_q1[layer], lambda_k1[layer], lambda_q2[layer],
                                 lambda_k2[layer], subln_w[layer], rel_bias, lambda_init)
        y_diff = (y_diff * jax.nn.silu(z_diff.astype(jnp.float32))).astype(h.dtype)
        mk, mv = jnp.split(mem @ w_mem_kv[layer], 2, axis=-1)
        y_mem = (_memory_attention(mq, mk, mv) * jax.nn.silu(z_mem)).astype(h.dtype)
        g = jax.nn.sigmoid(gates.astype(jnp.float32)).reshape(bsz, seq, N_BRANCH, D_MODEL)
        merged = (g[:, :, 0] * (y_ssm @ w_br_ssm[layer])
                  + g[:, :, 1] * (y_diff @ w_br_diff[layer])
                  + g[:, :, 2] * (y_mem @ w_br_mem[layer]))
        out = merged.astype(h.dtype) @ w_out[layer]
        h = _layer_norm(DEEPNORM_ALPHA * h + out, ln_g[layer], ln_b[layer])
    return h
```

```python
import math
from contextlib import ExitStack
import numpy as np
import concourse.bass as bass
import concourse.mybir as mybir
from concourse.bass_utils import run_bass_kernel_spmd

F32 = mybir.dt.float32
BF16 = mybir.dt.bfloat16
I32 = mybir.dt.int32
AF = mybir.ActivationFunctionType
ALU = mybir.AluOpType
AX = mybir.AxisListType

S = 4096
D = 1024
NT = 8
L = 16
NB = S // L
ALPHA = 2.0 ** 0.25
LAMBDA_INIT = 0.8 - 0.6 * math.exp(0.0)
PI = float(np.pi)


class Buf:
    __slots__ = ("name", "w", "r")

    def __init__(self, name):
        self.name = name
        self.w = None
        self.r = {}


class Eng:
    def __init__(self, name, eng, sem):
        self.name, self.eng, self.sem = name, eng, sem
        self.cnt = 0
        self.seen = {}


class FW:
    def __init__(self, nc, stack):
        self.nc, self.stack = nc, stack
        self.sems, self.engs, self.cnts = {}, {}, {}
        for name, eng in (("pe", nc.tensor), ("act", nc.scalar), ("dve", nc.vector),
                          ("pool", nc.gpsimd), ("sp", nc.sync)):
            sem = stack.enter_context(nc.semaphore("sem_" + name))
            self.sems[name] = sem
            self.engs[name] = Eng(name, eng, sem)
            self.cnts[name] = 0

    def dsem(self, name):
        self.sems[name] = self.stack.enter_context(self.nc.semaphore("d_" + name))
        self.cnts[name] = 0
        return name

    def _wait(self, e, key, val):
        if key == "pe" and e.name == "pe":
            return
        if e.seen.get(key, 0) >= val:
            return
        e.seen[key] = val
        e.eng.wait_ge(self.sems[key], val)

    def _deps(self, e, reads, writes):
        for b in reads:
            if b.w is not None:
                self._wait(e, *b.w)
        for b in writes:
            if b.w is not None:
                self._wait(e, *b.w)
            for k, v in b.r.items():
                self._wait(e, k, v)

    def _mark(self, tag, reads, writes):
        for b in reads:
            if b.r.get(tag[0], 0) < tag[1]:
                b.r[tag[0]] = tag[1]
        for b in writes:
            b.w = tag
            b.r = {}

    def op(self, en, fn, reads=(), writes=()):
        e = self.engs[en]
        self._deps(e, reads, writes)
        ins = fn(e.eng)
        self.cnts[en] += 1
        ins.then_inc(e.sem, 1)
        self._mark((en, self.cnts[en]), reads, writes)

    def dma(self, q, dsem, out, in_, reads=(), writes=()):
        e = self.engs[q]
        self._deps(e, reads, writes)
        ins = e.eng.dma_start(out=out, in_=in_)
        self.cnts[dsem] += 16
        ins.then_inc(self.sems[dsem], 16)
        self._mark((dsem, self.cnts[dsem]), reads, writes)

    def barrier(self, only=None, skip=()):
        for e in self.engs.values():
            if only is not None and e.name not in only:
                continue
            for k, v in self.cnts.items():
                if v > 0 and not k.startswith("cv_") and k not in skip:
                    self._wait(e, k, v)


def _t5_bucket_np(rel):
    ret = np.where(rel > 0, 16, 0)
    n = np.abs(rel)
    v = (np.log(np.maximum(n, 1).astype(np.float32) / np.float32(8)) / np.float32(math.log(16))
         * np.float32(8))
    large = np.minimum(8 + v.astype(np.int32), 15)
    return ret + np.where(n < 8, n, large)


def _shared_layouts(inp):
    f = np.float32
    o = {}
    lam_re, lam_im, log_dt = inp["lam_re"][0], inp["lam_im"][0], inp["log_dt"][0]
    b_re, b_im, c_re, c_im = inp["b_re"][0], inp["b_im"][0], inp["c_re"][0], inp["c_im"][0]
    def layA(v):
        return np.ascontiguousarray(v.reshape(16, 2, 64).transpose(1, 2, 0).reshape(128, 16)).astype(f)
    o["lamA"] = np.stack([layA(lam_re), layA(lam_im),
                          layA(np.broadcast_to(log_dt[:, None], (32, 64)))], axis=1)
    cbd = np.zeros((2, 2, 64, 16, 2, 16), f)
    bbd = np.zeros((2, 2, 64, 16, 2, 16), f)
    for ri, (cc, bb) in enumerate(((c_re, b_re), (c_im, b_im))):
        c4 = cc.reshape(16, 2, 16, 64)
        b4 = bb.reshape(16, 2, 64, 16)
        for g2 in range(2):
            cbd[ri, g2, :, :, g2, :] = c4[:, g2].transpose(2, 0, 1)
            bbd[ri, g2, :, :, g2, :] = b4[:, g2].transpose(1, 0, 2)
    o["cbdA"] = np.ascontiguousarray(cbd.reshape(2, 128, 16, 32).transpose(1, 0, 2, 3))
    o["bbdA"] = np.ascontiguousarray(bbd.reshape(2, 128, 16, 32).transpose(1, 0, 2, 3))
    def layB(v):
        a = v.reshape(4, 4, 2, 64).transpose(1, 0, 2, 3)
        a = np.broadcast_to(a[:, None], (4, 32, 4, 2, 64))
        return np.ascontiguousarray(a.reshape(128, 4, 128)).astype(f)
    o["lamB"] = np.stack([layB(lam_re), layB(lam_im),
                          layB(np.broadcast_to(log_dt[:, None], (32, 64)))], axis=1)
    btb = np.zeros((2, 4, 2, 16, 4, 2, 64), f)
    for ri, bb in enumerate((b_re, b_im)):
        b5 = bb.reshape(4, 4, 2, 64, 16)
        for g2 in range(2):
            btb[ri, :, g2, :, :, g2, :] = b5[:, :, g2].transpose(1, 3, 0, 2)
    o["btB"] = np.ascontiguousarray(btb.reshape(2, 128, 4, 128).transpose(1, 0, 2, 3))
    dsk = np.zeros((128, 4, 128), f)
    ds = inp["d_skip"][0].reshape(4, 128)
    for t in range(4):
        dsk[np.arange(128), t, np.arange(128)] = ds[t]
    o["dskD"] = dsk
    kl = np.arange(128)[:, None]
    qr = np.arange(256)[None, :]
    bidx = _t5_bucket_np(kl - qr)
    rb = inp["rel_bias"]
    o["tbias"] = np.ascontiguousarray(rb[bidx].transpose(0, 2, 1)).astype(f)
    maskc = np.where((kl >= 64) & (qr < 64), -30000.0, 0.0).astype(f)
    o["maskc"] = np.ascontiguousarray(np.broadcast_to(maskc[:, None, :], (128, 4, 256)))
    o["farb"] = np.ascontiguousarray(np.broadcast_to(rb[15][None, :], (128, 4))).astype(f)
    o["lamv"] = np.ascontiguousarray(np.broadcast_to(
        np.stack([inp["lambda_q1"][0], inp["lambda_k1"][0], inp["lambda_q2"][0], inp["lambda_k2"][0]])[None],
        (128, 4, 64))).astype(f)
    o["subw"] = np.ascontiguousarray(np.broadcast_to(inp["subln_w"][0][None], (128, 128))).astype(f)
    o["lng"] = np.ascontiguousarray(np.broadcast_to(inp["ln_g"][0][None], (128, D))).astype(f)
    o["lnb"] = np.ascontiguousarray(np.broadcast_to(inp["ln_b"][0][None], (128, D))).astype(f)
    o["ident"] = np.eye(128, dtype=f)
    o["w_in"] = np.ascontiguousarray(inp["w_in"][0])
    o["w_glu"] = np.ascontiguousarray(inp["w_glu"][0])
    o["w_mem_kv"] = np.ascontiguousarray(inp["w_mem_kv"][0])
    o["w_br"] = np.ascontiguousarray(np.stack([inp["w_br_ssm"][0], inp["w_br_diff"][0], inp["w_br_mem"][0]]))
    o["w_out"] = np.ascontiguousarray(inp["w_out"][0])
    return o


IN_SHAPES = {
    "xT": [D, S], "x": [S, D], "memT": [D, 256],
    "lamA": [128, 3, 16], "cbdA": [128, 2, 16, 32], "bbdA": [128, 2, 16, 32],
    "lamB": [128, 3, 4, 128], "btB": [128, 2, 4, 128], "dskD": [128, 4, 128],
    "tbias": [128, 4, 256], "maskc": [128, 4, 256], "farb": [128, 4], "lamv": [128, 4, 64],
    "subw": [128, 128], "lng": [128, D], "lnb": [128, D], "ident": [128, 128],
    "w_in": [D, 7168], "w_glu": [512, 1024], "w_mem_kv": [D, 1024], "w_br": [3, 512, 1024],
    "w_out": [D, D],
}


def build(stop_after=None, dumps=()):
    nc = bass.Bass("TRN2", target_bir_lowering=False)
    I = {k: nc.dram_tensor(k, v, F32, kind="ExternalInput").ap() for k, v in IN_SHAPES.items()}
    out_d = nc.dram_tensor("out", [S, D], F32, kind="ExternalOutput").ap()
    dump_d = {}
    xTb = nc.dram_tensor("xTb", [NT, D, 512], BF16).ap()
    wInb = nc.dram_tensor("wInb", [14, D, 512], BF16).ap()
    wGb = nc.dram_tensor("wGb", [2, 512, 512], BF16).ap()
    wMb = nc.dram_tensor("wMb", [2, D, 512], BF16).ap()
    wBb = nc.dram_tensor("wBb", [3, 2, 512, 512], BF16).ap()
    wOb = nc.dram_tensor("wOb", [2, D, 512], BF16).ap()
    memTb = nc.dram_tensor("memTb", [D, 256], BF16).ap()

    with ExitStack() as top:
        fw = FW(nc, top)
        op, dma = fw.op, fw.dma

        def sb(st, name, shape, dt):
            return st.enter_context(nc.sbuf_tensor("s_" + name, list(shape), dt))

        def dump(name, ap_sb, shape, buf):
            d = nc.dram_tensor("dump_" + name, list(shape), ap_sb.dtype, kind="ExternalOutput").ap()
            dump_d[name] = d
            dma("sp", fw.dsem("dump_" + name), d, ap_sb, reads=[buf])

        psA = top.enter_context(nc.psum_tensor("psA", [128, 1024], F32))
        psB = top.enter_context(nc.psum_tensor("psB", [128, 1024], F32))
        ps = [psA[:, 0:512], psA[:, 512:1024], psB[:, 0:512], psB[:, 512:1024]] + \
             [top.enter_context(nc.psum_tensor("ps%d" % i, [128, 512], F32)) for i in range(4, 8)]
        PS = [Buf("ps%d" % i) for i in range(8)]

        Bx = [Buf("xTb%d" % t) for t in range(NT)]
        Bw = [Buf("wInb%d" % j) for j in range(14)]
        Bg = [Buf("wGb%d" % j) for j in range(2)]
        Bm = [Buf("wMb%d" % j) for j in range(2)]
        Bb = [[Buf("wBb%d%d" % (b, j)) for j in range(2)] for b in range(3)]
        Bo = [Buf("wOb%d" % j) for j in range(2)]
        Bmem = Buf("memTb")

        def conv(dst, src, buf):
            dma("pool", fw.dsem("cv_" + buf.name), dst, src, writes=[buf])

        conv(wInb[0], I["w_in"][:, 0:512], Bw[0])
        conv(wInb[1], I["w_in"][:, 512:1024], Bw[1])
        for t in range(NT):
            conv(xTb[t], I["xT"][:, t * 512:(t + 1) * 512], Bx[t])
        for j in range(2):
            conv(wGb[j], I["w_glu"][:, j * 512:(j + 1) * 512], Bg[j])
        for j in (3, 4):
            conv(wInb[j], I["w_in"][:, j * 512:(j + 1) * 512], Bw[j])
        for j in range(2):
            conv(wMb[j], I["w_mem_kv"][:, j * 512:(j + 1) * 512], Bm[j])
        conv(memTb, I["memT"], Bmem)
        for j in (2, 5, 6, 7, 8, 9, 10, 11, 12, 13):
            conv(wInb[j], I["w_in"][:, j * 512:(j + 1) * 512], Bw[j])
        for b in range(3):
            for j in range(2):
                conv(wBb[b, j], I["w_br"][b, :, j * 512:(j + 1) * 512], Bb[b][j])
        for j in range(2):
            conv(wOb[j], I["w_out"][:, j * 512:(j + 1) * 512], Bo[j])

        def wsrc(piece, nkc):
            return piece.rearrange("(kc p) n -> p kc n", p=128)

        ident_f = sb(top, "ident_f", [128, 128], F32)
        ident = sb(top, "ident", [128, 128], BF16)
        Bid = Buf("ident")
        dma("sp", fw.dsem("c_id"), ident_f[:], I["ident"], writes=[Bid])
        op("dve", lambda e: e.tensor_copy(out=ident[:], in_=ident_f[:]), reads=[Bid], writes=[Bid])

        ZS = sb(top, "ZS", [128, 4, S], BF16)
        BZSn = Buf("ZSn")

        with ExitStack() as phS:
            UT = sb(phS, "UT", [128, 4, S], BF16)
            BUTp = [[Buf("UT%d_%d" % (i, j)) for j in range(8)] for i in range(4)]
            BUTall = [b for bl in BUTp for b in bl]
            ZSp = sb(phS, "ZSp", [128, 4, S], BF16)
            BZSp = Buf("ZSp")

            with ExitStack() as ph2:
                def cplx_setup(st, pref, lam_ap, shape, Bsrc, keep):
                    T = {}
                    for nm in ("dt", "th", "mg", "cs", "sn", "t1", "t2", "den"):
                        T[nm] = sb(st, pref + nm, shape, F32)
                    T.update(keep)
                    ti = sb(st, pref + "ti", shape, I32)
                    Bt = keep["buf"]
                    R, W = [Bsrc, Bt], [Bt]
                    lre, lim, ldt = lam_ap(0), lam_ap(1), lam_ap(2)
                    op("act", lambda e: e.activation(out=T["dt"][:], in_=ldt, func=AF.Exp), reads=R, writes=W)
                    op("dve", lambda e: e.tensor_tensor(out=T["th"][:], in0=lim, in1=T["dt"][:], op=ALU.mult), reads=R, writes=W)
                    op("dve", lambda e: e.tensor_tensor(out=T["t0"][:], in0=lre, in1=T["dt"][:], op=ALU.mult), reads=R, writes=W)
                    op("act", lambda e: e.activation(out=T["mg"][:], in_=T["t0"][:], func=AF.Exp), reads=R, writes=W)

                    def sin_of(dst, shift):
                        op("dve", lambda e: e.tensor_scalar(out=T["t0"][:], in0=T["th"][:], scalar1=shift, scalar2=None, op0=ALU.add), reads=R, writes=W)
                        op("dve", lambda e: e.tensor_scalar(out=T["t1"][:], in0=T["t0"][:], scalar1=1.0 / (2 * PI), scalar2=None, op0=ALU.mult), reads=R, writes=W)
                        op("dve", lambda e: e.tensor_copy(out=ti[:], in_=T["t1"][:]), reads=R, writes=W)
                        op("dve", lambda e: e.tensor_copy(out=T["t1"][:], in_=ti[:]), reads=R, writes=W)
                        op("dve", lambda e: e.scalar_tensor_tensor(out=T["t2"][:], in0=T["t1"][:], scalar=-2 * PI, in1=T["t0"][:], op0=ALU.mult, op1=ALU.add), reads=R, writes=W)
                        op("dve", lambda e: e.tensor_scalar(out=T["t1"][:], in0=T["t2"][:], scalar1=PI, scalar2=-2 * PI, op0=ALU.is_gt, op1=ALU.mult), reads=R, writes=W)
                        op("dve", lambda e: e.tensor_tensor(out=T["t2"][:], in0=T["t2"][:], in1=T["t1"][:], op=ALU.add), reads=R, writes=W)
                        op("dve", lambda e: e.tensor_scalar(out=T["t1"][:], in0=T["t2"][:], scalar1=-PI, scalar2=2 * PI, op0=ALU.is_lt, op1=ALU.mult), reads=R, writes=W)
                        op("dve", lambda e: e.tensor_tensor(out=T["t2"][:], in0=T["t2"][:], in1=T["t1"][:], op=ALU.add), reads=R, writes=W)
                        op("act", lambda e: e.activation(out=dst[:], in_=T["t2"][:], func=AF.Sin), reads=R, writes=W)
                    sin_of(T["sn"], 0.0)
                    sin_of(T["cs"], PI / 2)
                    op("dve", lambda e: e.tensor_tensor(out=T["lbr"][:], in0=T["mg"][:], in1=T["cs"][:], op=ALU.mult), reads=R, writes=W)
                    op("dve", lambda e: e.tensor_tensor(out=T["lbi"][:], in0=T["mg"][:], in1=T["sn"][:], op=ALU.mult), reads=R, writes=W)
                    op("dve", lambda e: e.tensor_tensor(out=T["den"][:], in0=lre, in1=lre, op=ALU.mult), reads=R, writes=W)
                    op("dve", lambda e: e.tensor_tensor(out=T["t0"][:], in0=lim, in1=lim, op=ALU.mult), reads=R, writes=W)
                    op("dve", lambda e: e.tensor_tensor(out=T["den"][:], in0=T["den"][:], in1=T["t0"][:], op=ALU.add), reads=R, writes=W)
                    op("dve", lambda e: e.reciprocal(out=T["den"][:], in_=T["den"][:]), reads=R, writes=W)
                    op("dve", lambda e: e.tensor_scalar(out=T["t0"][:], in0=T["lbr"][:], scalar1=-1.0, scalar2=None, op0=ALU.add), reads=R, writes=W)
                    op("dve", lambda e: e.tensor_tensor(out=T["t1"][:], in0=T["t0"][:], in1=lre, op=ALU.mult), reads=R, writes=W)
                    op("dve", lambda e: e.tensor_tensor(out=T["t2"][:], in0=T["lbi"][:], in1=lim, op=ALU.mult), reads=R, writes=W)
                    op("dve", lambda e: e.tensor_tensor(out=T["t1"][:], in0=T["t1"][:], in1=T["t2"][:], op=ALU.add), reads=R, writes=W)
                    op("dve", lambda e: e.tensor_tensor(out=T["cfr"][:], in0=T["t1"][:], in1=T["den"][:], op=ALU.mult), reads=R, writes=W)
                    op("dve", lambda e: e.tensor_tensor(out=T["t1"][:], in0=T["lbi"][:], in1=lre, op=ALU.mult), reads=R, writes=W)
                    op("dve", lambda e: e.tensor_tensor(out=T["t2"][:], in0=T["t0"][:], in1=lim, op=ALU.mult), reads=R, writes=W)
                    op("dve", lambda e: e.tensor_tensor(out=T["t1"][:], in0=T["t1"][:], in1=T["t2"][:], op=ALU.subtract), reads=R, writes=W)
                    op("dve", lambda e: e.tensor_tensor(out=T["cfi"][:], in0=T["t1"][:], in1=T["den"][:], op=ALU.mult), reads=R, writes=W)
                    return T, Bt

                def cmul(out_r, out_i, ar, ai, br, bi, t0, R, W, en="dve"):
                    op(en, lambda e: e.tensor_tensor(out=out_r, in0=ar, in1=br, op=ALU.mult), reads=R, writes=W)
                    op(en, lambda e: e.tensor_tensor(out=t0, in0=ai, in1=bi, op=ALU.mult), reads=R, writes=W)
                    op(en, lambda e: e.tensor_tensor(out=out_r, in0=out_r, in1=t0, op=ALU.subtract), reads=R, writes=W)
                    op(en, lambda e: e.tensor_tensor(out=out_i, in0=ar, in1=bi, op=ALU.mult), reads=R, writes=W)
                    op(en, lambda e: e.tensor_tensor(out=t0, in0=ai, in1=br, op=ALU.mult), reads=R, writes=W)
                    op(en, lambda e: e.tensor_tensor(out=out_i, in0=out_i, in1=t0, op=ALU.add), reads=R, writes=W)

                lamA = sb(ph2, "lamA", [128, 3, 16], F32)
                cbdA = sb(ph2, "cbdA", [128, 2, 16, 32], F32)
                keepA = {nm: sb(ph2, "A_" + nm, [128, 16], F32) for nm in ("lbr", "lbi", "cfr", "cfi", "t0")}
                keepA["buf"] = Buf("A_tmp")
                PW = sb(ph2, "PW", [128, 17, 2, 16], F32)
                AD = sb(ph2, "AD", [128, 8, 3, 16], F32)
                PWn = sb(ph2, "PWn", [128, 17, 16], F32)
                BbA = sb(ph2, "BbA", [128, 2, 16, 32], BF16)
                dskD = sb(ph2, "dskD", [128, 4, 128], F32)
                keepB = {nm: sb(ph2, "B_" + nm, [128, 512], F32) for nm in ("lbr", "lbi", "cfr", "cfi", "t0")}
                keepB["buf"] = Buf("B_tmp")
                XB0 = sb(ph2, "XB0", [128, 2, 512], F32)
                FA2 = [sb(ph2, "FA%d" % i, [128, 17, 2, 4, 32], BF16) for i in range(2)]
                BFA2 = [Buf("FA%d" % i) for i in range(2)]
                tP = [sb(ph2, "tP%d" % i, [128, 4, 32], F32) for i in range(3)]
                BtP = Buf("tP")
                BA, BB = Buf("layA"), Buf("layB")
                dsA, dsB = fw.dsem("layA"), fw.dsem("layB")
                dma("sp", dsA, lamA[:], I["lamA"], writes=[BA])
                dma("sp", dsA, cbdA[:], I["cbdA"], writes=[BA])
                dma("sp", dsB, dskD[:], I["dskD"], writes=[BB])

                def bc(ap2, n=16):
                    return ap2.unsqueeze(2).to_broadcast([128, n, 32])

                def setup_FA(T_):
                    par = T_ % 2
                    FA, BFA = FA2[par], BFA2[par]
                    Ps = slice(4 * T_, 4 * T_ + 4)
                    R_ = [BA, BtA, BtP, BFA]
                    cr, ci = cbdA[:, 0, Ps], cbdA[:, 1, Ps]
                    for a in range(17):
                        pr, pi_ = bc(PW[:, a, 0, Ps], 4), bc(PW[:, a, 1, Ps], 4)
                        op("pool", lambda e: e.tensor_tensor(out=tP[0][:], in0=cr, in1=pr, op=ALU.mult), reads=R_, writes=[BtP])
                        op("pool", lambda e: e.tensor_tensor(out=tP[1][:], in0=ci, in1=pi_, op=ALU.mult), reads=R_, writes=[BtP])
                        op("pool", lambda e, a=a: e.tensor_tensor(out=FA[:, a, 0], in0=tP[0][:], in1=tP[1][:], op=ALU.subtract), reads=R_, writes=[BFA, BtP])
                        npi = bc(PWn[:, a, Ps], 4)
                        op("pool", lambda e: e.tensor_tensor(out=tP[0][:], in0=cr, in1=npi, op=ALU.mult), reads=R_, writes=[BtP])
                        op("pool", lambda e: e.tensor_tensor(out=tP[1][:], in0=ci, in1=pr, op=ALU.mult), reads=R_, writes=[BtP])
                        op("pool", lambda e, a=a: e.tensor_tensor(out=FA[:, a, 1], in0=tP[0][:], in1=tP[1][:], op=ALU.subtract),
                           reads=R_, writes=[BFA, BtP])

                ph1 = ExitStack()
                wS = sb(ph1, "wS", [128, 8, 1024], BF16)
                BwS = Buf("wS")
                ds_w = fw.dsem("wS")
                for j in range(2):
                    dma("sp", ds_w, wS[:, :, j * 512:(j + 1) * 512], wsrc(wInb[j], 8), reads=[Bw[j]], writes=[BwS])
                xt1 = sb(ph1, "xtS", [128, 8, 512], BF16)
                Bxt1 = Buf("xtS")
                dsx1 = fw.dsem("xtS")

                def load_x(t):
                    dma("sp", dsx1, xt1[:], wsrc(xTb[t], 8), reads=[Bx[t]], writes=[Bxt1])
                load_x(0)

                with ExitStack() as tmpS:
                    bbdA = sb(tmpS, "bbdA", [128, 2, 16, 32], F32)
                    lamB = sb(tmpS, "lamB", [128, 3, 512], F32)
                    btB = sb(tmpS, "btB", [128, 2, 512], F32)
                    dma("sp", dsA, bbdA[:], I["bbdA"], writes=[BA])
                    dma("sp", dsB, lamB[:], I["lamB"].rearrange("p a t n -> p a (t n)"), writes=[BB])
                    dma("sp", dsB, btB[:], I["btB"].rearrange("p a t n -> p a (t n)"), writes=[BB])
                    TA, BtA = cplx_setup(tmpS, "A_", lambda i: lamA[:, i, :], [128, 16], BA, keepA)
                    TA = keepA
                    RA, WA_ = [BA, BtA], [BtA]
                    op("dve", lambda e: e.memset(PW[:, 0, 0, :], 1.0), reads=RA, writes=WA_)
                    op("dve", lambda e: e.memset(PW[:, 0, 1, :], 0.0), reads=RA, writes=WA_)
                    for a in range(1, 17):
                        cmul(PW[:, a, 0, :], PW[:, a, 1, :], PW[:, a - 1, 0, :], PW[:, a - 1, 1, :],
                             TA["lbr"][:], TA["lbi"][:], TA["t0"][:], RA, WA_)
                    op("dve", lambda e: e.tensor_copy(out=AD[:, 0, 0:2, :], in_=PW[:, 16, :, :]), reads=RA, writes=WA_)
                    op("dve", lambda e: e.tensor_scalar(out=PWn[:], in0=PW[:, :, 1, :], scalar1=-1.0, scalar2=None, op0=ALU.mult), reads=RA, writes=WA_)
                    for k in range(1, 8):
                        cmul(AD[:, k, 0, :], AD[:, k, 1, :], AD[:, k - 1, 0, :], AD[:, k - 1, 1, :],
                             AD[:, k - 1, 0, :], AD[:, k - 1, 1, :], TA["t0"][:], RA, WA_)
                    op("dve", lambda e: e.tensor_scalar(out=AD[:, :, 2, :], in0=AD[:, :, 1, :], scalar1=-1.0, scalar2=None, op0=ALU.mult), reads=RA, writes=WA_)
                    setup_FA(0)
                    setup_FA(1)
                    TBt, BtB_ = cplx_setup(tmpS, "B_", lambda i: lamB[:, i, :], [128, 512], BB, keepB)
                    TB = keepB
                    RB, WB_ = [BB, BtB_], [BtB_]
                    cmul(XB0[:, 0], XB0[:, 1], btB[:, 0], btB[:, 1], TB["cfr"][:], TB["cfi"][:], TB["t0"][:], RB, WB_)
                    tA = [TBt[nm][:].rearrange("p (a b) -> p a b", a=16) for nm in ("t1", "t2", "den")]
                    RAB, WAB = [BA, BtA, BtB_], [BtA, BtB_]
                    cmul(tA[0], tA[1], bbdA[:, 0], bbdA[:, 1], bc(TA["cfr"][:]), bc(TA["cfi"][:]), tA[2], RAB, WAB)
                    op("dve", lambda e: e.tensor_copy(out=BbA[:, 0], in_=tA[0]), reads=RAB, writes=WAB)
                    op("dve", lambda e: e.tensor_copy(out=BbA[:, 1], in_=tA[1]), reads=RAB, writes=WAB)

                    for t in range(NT):
                        x_t, Bx_t = xt1, Bxt1
                        for m in range(8):
                            bk = m % 8
                            for kc in range(8):
                                op("pe", lambda e, m=m, kc=kc, bk=bk: e.matmul(
                                    ps[bk][:], wS[:, kc, m * 128:(m + 1) * 128], x_t[:, kc, :],
                                    start=(kc == 0), stop=(kc == 7)),
                                   reads=[BwS, Bx_t], writes=[PS[bk]])
                            if m < 4:
                                op("act", lambda e, m=m, bk=bk: e.activation(func=AF.Copy,
                                    out=UT[:, m, :].rearrange("p (i c) -> p i c", i=L)[:, :, 32 * t:32 * t + 32],
                                    in_=ps[bk][:].rearrange("p (c i) -> p i c", i=L)),
                                   reads=[PS[bk]], writes=BUTp[m])
                            else:
                                op("act", lambda e, m=m, bk=bk: e.activation(
                                    out=ZSp[:, m - 4, :].rearrange("p (i c) -> p i c", i=L)[:, :, 32 * t:32 * t + 32],
                                    in_=ps[bk][:].rearrange("p (c i) -> p i c", i=L), func=AF.Silu),
                                   reads=[PS[bk]], writes=[BZSp])
                        if t + 1 < NT:
                            load_x(t + 1)
                    fw.barrier(skip=("pool",))
                ph1.close()
                fw.barrier(skip=("pool",))
                if "UT" in dumps:
                    dump("UT", UT[:], [128, 4, S], BUTp[3][0])
                    dump("ZS", ZSp[:], [128, 4, S], BZSp)
                EB = sb(ph2, "EB", [128, 16, 2, 128], BF16)
                KB2 = [sb(ph2, "KB%d" % i, [128, 16, 128], BF16) for i in range(2)]
                Wa = sb(ph2, "Wa", [128, 4, 2, NB], F32)
                Wb = sb(ph2, "Wb", [128, 4, 2, NB], F32)
                SP2 = [sb(ph2, "SP%d" % i, [128, 4, 2, NB], BF16) for i in range(2)]
                BEB = Buf("EB")
                BKB2 = [Buf("KB%d" % i) for i in range(2)]
                BWa, BWb = Buf("Wa"), Buf("Wb")
                BSP2 = [Buf("SP%d" % i) for i in range(2)]
                KF0 = sb(ph2, "KF0", [128, 128], F32)
                BKF = Buf("KF0")
                XB = [sb(ph2, "XBp%d" % i, [128, 2, 128], F32) for i in range(2)]
                BXB = Buf("XBp")
                gel = [sb(ph2, "gel%d" % i, [128, 512], F32) for i in range(2)]
                Bgel = Buf("gel")
                def eb_ops(T_):
                    cs = slice(T_ * 128, (T_ + 1) * 128)
                    RE = [BB, BtB_, BEB, BXB]
                    th = []
                    for a in range(16):
                        cur_r, cur_i = (XB0[:, 0, cs], XB0[:, 1, cs]) if a == 0 else (XB[a % 2][:, 0], XB[a % 2][:, 1])
                        th.append(lambda a=a, cur_r=cur_r: op("dve", lambda e: e.tensor_copy(out=EB[:, a, 0, :], in_=cur_r), reads=RE, writes=[BEB, BXB]))
                        th.append(lambda a=a, cur_i=cur_i: op("dve", lambda e: e.tensor_copy(out=EB[:, a, 1, :], in_=cur_i), reads=RE, writes=[BEB, BXB]))
                        if a < 15:
                            nxt = XB[(a + 1) % 2]
                            th.append(lambda nxt=nxt, cur_r=cur_r, cur_i=cur_i: cmul(
                                nxt[:, 0], nxt[:, 1], cur_r, cur_i, TB["lbr"][:, cs], TB["lbi"][:, cs], TB["t0"][:, cs], RE, [BXB]))
                    return th

                def emit_K(T_):
                    par = T_ % 2
                    FA, BFA, KB, BKB = FA2[par], BFA2[par], KB2[par], BKB2[par]
                    bk = 6 + (T_ % 2)
                    pk = ps[bk][:].rearrange("p (l n) -> p l n", l=16)
                    for lag in range(16):
                        for ri in range(2):
                            for m in range(4):
                                op("pe", lambda e, lag=lag, m=m, ri=ri: e.matmul(
                                    pk[32 * m:32 * m + 32, lag, :], BbA[:, ri, 4 * T_ + m, :], FA[:, lag, ri, m, :],
                                    start=(ri == 0), stop=(ri == 1), tile_position=(0, 32 * m), skip_group_check=True),
                                   reads=[BtA, BFA], writes=[PS[bk]])
                    op("dve", lambda e: e.memset(KB[:], 0.0), reads=[], writes=[BKB])
                    op("dve", lambda e: e.tensor_copy(out=KF0[:], in_=dskD[:, T_, :]), reads=[BB, BKF], writes=[BKF])
                    for m in range(4):
                        sl = slice(32 * m, 32 * m + 32)
                        op("dve", lambda e, sl=sl: e.tensor_copy(out=KB[sl, 1:16, sl], in_=pk[sl, 1:16, :]), reads=[PS[bk]], writes=[BKB])
                        op("dve", lambda e, sl=sl: e.tensor_tensor(out=KF0[sl, sl], in0=KF0[sl, sl], in1=pk[sl, 0, :], op=ALU.add), reads=[PS[bk], BKF], writes=[BKF])
                    op("dve", lambda e: e.tensor_copy(out=KB[:, 0, :], in_=KF0[:]), reads=[BKF, BKB], writes=[BKB])

                def emit_L1(T_):
                    UTv = UT[:, T_, :].rearrange("p (i c) -> p i c", i=L)
                    pws = [ps[m][:].rearrange("p (r n) -> p r n", r=2) for m in range(4)]
                    for ri in range(2):
                        for j in range(L):
                            for m in range(4):
                                rows = slice(32 * m, 32 * m + 32)
                                op("pe", lambda e, m=m, ri=ri, j=j, rows=rows: e.matmul(
                                    pws[m][:, ri, :], EB[rows, 15 - j, ri, :], UTv[rows, j, :],
                                    start=(j == 0), stop=(j == L - 1), tile_position=(32 * m, 0), skip_group_check=True),
                                   reads=[BEB] + BUTp[T_], writes=[PS[m]])
                    for m in range(4):
                        op("act", lambda e, m=m: e.activation(out=Wa[:, m], in_=pws[m], func=AF.Copy), reads=[PS[m]], writes=[BWa])

                def dbl_ops(T_):
                    par = T_ % 2
                    src, dst, Bs, Bd = Wa, Wb, BWa, BWb
                    SP, BSP = SP2[par], BSP2[par]
                    th = []
                    for k in range(8):
                        d = 1 << k
                        for m in range(4):
                            P_ = 4 * T_ + m
                            aR = AD[:, k, 0, P_:P_ + 1]
                            aI = AD[:, k, 1, P_:P_ + 1]
                            nI = AD[:, k, 2, P_:P_ + 1]
                            R_, W_ = [Bs, Bd, BtA], [Bd]

                            def step(m=m, d=d, aR=aR, aI=aI, nI=nI, src=src, dst=dst, R_=R_, W_=W_):
                                op("dve", lambda e: e.scalar_tensor_tensor(
                                    out=dst[:, m, 0, d:], in0=src[:, m, 0, :NB - d], scalar=aR, in1=src[:, m, 0, d:], op0=ALU.mult, op1=ALU.add), reads=R_, writes=W_)
                                op("dve", lambda e: e.scalar_tensor_tensor(
                                    out=dst[:, m, 0, d:], in0=src[:, m, 1, :NB - d], scalar=nI, in1=dst[:, m, 0, d:], op0=ALU.mult, op1=ALU.add), reads=R_, writes=W_)
                                op("dve", lambda e: e.scalar_tensor_tensor(
                                    out=dst[:, m, 1, d:], in0=src[:, m, 0, :NB - d], scalar=aI, in1=src[:, m, 1, d:], op0=ALU.mult, op1=ALU.add), reads=R_, writes=W_)
                                op("dve", lambda e: e.scalar_tensor_tensor(
                                    out=dst[:, m, 1, d:], in0=src[:, m, 1, :NB - d], scalar=aR, in1=dst[:, m, 1, d:], op0=ALU.mult, op1=ALU.add), reads=R_, writes=W_)
                                op("dve", lambda e: e.tensor_copy(out=dst[:, m, :, :d], in_=src[:, m, :, :d]), reads=R_, writes=W_)
                            th.append(step)
                        src, dst, Bs, Bd = dst, src, Bd, Bs

                    def fin(src=src, Bs=Bs):
                        op("dve", lambda e: e.memset(SP[:, :, :, 0:1], 0.0), reads=[], writes=[BSP])
                        op("dve", lambda e: e.tensor_copy(out=SP[:, :, :, 1:], in_=src[:, :, :, :NB - 1]), reads=[Bs], writes=[BSP])
                    th.append(fin)
                    return th

                def emit_L03(T_, inter):
                    par = T_ % 2
                    FA, BFA, KB, BKB, SP, BSP = FA2[par], BFA2[par], KB2[par], BKB2[par], SP2[par], BSP2[par]
                    UTv = UT[:, T_, :].rearrange("p (i c) -> p i c", i=L)
                    nint = (len(inter) + 7) // 8
                    for n_, ip in enumerate(range(7, -1, -1)):
                        bk = 4 + (ip % 4)
                        py = ps[bk][:].rearrange("p (i n) -> p i n", i=2)
                        i0 = 2 * ip
                        for lag in range(i0 + 2):
                            ist = i0 if lag <= i0 else i0 + 1
                            ni = i0 + 2 - ist
                            op("pe", lambda e, lag=lag, ist=ist, ni=ni, py=py: e.matmul(
                                py[:, ist - i0:ist - i0 + ni, :], KB[:, lag, :], UTv[:, ist - lag:ist - lag + ni, :],
                                start=(lag == 0), stop=False, skip_group_check=True),
                               reads=[BKB] + BUTp[T_][:ip + 1], writes=[PS[bk]])
                        for il in range(2):
                            for ri in range(2):
                                for m in range(4):
                                    last = (m == 3 and il == 1 and ri == 1)
                                    op("pe", lambda e, m=m, il=il, ri=ri, py=py, last=last: e.matmul(
                                        py[32 * m:32 * m + 32, il, :], FA[:, i0 + il + 1, ri, m, :], SP[:, m, ri, :],
                                        start=False, stop=last, tile_position=(0, 32 * m), skip_group_check=True),
                                       reads=[BFA, BSP], writes=[PS[bk]])
                        g0, g1 = gel[0][:], gel[1][:]
                        pyf = ps[bk][:]
                        op("act", lambda e, pyf=pyf: e.activation(out=g0, in_=pyf, func=AF.Square), reads=[PS[bk], Bgel], writes=[Bgel])
                        op("dve", lambda e: e.tensor_scalar(out=g0, in0=g0, scalar1=0.044715, scalar2=1.0, op0=ALU.mult, op1=ALU.add), reads=[Bgel], writes=[Bgel])
                        op("dve", lambda e, pyf=pyf: e.tensor_tensor(out=g0, in0=g0, in1=pyf, op=ALU.mult), reads=[Bgel, PS[bk]], writes=[Bgel])
                        op("act", lambda e: e.activation(out=g1, in_=g0, func=AF.Sigmoid, scale=1.5957691216057308), reads=[Bgel], writes=[Bgel])
                        op("dve", lambda e, py=py, i0=i0: e.tensor_tensor(
                            out=UTv[:, i0:i0 + 2, :], in0=gel[1][:].rearrange("p (i n) -> p i n", i=2), in1=py, op=ALU.mult),
                           reads=[Bgel, PS[bk]], writes=[BUTp[T_][ip]])
                        for f_ in inter[n_ * nint:(n_ + 1) * nint]:
                            f_()

                def run(th):
                    for f_ in th:
                        f_()

                run(eb_ops(0))
                emit_K(0)
                emit_L1(0)
                run(eb_ops(1))
                run(dbl_ops(0))
                emit_K(1)
                emit_L1(1)
                emit_L03(0, dbl_ops(1) + eb_ops(2))
                setup_FA(2)
                emit_K(2)
                emit_L1(2)
                emit_L03(1, dbl_ops(2) + eb_ops(3))
                setup_FA(3)
                emit_K(3)
                emit_L1(3)
                emit_L03(2, dbl_ops(3))
                emit_L03(3, [])
                if "YG" in dumps:
                    dump("YG", UT[:], [128, 4, S], BUTp[3][0])
                fw.barrier()
                ph2.close()
                wG = sb(ph2, "wG", [128, 4, 1024], BF16)
                BwG = Buf("wG")
                dsG = fw.dsem("wG")
                for j in range(2):
                    dma("sp", dsG, wG[:, :, j * 512:(j + 1) * 512], wsrc(wGb[j], 4), reads=[Bg[j]], writes=[BwG])
                sg = [sb(ph2, "sg%d" % i, [128, 512], F32) for i in range(2)]
                Bsg = [Buf("sg%d" % i) for i in range(2)]
                cnt = 0
                for t in range(NT):
                    cols = slice(t * 512, (t + 1) * 512)
                    for f in range(4):
                        bv, bg_ = (2 * cnt) % 8, (2 * cnt + 1) % 8
                        s_ = cnt % 2
                        cnt += 1
                        for kc in range(4):
                            op("pe", lambda e, f=f, kc=kc, bv=bv: e.matmul(
                                ps[bv][:], wG[:, kc, f * 128:(f + 1) * 128], UT[:, kc, cols], start=(kc == 0), stop=(kc == 3)),
                               reads=[BwG] + BUTall, writes=[PS[bv]])
                        for kc in range(4):
                            op("pe", lambda e, f=f, kc=kc, bg_=bg_: e.matmul(
                                ps[bg_][:], wG[:, kc, 512 + f * 128:512 + (f + 1) * 128], UT[:, kc, cols], start=(kc == 0), stop=(kc == 3)),
                               reads=[BwG] + BUTall, writes=[PS[bg_]])
                        op("act", lambda e, bg_=bg_, s_=s_: e.activation(out=sg[s_][:], in_=ps[bg_][:], func=AF.Sigmoid),
                           reads=[PS[bg_]], writes=[Bsg[s_]])
                        op("dve", lambda e, bv=bv, s_=s_: e.tensor_tensor(out=sg[s_][:], in0=sg[s_][:], in1=ps[bv][:], op=ALU.mult),
                           reads=[PS[bv], Bsg[s_]], writes=[Bsg[s_]])
                        op("dve", lambda e, f=f, s_=s_, t=t: e.tensor_tensor(
                            out=ZS[:, f, :].rearrange("p (c i) -> p i c", i=L)[:, 2 * t:2 * t + 2, :],
                            in0=sg[s_][:].rearrange("p (i c) -> p i c", i=2), in1=ZSp[:, f, cols].rearrange("p (i c) -> p i c", i=2), op=ALU.mult),
                           reads=[Bsg[s_], BZSp, BZSn], writes=[BZSn])
            fw.barrier()
        if "YS" in dumps:
            dump("YS", ZS[:], [128, 4, S], BZSn)
        if stop_after == "S":
            fw.barrier()
            return nc, dump_d

        KT = sb(top, "KT", [128, 4, S], BF16)
        VA = sb(top, "VA", [128, 32, 4, 130], BF16)
        MKT = sb(top, "MKT", [128, 4, 256], BF16)
        MVA = sb(top, "MVA", [128, 2, 4, 130], BF16)
        BKT, BVA, BMK = Buf("KT"), Buf("VA"), Buf("MK")
        op("dve", lambda e: e.memset(VA[:, :, :, 128:130], 1.0), reads=[], writes=[BVA])
        op("dve", lambda e: e.memset(MVA[:, :, :, 128:130], 1.0), reads=[], writes=[BMK])
        with ExitStack() as phK:
            wKV = sb(phK, "wKV", [128, 8, 1024], BF16)
            wM = sb(phK, "wM", [128, 8, 1024], BF16)
            mT = sb(phK, "mT", [128, 8, 256], BF16)
            BwKV, BwM, BmT = Buf("wKV"), Buf("wM"), Buf("mT")
            dsk_, dsm_, dst_ = fw.dsem("wKV"), fw.dsem("wM"), fw.dsem("mT")
            for j in range(2):
                dma("sp", dsk_, wKV[:, :, j * 512:(j + 1) * 512], wsrc(wInb[3 + j], 8), reads=[Bw[3 + j]], writes=[BwKV])
            for j in range(2):
                dma("sp", dsm_, wM[:, :, j * 512:(j + 1) * 512], wsrc(wMb[j], 8), reads=[Bm[j]], writes=[BwM])
            dma("sp", dst_, mT[:], wsrc(memTb, 8), reads=[Bmem], writes=[BmT])
            xt = [sb(phK, "xtK%d" % i, [128, 8, 512], BF16) for i in range(2)]
            Bxt = [Buf("xtK%d" % i) for i in range(2)]
            dsx = [fw.dsem("xtK%d" % i) for i in range(2)]

            def load_xk(t):
                dma("sp", dsx[t % 2], xt[t % 2][:], wsrc(xTb[t], 8), reads=[Bx[t]], writes=[Bxt[t % 2]])
            load_xk(0)
            cnt = 0
            for h in range(4):
                bk = cnt % 8
                cnt += 1
                for kc in range(8):
                    op("pe", lambda e, h=h, kc=kc, bk=bk: e.matmul(ps[bk][:, 0:256], wM[:, kc, h * 128:(h + 1) * 128], mT[:, kc, :],
                                                                   start=(kc == 0), stop=(kc == 7)), reads=[BwM, BmT], writes=[PS[bk]])
                op("act", lambda e, h=h, bk=bk: e.activation(out=MKT[:, h, :], in_=ps[bk][:, 0:256], func=AF.Copy), reads=[PS[bk]], writes=[BMK])
            for mb in range(2):
                bk = cnt % 8
                cnt += 1
                for kc in range(8):
                    op("pe", lambda e, mb=mb, kc=kc, bk=bk: e.matmul(ps[bk][:], mT[:, kc, mb * 128:(mb + 1) * 128], wM[:, kc, 512:1024],
                                                                     start=(kc == 0), stop=(kc == 7)), reads=[BwM, BmT], writes=[PS[bk]])
                op("act", lambda e, mb=mb, bk=bk: e.activation(out=MVA[:, mb, :, 0:128], in_=ps[bk][:].rearrange("p (h e) -> p h e", h=4), func=AF.Copy),
                   reads=[PS[bk]], writes=[BMK])
            for t in range(NT):
                if t + 1 < NT:
                    load_xk(t + 1)
                x_t, Bx_t = xt[t % 2], Bxt[t % 2]
                cols = slice(t * 512, (t + 1) * 512)
                for h in range(4):
                    bk = cnt % 8
                    cnt += 1
                    for kc in range(8):
                        op("pe", lambda e, h=h, kc=kc, bk=bk: e.matmul(ps[bk][:], wKV[:, kc, h * 128:(h + 1) * 128], x_t[:, kc, :],
                                                                       start=(kc == 0), stop=(kc == 7)), reads=[BwKV, Bx_t], writes=[PS[bk]])
                    op("act", lambda e, h=h, bk=bk: e.activation(out=KT[:, h, cols], in_=ps[bk][:], func=AF.Copy), reads=[PS[bk]], writes=[BKT])
                for tb in range(4):
                    bk = cnt % 8
                    cnt += 1
                    for kc in range(8):
                        op("pe", lambda e, tb=tb, kc=kc, bk=bk: e.matmul(ps[bk][:], x_t[:, kc, tb * 128:(tb + 1) * 128], wKV[:, kc, 512:1024],
                                                                         start=(kc == 0), stop=(kc == 7)), reads=[BwKV, Bx_t], writes=[PS[bk]])
                    op("dve", lambda e, tb=tb, bk=bk, t=t: e.tensor_copy(out=VA[:, 4 * t + tb, :, 0:128], in_=ps[bk][:].rearrange("p (h e) -> p h e", h=4)),
                       reads=[PS[bk]], writes=[BVA])
            fw.barrier()
        if "KT" in dumps:
            dump("KT", KT[:], [128, 4, S], BKT)
            dump("VA", VA[:], [128, 32, 4, 130], BVA)
            dump("MKT", MKT[:], [128, 4, 256], BMK)
            dump("MVA", MVA[:], [128, 2, 4, 130], BMK)
        if stop_after == "KV":
            fw.barrier()
            return nc, dump_d

        with ExitStack() as phF:
            farb = sb(phF, "farb", [128, 4], F32)
            lamv = sb(phF, "lamv", [128, 4, 64], F32)
            subw = sb(phF, "subw", [128, 128], F32)
            lng = sb(phF, "lng", [128, D], F32)
            lnb = sb(phF, "lnb", [128, D], F32)
            NBh = sb(phF, "NBh", [128, 4, 256], BF16)
            NBl = sb(phF, "NBl", [128, 4, 256], BF16)
            lsc = sb(phF, "lsc", [128, 8], F32)
            BC = Buf("constF")
            dsc = fw.dsem("constF")
            tmpF = ExitStack()
            tb_f = sb(tmpF, "tb_f", [128, 4, 256], F32)
            mk_f = sb(tmpF, "mk_f", [128, 4, 256], F32)
            for dst_, nm in ((tb_f, "tbias"), (mk_f, "maskc"), (farb, "farb"), (lamv, "lamv"), (subw, "subw"), (lng, "lng"), (lnb, "lnb")):
                dma("sp", dsc, dst_[:], I[nm], writes=[BC])
            RC, WC = [BC], [BC]
            for h in range(4):
                op("dve", lambda e, h=h: e.tensor_scalar(out=tb_f[:, h, :], in0=tb_f[:, h, :], scalar1=farb[:, h:h + 1], scalar2=None, op0=ALU.subtract), reads=RC, writes=WC)
            op("dve", lambda e: e.tensor_tensor(out=tb_f[:], in0=tb_f[:], in1=mk_f[:], op=ALU.add), reads=RC, writes=WC)
            op("dve", lambda e: e.tensor_copy(out=NBh[:], in_=tb_f[:]), reads=RC, writes=WC)
            op("dve", lambda e: e.tensor_copy(out=mk_f[:], in_=NBh[:]), reads=RC, writes=WC)
            op("dve", lambda e: e.tensor_tensor(out=mk_f[:], in0=tb_f[:], in1=mk_f[:], op=ALU.subtract), reads=RC, writes=WC)
            op("dve", lambda e: e.tensor_copy(out=NBl[:], in_=mk_f[:]), reads=RC, writes=WC)
            fw.barrier()
            tmpF.close()
            op("dve", lambda e: e.tensor_tensor(out=lamv[:, 0, :], in0=lamv[:, 0, :], in1=lamv[:, 1, :], op=ALU.mult), reads=RC, writes=WC)
            op("dve", lambda e: e.tensor_tensor(out=lamv[:, 2, :], in0=lamv[:, 2, :], in1=lamv[:, 3, :], op=ALU.mult), reads=RC, writes=WC)
            op("dve", lambda e: e.reduce_sum(out=lsc[:, 1:2], in_=lamv[:, 0, :], axis=AX.X), reads=RC, writes=WC)
            op("dve", lambda e: e.reduce_sum(out=lsc[:, 2:3], in_=lamv[:, 2, :], axis=AX.X), reads=RC, writes=WC)
            op("act", lambda e: e.activation(out=lsc[:, 3:5], in_=lsc[:, 1:3], func=AF.Exp), reads=RC, writes=WC)
            op("dve", lambda e: e.tensor_tensor(out=lsc[:, 0:1], in0=lsc[:, 4:5], in1=lsc[:, 3:4], op=ALU.subtract), reads=RC, writes=WC)
            op("dve", lambda e: e.tensor_scalar(out=lsc[:, 0:1], in0=lsc[:, 0:1], scalar1=-LAMBDA_INIT, scalar2=None, op0=ALU.add), reads=RC, writes=WC)
            op("dve", lambda e: e.tensor_scalar(out=subw[:], in0=subw[:], scalar1=1.0 - LAMBDA_INIT, scalar2=None, op0=ALU.mult), reads=RC, writes=WC)

            xq = sb(phF, "xq", [128, 8, 512], BF16)
            Bxq = Buf("xq")
            dsxq = fw.dsem("xq")
            NW = 4
            wt = [sb(phF, "wt%d" % i, [128, 8, 512], BF16) for i in range(NW)]
            Bwt = [Buf("wt%d" % i) for i in range(NW)]
            dswt = [fw.dsem("wt%d" % i) for i in range(NW)]
            wctr = [0]

            gseq = [(wInb[2], 8, Bw[2]), (wInb[5], 8, Bw[5]), (wInb[6], 8, Bw[6]), (wInb[7], 8, Bw[7])]
            for br_ in range(3):
                for half_ in range(2):
                    gseq.append((wInb[8 + 2 * br_ + half_], 8, Bw[8 + 2 * br_ + half_]))
                    gseq.append((wBb[br_, half_], 4, Bb[br_][half_]))
            gseq += [(wOb[0], 8, Bo[0]), (wOb[1], 8, Bo[1])]
            wseq = gseq * NT
            wstate = {"issued": 0, "consumed": 0}

            def _issue_w(k):
                piece, nkc, bsrc = wseq[k]
                i = k % NW
                dma("sp", dswt[i], wt[i][:, 0:nkc, :], wsrc(piece, nkc), reads=[bsrc], writes=[Bwt[i]])

            def next_w():
                k = wstate["consumed"]
                while wstate["issued"] < min(len(wseq), k + 3):
                    _issue_w(wstate["issued"])
                    wstate["issued"] += 1
                wstate["consumed"] += 1
                return wt[k % NW], Bwt[k % NW]

            QT = sb(phF, "QT", [128, 4, 512], BF16)
            MQT = QT
            scr = sb(phF, "scr", [128, 4096], F32)
            Bscr = Buf("scr")
            ZD = scr[:, 0:2048].rearrange("p (q f) -> p q f", q=4)
            ZM = ZD
            BQT = Buf("QT")
            BMQ, BZD, BZM = BQT, Bscr, Bscr
            NP = 4
            PT_all = sb(phF, "PT", [128, NP, 512], BF16)
            PT = [PT_all[:, i, :] for i in range(NP)]
            BPT = [Buf("PT%d" % i) for i in range(NP)]
            pctr = [0]
            YD = scr[:, 2048:3072].bitcast(BF16).rearrange("p (q f) -> p q f", q=4)
            YM = scr[:, 3072:4096].bitcast(BF16).rearrange("p (q f) -> p q f", q=4)
            YDT = sb(phF, "YDT", [128, 4, 512], BF16)
            YMT = sb(phF, "YMT", [128, 4, 512], BF16)
            BYD, BYM, BYDT, BYMT = Bscr, Bscr, Buf("YDT"), Buf("YMT")
            xres = YDT[:].rearrange("p h n -> p (h n)").bitcast(F32)
            Bxres = BYDT
            sm = sb(phF, "sm", [128, 32], F32)
            Bsm = Buf("sm")
            od = [None, sb(phF, "od1", [128, 128], F32)]
            Bod = Buf("od")
            od4 = sb(phF, "od4", [128, 4, 128], F32)
            OS = sb(phF, "OS", [128, 2, 520], F32)
            BOS = Buf("OS")
            ACC = scr[:].rearrange("p (q f) -> p q f", q=8)
            BACC = Bscr
            MT = sb(phF, "MT", [128, 8, 512], BF16)
            BMT = Buf("MT")
            sgf_all = sb(phF, "sgf", [128, 2, 512], F32)
            sgf = [sgf_all[:, 0, :], sgf_all[:, 1, :]]
            Bsgf = [Buf("sgf%d" % i) for i in range(2)]

            dsxr = fw.dsem("xres")
            dsout = [fw.dsem("out0"), fw.dsem("out1")]
            stats2 = [sb(phF, "stats%d" % i, [128, 2, 6], F32) for i in range(2)]
            mv2 = [sb(phF, "mv%d" % i, [128, 4], F32) for i in range(2)]
            Bst = [Buf("stats%d" % i) for i in range(2)]

            pctr_bank = [0]

            def gbank():
                b = (0, 1, 2, 3)[pctr_bank[0] % 4]
                pctr_bank[0] += 1
                return b

            OB = [[4, 5], [6, 7]]

            def oreg(c, ql):
                if ql < 3:
                    return ps[OB[c][0]][:, ql * 130:ql * 130 + 130], OB[c][0]
                return ps[OB[c][1]][:, 0:130], OB[c][1]

            def emit_qproj():
                wq, Bwq = next_w()
                for h in range(4):
                    bk = gbank()
                    for kc in range(8):
                        op("pe", lambda e, h=h, kc=kc, bk=bk: e.matmul(ps[bk][:], wq[:, kc, h * 128:(h + 1) * 128], xq[:, kc, :],
                                                                       start=(kc == 0), stop=(kc == 7)), reads=[Bwq, Bxq], writes=[PS[bk]])
                    op("act", lambda e, h=h, bk=bk: e.activation(out=QT[:, h, :], in_=ps[bk][:], func=AF.Copy, scale=0.125), reads=[PS[bk]], writes=[BQT])
                wz, Bwz = next_w()
                for ql in range(4):
                    bk = gbank()
                    for kc in range(8):
                        op("pe", lambda e, ql=ql, kc=kc, bk=bk: e.matmul(ps[bk][:], xq[:, kc, ql * 128:(ql + 1) * 128], wz[:, kc, :],
                                                                         start=(kc == 0), stop=(kc == 7)), reads=[Bwz, Bxq], writes=[PS[bk]])
                    op("act", lambda e, ql=ql, bk=bk: e.activation(out=ZD[:, ql, :], in_=ps[bk][:], func=AF.Silu), reads=[PS[bk]], writes=[BZD])

            for G in range(NT):
                if G == 0:
                    dma("sp", dsxq, xq[:], wsrc(xTb[G], 8), reads=[Bx[G]], writes=[Bxq])
                if G == 0:
                    emit_qproj()
                if stop_after == "F1":
                    dump("QT", QT[:], [128, 4, 512], BQT)
                    dump("ZD", scr[:, 0:2048], [128, 2048], Bscr)
                    fw.barrier()
                    return nc, dump_d
                nkb = 4 * G + 4
                steps = [(h, kb) for h in range(4) for kb in range(nkb)]
                SBANKS = ((0, 1), (2, 3))

                def emit_score(i):
                    h, kb = steps[i]
                    pair = i % 2
                    ql0 = max(0, kb - 4 * G)
                    cs_ = slice(ql0 * 128, 512)
                    nq = [ql for ql in range(4) if 4 * G + ql in (kb, kb + 1)]
                    for c in range(2):
                        rows = slice(64 * c, 64 * c + 64)
                        bk = SBANKS[pair][c]
                        op("pe", lambda e, h=h, kb=kb, bk=bk, cs_=cs_, rows=rows, nq=nq: e.matmul(
                            ps[bk][:, cs_], KT[rows, h, kb * 128:(kb + 1) * 128], QT[rows, h, cs_],
                            start=True, stop=(len(nq) == 0)), reads=[BKT, BQT], writes=[PS[bk]])
                    if nq:
                        qa = nq[0] * 128
                        qrel0 = (4 * G + nq[0] - kb) * 128
                        w_ = 128 * len(nq)
                        for c in range(2):
                            bk = SBANKS[pair][c]
                            for hl, NBx in enumerate((NBh, NBl)):
                                op("pe", lambda e, h=h, bk=bk, qa=qa, qrel0=qrel0, w_=w_, NBx=NBx, hl=hl: e.matmul(
                                    ps[bk][:, qa:qa + w_], ident[:], NBx[:, h, qrel0:qrel0 + w_],
                                    start=False, stop=(hl == 1)), reads=[Bid, BC], writes=[PS[bk]])
                    src2 = (psA if pair == 0 else psB)[:].rearrange("p (c n) -> p c n", c=2)[:, :, cs_]
                    dst2 = PT_all[:, 2 * pair:2 * pair + 2, cs_]
                    b0_, b1_ = SBANKS[pair]
                    op("act", lambda e, h=h, src2=src2, dst2=dst2: e.activation(
                        out=dst2, in_=src2, func=AF.Exp, bias=farb[:, h:h + 1], scale=1.0),
                       reads=[PS[b0_], PS[b1_], BC], writes=[BPT[2 * pair], BPT[2 * pair + 1]])

                def emit_pv(i):
                    h, kb = steps[i]
                    pair = i % 2
                    ql0 = max(0, kb - 4 * G)
                    for c in range(2):
                        pi = 2 * pair + c
                        for ql in range(ql0, 4):
                            oap, obk = oreg(c, ql)
                            st_ = (kb == 0 and ql in (0, 3))
                            last = (kb == 4 * G + ql)
                            op("pe", lambda e, h=h, kb=kb, ql=ql, pi=pi, oap=oap, st_=st_, last=last: e.matmul(
                                oap, PT[pi][:, ql * 128:(ql + 1) * 128], VA[:, kb, h, 0:130], start=st_, stop=last, skip_group_check=True),
                               reads=[BPT[pi], BVA], writes=[PS[obk]])
                    if kb == nkb - 1:
                        for pb in list(pend_b):
                            emit_norm_b(pb[0])
                            pend_b.remove(pb)
                        emit_norm_a(h)
                        pend_b.append([h, 10])

                def emit_norm_a(h):
                    for c in range(2):
                        op("dve", lambda e, c=c: e.tensor_copy(out=OS[:, c, 0:390], in_=ps[OB[c][0]][:, 0:390]), reads=[PS[OB[c][0]]], writes=[BOS])
                        op("dve", lambda e, c=c: e.tensor_copy(out=OS[:, c, 390:520], in_=ps[OB[c][1]][:, 0:130]), reads=[PS[OB[c][1]]], writes=[BOS])
                    R_, W_ = [BOS, Bsm, Bod, BC, BZD], [Bsm, Bod]
                    dens = OS[:].rearrange("p c (q e) -> p c q e", q=4)[:, :, :, 128]
                    op("dve", lambda e: e.reciprocal(out=sm[:, 0:8].rearrange("p (c q) -> p c q", c=2), in_=dens), reads=R_, writes=W_)
                    op("dve", lambda e: e.tensor_scalar(out=sm[:, 4:8], in0=sm[:, 4:8], scalar1=lsc[:, 0:1], scalar2=None, op0=ALU.mult), reads=R_, writes=W_)
                    for ql in range(4):
                        o0 = OS[:, 0, ql * 130:ql * 130 + 128]
                        o1 = OS[:, 1, ql * 130:ql * 130 + 128]
                        op("dve", lambda e, o0=o0, ql=ql: e.tensor_scalar(out=od4[:, ql, :], in0=o0, scalar1=sm[:, ql:ql + 1], scalar2=None, op0=ALU.mult), reads=R_, writes=W_)
                        op("dve", lambda e, o1=o1, ql=ql: e.scalar_tensor_tensor(out=od4[:, ql, :], in0=o1, scalar=sm[:, 4 + ql:5 + ql], in1=od4[:, ql, :], op0=ALU.mult, op1=ALU.add), reads=R_, writes=W_)
                        op("dve", lambda e, ql=ql: e.scalar_tensor_tensor(out=od[1][:], in0=od4[:, ql, :], scalar=1.0, in1=od4[:, ql, :], op0=ALU.mult, op1=ALU.mult,
                                                                         accum_out=sm[:, 8 + ql:9 + ql]), reads=R_, writes=W_)
                    op("dve", lambda e: e.tensor_scalar(out=sm[:, 12:16], in0=sm[:, 8:12], scalar1=1.0 / 128.0, scalar2=1e-5, op0=ALU.mult, op1=ALU.add), reads=R_, writes=W_)

                def emit_norm_b(h):
                    R_, W_ = [BOS, Bsm, Bod, BC, BZD], [Bsm, Bod]
                    op("act", lambda e: e.activation(out=sm[:, 16:20], in_=sm[:, 12:16], func=AF.Ln), reads=R_, writes=W_)
                    op("act", lambda e: e.activation(out=sm[:, 20:24], in_=sm[:, 16:20], func=AF.Exp, scale=-0.5), reads=R_, writes=W_)
                    for ql in range(4):
                        op("dve", lambda e, ql=ql: e.scalar_tensor_tensor(out=od4[:, ql, :], in0=od4[:, ql, :], scalar=sm[:, 20 + ql:21 + ql], in1=subw[:], op0=ALU.mult, op1=ALU.mult), reads=R_, writes=W_)
                    op("dve", lambda e, h=h: e.tensor_tensor(out=YD[:, :, h * 128:(h + 1) * 128], in0=od4[:], in1=ZD[:, :, h * 128:(h + 1) * 128], op=ALU.mult),
                       reads=R_ + [BYD], writes=[BYD, Bod])

                pend_b = []
                emit_score(0)
                for i in range(len(steps)):
                    if i + 1 < len(steps):
                        emit_score(i + 1)
                    emit_pv(i)
                    for pb in list(pend_b):
                        pb[1] -= 1
                        if pb[1] <= 0 or i == len(steps) - 1:
                            emit_norm_b(pb[0])
                            pend_b.remove(pb)

                if stop_after == "F2":
                    dump("YD", scr[:, 2048:3072], [128, 1024], Bscr)
                    dump("QT", QT[:], [128, 4, 512], BQT)
                    dump("ZD", scr[:, 0:2048], [128, 2048], Bscr)
                    op("dve", lambda e: e.tensor_copy(out=sgf[0][:], in_=ps[4][:]), reads=[PS[4]], writes=[Bsgf[0]])
                    op("dve", lambda e: e.tensor_copy(out=sgf[1][:], in_=ps[6][:]), reads=[PS[6]], writes=[Bsgf[1]])
                    dump("O0", sgf[0][:], [128, 512], Bsgf[0])
                    dump("O1", sgf[1][:], [128, 512], Bsgf[1])
                    dump("PT", PT[0], [128, 512], BPT[0])
                    dump("sm", sm[:], [128, 32], Bsm)
                    dump("lsc", lsc[:], [128, 8], BC)
                    fw.barrier()
                    return nc, dump_d
                wmq, Bwmq = next_w()
                for h in range(4):
                    bk = gbank()
                    for kc in range(8):
                        op("pe", lambda e, h=h, kc=kc, bk=bk: e.matmul(ps[bk][:], wmq[:, kc, h * 128:(h + 1) * 128], xq[:, kc, :],
                                                                       start=(kc == 0), stop=(kc == 7)), reads=[Bwmq, Bxq], writes=[PS[bk]])
                    op("act", lambda e, h=h, bk=bk: e.activation(out=MQT[:, h, :], in_=ps[bk][:], func=AF.Copy, scale=128.0 ** -0.5), reads=[PS[bk]], writes=[BMQ])
                wzm, Bwzm = next_w()
                for ql in range(4):
                    bk = gbank()
                    for kc in range(8):
                        op("pe", lambda e, ql=ql, kc=kc, bk=bk: e.matmul(ps[bk][:], xq[:, kc, ql * 128:(ql + 1) * 128], wzm[:, kc, :],
                                                                         start=(kc == 0), stop=(kc == 7)), reads=[Bwzm, Bxq], writes=[PS[bk]])
                    op("act", lambda e, ql=ql, bk=bk: e.activation(out=ZM[:, ql, :], in_=ps[bk][:], func=AF.Silu), reads=[PS[bk]], writes=[BZM])

                for h in range(4):
                    for mb in range(2):
                        bk = gbank()
                        op("pe", lambda e, h=h, mb=mb, bk=bk: e.matmul(ps[bk][:], MKT[:, h, mb * 128:(mb + 1) * 128], MQT[:, h, :], start=True, stop=True),
                           reads=[BMK, BMQ], writes=[PS[bk]])
                        pi = pctr[0] % NP
                        pctr[0] += 1
                        op("act", lambda e, bk=bk, pi=pi: e.activation(out=PT[pi][:], in_=ps[bk][:], func=AF.Exp), reads=[PS[bk]], writes=[BPT[pi]])
                        for ql in range(4):
                            oap, obk = oreg(0, ql)
                            op("pe", lambda e, h=h, mb=mb, ql=ql, pi=pi, oap=oap: e.matmul(
                                oap, PT[pi][:, ql * 128:(ql + 1) * 128], MVA[:, mb, h, 0:130], start=(mb == 0 and ql in (0, 3)), stop=(mb == 1), skip_group_check=True),
                               reads=[BPT[pi], BMK], writes=[PS[obk]])
                    for ql in range(4):
                        o0, b0 = oreg(0, ql)
                        R_, W_ = [PS[b0], Bsm, BZM], [Bsm]
                        op("dve", lambda e, o0=o0: e.reciprocal(out=sm[:, 8:9], in_=o0[:, 128:129]), reads=R_, writes=W_)
                        op("dve", lambda e, o0=o0, ql=ql, h=h: e.scalar_tensor_tensor(
                            out=YM[:, ql, h * 128:(h + 1) * 128], in0=o0[:, 0:128], scalar=sm[:, 8:9], in1=ZM[:, ql, h * 128:(h + 1) * 128], op0=ALU.mult, op1=ALU.mult),
                           reads=R_ + [BYM], writes=[BYM, Bsm])

                if stop_after == "F3":
                    dump("YM", scr[:, 3072:4096], [128, 1024], Bscr)
                    fw.barrier()
                    return nc, dump_d
                for (Ysrc, Bs_, Ydst, Bd_) in ((YD, BYD, YDT, BYDT), (YM, BYM, YMT, BYMT)):
                    for kc in range(4):
                        bk = gbank()
                        pst = ps[bk][:].bitcast(BF16)
                        for ql in range(4):
                            op("pe", lambda e, kc=kc, ql=ql, pst=pst, Ysrc=Ysrc: e.transpose(
                                pst[:, ql * 128:(ql + 1) * 128], Ysrc[:, ql, kc * 128:(kc + 1) * 128], ident[:]),
                               reads=[Bs_, Bid], writes=[PS[bk]])
                        op("dve", lambda e, kc=kc, pst=pst, Ydst=Ydst: e.tensor_copy(out=Ydst[:, kc, :], in_=pst[:, 0:512]), reads=[PS[bk]], writes=[Bd_])

                if stop_after == "F4":
                    dump("YDT", YDT[:], [128, 4, 512], BYDT)
                    dump("YMT", YMT[:], [128, 4, 512], BYMT)
                    fw.barrier()
                    return nc, dump_d
                for br in range(3):
                    if br == 0:
                        ysrc = lambda kc, G=G: ZS[:, kc, G * 512:(G + 1) * 512]
                        By = BZSn
                    elif br == 1:
                        ysrc = lambda kc: YDT[:, kc, :]
                        By = BYDT
                    else:
                        ysrc = lambda kc: YMT[:, kc, :]
                        By = BYMT
                    for half in range(2):
                        wg_, Bwg = next_w()
                        wb_, Bwb = next_w()
                        for fl in range(4):
                            ft = half * 4 + fl
                            bb_, bg_ = gbank(), gbank()
                            for kc in range(4):
                                op("pe", lambda e, kc=kc, fl=fl, bb_=bb_, ysrc=ysrc, wb_=wb_: e.matmul(
                                    ps[bb_][:], wb_[:, kc, fl * 128:(fl + 1) * 128], ysrc(kc), start=(kc == 0), stop=(kc == 3)),
                                   reads=[Bwb, By], writes=[PS[bb_]])
                            for kc in range(8):
                                op("pe", lambda e, kc=kc, fl=fl, bg_=bg_, wg_=wg_: e.matmul(
                                    ps[bg_][:], wg_[:, kc, fl * 128:(fl + 1) * 128], xq[:, kc, :], start=(kc == 0), stop=(kc == 7)),
                                   reads=[Bwg, Bxq], writes=[PS[bg_]])
                            s_ = ft % 2
                            op("act", lambda e, bg_=bg_, s_=s_: e.activation(out=sgf[s_][:], in_=ps[bg_][:], func=AF.Sigmoid), reads=[PS[bg_]], writes=[Bsgf[s_]])
                            if br == 0:
                                op("dve", lambda e, bb_=bb_, s_=s_, ft=ft: e.tensor_tensor(out=ACC[:, ft, :], in0=sgf[s_][:], in1=ps[bb_][:], op=ALU.mult),
                                   reads=[PS[bb_], Bsgf[s_]], writes=[BACC])
                            else:
                                op("dve", lambda e, bb_=bb_, s_=s_: e.tensor_tensor(out=sgf[s_][:], in0=sgf[s_][:], in1=ps[bb_][:], op=ALU.mult),
                                   reads=[PS[bb_], Bsgf[s_]], writes=[Bsgf[s_]])
                                if br == 1:
                                    op("dve", lambda e, s_=s_, ft=ft: e.tensor_tensor(out=ACC[:, ft, :], in0=ACC[:, ft, :], in1=sgf[s_][:], op=ALU.add),
                                       reads=[Bsgf[s_], BACC], writes=[BACC])
                                else:
                                    op("dve", lambda e, s_=s_, ft=ft: e.tensor_tensor(out=MT[:, ft, :], in0=ACC[:, ft, :], in1=sgf[s_][:], op=ALU.add),
                                       reads=[Bsgf[s_], BACC], writes=[BMT])

                if stop_after == "F5":
                    dump("MT", MT[:], [128, 8, 512], BMT)
                    fw.barrier()
                    return nc, dump_d
                if G + 1 < NT:
                    dma("sp", dsxq, xq[:], wsrc(xTb[G + 1], 8), reads=[Bx[G + 1]], writes=[Bxq])
                wo = [next_w() for j in range(2)]
                HH = [sgf_all[:].rearrange("p a n -> p (a n)"), YMT[:].rearrange("p h n -> p (h n)").bitcast(F32)]
                HHB = [[Bsgf[0], Bsgf[1]], [BYMT]]
                OBK = (0, 1, 2, 3, 4, 5, 6, 7)

                def ln_head(ql):
                    r0 = G * 512 + ql * 128
                    Hc, HB = HH[ql % 2], HHB[ql % 2]
                    dma("sp", dsxr, xres[:], I["x"][r0:r0 + 128, :], writes=[Bxres])
                    for half in range(2):
                        bk = OBK[2 * ql + half]
                        wo_, Bwo = wo[half]
                        for kc in range(8):
                            op("pe", lambda e, kc=kc, ql=ql, bk=bk, wo_=wo_: e.matmul(
                                ps[bk][:], MT[:, kc, ql * 128:(ql + 1) * 128], wo_[:, kc, :], start=(kc == 0), stop=(kc == 7)),
                               reads=[BMT, Bwo], writes=[PS[bk]])
                        hs = slice(half * 512, (half + 1) * 512)
                        op("dve", lambda e, bk=bk, hs=hs, Hc=Hc: e.scalar_tensor_tensor(out=Hc[:, hs], in0=xres[:, hs], scalar=ALPHA, in1=ps[bk][:], op0=ALU.mult, op1=ALU.add),
                           reads=[PS[bk], Bxres] + HB, writes=HB)
                    R_, W_ = HB + [Bst[ql % 2], BC], [Bst[ql % 2]] + HB
                    st_, mv_ = stats2[ql % 2], mv2[ql % 2]
                    for half in range(2):
                        op("dve", lambda e, half=half, Hc=Hc: e.bn_stats(out=st_[:, half, :], in_=Hc[:, half * 512:(half + 1) * 512]), reads=R_, writes=W_)
                    op("dve", lambda e: e.bn_aggr(out=mv_[:, 0:2], in_=st_[:].rearrange("p a b -> p (a b)")), reads=R_, writes=W_)
                    op("dve", lambda e: e.tensor_scalar(out=mv_[:, 2:3], in0=mv_[:, 1:2], scalar1=1e-5, scalar2=None, op0=ALU.add), reads=R_, writes=W_)

                def ln_tail(ql):
                    r0 = G * 512 + ql * 128
                    Hc, HB = HH[ql % 2], HHB[ql % 2]
                    R_, W_ = HB + [Bst[ql % 2], BC], [Bst[ql % 2]] + HB
                    mv_ = mv2[ql % 2]
                    op("act", lambda e: e.activation(out=mv_[:, 2:3], in_=mv_[:, 2:3], func=AF.Sqrt), reads=R_, writes=W_)
                    op("dve", lambda e: e.reciprocal(out=mv_[:, 3:4], in_=mv_[:, 2:3]), reads=R_, writes=W_)
                    op("dve", lambda e, Hc=Hc: e.tensor_scalar(out=Hc, in0=Hc, scalar1=mv_[:, 0:1], scalar2=mv_[:, 3:4], op0=ALU.subtract, op1=ALU.mult), reads=R_, writes=W_)
                    op("pool", lambda e, Hc=Hc: e.tensor_tensor(out=Hc, in0=Hc, in1=lng[:], op=ALU.mult), reads=HB + [BC], writes=HB)
                    op("pool", lambda e, Hc=Hc: e.tensor_tensor(out=Hc, in0=Hc, in1=lnb[:], op=ALU.add), reads=HB + [BC], writes=HB)
                    dma("pool", dsout[ql % 2], out_d[r0:r0 + 128, :], Hc, reads=HB)

                ln_head(0)
                ln_tail(0)
                ln_head(1)
                ln_tail(1)
                ln_head(2)
                ln_head(3)
                if G + 1 < NT:
                    emit_qproj()
                ln_tail(2)
                ln_tail(3)
                if stop_after == "F6":
                    fw.barrier()
                    return nc, dump_d
            fw.barrier()
        fw.barrier()
    return nc, dump_d


_NC_CACHE = {}


def kernel(**inputs):
    inp = {k: np.asarray(v) for k, v in inputs.items()}
    shared = _shared_layouts(inp)
    if "nc" not in _NC_CACHE:
        _NC_CACHE["nc"] = build()[0]
    nc = _NC_CACHE["nc"]
    in_maps = []
    for b in range(8):
        m = dict(shared)
        m["x"] = np.ascontiguousarray(inp["x"][b])
        m["xT"] = np.ascontiguousarray(inp["x"][b].T)
        m["memT"] = np.ascontiguousarray(inp["mem"][b].T)
        in_maps.append(m)
    res = run_bass_kernel_spmd(nc, in_maps, core_ids=list(range(8)))
    return np.stack([np.asarray(r["out"]) for r in res.results]).astype(np.float32)
```

```python
import math
from contextlib import ExitStack
import numpy as np
import concourse.bass as bass
import concourse.mybir as mybir
from concourse.bass_utils import run_bass_kernel_spmd

F32 = mybir.dt.float32
BF16 = mybir.dt.bfloat16
I32 = mybir.dt.int32
AF = mybir.ActivationFunctionType
ALU = mybir.AluOpType
AX = mybir.AxisListType

S = 4096
D = 1024
NT = 8
L = 16
NB = S // L
ALPHA = 2.0 ** 0.25
LAMBDA_INIT = 0.8 - 0.6 * math.exp(0.0)
PI = float(np.pi)


class Buf:
    __slots__ = ("name", "w", "r")

    def __init__(self, name):
        self.name = name
        self.w = None
        self.r = {}


class Eng:
    def __init__(self, name, eng, sem):
        self.name, self.eng, self.sem = name, eng, sem
        self.cnt = 0
        self.seen = {}


class FW:
    def __init__(self, nc, stack):
        self.nc, self.stack = nc, stack
        self.sems, self.engs, self.cnts = {}, {}, {}
        for name, eng in (("pe", nc.tensor), ("act", nc.scalar), ("dve", nc.vector),
                          ("pool", nc.gpsimd), ("sp", nc.sync)):
            sem = stack.enter_context(nc.semaphore("sem_" + name))
            self.sems[name] = sem
            self.engs[name] = Eng(name, eng, sem)
            self.cnts[name] = 0

    def dsem(self, name):
        self.sems[name] = self.stack.enter_context(self.nc.semaphore("d_" + name))
        self.cnts[name] = 0
        return name

    def _wait(self, e, key, val):
        if key == "pe" and e.name == "pe":
            return
        if e.seen.get(key, 0) >= val:
            return
        e.seen[key] = val
        e.eng.wait_ge(self.sems[key], val)

    def _deps(self, e, reads, writes):
        for b in reads:
            if b.w is not None:
                self._wait(e, *b.w)
        for b in writes:
            if b.w is not None:
                self._wait(e, *b.w)
            for k, v in b.r.items():
                self._wait(e, k, v)

    def _mark(self, tag, reads, writes):
        for b in reads:
            if b.r.get(tag[0], 0) < tag[1]:
                b.r[tag[0]] = tag[1]
        for b in writes:
            b.w = tag
            b.r = {}

    def op(self, en, fn, reads=(), writes=()):
        e = self.engs[en]
        self._deps(e, reads, writes)
        ins = fn(e.eng)
        self.cnts[en] += 1
        ins.then_inc(e.sem, 1)
        self._mark((en, self.cnts[en]), reads, writes)

    def dma(self, q, dsem, out, in_, reads=(), writes=()):
        e = self.engs[q]
        self._deps(e, reads, writes)
        ins = e.eng.dma_start(out=out, in_=in_)
        self.cnts[dsem] += 16
        ins.then_inc(self.sems[dsem], 16)
        self._mark((dsem, self.cnts[dsem]), reads, writes)

    def barrier(self, only=None, skip=()):
        for e in self.engs.values():
            if only is not None and e.name not in only:
                continue
            for k, v in self.cnts.items():
                if v > 0 and not k.startswith("cv_") and k not in skip:
                    self._wait(e, k, v)


def _t5_bucket_np(rel):
    ret = np.where(rel > 0, 16, 0)
    n = np.abs(rel)
    v = (np.log(np.maximum(n, 1).astype(np.float32) / np.float32(8)) / np.float32(math.log(16))
         * np.float32(8))
    large = np.minimum(8 + v.astype(np.int32), 15)
    return ret + np.where(n < 8, n, large)


def _shared_layouts(inp):
    f = np.float32
    o = {}
    lam_re, lam_im, log_dt = inp["lam_re"][0], inp["lam_im"][0], inp["log_dt"][0]
    b_re, b_im, c_re, c_im = inp["b_re"][0], inp["b_im"][0], inp["c_re"][0], inp["c_im"][0]
    def layA(v):
        return np.ascontiguousarray(v.reshape(16, 2, 64).transpose(1, 2, 0).reshape(128, 16)).astype(f)
    o["lamA"] = np.stack([layA(lam_re), layA(lam_im),
                          layA(np.broadcast_to(log_dt[:, None], (32, 64)))], axis=1)
    cbd = np.zeros((2, 2, 64, 16, 2, 16), f)
    bbd = np.zeros((2, 2, 64, 16, 2, 16), f)
    for ri, (cc, bb) in enumerate(((c_re, b_re), (c_im, b_im))):
        c4 = cc.reshape(16, 2, 16, 64)
        b4 = bb.reshape(16, 2, 64, 16)
        for g2 in range(2):
            cbd[ri, g2, :, :, g2, :] = c4[:, g2].transpose(2, 0, 1)
            bbd[ri, g2, :, :, g2, :] = b4[:, g2].transpose(1, 0, 2)
    o["cbdA"] = np.ascontiguousarray(cbd.reshape(2, 128, 16, 32).transpose(1, 0, 2, 3))
    o["bbdA"] = np.ascontiguousarray(bbd.reshape(2, 128, 16, 32).transpose(1, 0, 2, 3))
    def layB(v):
        a = v.reshape(4, 4, 2, 64).transpose(1, 0, 2, 3)
        a = np.broadcast_to(a[:, None], (4, 32, 4, 2, 64))
        return np.ascontiguousarray(a.reshape(128, 4, 128)).astype(f)
    o["lamB"] = np.stack([layB(lam_re), layB(lam_im),
                          layB(np.broadcast_to(log_dt[:, None], (32, 64)))], axis=1)
    btb = np.zeros((2, 4, 2, 16, 4, 2, 64), f)
    for ri, bb in enumerate((b_re, b_im)):
        b5 = bb.reshape(4, 4, 2, 64, 16)
        for g2 in range(2):
            btb[ri, :, g2, :, :, g2, :] = b5[:, :, g2].transpose(1, 3, 0, 2)
    o["btB"] = np.ascontiguousarray(btb.reshape(2, 128, 4, 128).transpose(1, 0, 2, 3))
    dsk = np.zeros((128, 4, 128), f)
    ds = inp["d_skip"][0].reshape(4, 128)
    for t in range(4):
        dsk[np.arange(128), t, np.arange(128)] = ds[t]
    o["dskD"] = dsk
    kl = np.arange(128)[:, None]
    qr = np.arange(256)[None, :]
    bidx = _t5_bucket_np(kl - qr)
    rb = inp["rel_bias"]
    o["tbias"] = np.ascontiguousarray(rb[bidx].transpose(0, 2, 1)).astype(f)
    maskc = np.where((kl >= 64) & (qr < 64), -30000.0, 0.0).astype(f)
    o["maskc"] = np.ascontiguousarray(np.broadcast_to(maskc[:, None, :], (128, 4, 256)))
    o["farb"] = np.ascontiguousarray(np.broadcast_to(rb[15][None, :], (128, 4))).astype(f)
    o["lamv"] = np.ascontiguousarray(np.broadcast_to(
        np.stack([inp["lambda_q1"][0], inp["lambda_k1"][0], inp["lambda_q2"][0], inp["lambda_k2"][0]])[None],
        (128, 4, 64))).astype(f)
    o["subw"] = np.ascontiguousarray(np.broadcast_to(inp["subln_w"][0][None], (128, 128))).astype(f)
    o["lng"] = np.ascontiguousarray(np.broadcast_to(inp["ln_g"][0][None], (128, D))).astype(f)
    o["lnb"] = np.ascontiguousarray(np.broadcast_to(inp["ln_b"][0][None], (128, D))).astype(f)
    o["ident"] = np.eye(128, dtype=f)
    o["w_in"] = np.ascontiguousarray(inp["w_in"][0])
    o["w_glu"] = np.ascontiguousarray(inp["w_glu"][0])
    o["w_mem_kv"] = np.ascontiguousarray(inp["w_mem_kv"][0])
    o["w_br"] = np.ascontiguousarray(np.stack([inp["w_br_ssm"][0], inp["w_br_diff"][0], inp["w_br_mem"][0]]))
    o["w_out"] = np.ascontiguousarray(inp["w_out"][0])
    return o


IN_SHAPES = {
    "xT": [D, S], "x": [S, D], "memT": [D, 256],
    "lamA": [128, 3, 16], "cbdA": [128, 2, 16, 32], "bbdA": [128, 2, 16, 32],
    "lamB": [128, 3, 4, 128], "btB": [128, 2, 4, 128], "dskD": [128, 4, 128],
    "tbias": [128, 4, 256], "maskc": [128, 4, 256], "farb": [128, 4], "lamv": [128, 4, 64],
    "subw": [128, 128], "lng": [128, D], "lnb": [128, D], "ident": [128, 128],
    "w_in": [D, 7168], "w_glu": [512, 1024], "w_mem_kv": [D, 1024], "w_br": [3, 512, 1024],
    "w_out": [D, D],
}


def build(stop_after=None, dumps=()):
    nc = bass.Bass("TRN2", target_bir_lowering=False)
    I = {k: nc.dram_tensor(k, v, F32, kind="ExternalInput").ap() for k, v in IN_SHAPES.items()}
    out_d = nc.dram_tensor("out", [S, D], F32, kind="ExternalOutput").ap()
    dump_d = {}
    xTb = nc.dram_tensor("xTb", [NT, D, 512], BF16).ap()
    wInb = nc.dram_tensor("wInb", [14, D, 512], BF16).ap()
    wGb = nc.dram_tensor("wGb", [2, 512, 512], BF16).ap()
    wMb = nc.dram_tensor("wMb", [2, D, 512], BF16).ap()
    wBb = nc.dram_tensor("wBb", [3, 2, 512, 512], BF16).ap()
    wOb = nc.dram_tensor("wOb", [2, D, 512], BF16).ap()
    memTb = nc.dram_tensor("memTb", [D, 256], BF16).ap()

    with ExitStack() as top:
        fw = FW(nc, top)
        op, dma = fw.op, fw.dma

        def sb(st, name, shape, dt):
            return st.enter_context(nc.sbuf_tensor("s_" + name, list(shape), dt))

        def dump(name, ap_sb, shape, buf):
            d = nc.dram_tensor("dump_" + name, list(shape), ap_sb.dtype, kind="ExternalOutput").ap()
            dump_d[name] = d
            dma("sp", fw.dsem("dump_" + name), d, ap_sb, reads=[buf])

        psA = top.enter_context(nc.psum_tensor("psA", [128, 1024], F32))
        psB = top.enter_context(nc.psum_tensor("psB", [128, 1024], F32))
        ps = [psA[:, 0:512], psA[:, 512:1024], psB[:, 0:512], psB[:, 512:1024]] + \
             [top.enter_context(nc.psum_tensor("ps%d" % i, [128, 512], F32)) for i in range(4, 8)]
        PS = [Buf("ps%d" % i) for i in range(8)]

        Bx = [Buf("xTb%d" % t) for t in range(NT)]
        Bw = [Buf("wInb%d" % j) for j in range(14)]
        Bg = [Buf("wGb%d" % j) for j in range(2)]
        Bm = [Buf("wMb%d" % j) for j in range(2)]
        Bb = [[Buf("wBb%d%d" % (b, j)) for j in range(2)] for b in range(3)]
        Bo = [Buf("wOb%d" % j) for j in range(2)]
        Bmem = Buf("memTb")

        def conv(dst, src, buf):
            dma("pool", fw.dsem("cv_" + buf.name), dst, src, writes=[buf])

        conv(wInb[0], I["w_in"][:, 0:512], Bw[0])
        conv(wInb[1], I["w_in"][:, 512:1024], Bw[1])
        for t in range(NT):
            conv(xTb[t], I["xT"][:, t * 512:(t + 1) * 512], Bx[t])
        for j in range(2):
            conv(wGb[j], I["w_glu"][:, j * 512:(j + 1) * 512], Bg[j])
        for j in (3, 4):
            conv(wInb[j], I["w_in"][:, j * 512:(j + 1) * 512], Bw[j])
        for j in range(2):
            conv(wMb[j], I["w_mem_kv"][:, j * 512:(j + 1) * 512], Bm[j])
        conv(memTb, I["memT"], Bmem)
        for j in (2, 5, 6, 7, 8, 9, 10, 11, 12, 13):
            conv(wInb[j], I["w_in"][:, j * 512:(j + 1) * 512], Bw[j])
        for b in range(3):
            for j in range(2):
                conv(wBb[b, j], I["w_br"][b, :, j * 512:(j + 1) * 512], Bb[b][j])
        for j in range(2):
            conv(wOb[j], I["w_out"][:, j * 512:(j + 1) * 512], Bo[j])

        def wsrc(piece, nkc):
            return piece.rearrange("(kc p) n -> p kc n", p=128)

        ident_f = sb(top, "ident_f", [128, 128], F32)
        ident = sb(top, "ident", [128, 128], BF16)
        Bid = Buf("ident")
        dma("sp", fw.dsem("c_id"), ident_f[:], I["ident"], writes=[Bid])
        op("dve", lambda e: e.tensor_copy(out=ident[:], in_=ident_f[:]), reads=[Bid], writes=[Bid])

        ZS = sb(top, "ZS", [128, 4, S], BF16)
        BZSn = Buf("ZSn")

        with ExitStack() as phS:
            UT = sb(phS, "UT", [128, 4, S], BF16)
            BUTp = [[Buf("UT%d_%d" % (i, j)) for j in range(8)] for i in range(4)]
            BUTall = [b for bl in BUTp for b in bl]
            ZSp = sb(phS, "ZSp", [128, 4, S], BF16)
            BZSp = Buf("ZSp")

            with ExitStack() as ph2:
                def cplx_setup(st, pref, lam_ap, shape, Bsrc, keep):
                    T = {}
                    for nm in ("dt", "th", "mg", "cs", "sn", "t1", "t2", "den"):
                        T[nm] = sb(st, pref + nm, shape, F32)
                    T.update(keep)
                    ti = sb(st, pref + "ti", shape, I32)
                    Bt = keep["buf"]
                    R, W = [Bsrc, Bt], [Bt]
                    lre, lim, ldt = lam_ap(0), lam_ap(1), lam_ap(2)
                    op("act", lambda e: e.activation(out=T["dt"][:], in_=ldt, func=AF.Exp), reads=R, writes=W)
                    op("dve", lambda e: e.tensor_tensor(out=T["th"][:], in0=lim, in1=T["dt"][:], op=ALU.mult), reads=R, writes=W)
                    op("dve", lambda e: e.tensor_tensor(out=T["t0"][:], in0=lre, in1=T["dt"][:], op=ALU.mult), reads=R, writes=W)
                    op("act", lambda e: e.activation(out=T["mg"][:], in_=T["t0"][:], func=AF.Exp), reads=R, writes=W)

                    def sin_of(dst, shift):
                        op("dve", lambda e: e.tensor_scalar(out=T["t0"][:], in0=T["th"][:], scalar1=shift, scalar2=None, op0=ALU.add), reads=R, writes=W)
                        op("dve", lambda e: e.tensor_scalar(out=T["t1"][:], in0=T["t0"][:], scalar1=1.0 / (2 * PI), scalar2=None, op0=ALU.mult), reads=R, writes=W)
                        op("dve", lambda e: e.tensor_copy(out=ti[:], in_=T["t1"][:]), reads=R, writes=W)
                        op("dve", lambda e: e.tensor_copy(out=T["t1"][:], in_=ti[:]), reads=R, writes=W)
                        op("dve", lambda e: e.scalar_tensor_tensor(out=T["t2"][:], in0=T["t1"][:], scalar=-2 * PI, in1=T["t0"][:], op0=ALU.mult, op1=ALU.add), reads=R, writes=W)
                        op("dve", lambda e: e.tensor_scalar(out=T["t1"][:], in0=T["t2"][:], scalar1=PI, scalar2=-2 * PI, op0=ALU.is_gt, op1=ALU.mult), reads=R, writes=W)
                        op("dve", lambda e: e.tensor_tensor(out=T["t2"][:], in0=T["t2"][:], in1=T["t1"][:], op=ALU.add), reads=R, writes=W)
                        op("dve", lambda e: e.tensor_scalar(out=T["t1"][:], in0=T["t2"][:], scalar1=-PI, scalar2=2 * PI, op0=ALU.is_lt, op1=ALU.mult), reads=R, writes=W)
                        op("dve", lambda e: e.tensor_tensor(out=T["t2"][:], in0=T["t2"][:], in1=T["t1"][:], op=ALU.add), reads=R, writes=W)
                        op("act", lambda e: e.activation(out=dst[:], in_=T["t2"][:], func=AF.Sin), reads=R, writes=W)
                    sin_of(T["sn"], 0.0)
                    sin_of(T["cs"], PI / 2)
                    op("dve", lambda e: e.tensor_tensor(out=T["lbr"][:], in0=T["mg"][:], in1=T["cs"][:], op=ALU.mult), reads=R, writes=W)
                    op("dve", lambda e: e.tensor_tensor(out=T["lbi"][:], in0=T["mg"][:], in1=T["sn"][:], op=ALU.mult), reads=R, writes=W)
                    op("dve", lambda e: e.tensor_tensor(out=T["den"][:], in0=lre, in1=lre, op=ALU.mult), reads=R, writes=W)
                    op("dve", lambda e: e.tensor_tensor(out=T["t0"][:], in0=lim, in1=lim, op=ALU.mult), reads=R, writes=W)
                    op("dve", lambda e: e.tensor_tensor(out=T["den"][:], in0=T["den"][:], in1=T["t0"][:], op=ALU.add), reads=R, writes=W)
                    op("dve", lambda e: e.reciprocal(out=T["den"][:], in_=T["den"][:]), reads=R, writes=W)
                    op("dve", lambda e: e.tensor_scalar(out=T["t0"][:], in0=T["lbr"][:], scalar1=-1.0, scalar2=None, op0=ALU.add), reads=R, writes=W)
                    op("dve", lambda e: e.tensor_tensor(out=T["t1"][:], in0=T["t0"][:], in1=lre, op=ALU.mult), reads=R, writes=W)
                    op("dve", lambda e: e.tensor_tensor(out=T["t2"][:], in0=T["lbi"][:], in1=lim, op=ALU.mult), reads=R, writes=W)
                    op("dve", lambda e: e.tensor_tensor(out=T["t1"][:], in0=T["t1"][:], in1=T["t2"][:], op=ALU.add), reads=R, writes=W)
                    op("dve", lambda e: e.tensor_tensor(out=T["cfr"][:], in0=T["t1"][:], in1=T["den"][:], op=ALU.mult), reads=R, writes=W)
                    op("dve", lambda e: e.tensor_tensor(out=T["t1"][:], in0=T["lbi"][:], in1=lre, op=ALU.mult), reads=R, writes=W)
                    op("dve", lambda e: e.tensor_tensor(out=T["t2"][:], in0=T["t0"][:], in1=lim, op=ALU.mult), reads=R, writes=W)
                    op("dve", lambda e: e.tensor_tensor(out=T["t1"][:], in0=T["t1"][:], in1=T["t2"][:], op=ALU.subtract), reads=R, writes=W)
                    op("dve", lambda e: e.tensor_tensor(out=T["cfi"][:], in0=T["t1"][:], in1=T["den"][:], op=ALU.mult), reads=R, writes=W)
                    return T, Bt

                def cmul(out_r, out_i, ar, ai, br, bi, t0, R, W, en="dve"):
                    op(en, lambda e: e.tensor_tensor(out=out_r, in0=ar, in1=br, op=ALU.mult), reads=R, writes=W)
                    op(en, lambda e: e.tensor_tensor(out=t0, in0=ai, in1=bi, op=ALU.mult), reads=R, writes=W)
                    op(en, lambda e: e.tensor_tensor(out=out_r, in0=out_r, in1=t0, op=ALU.subtract), reads=R, writes=W)
                    op(en, lambda e: e.tensor_tensor(out=out_i, in0=ar, in1=bi, op=ALU.mult), reads=R, writes=W)
                    op(en, lambda e: e.tensor_tensor(out=t0, in0=ai, in1=br, op=ALU.mult), reads=R, writes=W)
                    op(en, lambda e: e.tensor_tensor(out=out_i, in0=out_i, in1=t0, op=ALU.add), reads=R, writes=W)

                lamA = sb(ph2, "lamA", [128, 3, 16], F32)
                cbdA = sb(ph2, "cbdA", [128, 2, 16, 32], F32)
                keepA = {nm: sb(ph2, "A_" + nm, [128, 16], F32) for nm in ("lbr", "lbi", "cfr", "cfi", "t0")}
                keepA["buf"] = Buf("A_tmp")
                PW = sb(ph2, "PW", [128, 17, 2, 16], F32)
                AD = sb(ph2, "AD", [128, 8, 3, 16], F32)
                PWn = sb(ph2, "PWn", [128, 17, 16], F32)
                BbA = sb(ph2, "BbA", [128, 2, 16, 32], BF16)
                dskD = sb(ph2, "dskD", [128, 4, 128], F32)
                keepB = {nm: sb(ph2, "B_" + nm, [128, 512], F32) for nm in ("lbr", "lbi", "cfr", "cfi", "t0")}
                keepB["buf"] = Buf("B_tmp")
                XB0 = sb(ph2, "XB0", [128, 2, 512], F32)
                FA2 = [sb(ph2, "FA%d" % i, [128, 17, 2, 4, 32], BF16) for i in range(2)]
                BFA2 = [Buf("FA%d" % i) for i in range(2)]
                tP = [sb(ph2, "tP%d" % i, [128, 4, 32], F32) for i in range(3)]
                BtP = Buf("tP")
                BA, BB = Buf("layA"), Buf("layB")
                dsA, dsB = fw.dsem("layA"), fw.dsem("layB")
                dma("sp", dsA, lamA[:], I["lamA"], writes=[BA])
                dma("sp", dsA, cbdA[:], I["cbdA"], writes=[BA])
                dma("sp", dsB, dskD[:], I["dskD"], writes=[BB])

                def bc(ap2, n=16):
                    return ap2.unsqueeze(2).to_broadcast([128, n, 32])

                def setup_FA(T_):
                    par = T_ % 2
                    FA, BFA = FA2[par], BFA2[par]
                    Ps = slice(4 * T_, 4 * T_ + 4)
                    R_ = [BA, BtA, BtP, BFA]
                    cr, ci = cbdA[:, 0, Ps], cbdA[:, 1, Ps]
                    for a in range(17):
                        pr, pi_ = bc(PW[:, a, 0, Ps], 4), bc(PW[:, a, 1, Ps], 4)
                        op("pool", lambda e: e.tensor_tensor(out=tP[0][:], in0=cr, in1=pr, op=ALU.mult), reads=R_, writes=[BtP])
                        op("pool", lambda e: e.tensor_tensor(out=tP[1][:], in0=ci, in1=pi_, op=ALU.mult), reads=R_, writes=[BtP])
                        op("pool", lambda e, a=a: e.tensor_tensor(out=FA[:, a, 0], in0=tP[0][:], in1=tP[1][:], op=ALU.subtract), reads=R_, writes=[BFA, BtP])
                        npi = bc(PWn[:, a, Ps], 4)
                        op("pool", lambda e: e.tensor_tensor(out=tP[0][:], in0=cr, in1=npi, op=ALU.mult), reads=R_, writes=[BtP])
                        op("pool", lambda e: e.tensor_tensor(out=tP[1][:], in0=ci, in1=pr, op=ALU.mult), reads=R_, writes=[BtP])
                        op("pool", lambda e, a=a: e.tensor_tensor(out=FA[:, a, 1], in0=tP[0][:], in1=tP[1][:], op=ALU.subtract),
                           reads=R_, writes=[BFA, BtP])

                ph1 = ExitStack()
                wS = sb(ph1, "wS", [128, 8, 1024], BF16)
                BwS = Buf("wS")
                ds_w = fw.dsem("wS")
                for j in range(2):
                    dma("sp", ds_w, wS[:, :, j * 512:(j + 1) * 512], wsrc(wInb[j], 8), reads=[Bw[j]], writes=[BwS])
                xt1 = sb(ph1, "xtS", [128, 8, 512], BF16)
                Bxt1 = Buf("xtS")
                dsx1 = fw.dsem("xtS")

                def load_x(t):
                    dma("sp", dsx1, xt1[:], wsrc(xTb[t], 8), reads=[Bx[t]], writes=[Bxt1])
                load_x(0)

                with ExitStack() as tmpS:
                    bbdA = sb(tmpS, "bbdA", [128, 2, 16, 32], F32)
                    lamB = sb(tmpS, "lamB", [128, 3, 512], F32)
                    btB = sb(tmpS, "btB", [128, 2, 512], F32)
                    dma("sp", dsA, bbdA[:], I["bbdA"], writes=[BA])
                    dma("sp", dsB, lamB[:], I["lamB"].rearrange("p a t n -> p a (t n)"), writes=[BB])
                    dma("sp", dsB, btB[:], I["btB"].rearrange("p a t n -> p a (t n)"), writes=[BB])
                    TA, BtA = cplx_setup(tmpS, "A_", lambda i: lamA[:, i, :], [128, 16], BA, keepA)
                    TA = keepA
                    RA, WA_ = [BA, BtA], [BtA]
                    op("dve", lambda e: e.memset(PW[:, 0, 0, :], 1.0), reads=RA, writes=WA_)
                    op("dve", lambda e: e.memset(PW[:, 0, 1, :], 0.0), reads=RA, writes=WA_)
                    for a in range(1, 17):
                        cmul(PW[:, a, 0, :], PW[:, a, 1, :], PW[:, a - 1, 0, :], PW[:, a - 1, 1, :],
                             TA["lbr"][:], TA["lbi"][:], TA["t0"][:], RA, WA_)
                    op("dve", lambda e: e.tensor_copy(out=AD[:, 0, 0:2, :], in_=PW[:, 16, :, :]), reads=RA, writes=WA_)
                    op("dve", lambda e: e.tensor_scalar(out=PWn[:], in0=PW[:, :, 1, :], scalar1=-1.0, scalar2=None, op0=ALU.mult), reads=RA, writes=WA_)
                    for k in range(1, 8):
                        cmul(AD[:, k, 0, :], AD[:, k, 1, :], AD[:, k - 1, 0, :], AD[:, k - 1, 1, :],
                             AD[:, k - 1, 0, :], AD[:, k - 1, 1, :], TA["t0"][:], RA, WA_)
                    op("dve", lambda e: e.tensor_scalar(out=AD[:, :, 2, :], in0=AD[:, :, 1, :], scalar1=-1.0, scalar2=None, op0=ALU.mult), reads=RA, writes=WA_)
                    setup_FA(0)
                    setup_FA(1)
                    TBt, BtB_ = cplx_setup(tmpS, "B_", lambda i: lamB[:, i, :], [128, 512], BB, keepB)
                    TB = keepB
                    RB, WB_ = [BB, BtB_], [BtB_]
                    cmul(XB0[:, 0], XB0[:, 1], btB[:, 0], btB[:, 1], TB["cfr"][:], TB["cfi"][:], TB["t0"][:], RB, WB_)
                    tA = [TBt[nm][:].rearrange("p (a b) -> p a b", a=16) for nm in ("t1", "t2", "den")]
                    RAB, WAB = [BA, BtA, BtB_], [BtA, BtB_]
                    cmul(tA[0], tA[1], bbdA[:, 0], bbdA[:, 1], bc(TA["cfr"][:]), bc(TA["cfi"][:]), tA[2], RAB, WAB)
                    op("dve", lambda e: e.tensor_copy(out=BbA[:, 0], in_=tA[0]), reads=RAB, writes=WAB)
                    op("dve", lambda e: e.tensor_copy(out=BbA[:, 1], in_=tA[1]), reads=RAB, writes=WAB)

                    for t in range(NT):
                        x_t, Bx_t = xt1, Bxt1
                        for m in range(8):
                            bk = m % 8
                            for kc in range(8):
                                op("pe", lambda e, m=m, kc=kc, bk=bk: e.matmul(
                                    ps[bk][:], wS[:, kc, m * 128:(m + 1) * 128], x_t[:, kc, :],
                                    start=(kc == 0), stop=(kc == 7)),
                                   reads=[BwS, Bx_t], writes=[PS[bk]])
                            if m < 4:
                                op("act", lambda e, m=m, bk=bk: e.activation(func=AF.Copy,
                                    out=UT[:, m, :].rearrange("p (i c) -> p i c", i=L)[:, :, 32 * t:32 * t + 32],
                                    in_=ps[bk][:].rearrange("p (c i) -> p i c", i=L)),
                                   reads=[PS[bk]], writes=BUTp[m])
                            else:
                                op("act", lambda e, m=m, bk=bk: e.activation(
                                    out=ZSp[:, m - 4, :].rearrange("p (i c) -> p i c", i=L)[:, :, 32 * t:32 * t + 32],
                                    in_=ps[bk][:].rearrange("p (c i) -> p i c", i=L), func=AF.Silu),
                                   reads=[PS[bk]], writes=[BZSp])
                        if t + 1 < NT:
                            load_x(t + 1)
                    fw.barrier(skip=("pool",))
                ph1.close()
                fw.barrier(skip=("pool",))
                if "UT" in dumps:
                    dump("UT", UT[:], [128, 4, S], BUTp[3][0])
                    dump("ZS", ZSp[:], [128, 4, S], BZSp)
                EB = sb(ph2, "EB", [128, 16, 2, 128], BF16)
                KB2 = [sb(ph2, "KB%d" % i, [128, 16, 128], BF16) for i in range(2)]
                Wa = sb(ph2, "Wa", [128, 4, 2, NB], F32)
                Wb = sb(ph2, "Wb", [128, 4, 2, NB], F32)
                SP2 = [sb(ph2, "SP%d" % i, [128, 4, 2, NB], BF16) for i in range(2)]
                BEB = Buf("EB")
                BKB2 = [Buf("KB%d" % i) for i in range(2)]
                BWa, BWb = Buf("Wa"), Buf("Wb")
                BSP2 = [Buf("SP%d" % i) for i in range(2)]
                KF0 = sb(ph2, "KF0", [128, 128], F32)
                BKF = Buf("KF0")
                XB = [sb(ph2, "XBp%d" % i, [128, 2, 128], F32) for i in range(2)]
                BXB = Buf("XBp")
                gel = [sb(ph2, "gel%d" % i, [128, 512], F32) for i in range(2)]
                Bgel = Buf("gel")
                def eb_ops(T_):
                    cs = slice(T_ * 128, (T_ + 1) * 128)
                    RE = [BB, BtB_, BEB, BXB]
                    th = []
                    for a in range(16):
                        cur_r, cur_i = (XB0[:, 0, cs], XB0[:, 1, cs]) if a == 0 else (XB[a % 2][:, 0], XB[a % 2][:, 1])
                        th.append(lambda a=a, cur_r=cur_r: op("dve", lambda e: e.tensor_copy(out=EB[:, a, 0, :], in_=cur_r), reads=RE, writes=[BEB, BXB]))
                        th.append(lambda a=a, cur_i=cur_i: op("dve", lambda e: e.tensor_copy(out=EB[:, a, 1, :], in_=cur_i), reads=RE, writes=[BEB, BXB]))
                        if a < 15:
                            nxt = XB[(a + 1) % 2]
                            th.append(lambda nxt=nxt, cur_r=cur_r, cur_i=cur_i: cmul(
                                nxt[:, 0], nxt[:, 1], cur_r, cur_i, TB["lbr"][:, cs], TB["lbi"][:, cs], TB["t0"][:, cs], RE, [BXB]))
                    return th

                def emit_K(T_):
                    par = T_ % 2
                    FA, BFA, KB, BKB = FA2[par], BFA2[par], KB2[par], BKB2[par]
                    bk = 6 + (T_ % 2)
                    pk = ps[bk][:].rearrange("p (l n) -> p l n", l=16)
                    for lag in range(16):
                        for ri in range(2):
                            for m in range(4):
                                op("pe", lambda e, lag=lag, m=m, ri=ri: e.matmul(
                                    pk[32 * m:32 * m + 32, lag, :], BbA[:, ri, 4 * T_ + m, :], FA[:, lag, ri, m, :],
                                    start=(ri == 0), stop=(ri == 1), tile_position=(0, 32 * m), skip_group_check=True),
                                   reads=[BtA, BFA], writes=[PS[bk]])
                    op("dve", lambda e: e.memset(KB[:], 0.0), reads=[], writes=[BKB])
                    op("dve", lambda e: e.tensor_copy(out=KF0[:], in_=dskD[:, T_, :]), reads=[BB, BKF], writes=[BKF])
                    for m in range(4):
                        sl = slice(32 * m, 32 * m + 32)
                        op("dve", lambda e, sl=sl: e.tensor_copy(out=KB[sl, 1:16, sl], in_=pk[sl, 1:16, :]), reads=[PS[bk]], writes=[BKB])
                        op("dve", lambda e, sl=sl: e.tensor_tensor(out=KF0[sl, sl], in0=KF0[sl, sl], in1=pk[sl, 0, :], op=ALU.add), reads=[PS[bk], BKF], writes=[BKF])
                    op("dve", lambda e: e.tensor_copy(out=KB[:, 0, :], in_=KF0[:]), reads=[BKF, BKB], writes=[BKB])

                def emit_L1(T_):
                    UTv = UT[:, T_, :].rearrange("p (i c) -> p i c", i=L)
                    pws = [ps[m][:].rearrange("p (r n) -> p r n", r=2) for m in range(4)]
                    for ri in range(2):
                        for j in range(L):
                            for m in range(4):
                                rows = slice(32 * m, 32 * m + 32)
                                op("pe", lambda e, m=m, ri=ri, j=j, rows=rows: e.matmul(
                                    pws[m][:, ri, :], EB[rows, 15 - j, ri, :], UTv[rows, j, :],
                                    start=(j == 0), stop=(j == L - 1), tile_position=(32 * m, 0), skip_group_check=True),
                                   reads=[BEB] + BUTp[T_], writes=[PS[m]])
                    for m in range(4):
                        op("act", lambda e, m=m: e.activation(out=Wa[:, m], in_=pws[m], func=AF.Copy), reads=[PS[m]], writes=[BWa])

                def dbl_ops(T_):
                    par = T_ % 2
                    src, dst, Bs, Bd = Wa, Wb, BWa, BWb
                    SP, BSP = SP2[par], BSP2[par]
                    th = []
                    for k in range(8):
                        d = 1 << k
                        for m in range(4):
                            P_ = 4 * T_ + m
                            aR = AD[:, k, 0, P_:P_ + 1]
                            aI = AD[:, k, 1, P_:P_ + 1]
                            nI = AD[:, k, 2, P_:P_ + 1]
                            R_, W_ = [Bs, Bd, BtA], [Bd]

                            def step(m=m, d=d, aR=aR, aI=aI, nI=nI, src=src, dst=dst, R_=R_, W_=W_):
                                op("dve", lambda e: e.scalar_tensor_tensor(
                                    out=dst[:, m, 0, d:], in0=src[:, m, 0, :NB - d], scalar=aR, in1=src[:, m, 0, d:], op0=ALU.mult, op1=ALU.add), reads=R_, writes=W_)
                                op("dve", lambda e: e.scalar_tensor_tensor(
                                    out=dst[:, m, 0, d:], in0=src[:, m, 1, :NB - d], scalar=nI, in1=dst[:, m, 0, d:], op0=ALU.mult, op1=ALU.add), reads=R_, writes=W_)
                                op("dve", lambda e: e.scalar_tensor_tensor(
                                    out=dst[:, m, 1, d:], in0=src[:, m, 0, :NB - d], scalar=aI, in1=src[:, m, 1, d:], op0=ALU.mult, op1=ALU.add), reads=R_, writes=W_)
                                op("dve", lambda e: e.scalar_tensor_tensor(
                                    out=dst[:, m, 1, d:], in0=src[:, m, 1, :NB - d], scalar=aR, in1=dst[:, m, 1, d:], op0=ALU.mult, op1=ALU.add), reads=R_, writes=W_)
                                op("dve", lambda e: e.tensor_copy(out=dst[:, m, :, :d], in_=src[:, m, :, :d]), reads=R_, writes=W_)
                            th.append(step)
                        src, dst, Bs, Bd = dst, src, Bd, Bs

                    def fin(src=src, Bs=Bs):
                        op("dve", lambda e: e.memset(SP[:, :, :, 0:1], 0.0), reads=[], writes=[BSP])
                        op("dve", lambda e: e.tensor_copy(out=SP[:, :, :, 1:], in_=src[:, :, :, :NB - 1]), reads=[Bs], writes=[BSP])
                    th.append(fin)
                    return th

                def emit_L03(T_, inter):
                    par = T_ % 2
                    FA, BFA, KB, BKB, SP, BSP = FA2[par], BFA2[par], KB2[par], BKB2[par], SP2[par], BSP2[par]
                    UTv = UT[:, T_, :].rearrange("p (i c) -> p i c", i=L)
                    nint = (len(inter) + 7) // 8
                    for n_, ip in enumerate(range(7, -1, -1)):
                        bk = 4 + (ip % 4)
                        py = ps[bk][:].rearrange("p (i n) -> p i n", i=2)
                        i0 = 2 * ip
                        for lag in range(i0 + 2):
                            ist = i0 if lag <= i0 else i0 + 1
                            ni = i0 + 2 - ist
                            op("pe", lambda e, lag=lag, ist=ist, ni=ni, py=py: e.matmul(
                                py[:, ist - i0:ist - i0 + ni, :], KB[:, lag, :], UTv[:, ist - lag:ist - lag + ni, :],
                                start=(lag == 0), stop=False, skip_group_check=True),
                               reads=[BKB] + BUTp[T_][:ip + 1], writes=[PS[bk]])
                        for il in range(2):
                            for ri in range(2):
                                for m in range(4):
                                    last = (m == 3 and il == 1 and ri == 1)
                                    op("pe", lambda e, m=m, il=il, ri=ri, py=py, last=last: e.matmul(
                                        py[32 * m:32 * m + 32, il, :], FA[:, i0 + il + 1, ri, m, :], SP[:, m, ri, :],
                                        start=False, stop=last, tile_position=(0, 32 * m), skip_group_check=True),
                                       reads=[BFA, BSP], writes=[PS[bk]])
                        g0, g1 = gel[0][:], gel[1][:]
                        pyf = ps[bk][:]
                        op("act", lambda e, pyf=pyf: e.activation(out=g0, in_=pyf, func=AF.Square), reads=[PS[bk], Bgel], writes=[Bgel])
                        op("dve", lambda e: e.tensor_scalar(out=g0, in0=g0, scalar1=0.044715, scalar2=1.0, op0=ALU.mult, op1=ALU.add), reads=[Bgel], writes=[Bgel])
                        op("dve", lambda e, pyf=pyf: e.tensor_tensor(out=g0, in0=g0, in1=pyf, op=ALU.mult), reads=[Bgel, PS[bk]], writes=[Bgel])
                        op("act", lambda e: e.activation(out=g1, in_=g0, func=AF.Sigmoid, scale=1.5957691216057308), reads=[Bgel], writes=[Bgel])
                        op("dve", lambda e, py=py, i0=i0: e.tensor_tensor(
                            out=UTv[:, i0:i0 + 2, :], in0=gel[1][:].rearrange("p (i n) -> p i n", i=2), in1=py, op=ALU.mult),
                           reads=[Bgel, PS[bk]], writes=[BUTp[T_][ip]])
                        for f_ in inter[n_ * nint:(n_ + 1) * nint]:
                            f_()

                def run(th):
                    for f_ in th:
                        f_()

                run(eb_ops(0))
                emit_K(0)
                emit_L1(0)
                run(eb_ops(1))
                run(dbl_ops(0))
                emit_K(1)
                emit_L1(1)
                emit_L03(0, dbl_ops(1) + eb_ops(2))
                setup_FA(2)
                emit_K(2)
                emit_L1(2)
                emit_L03(1, dbl_ops(2) + eb_ops(3))
                setup_FA(3)
                emit_K(3)
                emit_L1(3)
                emit_L03(2, dbl_ops(3))
                emit_L03(3, [])
                if "YG" in dumps:
                    dump("YG", UT[:], [128, 4, S], BUTp[3][0])
                fw.barrier()
                ph2.close()
                wG = sb(ph2, "wG", [128, 4, 1024], BF16)
                BwG = Buf("wG")
                dsG = fw.dsem("wG")
                for j in range(2):
                    dma("sp", dsG, wG[:, :, j * 512:(j + 1) * 512], wsrc(wGb[j], 4), reads=[Bg[j]], writes=[BwG])
                sg = [sb(ph2, "sg%d" % i, [128, 512], F32) for i in range(2)]
                Bsg = [Buf("sg%d" % i) for i in range(2)]
                cnt = 0
                for t in range(NT):
                    cols = slice(t * 512, (t + 1) * 512)
                    for f in range(4):
                        bv, bg_ = (2 * cnt) % 8, (2 * cnt + 1) % 8
                        s_ = cnt % 2
                        cnt += 1
                        for kc in range(4):
                            op("pe", lambda e, f=f, kc=kc, bv=bv: e.matmul(
                                ps[bv][:], wG[:, kc, f * 128:(f + 1) * 128], UT[:, kc, cols], start=(kc == 0), stop=(kc == 3)),
                               reads=[BwG] + BUTall, writes=[PS[bv]])
                        for kc in range(4):
                            op("pe", lambda e, f=f, kc=kc, bg_=bg_: e.matmul(
                                ps[bg_][:], wG[:, kc, 512 + f * 128:512 + (f + 1) * 128], UT[:, kc, cols], start=(kc == 0), stop=(kc == 3)),
                               reads=[BwG] + BUTall, writes=[PS[bg_]])
                        op("act", lambda e, bg_=bg_, s_=s_: e.activation(out=sg[s_][:], in_=ps[bg_][:], func=AF.Sigmoid),
                           reads=[PS[bg_]], writes=[Bsg[s_]])
                        op("dve", lambda e, bv=bv, s_=s_: e.tensor_tensor(out=sg[s_][:], in0=sg[s_][:], in1=ps[bv][:], op=ALU.mult),
                           reads=[PS[bv], Bsg[s_]], writes=[Bsg[s_]])
                        op("dve", lambda e, f=f, s_=s_, t=t: e.tensor_tensor(
                            out=ZS[:, f, :].rearrange("p (c i) -> p i c", i=L)[:, 2 * t:2 * t + 2, :],
                            in0=sg[s_][:].rearrange("p (i c) -> p i c", i=2), in1=ZSp[:, f, cols].rearrange("p (i c) -> p i c", i=2), op=ALU.mult),
                           reads=[Bsg[s_], BZSp, BZSn], writes=[BZSn])
            fw.barrier()
        if "YS" in dumps:
            dump("YS", ZS[:], [128, 4, S], BZSn)
        if stop_after == "S":
            fw.barrier()
            return nc, dump_d

        KT = sb(top, "KT", [128, 4, S], BF16)
        VA = sb(top, "VA", [128, 32, 4, 130], BF16)
        MKT = sb(top, "MKT", [128, 4, 256], BF16)
        MVA = sb(top, "MVA", [128, 2, 4, 130], BF16)
        BKT, BVA, BMK = Buf("KT"), Buf("VA"), Buf("MK")
        op("dve", lambda e: e.memset(VA[:, :, :, 128:130], 1.0), reads=[], writes=[BVA])
        op("dve", lambda e: e.memset(MVA[:, :, :, 128:130], 1.0), reads=[], writes=[BMK])
        with ExitStack() as phK:
            wKV = sb(phK, "wKV", [128, 8, 1024], BF16)
            wM = sb(phK, "wM", [128, 8, 1024], BF16)
            mT = sb(phK, "mT", [128, 8, 256], BF16)
            BwKV, BwM, BmT = Buf("wKV"), Buf("wM"), Buf("mT")
            dsk_, dsm_, dst_ = fw.dsem("wKV"), fw.dsem("wM"), fw.dsem("mT")
            for j in range(2):
                dma("sp", dsk_, wKV[:, :, j * 512:(j + 1) * 512], wsrc(wInb[3 + j], 8), reads=[Bw[3 + j]], writes=[BwKV])
            for j in range(2):
                dma("sp", dsm_, wM[:, :, j * 512:(j + 1) * 512], wsrc(wMb[j], 8), reads=[Bm[j]], writes=[BwM])
            dma("sp", dst_, mT[:], wsrc(memTb, 8), reads=[Bmem], writes=[BmT])
            xt = [sb(phK, "xtK%d" % i, [128, 8, 512], BF16) for i in range(2)]
            Bxt = [Buf("xtK%d" % i) for i in range(2)]
            dsx = [fw.dsem("xtK%d" % i) for i in range(2)]

            def load_xk(t):
                dma("sp", dsx[t % 2], xt[t % 2][:], wsrc(xTb[t], 8), reads=[Bx[t]], writes=[Bxt[t % 2]])
            load_xk(0)
            cnt = 0
            for h in range(4):
                bk = cnt % 8
                cnt += 1
                for kc in range(8):
                    op("pe", lambda e, h=h, kc=kc, bk=bk: e.matmul(ps[bk][:, 0:256], wM[:, kc, h * 128:(h + 1) * 128], mT[:, kc, :],
                                                                   start=(kc == 0), stop=(kc == 7)), reads=[BwM, BmT], writes=[PS[bk]])
                op("act", lambda e, h=h, bk=bk: e.activation(out=MKT[:, h, :], in_=ps[bk][:, 0:256], func=AF.Copy), reads=[PS[bk]], writes=[BMK])
            for mb in range(2):
                bk = cnt % 8
                cnt += 1
                for kc in range(8):
                    op("pe", lambda e, mb=mb, kc=kc, bk=bk: e.matmul(ps[bk][:], mT[:, kc, mb * 128:(mb + 1) * 128], wM[:, kc, 512:1024],
                                                                     start=(kc == 0), stop=(kc == 7)), reads=[BwM, BmT], writes=[PS[bk]])
                op("act", lambda e, mb=mb, bk=bk: e.activation(out=MVA[:, mb, :, 0:128], in_=ps[bk][:].rearrange("p (h e) -> p h e", h=4), func=AF.Copy),
                   reads=[PS[bk]], writes=[BMK])
            for t in range(NT):
                if t + 1 < NT:
                    load_xk(t + 1)
                x_t, Bx_t = xt[t % 2], Bxt[t % 2]
                cols = slice(t * 512, (t + 1) * 512)
                for h in range(4):
                    bk = cnt % 8
                    cnt += 1
                    for kc in range(8):
                        op("pe", lambda e, h=h, kc=kc, bk=bk: e.matmul(ps[bk][:], wKV[:, kc, h * 128:(h + 1) * 128], x_t[:, kc, :],
                                                                       start=(kc == 0), stop=(kc == 7)), reads=[BwKV, Bx_t], writes=[PS[bk]])
                    op("act", lambda e, h=h, bk=bk: e.activation(out=KT[:, h, cols], in_=ps[bk][:], func=AF.Copy), reads=[PS[bk]], writes=[BKT])
                for tb in range(4):
                    bk = cnt % 8
                    cnt += 1
                    for kc in range(8):
                        op("pe", lambda e, tb=tb, kc=kc, bk=bk: e.matmul(ps[bk][:], x_t[:, kc, tb * 128:(tb + 1) * 128], wKV[:, kc, 512:1024],
                                                                         start=(kc == 0), stop=(kc == 7)), reads=[BwKV, Bx_t], writes=[PS[bk]])
                    op("dve", lambda e, tb=tb, bk=bk, t=t: e.tensor_copy(out=VA[:, 4 * t + tb, :, 0:128], in_=ps[bk][:].rearrange("p (h e) -> p h e", h=4)),
                       reads=[PS[bk]], writes=[BVA])
            fw.barrier()
        if "KT" in dumps:
            dump("KT", KT[:], [128, 4, S], BKT)
            dump("VA", VA[:], [128, 32, 4, 130], BVA)
            dump("MKT", MKT[:], [128, 4, 256], BMK)
            dump("MVA", MVA[:], [128, 2, 4, 130], BMK)
        if stop_after == "KV":
            fw.barrier()
            return nc, dump_d

        with ExitStack() as phF:
            farb = sb(phF, "farb", [128, 4], F32)
            lamv = sb(phF, "lamv", [128, 4, 64], F32)
            subw = sb(phF, "subw", [128, 128], F32)
            lng = sb(phF, "lng", [128, D], F32)
            lnb = sb(phF, "lnb", [128, D], F32)
            NBh = sb(phF, "NBh", [128, 4, 256], BF16)
            NBl = sb(phF, "NBl", [128, 4, 256], BF16)
            lsc = sb(phF, "lsc", [128, 8], F32)
            BC = Buf("constF")
            dsc = fw.dsem("constF")
            tmpF = ExitStack()
            tb_f = sb(tmpF, "tb_f", [128, 4, 256], F32)
            mk_f = sb(tmpF, "mk_f", [128, 4, 256], F32)
            for dst_, nm in ((tb_f, "tbias"), (mk_f, "maskc"), (farb, "farb"), (lamv, "lamv"), (subw, "subw"), (lng, "lng"), (lnb, "lnb")):
                dma("sp", dsc, dst_[:], I[nm], writes=[BC])
            RC, WC = [BC], [BC]
            for h in range(4):
                op("dve", lambda e, h=h: e.tensor_scalar(out=tb_f[:, h, :], in0=tb_f[:, h, :], scalar1=farb[:, h:h + 1], scalar2=None, op0=ALU.subtract), reads=RC, writes=WC)
            op("dve", lambda e: e.tensor_tensor(out=tb_f[:], in0=tb_f[:], in1=mk_f[:], op=ALU.add), reads=RC, writes=WC)
            op("dve", lambda e: e.tensor_copy(out=NBh[:], in_=tb_f[:]), reads=RC, writes=WC)
            op("dve", lambda e: e.tensor_copy(out=mk_f[:], in_=NBh[:]), reads=RC, writes=WC)
            op("dve", lambda e: e.tensor_tensor(out=mk_f[:], in0=tb_f[:], in1=mk_f[:], op=ALU.subtract), reads=RC, writes=WC)
            op("dve", lambda e: e.tensor_copy(out=NBl[:], in_=mk_f[:]), reads=RC, writes=WC)
            fw.barrier()
            tmpF.close()
            op("dve", lambda e: e.tensor_tensor(out=lamv[:, 0, :], in0=lamv[:, 0, :], in1=lamv[:, 1, :], op=ALU.mult), reads=RC, writes=WC)
            op("dve", lambda e: e.tensor_tensor(out=lamv[:, 2, :], in0=lamv[:, 2, :], in1=lamv[:, 3, :], op=ALU.mult), reads=RC, writes=WC)
            op("dve", lambda e: e.reduce_sum(out=lsc[:, 1:2], in_=lamv[:, 0, :], axis=AX.X), reads=RC, writes=WC)
            op("dve", lambda e: e.reduce_sum(out=lsc[:, 2:3], in_=lamv[:, 2, :], axis=AX.X), reads=RC, writes=WC)
            op("act", lambda e: e.activation(out=lsc[:, 3:5], in_=lsc[:, 1:3], func=AF.Exp), reads=RC, writes=WC)
            op("dve", lambda e: e.tensor_tensor(out=lsc[:, 0:1], in0=lsc[:, 4:5], in1=lsc[:, 3:4], op=ALU.subtract), reads=RC, writes=WC)
            op("dve", lambda e: e.tensor_scalar(out=lsc[:, 0:1], in0=lsc[:, 0:1], scalar1=-LAMBDA_INIT, scalar2=None, op0=ALU.add), reads=RC, writes=WC)
            op("dve", lambda e: e.tensor_scalar(out=subw[:], in0=subw[:], scalar1=1.0 - LAMBDA_INIT, scalar2=None, op0=ALU.mult), reads=RC, writes=WC)

            xq = sb(phF, "xq", [128, 8, 512], BF16)
            Bxq = Buf("xq")
            dsxq = fw.dsem("xq")
            NW = 4
            wt = [sb(phF, "wt%d" % i, [128, 8, 512], BF16) for i in range(NW)]
            Bwt = [Buf("wt%d" % i) for i in range(NW)]
            dswt = [fw.dsem("wt%d" % i) for i in range(NW)]
            wctr = [0]

            gseq = [(wInb[2], 8, Bw[2]), (wInb[5], 8, Bw[5]), (wInb[6], 8, Bw[6]), (wInb[7], 8, Bw[7])]
            for br_ in range(3):
                for half_ in range(2):
                    gseq.append((wInb[8 + 2 * br_ + half_], 8, Bw[8 + 2 * br_ + half_]))
                    gseq.append((wBb[br_, half_], 4, Bb[br_][half_]))
            gseq += [(wOb[0], 8, Bo[0]), (wOb[1], 8, Bo[1])]
            wseq = gseq * NT
            wstate = {"issued": 0, "consumed": 0}

            def _issue_w(k):
                piece, nkc, bsrc = wseq[k]
                i = k % NW
                dma("sp", dswt[i], wt[i][:, 0:nkc, :], wsrc(piece, nkc), reads=[bsrc], writes=[Bwt[i]])

            def next_w():
                k = wstate["consumed"]
                while wstate["issued"] < min(len(wseq), k + 3):
                    _issue_w(wstate["issued"])
                    wstate["issued"] += 1
                wstate["consumed"] += 1
                return wt[k % NW], Bwt[k % NW]

            QT = sb(phF, "QT", [128, 4, 512], BF16)
            MQT = QT
            scr = sb(phF, "scr", [128, 4096], F32)
            Bscr = Buf("scr")
            ZD = scr[:, 0:2048].rearrange("p (q f) -> p q f", q=4)
            ZM = ZD
            BQT = Buf("QT")
            BMQ, BZD, BZM = BQT, Bscr, Bscr
            NP = 4
            PT_all = sb(phF, "PT", [128, NP, 512], BF16)
            PT = [PT_all[:, i, :] for i in range(NP)]
            BPT = [Buf("PT%d" % i) for i in range(NP)]
            pctr = [0]
            YD = scr[:, 2048:3072].bitcast(BF16).rearrange("p (q f) -> p q f", q=4)
            YM = scr[:, 3072:4096].bitcast(BF16).rearrange("p (q f) -> p q f", q=4)
            YDT = sb(phF, "YDT", [128, 4, 512], BF16)
            YMT = sb(phF, "YMT", [128, 4, 512], BF16)
            BYD, BYM, BYDT, BYMT = Bscr, Bscr, Buf("YDT"), Buf("YMT")
            xres = YDT[:].rearrange("p h n -> p (h n)").bitcast(F32)
            Bxres = BYDT
            sm = sb(phF, "sm", [128, 32], F32)
            Bsm = Buf("sm")
            od = [None, sb(phF, "od1", [128, 128], F32)]
            Bod = Buf("od")
            od4 = sb(phF, "od4", [128, 4, 128], F32)
            OS = sb(phF, "OS", [128, 2, 520], F32)
            BOS = Buf("OS")
            ACC = scr[:].rearrange("p (q f) -> p q f", q=8)
            BACC = Bscr
            MT = sb(phF, "MT", [128, 8, 512], BF16)
            BMT = Buf("MT")
            sgf_all = sb(phF, "sgf", [128, 2, 512], F32)
            sgf = [sgf_all[:, 0, :], sgf_all[:, 1, :]]
            Bsgf = [Buf("sgf%d" % i) for i in range(2)]

            dsxr = fw.dsem("xres")
            dsout = [fw.dsem("out0"), fw.dsem("out1")]
            stats2 = [sb(phF, "stats%d" % i, [128, 2, 6], F32) for i in range(2)]
            mv2 = [sb(phF, "mv%d" % i, [128, 4], F32) for i in range(2)]
            Bst = [Buf("stats%d" % i) for i in range(2)]

            pctr_bank = [0]

            def gbank():
                b = (0, 1, 2, 3)[pctr_bank[0] % 4]
                pctr_bank[0] += 1
                return b

            OB = [[4, 5], [6, 7]]

            def oreg(c, ql):
                if ql < 3:
                    return ps[OB[c][0]][:, ql * 130:ql * 130 + 130], OB[c][0]
                return ps[OB[c][1]][:, 0:130], OB[c][1]

            def emit_qproj():
                wq, Bwq = next_w()
                for h in range(4):
                    bk = gbank()
                    for kc in range(8):
                        op("pe", lambda e, h=h, kc=kc, bk=bk: e.matmul(ps[bk][:], wq[:, kc, h * 128:(h + 1) * 128], xq[:, kc, :],
                                                                       start=(kc == 0), stop=(kc == 7)), reads=[Bwq, Bxq], writes=[PS[bk]])
                    op("act", lambda e, h=h, bk=bk: e.activation(out=QT[:, h, :], in_=ps[bk][:], func=AF.Copy, scale=0.125), reads=[PS[bk]], writes=[BQT])
                wz, Bwz = next_w()
                for ql in range(4):
                    bk = gbank()
                    for kc in range(8):
                        op("pe", lambda e, ql=ql, kc=kc, bk=bk: e.matmul(ps[bk][:], xq[:, kc, ql * 128:(ql + 1) * 128], wz[:, kc, :],
                                                                         start=(kc == 0), stop=(kc == 7)), reads=[Bwz, Bxq], writes=[PS[bk]])
                    op("act", lambda e, ql=ql, bk=bk: e.activation(out=ZD[:, ql, :], in_=ps[bk][:], func=AF.Silu), reads=[PS[bk]], writes=[BZD])

            pend_ln = []
            for G in range(NT):
                if G == 0:
                    dma("sp", dsxq, xq[:], wsrc(xTb[G], 8), reads=[Bx[G]], writes=[Bxq])
                if G == 0:
                    emit_qproj()
                if stop_after == "F1":
                    dump("QT", QT[:], [128, 4, 512], BQT)
                    dump("ZD", scr[:, 0:2048], [128, 2048], Bscr)
                    fw.barrier()
                    return nc, dump_d
                nkb = 4 * G + 4
                steps = [(h, kb) for h in range(4) for kb in range(nkb)]
                SBANKS = ((0, 1), (2, 3))

                def emit_score(i):
                    h, kb = steps[i]
                    pair = i % 2
                    ql0 = max(0, kb - 4 * G)
                    cs_ = slice(ql0 * 128, 512)
                    nq = [ql for ql in range(4) if 4 * G + ql in (kb, kb + 1)]
                    for c in range(2):
                        rows = slice(64 * c, 64 * c + 64)
                        bk = SBANKS[pair][c]
                        op("pe", lambda e, h=h, kb=kb, bk=bk, cs_=cs_, rows=rows, nq=nq: e.matmul(
                            ps[bk][:, cs_], KT[rows, h, kb * 128:(kb + 1) * 128], QT[rows, h, cs_],
                            start=True, stop=(len(nq) == 0)), reads=[BKT, BQT], writes=[PS[bk]])
                    if nq:
                        qa = nq[0] * 128
                        qrel0 = (4 * G + nq[0] - kb) * 128
                        w_ = 128 * len(nq)
                        for c in range(2):
                            bk = SBANKS[pair][c]
                            for hl, NBx in enumerate((NBh, NBl)):
                                op("pe", lambda e, h=h, bk=bk, qa=qa, qrel0=qrel0, w_=w_, NBx=NBx, hl=hl: e.matmul(
                                    ps[bk][:, qa:qa + w_], ident[:], NBx[:, h, qrel0:qrel0 + w_],
                                    start=False, stop=(hl == 1)), reads=[Bid, BC], writes=[PS[bk]])
                    src2 = (psA if pair == 0 else psB)[:].rearrange("p (c n) -> p c n", c=2)[:, :, cs_]
                    dst2 = PT_all[:, 2 * pair:2 * pair + 2, cs_]
                    b0_, b1_ = SBANKS[pair]
                    op("act", lambda e, h=h, src2=src2, dst2=dst2: e.activation(
                        out=dst2, in_=src2, func=AF.Exp, bias=farb[:, h:h + 1], scale=1.0),
                       reads=[PS[b0_], PS[b1_], BC], writes=[BPT[2 * pair], BPT[2 * pair + 1]])

                def emit_pv(i):
                    h, kb = steps[i]
                    pair = i % 2
                    ql0 = max(0, kb - 4 * G)
                    for c in range(2):
                        pi = 2 * pair + c
                        for ql in range(ql0, 4):
                            oap, obk = oreg(c, ql)
                            st_ = (kb == 0 and ql in (0, 3))
                            last = (kb == 4 * G + ql)
                            op("pe", lambda e, h=h, kb=kb, ql=ql, pi=pi, oap=oap, st_=st_, last=last: e.matmul(
                                oap, PT[pi][:, ql * 128:(ql + 1) * 128], VA[:, kb, h, 0:130], start=st_, stop=last, skip_group_check=True),
                               reads=[BPT[pi], BVA], writes=[PS[obk]])
                    if kb == nkb - 1:
                        for pb in list(pend_b):
                            emit_norm_b(pb[0])
                            pend_b.remove(pb)
                        emit_norm_a(h)
                        pend_b.append([h, 10])

                def emit_norm_a(h):
                    for c in range(2):
                        op("dve", lambda e, c=c: e.tensor_copy(out=OS[:, c, 0:390], in_=ps[OB[c][0]][:, 0:390]), reads=[PS[OB[c][0]]], writes=[BOS])
                        op("dve", lambda e, c=c: e.tensor_copy(out=OS[:, c, 390:520], in_=ps[OB[c][1]][:, 0:130]), reads=[PS[OB[c][1]]], writes=[BOS])
                    R_, W_ = [BOS, Bsm, Bod, BC, BZD], [Bsm, Bod]
                    dens = OS[:].rearrange("p c (q e) -> p c q e", q=4)[:, :, :, 128]
                    op("dve", lambda e: e.reciprocal(out=sm[:, 0:8].rearrange("p (c q) -> p c q", c=2), in_=dens), reads=R_, writes=W_)
                    op("dve", lambda e: e.tensor_scalar(out=sm[:, 4:8], in0=sm[:, 4:8], scalar1=lsc[:, 0:1], scalar2=None, op0=ALU.mult), reads=R_, writes=W_)
                    for ql in range(4):
                        o0 = OS[:, 0, ql * 130:ql * 130 + 128]
                        o1 = OS[:, 1, ql * 130:ql * 130 + 128]
                        op("dve", lambda e, o0=o0, ql=ql: e.tensor_scalar(out=od4[:, ql, :], in0=o0, scalar1=sm[:, ql:ql + 1], scalar2=None, op0=ALU.mult), reads=R_, writes=W_)
                        op("dve", lambda e, o1=o1, ql=ql: e.scalar_tensor_tensor(out=od4[:, ql, :], in0=o1, scalar=sm[:, 4 + ql:5 + ql], in1=od4[:, ql, :], op0=ALU.mult, op1=ALU.add), reads=R_, writes=W_)
                        op("dve", lambda e, ql=ql: e.scalar_tensor_tensor(out=od[1][:], in0=od4[:, ql, :], scalar=1.0, in1=od4[:, ql, :], op0=ALU.mult, op1=ALU.mult,
                                                                         accum_out=sm[:, 8 + ql:9 + ql]), reads=R_, writes=W_)
                    op("dve", lambda e: e.tensor_scalar(out=sm[:, 12:16], in0=sm[:, 8:12], scalar1=1.0 / 128.0, scalar2=1e-5, op0=ALU.mult, op1=ALU.add), reads=R_, writes=W_)

                def emit_norm_b(h):
                    R_, W_ = [BOS, Bsm, Bod, BC, BZD], [Bsm, Bod]
                    op("act", lambda e: e.activation(out=sm[:, 16:20], in_=sm[:, 12:16], func=AF.Ln), reads=R_, writes=W_)
                    op("act", lambda e: e.activation(out=sm[:, 20:24], in_=sm[:, 16:20], func=AF.Exp, scale=-0.5), reads=R_, writes=W_)
                    for ql in range(4):
                        op("dve", lambda e, ql=ql: e.scalar_tensor_tensor(out=od4[:, ql, :], in0=od4[:, ql, :], scalar=sm[:, 20 + ql:21 + ql], in1=subw[:], op0=ALU.mult, op1=ALU.mult), reads=R_, writes=W_)
                    op("dve", lambda e, h=h: e.tensor_tensor(out=YD[:, :, h * 128:(h + 1) * 128], in0=od4[:], in1=ZD[:, :, h * 128:(h + 1) * 128], op=ALU.mult),
                       reads=R_ + [BYD], writes=[BYD, Bod])

                pend_b = []
                emit_score(0)
                for i in range(len(steps)):
                    if i + 1 < len(steps):
                        emit_score(i + 1)
                    emit_pv(i)
                    if pend_ln and (i == 10 or i == len(steps) - 1):
                        for f_ in pend_ln:
                            f_()
                        del pend_ln[:]
                    for pb in list(pend_b):
                        pb[1] -= 1
                        if pb[1] <= 0 or i == len(steps) - 1:
                            emit_norm_b(pb[0])
                            pend_b.remove(pb)

                if stop_after == "F2":
                    dump("YD", scr[:, 2048:3072], [128, 1024], Bscr)
                    dump("QT", QT[:], [128, 4, 512], BQT)
                    dump("ZD", scr[:, 0:2048], [128, 2048], Bscr)
                    op("dve", lambda e: e.tensor_copy(out=sgf[0][:], in_=ps[4][:]), reads=[PS[4]], writes=[Bsgf[0]])
                    op("dve", lambda e: e.tensor_copy(out=sgf[1][:], in_=ps[6][:]), reads=[PS[6]], writes=[Bsgf[1]])
                    dump("O0", sgf[0][:], [128, 512], Bsgf[0])
                    dump("O1", sgf[1][:], [128, 512], Bsgf[1])
                    dump("PT", PT[0], [128, 512], BPT[0])
                    dump("sm", sm[:], [128, 32], Bsm)
                    dump("lsc", lsc[:], [128, 8], BC)
                    fw.barrier()
                    return nc, dump_d
                wmq, Bwmq = next_w()
                for h in range(4):
                    bk = gbank()
                    for kc in range(8):
                        op("pe", lambda e, h=h, kc=kc, bk=bk: e.matmul(ps[bk][:], wmq[:, kc, h * 128:(h + 1) * 128], xq[:, kc, :],
                                                                       start=(kc == 0), stop=(kc == 7)), reads=[Bwmq, Bxq], writes=[PS[bk]])
                    op("act", lambda e, h=h, bk=bk: e.activation(out=MQT[:, h, :], in_=ps[bk][:], func=AF.Copy, scale=128.0 ** -0.5), reads=[PS[bk]], writes=[BMQ])
                wzm, Bwzm = next_w()
                for ql in range(4):
                    bk = gbank()
                    for kc in range(8):
                        op("pe", lambda e, ql=ql, kc=kc, bk=bk: e.matmul(ps[bk][:], xq[:, kc, ql * 128:(ql + 1) * 128], wzm[:, kc, :],
                                                                         start=(kc == 0), stop=(kc == 7)), reads=[Bwzm, Bxq], writes=[PS[bk]])
                    op("act", lambda e, ql=ql, bk=bk: e.activation(out=ZM[:, ql, :], in_=ps[bk][:], func=AF.Silu), reads=[PS[bk]], writes=[BZM])

                for h in range(4):
                    for mb in range(2):
                        bk = gbank()
                        op("pe", lambda e, h=h, mb=mb, bk=bk: e.matmul(ps[bk][:], MKT[:, h, mb * 128:(mb + 1) * 128], MQT[:, h, :], start=True, stop=True),
                           reads=[BMK, BMQ], writes=[PS[bk]])
                        pi = pctr[0] % NP
                        pctr[0] += 1
                        op("act", lambda e, bk=bk, pi=pi: e.activation(out=PT[pi][:], in_=ps[bk][:], func=AF.Exp), reads=[PS[bk]], writes=[BPT[pi]])
                        for ql in range(4):
                            oap, obk = oreg(0, ql)
                            op("pe", lambda e, h=h, mb=mb, ql=ql, pi=pi, oap=oap: e.matmul(
                                oap, PT[pi][:, ql * 128:(ql + 1) * 128], MVA[:, mb, h, 0:130], start=(mb == 0 and ql in (0, 3)), stop=(mb == 1), skip_group_check=True),
                               reads=[BPT[pi], BMK], writes=[PS[obk]])
                    for ql in range(4):
                        o0, b0 = oreg(0, ql)
                        R_, W_ = [PS[b0], Bsm, BZM], [Bsm]
                        op("dve", lambda e, o0=o0: e.reciprocal(out=sm[:, 8:9], in_=o0[:, 128:129]), reads=R_, writes=W_)
                        op("dve", lambda e, o0=o0, ql=ql, h=h: e.scalar_tensor_tensor(
                            out=YM[:, ql, h * 128:(h + 1) * 128], in0=o0[:, 0:128], scalar=sm[:, 8:9], in1=ZM[:, ql, h * 128:(h + 1) * 128], op0=ALU.mult, op1=ALU.mult),
                           reads=R_ + [BYM], writes=[BYM, Bsm])

                if stop_after == "F3":
                    dump("YM", scr[:, 3072:4096], [128, 1024], Bscr)
                    fw.barrier()
                    return nc, dump_d
                for (Ysrc, Bs_, Ydst, Bd_) in ((YD, BYD, YDT, BYDT), (YM, BYM, YMT, BYMT)):
                    for kc in range(4):
                        bk = gbank()
                        pst = ps[bk][:].bitcast(BF16)
                        for ql in range(4):
                            op("pe", lambda e, kc=kc, ql=ql, pst=pst, Ysrc=Ysrc: e.transpose(
                                pst[:, ql * 128:(ql + 1) * 128], Ysrc[:, ql, kc * 128:(kc + 1) * 128], ident[:]),
                               reads=[Bs_, Bid], writes=[PS[bk]])
                        op("dve", lambda e, kc=kc, pst=pst, Ydst=Ydst: e.tensor_copy(out=Ydst[:, kc, :], in_=pst[:, 0:512]), reads=[PS[bk]], writes=[Bd_])

                if stop_after == "F4":
                    dump("YDT", YDT[:], [128, 4, 512], BYDT)
                    dump("YMT", YMT[:], [128, 4, 512], BYMT)
                    fw.barrier()
                    return nc, dump_d
                for br in range(3):
                    if br == 0:
                        ysrc = lambda kc, G=G: ZS[:, kc, G * 512:(G + 1) * 512]
                        By = BZSn
                    elif br == 1:
                        ysrc = lambda kc: YDT[:, kc, :]
                        By = BYDT
                    else:
                        ysrc = lambda kc: YMT[:, kc, :]
                        By = BYMT
                    for half in range(2):
                        wg_, Bwg = next_w()
                        wb_, Bwb = next_w()
                        for fl in range(4):
                            ft = half * 4 + fl
                            bb_, bg_ = gbank(), gbank()
                            for kc in range(4):
                                op("pe", lambda e, kc=kc, fl=fl, bb_=bb_, ysrc=ysrc, wb_=wb_: e.matmul(
                                    ps[bb_][:], wb_[:, kc, fl * 128:(fl + 1) * 128], ysrc(kc), start=(kc == 0), stop=(kc == 3)),
                                   reads=[Bwb, By], writes=[PS[bb_]])
                            for kc in range(8):
                                op("pe", lambda e, kc=kc, fl=fl, bg_=bg_, wg_=wg_: e.matmul(
                                    ps[bg_][:], wg_[:, kc, fl * 128:(fl + 1) * 128], xq[:, kc, :], start=(kc == 0), stop=(kc == 7)),
                                   reads=[Bwg, Bxq], writes=[PS[bg_]])
                            s_ = ft % 2
                            op("act", lambda e, bg_=bg_, s_=s_: e.activation(out=sgf[s_][:], in_=ps[bg_][:], func=AF.Sigmoid), reads=[PS[bg_]], writes=[Bsgf[s_]])
                            if br == 0:
                                op("dve", lambda e, bb_=bb_, s_=s_, ft=ft: e.tensor_tensor(out=ACC[:, ft, :], in0=sgf[s_][:], in1=ps[bb_][:], op=ALU.mult),
                                   reads=[PS[bb_], Bsgf[s_]], writes=[BACC])
                            else:
                                op("dve", lambda e, bb_=bb_, s_=s_: e.tensor_tensor(out=sgf[s_][:], in0=sgf[s_][:], in1=ps[bb_][:], op=ALU.mult),
                                   reads=[PS[bb_], Bsgf[s_]], writes=[Bsgf[s_]])
                                if br == 1:
                                    op("dve", lambda e, s_=s_, ft=ft: e.tensor_tensor(out=ACC[:, ft, :], in0=ACC[:, ft, :], in1=sgf[s_][:], op=ALU.add),
                                       reads=[Bsgf[s_], BACC], writes=[BACC])
                                else:
                                    op("dve", lambda e, s_=s_, ft=ft: e.tensor_tensor(out=MT[:, ft, :], in0=ACC[:, ft, :], in1=sgf[s_][:], op=ALU.add),
                                       reads=[Bsgf[s_], BACC], writes=[BMT])

                if stop_after == "F5":
                    dump("MT", MT[:], [128, 8, 512], BMT)
                    fw.barrier()
                    return nc, dump_d
                if G + 1 < NT:
                    dma("sp", dsxq, xq[:], wsrc(xTb[G + 1], 8), reads=[Bx[G + 1]], writes=[Bxq])
                wo = [next_w() for j in range(2)]
                HH = [sgf_all[:].rearrange("p a n -> p (a n)"), YMT[:].rearrange("p h n -> p (h n)").bitcast(F32)]
                HHB = [[Bsgf[0], Bsgf[1]], [BYMT]]
                OBK = (0, 1, 2, 3, 4, 5, 6, 7)

                def ln_head(ql):
                    r0 = G * 512 + ql * 128
                    Hc, HB = HH[ql % 2], HHB[ql % 2]
                    dma("sp", dsxr, xres[:], I["x"][r0:r0 + 128, :], writes=[Bxres])
                    for half in range(2):
                        bk = OBK[2 * ql + half]
                        wo_, Bwo = wo[half]
                        for kc in range(8):
                            op("pe", lambda e, kc=kc, ql=ql, bk=bk, wo_=wo_: e.matmul(
                                ps[bk][:], MT[:, kc, ql * 128:(ql + 1) * 128], wo_[:, kc, :], start=(kc == 0), stop=(kc == 7)),
                               reads=[BMT, Bwo], writes=[PS[bk]])
                        hs = slice(half * 512, (half + 1) * 512)
                        op("dve", lambda e, bk=bk, hs=hs, Hc=Hc: e.scalar_tensor_tensor(out=Hc[:, hs], in0=xres[:, hs], scalar=ALPHA, in1=ps[bk][:], op0=ALU.mult, op1=ALU.add),
                           reads=[PS[bk], Bxres] + HB, writes=HB)
                    R_, W_ = HB + [Bst[ql % 2], BC], [Bst[ql % 2]] + HB
                    st_, mv_ = stats2[ql % 2], mv2[ql % 2]
                    for half in range(2):
                        op("dve", lambda e, half=half, Hc=Hc: e.bn_stats(out=st_[:, half, :], in_=Hc[:, half * 512:(half + 1) * 512]), reads=R_, writes=W_)
                    op("dve", lambda e: e.bn_aggr(out=mv_[:, 0:2], in_=st_[:].rearrange("p a b -> p (a b)")), reads=R_, writes=W_)
                    op("dve", lambda e: e.tensor_scalar(out=mv_[:, 2:3], in0=mv_[:, 1:2], scalar1=1e-5, scalar2=None, op0=ALU.add), reads=R_, writes=W_)

                def ln_tail(ql, G=G, HH=HH, HHB=HHB):
                    r0 = G * 512 + ql * 128
                    Hc, HB = HH[ql % 2], HHB[ql % 2]
                    R_, W_ = HB + [Bst[ql % 2], BC], [Bst[ql % 2]] + HB
                    mv_ = mv2[ql % 2]
                    op("act", lambda e: e.activation(out=mv_[:, 2:3], in_=mv_[:, 2:3], func=AF.Ln), reads=R_, writes=W_)
                    op("act", lambda e: e.activation(out=mv_[:, 3:4], in_=mv_[:, 2:3], func=AF.Exp, scale=-0.5), reads=R_, writes=W_)
                    op("dve", lambda e, Hc=Hc: e.tensor_scalar(out=Hc, in0=Hc, scalar1=mv_[:, 0:1], scalar2=mv_[:, 3:4], op0=ALU.subtract, op1=ALU.mult), reads=R_, writes=W_)
                    op("pool", lambda e, Hc=Hc: e.tensor_tensor(out=Hc, in0=Hc, in1=lng[:], op=ALU.mult), reads=HB + [BC], writes=HB)
                    op("pool", lambda e, Hc=Hc: e.tensor_tensor(out=Hc, in0=Hc, in1=lnb[:], op=ALU.add), reads=HB + [BC], writes=HB)
                    dma("pool", dsout[ql % 2], out_d[r0:r0 + 128, :], Hc, reads=HB)

                ln_head(0)
                ln_tail(0)
                ln_head(1)
                ln_tail(1)
                ln_head(2)
                ln_head(3)
                if G + 1 < NT:
                    emit_qproj()
                    pend_ln.extend([lambda f=ln_tail: f(2), lambda f=ln_tail: f(3)])
                else:
                    ln_tail(2)
                    ln_tail(3)
                if stop_after == "F6":
                    fw.barrier()
                    return nc, dump_d
            fw.barrier()
        fw.barrier()
    return nc, dump_d


_NC_CACHE = {}


def kernel(**inputs):
    inp = {k: np.asarray(v) for k, v in inputs.items()}
    shared = _shared_layouts(inp)
    if "nc" not in _NC_CACHE:
        _NC_CACHE["nc"] = build()[0]
    nc = _NC_CACHE["nc"]
    in_maps = []
    for b in range(8):
        m = dict(shared)
        m["x"] = np.ascontiguousarray(inp["x"][b])
        m["xT"] = np.ascontiguousarray(inp["x"][b].T)
        m["memT"] = np.ascontiguousarray(inp["mem"][b].T)
        in_maps.append(m)
    res = run_bass_kernel_spmd(nc, in_maps, core_ids=list(range(8)))
    return np.stack([np.asarray(r["out"]) for r in res.results]).astype(np.float32)
```

```python
import math
from contextlib import ExitStack
import numpy as np
import concourse.bass as bass
import concourse.mybir as mybir
from concourse.bass_utils import run_bass_kernel_spmd

F32 = mybir.dt.float32
BF16 = mybir.dt.bfloat16
I32 = mybir.dt.int32
AF = mybir.ActivationFunctionType
ALU = mybir.AluOpType
AX = mybir.AxisListType

S = 4096
D = 1024
NT = 8
L = 16
NB = S // L
ALPHA = 2.0 ** 0.25
LAMBDA_INIT = 0.8 - 0.6 * math.exp(0.0)
PI = float(np.pi)


class Buf:
    __slots__ = ("name", "w", "r")

    def __init__(self, name):
        self.name = name
        self.w = None
        self.r = {}


class Eng:
    def __init__(self, name, eng, sem):
        self.name, self.eng, self.sem = name, eng, sem
        self.cnt = 0
        self.seen = {}


class FW:
    def __init__(self, nc, stack):
        self.nc, self.stack = nc, stack
        self.sems, self.engs, self.cnts = {}, {}, {}
        for name, eng in (("pe", nc.tensor), ("act", nc.scalar), ("dve", nc.vector),
                          ("pool", nc.gpsimd), ("sp", nc.sync)):
            sem = stack.enter_context(nc.semaphore("sem_" + name))
            self.sems[name] = sem
            self.engs[name] = Eng(name, eng, sem)
            self.cnts[name] = 0

    def dsem(self, name):
        self.sems[name] = self.stack.enter_context(self.nc.semaphore("d_" + name))
        self.cnts[name] = 0
        return name

    def _wait(self, e, key, val):
        if key == "pe" and e.name == "pe":
            return
        if e.seen.get(key, 0) >= val:
            return
        e.seen[key] = val
        e.eng.wait_ge(self.sems[key], val)

    def _deps(self, e, reads, writes):
        for b in reads:
            if b.w is not None:
                self._wait(e, *b.w)
        for b in writes:
            if b.w is not None:
                self._wait(e, *b.w)
            for k, v in b.r.items():
                self._wait(e, k, v)

    def _mark(self, tag, reads, writes):
        for b in reads:
            if b.r.get(tag[0], 0) < tag[1]:
                b.r[tag[0]] = tag[1]
        for b in writes:
            b.w = tag
            b.r = {}

    def op(self, en, fn, reads=(), writes=()):
        e = self.engs[en]
        self._deps(e, reads, writes)
        ins = fn(e.eng)
        self.cnts[en] += 1
        ins.then_inc(e.sem, 1)
        self._mark((en, self.cnts[en]), reads, writes)

    def dma(self, q, dsem, out, in_, reads=(), writes=()):
        e = self.engs[q]
        self._deps(e, reads, writes)
        ins = e.eng.dma_start(out=out, in_=in_)
        self.cnts[dsem] += 16
        ins.then_inc(self.sems[dsem], 16)
        self._mark((dsem, self.cnts[dsem]), reads, writes)

    def barrier(self, only=None, skip=()):
        for e in self.engs.values():
            if only is not None and e.name not in only:
                continue
            for k, v in self.cnts.items():
                if v > 0 and not k.startswith("cv_") and k not in skip:
                    self._wait(e, k, v)


def _t5_bucket_np(rel):
    ret = np.where(rel > 0, 16, 0)
    n = np.abs(rel)
    v = (np.log(np.maximum(n, 1).astype(np.float32) / np.float32(8)) / np.float32(math.log(16))
         * np.float32(8))
    large = np.minimum(8 + v.astype(np.int32), 15)
    return ret + np.where(n < 8, n, large)


def _shared_layouts(inp):
    f = np.float32
    o = {}
    lam_re, lam_im, log_dt = inp["lam_re"][0], inp["lam_im"][0], inp["log_dt"][0]
    b_re, b_im, c_re, c_im = inp["b_re"][0], inp["b_im"][0], inp["c_re"][0], inp["c_im"][0]
    def layA(v):
        return np.ascontiguousarray(v.reshape(16, 2, 64).transpose(1, 2, 0).reshape(128, 16)).astype(f)
    o["lamA"] = np.stack([layA(lam_re), layA(lam_im),
                          layA(np.broadcast_to(log_dt[:, None], (32, 64)))], axis=1)
    cbd = np.zeros((2, 2, 64, 16, 2, 16), f)
    bbd = np.zeros((2, 2, 64, 16, 2, 16), f)
    for ri, (cc, bb) in enumerate(((c_re, b_re), (c_im, b_im))):
        c4 = cc.reshape(16, 2, 16, 64)
        b4 = bb.reshape(16, 2, 64, 16)
        for g2 in range(2):
            cbd[ri, g2, :, :, g2, :] = c4[:, g2].transpose(2, 0, 1)
            bbd[ri, g2, :, :, g2, :] = b4[:, g2].transpose(1, 0, 2)
    o["cbdA"] = np.ascontiguousarray(cbd.reshape(2, 128, 16, 32).transpose(1, 0, 2, 3))
    o["bbdA"] = np.ascontiguousarray(bbd.reshape(2, 128, 16, 32).transpose(1, 0, 2, 3))
    def layB(v):
        a = v.reshape(4, 4, 2, 64).transpose(1, 0, 2, 3)
        a = np.broadcast_to(a[:, None], (4, 32, 4, 2, 64))
        return np.ascontiguousarray(a.reshape(128, 4, 128)).astype(f)
    o["lamB"] = np.stack([layB(lam_re), layB(lam_im),
                          layB(np.broadcast_to(log_dt[:, None], (32, 64)))], axis=1)
    btb = np.zeros((2, 4, 2, 16, 4, 2, 64), f)
    for ri, bb in enumerate((b_re, b_im)):
        b5 = bb.reshape(4, 4, 2, 64, 16)
        for g2 in range(2):
            btb[ri, :, g2, :, :, g2, :] = b5[:, :, g2].transpose(1, 3, 0, 2)
    o["btB"] = np.ascontiguousarray(btb.reshape(2, 128, 4, 128).transpose(1, 0, 2, 3))
    dsk = np.zeros((128, 4, 128), f)
    ds = inp["d_skip"][0].reshape(4, 128)
    for t in range(4):
        dsk[np.arange(128), t, np.arange(128)] = ds[t]
    o["dskD"] = dsk
    kl = np.arange(128)[:, None]
    qr = np.arange(256)[None, :]
    bidx = _t5_bucket_np(kl - qr)
    rb = inp["rel_bias"]
    o["tbias"] = np.ascontiguousarray(rb[bidx].transpose(0, 2, 1)).astype(f)
    maskc = np.where((kl >= 64) & (qr < 64), -30000.0, 0.0).astype(f)
    o["maskc"] = np.ascontiguousarray(np.broadcast_to(maskc[:, None, :], (128, 4, 256)))
    o["farb"] = np.ascontiguousarray(np.broadcast_to(rb[15][None, :], (128, 4))).astype(f)
    o["lamv"] = np.ascontiguousarray(np.broadcast_to(
        np.stack([inp["lambda_q1"][0], inp["lambda_k1"][0], inp["lambda_q2"][0], inp["lambda_k2"][0]])[None],
        (128, 4, 64))).astype(f)
    o["subw"] = np.ascontiguousarray(np.broadcast_to(inp["subln_w"][0][None], (128, 128))).astype(f)
    o["lng"] = np.ascontiguousarray(np.broadcast_to(inp["ln_g"][0][None], (128, D))).astype(f)
    o["lnb"] = np.ascontiguousarray(np.broadcast_to(inp["ln_b"][0][None], (128, D))).astype(f)
    o["ident"] = np.eye(128, dtype=f)
    o["w_in"] = np.ascontiguousarray(inp["w_in"][0])
    o["w_glu"] = np.ascontiguousarray(inp["w_glu"][0])
    o["w_mem_kv"] = np.ascontiguousarray(inp["w_mem_kv"][0])
    o["w_br"] = np.ascontiguousarray(np.stack([inp["w_br_ssm"][0], inp["w_br_diff"][0], inp["w_br_mem"][0]]))
    o["w_out"] = np.ascontiguousarray(inp["w_out"][0])
    return o


IN_SHAPES = {
    "xT": [D, S], "x": [S, D], "memT": [D, 256],
    "lamA": [128, 3, 16], "cbdA": [128, 2, 16, 32], "bbdA": [128, 2, 16, 32],
    "lamB": [128, 3, 4, 128], "btB": [128, 2, 4, 128], "dskD": [128, 4, 128],
    "tbias": [128, 4, 256], "maskc": [128, 4, 256], "farb": [128, 4], "lamv": [128, 4, 64],
    "subw": [128, 128], "lng": [128, D], "lnb": [128, D], "ident": [128, 128],
    "w_in": [D, 7168], "w_glu": [512, 1024], "w_mem_kv": [D, 1024], "w_br": [3, 512, 1024],
    "w_out": [D, D],
}


def build(stop_after=None, dumps=()):
    nc = bass.Bass("TRN2", target_bir_lowering=False)
    I = {k: nc.dram_tensor(k, v, F32, kind="ExternalInput").ap() for k, v in IN_SHAPES.items()}
    out_d = nc.dram_tensor("out", [S, D], F32, kind="ExternalOutput").ap()
    dump_d = {}
    xTb = nc.dram_tensor("xTb", [NT, D, 512], BF16).ap()
    wInb = nc.dram_tensor("wInb", [14, D, 512], BF16).ap()
    wGb = nc.dram_tensor("wGb", [2, 512, 512], BF16).ap()
    wMb = nc.dram_tensor("wMb", [2, D, 512], BF16).ap()
    wBb = nc.dram_tensor("wBb", [3, 2, 512, 512], BF16).ap()
    wOb = nc.dram_tensor("wOb", [2, D, 512], BF16).ap()
    memTb = nc.dram_tensor("memTb", [D, 256], BF16).ap()

    with ExitStack() as top:
        fw = FW(nc, top)
        op, dma = fw.op, fw.dma

        def sb(st, name, shape, dt):
            return st.enter_context(nc.sbuf_tensor("s_" + name, list(shape), dt))

        def dump(name, ap_sb, shape, buf):
            d = nc.dram_tensor("dump_" + name, list(shape), ap_sb.dtype, kind="ExternalOutput").ap()
            dump_d[name] = d
            dma("sp", fw.dsem("dump_" + name), d, ap_sb, reads=[buf])

        psA = top.enter_context(nc.psum_tensor("psA", [128, 1024], F32))
        psB = top.enter_context(nc.psum_tensor("psB", [128, 1024], F32))
        ps = [psA[:, 0:512], psA[:, 512:1024], psB[:, 0:512], psB[:, 512:1024]] + \
             [top.enter_context(nc.psum_tensor("ps%d" % i, [128, 512], F32)) for i in range(4, 8)]
        PS = [Buf("ps%d" % i) for i in range(8)]

        Bx = [Buf("xTb%d" % t) for t in range(NT)]
        Bw = [Buf("wInb%d" % j) for j in range(14)]
        Bg = [Buf("wGb%d" % j) for j in range(2)]
        Bm = [Buf("wMb%d" % j) for j in range(2)]
        Bb = [[Buf("wBb%d%d" % (b, j)) for j in range(2)] for b in range(3)]
        Bo = [Buf("wOb%d" % j) for j in range(2)]
        Bmem = Buf("memTb")

        def conv(dst, src, buf):
            dma("pool", fw.dsem("cv_" + buf.name), dst, src, writes=[buf])

        conv(wInb[0], I["w_in"][:, 0:512], Bw[0])
        conv(wInb[1], I["w_in"][:, 512:1024], Bw[1])
        for t in range(NT):
            conv(xTb[t], I["xT"][:, t * 512:(t + 1) * 512], Bx[t])
        for j in range(2):
            conv(wGb[j], I["w_glu"][:, j * 512:(j + 1) * 512], Bg[j])
        for j in (3, 4):
            conv(wInb[j], I["w_in"][:, j * 512:(j + 1) * 512], Bw[j])
        for j in range(2):
            conv(wMb[j], I["w_mem_kv"][:, j * 512:(j + 1) * 512], Bm[j])
        conv(memTb, I["memT"], Bmem)
        for j in (2, 5, 6, 7, 8, 9, 10, 11, 12, 13):
            conv(wInb[j], I["w_in"][:, j * 512:(j + 1) * 512], Bw[j])
        for b in range(3):
            for j in range(2):
                conv(wBb[b, j], I["w_br"][b, :, j * 512:(j + 1) * 512], Bb[b][j])
        for j in range(2):
            conv(wOb[j], I["w_out"][:, j * 512:(j + 1) * 512], Bo[j])

        def wsrc(piece, nkc):
            return piece.rearrange("(kc p) n -> p kc n", p=128)

        ident_f = sb(top, "ident_f", [128, 128], F32)
        ident = sb(top, "ident", [128, 128], BF16)
        Bid = Buf("ident")
        dma("sp", fw.dsem("c_id"), ident_f[:], I["ident"], writes=[Bid])
        op("dve", lambda e: e.tensor_copy(out=ident[:], in_=ident_f[:]), reads=[Bid], writes=[Bid])

        ZS = sb(top, "ZS", [128, 4, S], BF16)
        BZSn = Buf("ZSn")

        with ExitStack() as phS:
            UT = sb(phS, "UT", [128, 4, S], BF16)
            BUTp = [[Buf("UT%d_%d" % (i, j)) for j in range(8)] for i in range(4)]
            BUTall = [b for bl in BUTp for b in bl]
            ZSp = sb(phS, "ZSp", [128, 4, S], BF16)
            BZSp = Buf("ZSp")

            with ExitStack() as ph2:
                def cplx_setup(st, pref, lam_ap, shape, Bsrc, keep):
                    T = {}
                    for nm in ("dt", "th", "mg", "cs", "sn", "t1", "t2", "den"):
                        T[nm] = sb(st, pref + nm, shape, F32)
                    T.update(keep)
                    ti = sb(st, pref + "ti", shape, I32)
                    Bt = keep["buf"]
                    R, W = [Bsrc, Bt], [Bt]
                    lre, lim, ldt = lam_ap(0), lam_ap(1), lam_ap(2)
                    op("act", lambda e: e.activation(out=T["dt"][:], in_=ldt, func=AF.Exp), reads=R, writes=W)
                    op("dve", lambda e: e.tensor_tensor(out=T["th"][:], in0=lim, in1=T["dt"][:], op=ALU.mult), reads=R, writes=W)
                    op("dve", lambda e: e.tensor_tensor(out=T["t0"][:], in0=lre, in1=T["dt"][:], op=ALU.mult), reads=R, writes=W)
                    op("act", lambda e: e.activation(out=T["mg"][:], in_=T["t0"][:], func=AF.Exp), reads=R, writes=W)

                    def sin_of(dst, shift):
                        op("dve", lambda e: e.tensor_scalar(out=T["t0"][:], in0=T["th"][:], scalar1=shift, scalar2=None, op0=ALU.add), reads=R, writes=W)
                        op("dve", lambda e: e.tensor_scalar(out=T["t1"][:], in0=T["t0"][:], scalar1=1.0 / (2 * PI), scalar2=None, op0=ALU.mult), reads=R, writes=W)
                        op("dve", lambda e: e.tensor_copy(out=ti[:], in_=T["t1"][:]), reads=R, writes=W)
                        op("dve", lambda e: e.tensor_copy(out=T["t1"][:], in_=ti[:]), reads=R, writes=W)
                        op("dve", lambda e: e.scalar_tensor_tensor(out=T["t2"][:], in0=T["t1"][:], scalar=-2 * PI, in1=T["t0"][:], op0=ALU.mult, op1=ALU.add), reads=R, writes=W)
                        op("dve", lambda e: e.tensor_scalar(out=T["t1"][:], in0=T["t2"][:], scalar1=PI, scalar2=-2 * PI, op0=ALU.is_gt, op1=ALU.mult), reads=R, writes=W)
                        op("dve", lambda e: e.tensor_tensor(out=T["t2"][:], in0=T["t2"][:], in1=T["t1"][:], op=ALU.add), reads=R, writes=W)
                        op("dve", lambda e: e.tensor_scalar(out=T["t1"][:], in0=T["t2"][:], scalar1=-PI, scalar2=2 * PI, op0=ALU.is_lt, op1=ALU.mult), reads=R, writes=W)
                        op("dve", lambda e: e.tensor_tensor(out=T["t2"][:], in0=T["t2"][:], in1=T["t1"][:], op=ALU.add), reads=R, writes=W)
                        op("act", lambda e: e.activation(out=dst[:], in_=T["t2"][:], func=AF.Sin), reads=R, writes=W)
                    sin_of(T["sn"], 0.0)
                    sin_of(T["cs"], PI / 2)
                    op("dve", lambda e: e.tensor_tensor(out=T["lbr"][:], in0=T["mg"][:], in1=T["cs"][:], op=ALU.mult), reads=R, writes=W)
                    op("dve", lambda e: e.tensor_tensor(out=T["lbi"][:], in0=T["mg"][:], in1=T["sn"][:], op=ALU.mult), reads=R, writes=W)
                    op("dve", lambda e: e.tensor_tensor(out=T["den"][:], in0=lre, in1=lre, op=ALU.mult), reads=R, writes=W)
                    op("dve", lambda e: e.tensor_tensor(out=T["t0"][:], in0=lim, in1=lim, op=ALU.mult), reads=R, writes=W)
                    op("dve", lambda e: e.tensor_tensor(out=T["den"][:], in0=T["den"][:], in1=T["t0"][:], op=ALU.add), reads=R, writes=W)
                    op("dve", lambda e: e.reciprocal(out=T["den"][:], in_=T["den"][:]), reads=R, writes=W)
                    op("dve", lambda e: e.tensor_scalar(out=T["t0"][:], in0=T["lbr"][:], scalar1=-1.0, scalar2=None, op0=ALU.add), reads=R, writes=W)
                    op("dve", lambda e: e.tensor_tensor(out=T["t1"][:], in0=T["t0"][:], in1=lre, op=ALU.mult), reads=R, writes=W)
                    op("dve", lambda e: e.tensor_tensor(out=T["t2"][:], in0=T["lbi"][:], in1=lim, op=ALU.mult), reads=R, writes=W)
                    op("dve", lambda e: e.tensor_tensor(out=T["t1"][:], in0=T["t1"][:], in1=T["t2"][:], op=ALU.add), reads=R, writes=W)
                    op("dve", lambda e: e.tensor_tensor(out=T["cfr"][:], in0=T["t1"][:], in1=T["den"][:], op=ALU.mult), reads=R, writes=W)
                    op("dve", lambda e: e.tensor_tensor(out=T["t1"][:], in0=T["lbi"][:], in1=lre, op=ALU.mult), reads=R, writes=W)
                    op("dve", lambda e: e.tensor_tensor(out=T["t2"][:], in0=T["t0"][:], in1=lim, op=ALU.mult), reads=R, writes=W)
                    op("dve", lambda e: e.tensor_tensor(out=T["t1"][:], in0=T["t1"][:], in1=T["t2"][:], op=ALU.subtract), reads=R, writes=W)
                    op("dve", lambda e: e.tensor_tensor(out=T["cfi"][:], in0=T["t1"][:], in1=T["den"][:], op=ALU.mult), reads=R, writes=W)
                    return T, Bt

                def cmul(out_r, out_i, ar, ai, br, bi, t0, R, W, en="dve"):
                    op(en, lambda e: e.tensor_tensor(out=out_r, in0=ar, in1=br, op=ALU.mult), reads=R, writes=W)
                    op(en, lambda e: e.tensor_tensor(out=t0, in0=ai, in1=bi, op=ALU.mult), reads=R, writes=W)
                    op(en, lambda e: e.tensor_tensor(out=out_r, in0=out_r, in1=t0, op=ALU.subtract), reads=R, writes=W)
                    op(en, lambda e: e.tensor_tensor(out=out_i, in0=ar, in1=bi, op=ALU.mult), reads=R, writes=W)
                    op(en, lambda e: e.tensor_tensor(out=t0, in0=ai, in1=br, op=ALU.mult), reads=R, writes=W)
                    op(en, lambda e: e.tensor_tensor(out=out_i, in0=out_i, in1=t0, op=ALU.add), reads=R, writes=W)

                lamA = sb(ph2, "lamA", [128, 3, 16], F32)
                cbdA = sb(ph2, "cbdA", [128, 2, 16, 32], F32)
                keepA = {nm: sb(ph2, "A_" + nm, [128, 16], F32) for nm in ("lbr", "lbi", "cfr", "cfi", "t0")}
                keepA["buf"] = Buf("A_tmp")
                PW = sb(ph2, "PW", [128, 17, 2, 16], F32)
                AD = sb(ph2, "AD", [128, 8, 3, 16], F32)
                PWn = sb(ph2, "PWn", [128, 17, 16], F32)
                BbA = sb(ph2, "BbA", [128, 2, 16, 32], BF16)
                dskD = sb(ph2, "dskD", [128, 4, 128], F32)
                XB0 = sb(ph2, "XB0", [128, 2, 512], F32)
                LB1 = sb(ph2, "LB1", [128, 2, 512], F32)
                LB2 = sb(ph2, "LB2", [128, 2, 512], F32)
                keepB = {nm: sb(ph2, "B_" + nm, [128, 512], F32) for nm in ("cfr", "cfi", "t0")}
                keepB["lbr"], keepB["lbi"] = LB1[:, 0, :], LB1[:, 1, :]
                keepB["buf"] = Buf("B_tmp")
                FA2 = [sb(ph2, "FA%d" % i, [128, 17, 2, 4, 32], BF16) for i in range(2)]
                BFA2 = [Buf("FA%d" % i) for i in range(2)]
                tP8 = [sb(ph2, "tP%d" % i, [128, 4, 32], F32) for i in range(8)]
                BtP8 = [Buf("tP%d" % i) for i in range(8)]
                BA, BB = Buf("layA"), Buf("layB")
                dsA, dsB = fw.dsem("layA"), fw.dsem("layB")
                dma("sp", dsA, lamA[:], I["lamA"], writes=[BA])
                dma("sp", dsA, cbdA[:], I["cbdA"], writes=[BA])
                dma("sp", dsB, dskD[:], I["dskD"], writes=[BB])

                def bc(ap2, n=16):
                    return ap2.unsqueeze(2).to_broadcast([128, n, 32])

                def setup_FA(T_):
                    par = T_ % 2
                    FA, BFA = FA2[par], BFA2[par]
                    Ps = slice(4 * T_, 4 * T_ + 4)
                    cr, ci = cbdA[:, 0, Ps], cbdA[:, 1, Ps]
                    for a in range(17):
                        tt = tP8[4 * (a % 2):4 * (a % 2) + 4]
                        tb = BtP8[4 * (a % 2):4 * (a % 2) + 4]
                        pr, pi_, npi = bc(PW[:, a, 0, Ps], 4), bc(PW[:, a, 1, Ps], 4), bc(PWn[:, a, Ps], 4)
                        R0 = [BA, BtA]
                        op("pool", lambda e: e.tensor_tensor(out=tt[0][:], in0=cr, in1=pr, op=ALU.mult), reads=R0, writes=[tb[0]])
                        op("pool", lambda e: e.tensor_tensor(out=tt[1][:], in0=ci, in1=pi_, op=ALU.mult), reads=R0, writes=[tb[1]])
                        op("pool", lambda e: e.tensor_tensor(out=tt[2][:], in0=cr, in1=npi, op=ALU.mult), reads=R0, writes=[tb[2]])
                        op("pool", lambda e: e.tensor_tensor(out=tt[3][:], in0=ci, in1=pr, op=ALU.mult), reads=R0, writes=[tb[3]])
                        op("pool", lambda e, a=a: e.tensor_tensor(out=FA[:, a, 0], in0=tt[0][:], in1=tt[1][:], op=ALU.subtract),
                           reads=[tb[0], tb[1], BFA], writes=[BFA])
                        op("pool", lambda e, a=a: e.tensor_tensor(out=FA[:, a, 1], in0=tt[2][:], in1=tt[3][:], op=ALU.subtract),
                           reads=[tb[2], tb[3], BFA], writes=[BFA])

                ph1 = ExitStack()
                wS = sb(ph1, "wS", [128, 8, 1024], BF16)
                BwS = Buf("wS")
                ds_w = fw.dsem("wS")
                for j in range(2):
                    dma("sp", ds_w, wS[:, :, j * 512:(j + 1) * 512], wsrc(wInb[j], 8), reads=[Bw[j]], writes=[BwS])
                xt1 = sb(ph1, "xtS", [128, 8, 512], BF16)
                Bxt1 = Buf("xtS")
                dsx1 = fw.dsem("xtS")

                def load_x(t):
                    dma("sp", dsx1, xt1[:], wsrc(xTb[t], 8), reads=[Bx[t]], writes=[Bxt1])
                load_x(0)

                with ExitStack() as tmpS:
                    bbdA = sb(tmpS, "bbdA", [128, 2, 16, 32], F32)
                    lamB = sb(tmpS, "lamB", [128, 3, 512], F32)
                    btB = sb(tmpS, "btB", [128, 2, 512], F32)
                    dma("sp", dsA, bbdA[:], I["bbdA"], writes=[BA])
                    dma("sp", dsB, lamB[:], I["lamB"].rearrange("p a t n -> p a (t n)"), writes=[BB])
                    dma("sp", dsB, btB[:], I["btB"].rearrange("p a t n -> p a (t n)"), writes=[BB])
                    TA, BtA = cplx_setup(tmpS, "A_", lambda i: lamA[:, i, :], [128, 16], BA, keepA)
                    TA = keepA
                    RA, WA_ = [BA, BtA], [BtA]
                    op("dve", lambda e: e.memset(PW[:, 0, 0, :], 1.0), reads=RA, writes=WA_)
                    op("dve", lambda e: e.memset(PW[:, 0, 1, :], 0.0), reads=RA, writes=WA_)
                    for a in range(1, 17):
                        cmul(PW[:, a, 0, :], PW[:, a, 1, :], PW[:, a - 1, 0, :], PW[:, a - 1, 1, :],
                             TA["lbr"][:], TA["lbi"][:], TA["t0"][:], RA, WA_)
                    op("dve", lambda e: e.tensor_copy(out=AD[:, 0, 0:2, :], in_=PW[:, 16, :, :]), reads=RA, writes=WA_)
                    op("dve", lambda e: e.tensor_scalar(out=PWn[:], in0=PW[:, :, 1, :], scalar1=-1.0, scalar2=None, op0=ALU.mult), reads=RA, writes=WA_)
                    for k in range(1, 8):
                        cmul(AD[:, k, 0, :], AD[:, k, 1, :], AD[:, k - 1, 0, :], AD[:, k - 1, 1, :],
                             AD[:, k - 1, 0, :], AD[:, k - 1, 1, :], TA["t0"][:], RA, WA_)
                    op("dve", lambda e: e.tensor_scalar(out=AD[:, :, 2, :], in0=AD[:, :, 1, :], scalar1=-1.0, scalar2=None, op0=ALU.mult), reads=RA, writes=WA_)
                    setup_FA(0)
                    setup_FA(1)
                    TBt, BtB_ = cplx_setup(tmpS, "B_", lambda i: lamB[:, i, :], [128, 512], BB, keepB)
                    TB = keepB
                    RB, WB_ = [BB, BtB_], [BtB_]
                    cmul(XB0[:, 0], XB0[:, 1], btB[:, 0], btB[:, 1], TB["cfr"][:], TB["cfi"][:], TB["t0"][:], RB, WB_)
                    op("dve", lambda e: e.tensor_scalar(out=LB2[:, 0, :], in0=TB["lbi"][:], scalar1=-1.0, scalar2=None, op0=ALU.mult), reads=RB, writes=WB_)
                    op("dve", lambda e: e.tensor_copy(out=LB2[:, 1, :], in_=TB["lbr"][:]), reads=RB, writes=WB_)
                    tA = [TBt[nm][:].rearrange("p (a b) -> p a b", a=16) for nm in ("t1", "t2", "den")]
                    RAB, WAB = [BA, BtA, BtB_], [BtA, BtB_]
                    cmul(tA[0], tA[1], bbdA[:, 0], bbdA[:, 1], bc(TA["cfr"][:]), bc(TA["cfi"][:]), tA[2], RAB, WAB)
                    op("dve", lambda e: e.tensor_copy(out=BbA[:, 0], in_=tA[0]), reads=RAB, writes=WAB)
                    op("dve", lambda e: e.tensor_copy(out=BbA[:, 1], in_=tA[1]), reads=RAB, writes=WAB)

                    for t in range(NT):
                        x_t, Bx_t = xt1, Bxt1
                        for m in range(8):
                            bk = m % 8
                            for kc in range(8):
                                op("pe", lambda e, m=m, kc=kc, bk=bk: e.matmul(
                                    ps[bk][:], wS[:, kc, m * 128:(m + 1) * 128], x_t[:, kc, :],
                                    start=(kc == 0), stop=(kc == 7)),
                                   reads=[BwS, Bx_t], writes=[PS[bk]])
                            if m < 4:
                                op("act", lambda e, m=m, bk=bk: e.activation(func=AF.Copy,
                                    out=UT[:, m, :].rearrange("p (i c) -> p i c", i=L)[:, :, 32 * t:32 * t + 32],
                                    in_=ps[bk][:].rearrange("p (c i) -> p i c", i=L)),
                                   reads=[PS[bk]], writes=BUTp[m])
                            else:
                                op("act", lambda e, m=m, bk=bk: e.activation(
                                    out=ZSp[:, m - 4, :].rearrange("p (i c) -> p i c", i=L)[:, :, 32 * t:32 * t + 32],
                                    in_=ps[bk][:].rearrange("p (c i) -> p i c", i=L), func=AF.Silu),
                                   reads=[PS[bk]], writes=[BZSp])
                        if t + 1 < NT:
                            load_x(t + 1)
                    fw.barrier(skip=("pool",))
                ph1.close()
                fw.barrier(skip=("pool",))
                if "UT" in dumps:
                    dump("UT", UT[:], [128, 4, S], BUTp[3][0])
                    dump("ZS", ZSp[:], [128, 4, S], BZSp)
                EB = sb(ph2, "EB", [128, 16, 2, 128], BF16)
                KB2 = [sb(ph2, "KB%d" % i, [128, 16, 128], BF16) for i in range(2)]
                Wa = sb(ph2, "Wa", [128, 4, 2, NB], F32)
                Wb = sb(ph2, "Wb", [128, 4, 2, NB], F32)
                SP2 = [sb(ph2, "SP%d" % i, [128, 4, 2, NB], BF16) for i in range(2)]
                BEBa = [Buf("EB%d" % a) for a in range(16)]
                BKB2 = [Buf("KB%d" % i) for i in range(2)]
                BWam = [Buf("Wa%d" % m) for m in range(4)]
                BWbm = [Buf("Wb%d" % m) for m in range(4)]
                BSP2 = [Buf("SP%d" % i) for i in range(2)]
                KF0 = sb(ph2, "KF0", [128, 128], F32)
                BKF = Buf("KF0")
                XB = [sb(ph2, "XBp%d" % i, [128, 2, 128], F32) for i in range(2)]
                BXBf = [Buf("XBp%d" % i) for i in range(2)]
                XBt = sb(ph2, "XBt", [128, 2, 128], F32)
                BXBt = Buf("XBt")
                gel = [sb(ph2, "gel%d" % i, [128, 512], F32) for i in range(2)]
                Bgel = Buf("gel")
                def eb_ops(T_):
                    cs = slice(T_ * 128, (T_ + 1) * 128)
                    R0 = [BB, BtB_]
                    l1, l2 = LB1[:, :, cs], LB2[:, :, cs]
                    th = []
                    for a in range(16):
                        if a == 0:
                            cur, bc_ = XB0[:, :, cs], BtB_
                        else:
                            cur, bc_ = XB[a % 2][:], BXBf[a % 2]
                        th.append(lambda a=a, cur=cur, bc_=bc_: op("dve", lambda e: e.tensor_copy(out=EB[:, a, :, :], in_=cur), reads=R0 + [bc_, BEBa[a]], writes=[BEBa[a]]))
                        if a < 15:
                            nxt, bn_ = XB[(a + 1) % 2], BXBf[(a + 1) % 2]
                            cr_b = cur[:, 0:1, :].to_broadcast([128, 2, 128])
                            ci_b = cur[:, 1:2, :].to_broadcast([128, 2, 128])
                            th.append(lambda nxt=nxt, cr_b=cr_b, bc_=bc_, bn_=bn_: op("dve", lambda e: e.tensor_tensor(out=nxt[:], in0=cr_b, in1=l1, op=ALU.mult), reads=R0 + [bc_], writes=[bn_]))
                            th.append(lambda ci_b=ci_b, bc_=bc_: op("dve", lambda e: e.tensor_tensor(out=XBt[:], in0=ci_b, in1=l2, op=ALU.mult), reads=R0 + [bc_], writes=[BXBt]))
                            th.append(lambda nxt=nxt, bn_=bn_: op("dve", lambda e: e.tensor_tensor(out=nxt[:], in0=nxt[:], in1=XBt[:], op=ALU.add), reads=[bn_, BXBt], writes=[bn_]))
                    return th

                def emit_K(T_):
                    par = T_ % 2
                    FA, BFA, KB, BKB = FA2[par], BFA2[par], KB2[par], BKB2[par]
                    bk = 6 + (T_ % 2)
                    pk = ps[bk][:].rearrange("p (l n) -> p l n", l=16)
                    for lag in range(16):
                        for ri in range(2):
                            for m in range(4):
                                op("pe", lambda e, lag=lag, m=m, ri=ri: e.matmul(
                                    pk[32 * m:32 * m + 32, lag, :], BbA[:, ri, 4 * T_ + m, :], FA[:, lag, ri, m, :],
                                    start=(ri == 0), stop=(ri == 1), tile_position=(0, 32 * m), skip_group_check=True),
                                   reads=[BtA, BFA], writes=[PS[bk]])
                    op("dve", lambda e: e.memset(KB[:], 0.0), reads=[], writes=[BKB])
                    op("dve", lambda e: e.tensor_copy(out=KF0[:], in_=dskD[:, T_, :]), reads=[BB, BKF], writes=[BKF])
                    for m in range(4):
                        sl = slice(32 * m, 32 * m + 32)
                        op("dve", lambda e, sl=sl: e.tensor_copy(out=KB[sl, 1:16, sl], in_=pk[sl, 1:16, :]), reads=[PS[bk]], writes=[BKB])
                        op("dve", lambda e, sl=sl: e.tensor_tensor(out=KF0[sl, sl], in0=KF0[sl, sl], in1=pk[sl, 0, :], op=ALU.add), reads=[PS[bk], BKF], writes=[BKF])
                    op("dve", lambda e: e.tensor_copy(out=KB[:, 0, :], in_=KF0[:]), reads=[BKF, BKB], writes=[BKB])

                def emit_L1(T_):
                    UTv = UT[:, T_, :].rearrange("p (i c) -> p i c", i=L)
                    pws = [ps[m][:].rearrange("p (r n) -> p r n", r=2) for m in range(4)]
                    for ri in range(2):
                        for j in range(L):
                            for m in range(4):
                                rows = slice(32 * m, 32 * m + 32)
                                op("pe", lambda e, m=m, ri=ri, j=j, rows=rows: e.matmul(
                                    pws[m][:, ri, :], EB[rows, 15 - j, ri, :], UTv[rows, j, :],
                                    start=(j == 0), stop=(j == L - 1), tile_position=(32 * m, 0), skip_group_check=True),
                                   reads=[BEBa[15 - j]] + BUTp[T_], writes=[PS[m]])
                    for m in range(4):
                        op("act", lambda e, m=m: e.activation(out=Wa[:, m], in_=pws[m], func=AF.Copy), reads=[PS[m]], writes=[BWam[m]])

                def dbl_ops(T_):
                    par = T_ % 2
                    src, dst, Bs, Bd = Wa, Wb, BWam, BWbm
                    SP, BSP = SP2[par], BSP2[par]
                    th = []
                    for k in range(8):
                        d = 1 << k

                        def mk(kind, m, d=d, k=k, src=src, dst=dst, Bs=Bs, Bd=Bd):
                            P_ = 4 * T_ + m
                            aR, aI, nI = AD[:, k, 0, P_:P_ + 1], AD[:, k, 1, P_:P_ + 1], AD[:, k, 2, P_:P_ + 1]
                            R_, W_ = [Bs[m], Bd[m], BtA], [Bd[m]]
                            if kind == 0:
                                return lambda: op("dve", lambda e: e.scalar_tensor_tensor(
                                    out=dst[:, m, 0, d:], in0=src[:, m, 0, :NB - d], scalar=aR, in1=src[:, m, 0, d:], op0=ALU.mult, op1=ALU.add), reads=R_, writes=W_)
                            if kind == 1:
                                return lambda: op("dve", lambda e: e.scalar_tensor_tensor(
                                    out=dst[:, m, 1, d:], in0=src[:, m, 0, :NB - d], scalar=aI, in1=src[:, m, 1, d:], op0=ALU.mult, op1=ALU.add), reads=R_, writes=W_)
                            if kind == 2:
                                return lambda: op("dve", lambda e: e.tensor_copy(out=dst[:, m, :, :d], in_=src[:, m, :, :d]), reads=R_, writes=W_)
                            if kind == 3:
                                return lambda: op("dve", lambda e: e.scalar_tensor_tensor(
                                    out=dst[:, m, 0, d:], in0=src[:, m, 1, :NB - d], scalar=nI, in1=dst[:, m, 0, d:], op0=ALU.mult, op1=ALU.add), reads=R_, writes=W_)
                            return lambda: op("dve", lambda e: e.scalar_tensor_tensor(
                                out=dst[:, m, 1, d:], in0=src[:, m, 1, :NB - d], scalar=aR, in1=dst[:, m, 1, d:], op0=ALU.mult, op1=ALU.add), reads=R_, writes=W_)
                        th.append(lambda d=d, src=src, dst=dst, Bs=Bs, Bd=Bd: op("dve", lambda e: e.tensor_copy(
                            out=dst[:, :, :, :d], in_=src[:, :, :, :d]), reads=list(Bs) + list(Bd), writes=list(Bd)))
                        for kind in (0, 1, 3, 4):
                            for m in range(4):
                                th.append(mk(kind, m))
                        src, dst, Bs, Bd = dst, src, Bd, Bs

                    def fin(src=src, Bs=Bs):
                        op("dve", lambda e: e.memset(SP[:, :, :, 0:1], 0.0), reads=[], writes=[BSP])
                        op("dve", lambda e: e.tensor_copy(out=SP[:, :, :, 1:], in_=src[:, :, :, :NB - 1]), reads=list(Bs), writes=[BSP])
                    th.append(fin)
                    return th

                def emit_L03(T_, inter):
                    par = T_ % 2
                    FA, BFA, KB, BKB, SP, BSP = FA2[par], BFA2[par], KB2[par], BKB2[par], SP2[par], BSP2[par]
                    UTv = UT[:, T_, :].rearrange("p (i c) -> p i c", i=L)
                    nint = (len(inter) + 7) // 8
                    for n_, ip in enumerate(range(7, -1, -1)):
                        bk = 4 + (ip % 4)
                        py = ps[bk][:].rearrange("p (i n) -> p i n", i=2)
                        i0 = 2 * ip
                        for lag in range(i0 + 2):
                            ist = i0 if lag <= i0 else i0 + 1
                            ni = i0 + 2 - ist
                            op("pe", lambda e, lag=lag, ist=ist, ni=ni, py=py: e.matmul(
                                py[:, ist - i0:ist - i0 + ni, :], KB[:, lag, :], UTv[:, ist - lag:ist - lag + ni, :],
                                start=(lag == 0), stop=False, skip_group_check=True),
                               reads=[BKB] + BUTp[T_][:ip + 1], writes=[PS[bk]])
                        for il in range(2):
                            for ri in range(2):
                                for m in range(4):
                                    last = (m == 3 and il == 1 and ri == 1)
                                    op("pe", lambda e, m=m, il=il, ri=ri, py=py, last=last: e.matmul(
                                        py[32 * m:32 * m + 32, il, :], FA[:, i0 + il + 1, ri, m, :], SP[:, m, ri, :],
                                        start=False, stop=last, tile_position=(0, 32 * m), skip_group_check=True),
                                       reads=[BFA, BSP], writes=[PS[bk]])
                        g0, g1 = gel[0][:], gel[1][:]
                        pyf = ps[bk][:]
                        op("act", lambda e, pyf=pyf: e.activation(out=g0, in_=pyf, func=AF.Square), reads=[PS[bk], Bgel], writes=[Bgel])
                        op("dve", lambda e: e.tensor_scalar(out=g0, in0=g0, scalar1=0.044715, scalar2=1.0, op0=ALU.mult, op1=ALU.add), reads=[Bgel], writes=[Bgel])
                        op("dve", lambda e, pyf=pyf: e.tensor_tensor(out=g0, in0=g0, in1=pyf, op=ALU.mult), reads=[Bgel, PS[bk]], writes=[Bgel])
                        op("act", lambda e: e.activation(out=g1, in_=g0, func=AF.Sigmoid, scale=1.5957691216057308), reads=[Bgel], writes=[Bgel])
                        op("dve", lambda e, py=py, i0=i0: e.tensor_tensor(
                            out=UTv[:, i0:i0 + 2, :], in0=gel[1][:].rearrange("p (i n) -> p i n", i=2), in1=py, op=ALU.mult),
                           reads=[Bgel, PS[bk]], writes=[BUTp[T_][ip]])
                        for f_ in inter[n_ * nint:(n_ + 1) * nint]:
                            f_()

                def run(th):
                    for f_ in th:
                        f_()

                def mix(a_, b_):
                    out_, ia, ib = [], 0, 0
                    while ia < len(a_) or ib < len(b_):
                        if ib >= len(b_) or (ia < len(a_) and ia * len(b_) <= ib * len(a_)):
                            out_.append(a_[ia]); ia += 1
                        else:
                            out_.append(b_[ib]); ib += 1
                    return out_

                run(eb_ops(0))
                emit_K(0)
                emit_L1(0)
                run(mix(dbl_ops(0), eb_ops(1)))
                emit_K(1)
                emit_L1(1)
                emit_L03(0, mix(dbl_ops(1), eb_ops(2)))
                setup_FA(2)
                emit_K(2)
                emit_L1(2)
                emit_L03(1, mix(dbl_ops(2), eb_ops(3)))
                setup_FA(3)
                emit_K(3)
                emit_L1(3)
                emit_L03(2, dbl_ops(3))
                emit_L03(3, [])
                if "YG" in dumps:
                    dump("YG", UT[:], [128, 4, S], BUTp[3][0])
                fw.barrier()
                ph2.close()
                wG = sb(ph2, "wG", [128, 4, 1024], BF16)
                BwG = Buf("wG")
                dsG = fw.dsem("wG")
                for j in range(2):
                    dma("sp", dsG, wG[:, :, j * 512:(j + 1) * 512], wsrc(wGb[j], 4), reads=[Bg[j]], writes=[BwG])
                sg = [sb(ph2, "sg%d" % i, [128, 512], F32) for i in range(2)]
                Bsg = [Buf("sg%d" % i) for i in range(2)]
                cnt = 0
                for t in range(NT):
                    cols = slice(t * 512, (t + 1) * 512)
                    for f in range(4):
                        bv, bg_ = (2 * cnt) % 8, (2 * cnt + 1) % 8
                        s_ = cnt % 2
                        cnt += 1
                        for kc in range(4):
                            op("pe", lambda e, f=f, kc=kc, bv=bv: e.matmul(
                                ps[bv][:], wG[:, kc, f * 128:(f + 1) * 128], UT[:, kc, cols], start=(kc == 0), stop=(kc == 3)),
                               reads=[BwG] + BUTall, writes=[PS[bv]])
                        for kc in range(4):
                            op("pe", lambda e, f=f, kc=kc, bg_=bg_: e.matmul(
                                ps[bg_][:], wG[:, kc, 512 + f * 128:512 + (f + 1) * 128], UT[:, kc, cols], start=(kc == 0), stop=(kc == 3)),
                               reads=[BwG] + BUTall, writes=[PS[bg_]])
                        op("act", lambda e, bg_=bg_, s_=s_: e.activation(out=sg[s_][:], in_=ps[bg_][:], func=AF.Sigmoid),
                           reads=[PS[bg_]], writes=[Bsg[s_]])
                        op("dve", lambda e, bv=bv, s_=s_: e.tensor_tensor(out=sg[s_][:], in0=sg[s_][:], in1=ps[bv][:], op=ALU.mult),
                           reads=[PS[bv], Bsg[s_]], writes=[Bsg[s_]])
                        op("dve", lambda e, f=f, s_=s_, t=t: e.tensor_tensor(
                            out=ZS[:, f, :].rearrange("p (c i) -> p i c", i=L)[:, 2 * t:2 * t + 2, :],
                            in0=sg[s_][:].rearrange("p (i c) -> p i c", i=2), in1=ZSp[:, f, cols].rearrange("p (i c) -> p i c", i=2), op=ALU.mult),
                           reads=[Bsg[s_], BZSp, BZSn], writes=[BZSn])
            fw.barrier()
        if "YS" in dumps:
            dump("YS", ZS[:], [128, 4, S], BZSn)
        if stop_after == "S":
            fw.barrier()
            return nc, dump_d

        KT = sb(top, "KT", [128, 4, S], BF16)
        VA = sb(top, "VA", [128, 32, 4, 130], BF16)
        MKT = sb(top, "MKT", [128, 4, 256], BF16)
        MVA = sb(top, "MVA", [128, 2, 4, 130], BF16)
        BKT, BVA, BMK = Buf("KT"), Buf("VA"), Buf("MK")
        op("dve", lambda e: e.memset(VA[:, :, :, 128:130], 1.0), reads=[], writes=[BVA])
        op("dve", lambda e: e.memset(MVA[:, :, :, 128:130], 1.0), reads=[], writes=[BMK])
        with ExitStack() as phK:
            wKV = sb(phK, "wKV", [128, 8, 1024], BF16)
            wM = sb(phK, "wM", [128, 8, 1024], BF16)
            mT = sb(phK, "mT", [128, 8, 256], BF16)
            BwKV, BwM, BmT = Buf("wKV"), Buf("wM"), Buf("mT")
            dsk_, dsm_, dst_ = fw.dsem("wKV"), fw.dsem("wM"), fw.dsem("mT")
            for j in range(2):
                dma("sp", dsk_, wKV[:, :, j * 512:(j + 1) * 512], wsrc(wInb[3 + j], 8), reads=[Bw[3 + j]], writes=[BwKV])
            for j in range(2):
                dma("sp", dsm_, wM[:, :, j * 512:(j + 1) * 512], wsrc(wMb[j], 8), reads=[Bm[j]], writes=[BwM])
            dma("sp", dst_, mT[:], wsrc(memTb, 8), reads=[Bmem], writes=[BmT])
            xt = [sb(phK, "xtK%d" % i, [128, 8, 512], BF16) for i in range(2)]
            Bxt = [Buf("xtK%d" % i) for i in range(2)]
            dsx = [fw.dsem("xtK%d" % i) for i in range(2)]

            def load_xk(t):
                dma("sp", dsx[t % 2], xt[t % 2][:], wsrc(xTb[t], 8), reads=[Bx[t]], writes=[Bxt[t % 2]])
            load_xk(0)
            cnt = 0
            for h in range(4):
                bk = cnt % 8
                cnt += 1
                for kc in range(8):
                    op("pe", lambda e, h=h, kc=kc, bk=bk: e.matmul(ps[bk][:, 0:256], wM[:, kc, h * 128:(h + 1) * 128], mT[:, kc, :],
                                                                   start=(kc == 0), stop=(kc == 7)), reads=[BwM, BmT], writes=[PS[bk]])
                op("act", lambda e, h=h, bk=bk: e.activation(out=MKT[:, h, :], in_=ps[bk][:, 0:256], func=AF.Copy), reads=[PS[bk]], writes=[BMK])
            for mb in range(2):
                bk = cnt % 8
                cnt += 1
                for kc in range(8):
                    op("pe", lambda e, mb=mb, kc=kc, bk=bk: e.matmul(ps[bk][:], mT[:, kc, mb * 128:(mb + 1) * 128], wM[:, kc, 512:1024],
                                                                     start=(kc == 0), stop=(kc == 7)), reads=[BwM, BmT], writes=[PS[bk]])
                op("act", lambda e, mb=mb, bk=bk: e.activation(out=MVA[:, mb, :, 0:128], in_=ps[bk][:].rearrange("p (h e) -> p h e", h=4), func=AF.Copy),
                   reads=[PS[bk]], writes=[BMK])
            for t in range(NT):
                if t + 1 < NT:
                    load_xk(t + 1)
                x_t, Bx_t = xt[t % 2], Bxt[t % 2]
                cols = slice(t * 512, (t + 1) * 512)
                for h in range(4):
                    bk = cnt % 8
                    cnt += 1
                    for kc in range(8):
                        op("pe", lambda e, h=h, kc=kc, bk=bk: e.matmul(ps[bk][:], wKV[:, kc, h * 128:(h + 1) * 128], x_t[:, kc, :],
                                                                       start=(kc == 0), stop=(kc == 7)), reads=[BwKV, Bx_t], writes=[PS[bk]])
                    op("act", lambda e, h=h, bk=bk: e.activation(out=KT[:, h, cols], in_=ps[bk][:], func=AF.Copy), reads=[PS[bk]], writes=[BKT])
                for tb in range(4):
                    bk = cnt % 8
                    cnt += 1
                    for kc in range(8):
                        op("pe", lambda e, tb=tb, kc=kc, bk=bk: e.matmul(ps[bk][:], x_t[:, kc, tb * 128:(tb + 1) * 128], wKV[:, kc, 512:1024],
                                                                         start=(kc == 0), stop=(kc == 7)), reads=[BwKV, Bx_t], writes=[PS[bk]])
                    op("dve", lambda e, tb=tb, bk=bk, t=t: e.tensor_copy(out=VA[:, 4 * t + tb, :, 0:128], in_=ps[bk][:].rearrange("p (h e) -> p h e", h=4)),
                       reads=[PS[bk]], writes=[BVA])
            fw.barrier()
        if "KT" in dumps:
            dump("KT", KT[:], [128, 4, S], BKT)
            dump("VA", VA[:], [128, 32, 4, 130], BVA)
            dump("MKT", MKT[:], [128, 4, 256], BMK)
            dump("MVA", MVA[:], [128, 2, 4, 130], BMK)
        if stop_after == "KV":
            fw.barrier()
            return nc, dump_d

        with ExitStack() as phF:
            farb = sb(phF, "farb", [128, 4], F32)
            lamv = sb(phF, "lamv", [128, 4, 64], F32)
            subw = sb(phF, "subw", [128, 128], F32)
            lng = sb(phF, "lng", [128, D], F32)
            lnb = sb(phF, "lnb", [128, D], F32)
            NBh = sb(phF, "NBh", [128, 4, 256], BF16)
            NBl = sb(phF, "NBl", [128, 4, 256], BF16)
            lsc = sb(phF, "lsc", [128, 8], F32)
            BC = Buf("constF")
            dsc = fw.dsem("constF")
            tmpF = ExitStack()
            tb_f = sb(tmpF, "tb_f", [128, 4, 256], F32)
            mk_f = sb(tmpF, "mk_f", [128, 4, 256], F32)
            for dst_, nm in ((tb_f, "tbias"), (mk_f, "maskc"), (farb, "farb"), (lamv, "lamv"), (subw, "subw"), (lng, "lng"), (lnb, "lnb")):
                dma("sp", dsc, dst_[:], I[nm], writes=[BC])
            RC, WC = [BC], [BC]
            for h in range(4):
                op("dve", lambda e, h=h: e.tensor_scalar(out=tb_f[:, h, :], in0=tb_f[:, h, :], scalar1=farb[:, h:h + 1], scalar2=None, op0=ALU.subtract), reads=RC, writes=WC)
            op("dve", lambda e: e.tensor_tensor(out=tb_f[:], in0=tb_f[:], in1=mk_f[:], op=ALU.add), reads=RC, writes=WC)
            op("dve", lambda e: e.tensor_copy(out=NBh[:], in_=tb_f[:]), reads=RC, writes=WC)
            op("dve", lambda e: e.tensor_copy(out=mk_f[:], in_=NBh[:]), reads=RC, writes=WC)
            op("dve", lambda e: e.tensor_tensor(out=mk_f[:], in0=tb_f[:], in1=mk_f[:], op=ALU.subtract), reads=RC, writes=WC)
            op("dve", lambda e: e.tensor_copy(out=NBl[:], in_=mk_f[:]), reads=RC, writes=WC)
            fw.barrier()
            tmpF.close()
            op("dve", lambda e: e.tensor_tensor(out=lamv[:, 0, :], in0=lamv[:, 0, :], in1=lamv[:, 1, :], op=ALU.mult), reads=RC, writes=WC)
            op("dve", lambda e: e.tensor_tensor(out=lamv[:, 2, :], in0=lamv[:, 2, :], in1=lamv[:, 3, :], op=ALU.mult), reads=RC, writes=WC)
            op("dve", lambda e: e.reduce_sum(out=lsc[:, 1:2], in_=lamv[:, 0, :], axis=AX.X), reads=RC, writes=WC)
            op("dve", lambda e: e.reduce_sum(out=lsc[:, 2:3], in_=lamv[:, 2, :], axis=AX.X), reads=RC, writes=WC)
            op("act", lambda e: e.activation(out=lsc[:, 3:5], in_=lsc[:, 1:3], func=AF.Exp), reads=RC, writes=WC)
            op("dve", lambda e: e.tensor_tensor(out=lsc[:, 0:1], in0=lsc[:, 4:5], in1=lsc[:, 3:4], op=ALU.subtract), reads=RC, writes=WC)
            op("dve", lambda e: e.tensor_scalar(out=lsc[:, 0:1], in0=lsc[:, 0:1], scalar1=-LAMBDA_INIT, scalar2=None, op0=ALU.add), reads=RC, writes=WC)
            op("dve", lambda e: e.tensor_scalar(out=subw[:], in0=subw[:], scalar1=1.0 - LAMBDA_INIT, scalar2=None, op0=ALU.mult), reads=RC, writes=WC)

            xq = sb(phF, "xq", [128, 8, 512], BF16)
            Bxq = Buf("xq")
            dsxq = fw.dsem("xq")
            NW = 4
            wt = [sb(phF, "wt%d" % i, [128, 8, 512], BF16) for i in range(NW)]
            Bwt = [Buf("wt%d" % i) for i in range(NW)]
            dswt = [fw.dsem("wt%d" % i) for i in range(NW)]
            wctr = [0]

            gseq = [(wInb[2], 8, Bw[2]), (wInb[5], 8, Bw[5]), (wInb[6], 8, Bw[6]), (wInb[7], 8, Bw[7])]
            for br_ in range(3):
                for half_ in range(2):
                    gseq.append((wInb[8 + 2 * br_ + half_], 8, Bw[8 + 2 * br_ + half_]))
                    gseq.append((wBb[br_, half_], 4, Bb[br_][half_]))
            gseq += [(wOb[0], 8, Bo[0]), (wOb[1], 8, Bo[1])]
            wseq = gseq * NT
            wstate = {"issued": 0, "consumed": 0}

            def _issue_w(k):
                piece, nkc, bsrc = wseq[k]
                i = k % NW
                dma("sp", dswt[i], wt[i][:, 0:nkc, :], wsrc(piece, nkc), reads=[bsrc], writes=[Bwt[i]])

            def next_w():
                k = wstate["consumed"]
                while wstate["issued"] < min(len(wseq), k + 3):
                    _issue_w(wstate["issued"])
                    wstate["issued"] += 1
                wstate["consumed"] += 1
                return wt[k % NW], Bwt[k % NW]

            QT = sb(phF, "QT", [128, 4, 512], BF16)
            MQT = QT
            scr = sb(phF, "scr", [128, 4096], F32)
            Bscr = Buf("scr")
            ZD = scr[:, 0:2048].rearrange("p (q f) -> p q f", q=4)
            ZM = ZD
            BQT = Buf("QT")
            BMQ, BZD, BZM = BQT, Bscr, Bscr
            NP = 4
            PT_all = sb(phF, "PT", [128, NP, 512], BF16)
            PT = [PT_all[:, i, :] for i in range(NP)]
            BPT = [Buf("PT%d" % i) for i in range(NP)]
            pctr = [0]
            YD = scr[:, 2048:3072].bitcast(BF16).rearrange("p (q f) -> p q f", q=4)
            YM = scr[:, 3072:4096].bitcast(BF16).rearrange("p (q f) -> p q f", q=4)
            YDT = sb(phF, "YDT", [128, 4, 512], BF16)
            YMT = sb(phF, "YMT", [128, 4, 512], BF16)
            BYD, BYM, BYDT, BYMT = Bscr, Bscr, Buf("YDT"), Buf("YMT")
            xres = YDT[:].rearrange("p h n -> p (h n)").bitcast(F32)
            Bxres = BYDT
            sm = sb(phF, "sm", [128, 32], F32)
            Bsm = Buf("sm")
            od = [None, sb(phF, "od1", [128, 128], F32)]
            Bod = Buf("od")
            od4 = sb(phF, "od4", [128, 4, 128], F32)
            OS = sb(phF, "OS", [128, 2, 520], F32)
            BOS = Buf("OS")
            ACC = scr[:].rearrange("p (q f) -> p q f", q=8)
            BACC = Bscr
            MT = sb(phF, "MT", [128, 8, 512], BF16)
            BMT = Buf("MT")
            sgf_all = sb(phF, "sgf", [128, 2, 512], F32)
            sgf = [sgf_all[:, 0, :], sgf_all[:, 1, :]]
            Bsgf = [Buf("sgf%d" % i) for i in range(2)]

            dsxr = fw.dsem("xres")
            dsout = [fw.dsem("out0"), fw.dsem("out1")]
            stats2 = [sb(phF, "stats%d" % i, [128, 2, 6], F32) for i in range(2)]
            mv2 = [sb(phF, "mv%d" % i, [128, 4], F32) for i in range(2)]
            Bst = [Buf("stats%d" % i) for i in range(2)]

            pctr_bank = [0]

            def gbank():
                b = (0, 1, 2, 3)[pctr_bank[0] % 4]
                pctr_bank[0] += 1
                return b

            OB = [[4, 5], [6, 7]]

            def oreg(c, ql):
                if ql < 3:
                    return ps[OB[c][0]][:, ql * 130:ql * 130 + 130], OB[c][0]
                return ps[OB[c][1]][:, 0:130], OB[c][1]

            def emit_qproj():
                wq, Bwq = next_w()
                for h in range(4):
                    bk = gbank()
                    for kc in range(8):
                        op("pe", lambda e, h=h, kc=kc, bk=bk: e.matmul(ps[bk][:], wq[:, kc, h * 128:(h + 1) * 128], xq[:, kc, :],
                                                                       start=(kc == 0), stop=(kc == 7)), reads=[Bwq, Bxq], writes=[PS[bk]])
                    op("act", lambda e, h=h, bk=bk: e.activation(out=QT[:, h, :], in_=ps[bk][:], func=AF.Copy, scale=0.125), reads=[PS[bk]], writes=[BQT])
                wz, Bwz = next_w()
                for ql in range(4):
                    bk = gbank()
                    for kc in range(8):
                        op("pe", lambda e, ql=ql, kc=kc, bk=bk: e.matmul(ps[bk][:], xq[:, kc, ql * 128:(ql + 1) * 128], wz[:, kc, :],
                                                                         start=(kc == 0), stop=(kc == 7)), reads=[Bwz, Bxq], writes=[PS[bk]])
                    op("act", lambda e, ql=ql, bk=bk: e.activation(out=ZD[:, ql, :], in_=ps[bk][:], func=AF.Silu), reads=[PS[bk]], writes=[BZD])

            pend_ln = []
            for G in range(NT):
                if G == 0:
                    dma("sp", dsxq, xq[:], wsrc(xTb[G], 8), reads=[Bx[G]], writes=[Bxq])
                if G == 0:
                    emit_qproj()
                if stop_after == "F1":
                    dump("QT", QT[:], [128, 4, 512], BQT)
                    dump("ZD", scr[:, 0:2048], [128, 2048], Bscr)
                    fw.barrier()
                    return nc, dump_d
                nkb = 4 * G + 4
                steps = [(h, kb) for h in range(4) for kb in range(nkb)]
                SBANKS = ((0, 1), (2, 3))

                def emit_score(i):
                    h, kb = steps[i]
                    pair = i % 2
                    ql0 = max(0, kb - 4 * G)
                    cs_ = slice(ql0 * 128, 512)
                    nq = [ql for ql in range(4) if 4 * G + ql in (kb, kb + 1)]
                    for c in range(2):
                        rows = slice(64 * c, 64 * c + 64)
                        bk = SBANKS[pair][c]
                        op("pe", lambda e, h=h, kb=kb, bk=bk, cs_=cs_, rows=rows, nq=nq: e.matmul(
                            ps[bk][:, cs_], KT[rows, h, kb * 128:(kb + 1) * 128], QT[rows, h, cs_],
                            start=True, stop=(len(nq) == 0)), reads=[BKT, BQT], writes=[PS[bk]])
                    if nq:
                        qa = nq[0] * 128
                        qrel0 = (4 * G + nq[0] - kb) * 128
                        w_ = 128 * len(nq)
                        for c in range(2):
                            bk = SBANKS[pair][c]
                            for hl, NBx in enumerate((NBh, NBl)):
                                op("pe", lambda e, h=h, bk=bk, qa=qa, qrel0=qrel0, w_=w_, NBx=NBx, hl=hl: e.matmul(
                                    ps[bk][:, qa:qa + w_], ident[:], NBx[:, h, qrel0:qrel0 + w_],
                                    start=False, stop=(hl == 1)), reads=[Bid, BC], writes=[PS[bk]])
                    src2 = (psA if pair == 0 else psB)[:].rearrange("p (c n) -> p c n", c=2)[:, :, cs_]
                    dst2 = PT_all[:, 2 * pair:2 * pair + 2, cs_]
                    b0_, b1_ = SBANKS[pair]
                    op("act", lambda e, h=h, src2=src2, dst2=dst2: e.activation(
                        out=dst2, in_=src2, func=AF.Exp, bias=farb[:, h:h + 1], scale=1.0),
                       reads=[PS[b0_], PS[b1_], BC], writes=[BPT[2 * pair], BPT[2 * pair + 1]])

                def emit_pv(i):
                    h, kb = steps[i]
                    pair = i % 2
                    ql0 = max(0, kb - 4 * G)
                    for c in range(2):
                        pi = 2 * pair + c
                        for ql in range(ql0, 4):
                            oap, obk = oreg(c, ql)
                            st_ = (kb == 0 and ql in (0, 3))
                            last = (kb == 4 * G + ql)
                            op("pe", lambda e, h=h, kb=kb, ql=ql, pi=pi, oap=oap, st_=st_, last=last: e.matmul(
                                oap, PT[pi][:, ql * 128:(ql + 1) * 128], VA[:, kb, h, 0:130], start=st_, stop=last, skip_group_check=True),
                               reads=[BPT[pi], BVA], writes=[PS[obk]])
                    if kb == nkb - 1:
                        for pb in list(pend_b):
                            emit_norm_b(pb[0])
                            pend_b.remove(pb)
                        emit_norm_a(h)
                        pend_b.append([h, 10])

                def emit_norm_a(h):
                    for c in range(2):
                        op("dve", lambda e, c=c: e.tensor_copy(out=OS[:, c, 0:390], in_=ps[OB[c][0]][:, 0:390]), reads=[PS[OB[c][0]]], writes=[BOS])
                        op("dve", lambda e, c=c: e.tensor_copy(out=OS[:, c, 390:520], in_=ps[OB[c][1]][:, 0:130]), reads=[PS[OB[c][1]]], writes=[BOS])
                    R_, W_ = [BOS, Bsm, Bod, BC, BZD], [Bsm, Bod]
                    dens = OS[:].rearrange("p c (q e) -> p c q e", q=4)[:, :, :, 128]
                    op("dve", lambda e: e.reciprocal(out=sm[:, 0:8].rearrange("p (c q) -> p c q", c=2), in_=dens), reads=R_, writes=W_)
                    op("dve", lambda e: e.tensor_scalar(out=sm[:, 4:8], in0=sm[:, 4:8], scalar1=lsc[:, 0:1], scalar2=None, op0=ALU.mult), reads=R_, writes=W_)
                    for ql in range(4):
                        o0 = OS[:, 0, ql * 130:ql * 130 + 128]
                        o1 = OS[:, 1, ql * 130:ql * 130 + 128]
                        op("dve", lambda e, o0=o0, ql=ql: e.tensor_scalar(out=od4[:, ql, :], in0=o0, scalar1=sm[:, ql:ql + 1], scalar2=None, op0=ALU.mult), reads=R_, writes=W_)
                        op("dve", lambda e, o1=o1, ql=ql: e.scalar_tensor_tensor(out=od4[:, ql, :], in0=o1, scalar=sm[:, 4 + ql:5 + ql], in1=od4[:, ql, :], op0=ALU.mult, op1=ALU.add), reads=R_, writes=W_)
                        op("dve", lambda e, ql=ql: e.scalar_tensor_tensor(out=od[1][:], in0=od4[:, ql, :], scalar=1.0, in1=od4[:, ql, :], op0=ALU.mult, op1=ALU.mult,
                                                                         accum_out=sm[:, 8 + ql:9 + ql]), reads=R_, writes=W_)
                    op("dve", lambda e: e.tensor_scalar(out=sm[:, 12:16], in0=sm[:, 8:12], scalar1=1.0 / 128.0, scalar2=1e-5, op0=ALU.mult, op1=ALU.add), reads=R_, writes=W_)

                def emit_norm_b(h):
                    R_, W_ = [BOS, Bsm, Bod, BC, BZD], [Bsm, Bod]
                    op("act", lambda e: e.activation(out=sm[:, 16:20], in_=sm[:, 12:16], func=AF.Ln), reads=R_, writes=W_)
                    op("act", lambda e: e.activation(out=sm[:, 20:24], in_=sm[:, 16:20], func=AF.Exp, scale=-0.5), reads=R_, writes=W_)
                    for ql in range(4):
                        op("dve", lambda e, ql=ql: e.scalar_tensor_tensor(out=od4[:, ql, :], in0=od4[:, ql, :], scalar=sm[:, 20 + ql:21 + ql], in1=subw[:], op0=ALU.mult, op1=ALU.mult), reads=R_, writes=W_)
                    op("dve", lambda e, h=h: e.tensor_tensor(out=YD[:, :, h * 128:(h + 1) * 128], in0=od4[:], in1=ZD[:, :, h * 128:(h + 1) * 128], op=ALU.mult),
                       reads=R_ + [BYD], writes=[BYD, Bod])

                pend_b = []
                emit_score(0)
                for i in range(len(steps)):
                    if i + 1 < len(steps):
                        emit_score(i + 1)
                    emit_pv(i)
                    if pend_ln and (i == 10 or i == len(steps) - 1):
                        for f_ in pend_ln:
                            f_()
                        del pend_ln[:]
                    for pb in list(pend_b):
                        pb[1] -= 1
                        if pb[1] <= 0 or i == len(steps) - 1:
                            emit_norm_b(pb[0])
                            pend_b.remove(pb)

                if stop_after == "F2":
                    dump("YD", scr[:, 2048:3072], [128, 1024], Bscr)
                    dump("QT", QT[:], [128, 4, 512], BQT)
                    dump("ZD", scr[:, 0:2048], [128, 2048], Bscr)
                    op("dve", lambda e: e.tensor_copy(out=sgf[0][:], in_=ps[4][:]), reads=[PS[4]], writes=[Bsgf[0]])
                    op("dve", lambda e: e.tensor_copy(out=sgf[1][:], in_=ps[6][:]), reads=[PS[6]], writes=[Bsgf[1]])
                    dump("O0", sgf[0][:], [128, 512], Bsgf[0])
                    dump("O1", sgf[1][:], [128, 512], Bsgf[1])
                    dump("PT", PT[0], [128, 512], BPT[0])
                    dump("sm", sm[:], [128, 32], Bsm)
                    dump("lsc", lsc[:], [128, 8], BC)
                    fw.barrier()
                    return nc, dump_d
                wmq, Bwmq = next_w()
                for h in range(4):
                    bk = gbank()
                    for kc in range(8):
                        op("pe", lambda e, h=h, kc=kc, bk=bk: e.matmul(ps[bk][:], wmq[:, kc, h * 128:(h + 1) * 128], xq[:, kc, :],
                                                                       start=(kc == 0), stop=(kc == 7)), reads=[Bwmq, Bxq], writes=[PS[bk]])
                    op("act", lambda e, h=h, bk=bk: e.activation(out=MQT[:, h, :], in_=ps[bk][:], func=AF.Copy, scale=128.0 ** -0.5), reads=[PS[bk]], writes=[BMQ])
                wzm, Bwzm = next_w()
                for ql in range(4):
                    bk = gbank()
                    for kc in range(8):
                        op("pe", lambda e, ql=ql, kc=kc, bk=bk: e.matmul(ps[bk][:], xq[:, kc, ql * 128:(ql + 1) * 128], wzm[:, kc, :],
                                                                         start=(kc == 0), stop=(kc == 7)), reads=[Bwzm, Bxq], writes=[PS[bk]])
                    op("act", lambda e, ql=ql, bk=bk: e.activation(out=ZM[:, ql, :], in_=ps[bk][:], func=AF.Silu), reads=[PS[bk]], writes=[BZM])

                for h in range(4):
                    for mb in range(2):
                        bk = gbank()
                        op("pe", lambda e, h=h, mb=mb, bk=bk: e.matmul(ps[bk][:], MKT[:, h, mb * 128:(mb + 1) * 128], MQT[:, h, :], start=True, stop=True),
                           reads=[BMK, BMQ], writes=[PS[bk]])
                        pi = pctr[0] % NP
                        pctr[0] += 1
                        op("act", lambda e, bk=bk, pi=pi: e.activation(out=PT[pi][:], in_=ps[bk][:], func=AF.Exp), reads=[PS[bk]], writes=[BPT[pi]])
                        for ql in range(4):
                            oap, obk = oreg(0, ql)
                            op("pe", lambda e, h=h, mb=mb, ql=ql, pi=pi, oap=oap: e.matmul(
                                oap, PT[pi][:, ql * 128:(ql + 1) * 128], MVA[:, mb, h, 0:130], start=(mb == 0 and ql in (0, 3)), stop=(mb == 1), skip_group_check=True),
                               reads=[BPT[pi], BMK], writes=[PS[obk]])
                    for ql in range(4):
                        o0, b0 = oreg(0, ql)
                        R_, W_ = [PS[b0], Bsm, BZM], [Bsm]
                        op("dve", lambda e, o0=o0: e.reciprocal(out=sm[:, 8:9], in_=o0[:, 128:129]), reads=R_, writes=W_)
                        op("dve", lambda e, o0=o0, ql=ql, h=h: e.scalar_tensor_tensor(
                            out=YM[:, ql, h * 128:(h + 1) * 128], in0=o0[:, 0:128], scalar=sm[:, 8:9], in1=ZM[:, ql, h * 128:(h + 1) * 128], op0=ALU.mult, op1=ALU.mult),
                           reads=R_ + [BYM], writes=[BYM, Bsm])

                if stop_after == "F3":
                    dump("YM", scr[:, 3072:4096], [128, 1024], Bscr)
                    fw.barrier()
                    return nc, dump_d
                for (Ysrc, Bs_, Ydst, Bd_) in ((YD, BYD, YDT, BYDT), (YM, BYM, YMT, BYMT)):
                    for kc in range(4):
                        bk = gbank()
                        pst = ps[bk][:].bitcast(BF16)
                        for ql in range(4):
                            op("pe", lambda e, kc=kc, ql=ql, pst=pst, Ysrc=Ysrc: e.transpose(
                                pst[:, ql * 128:(ql + 1) * 128], Ysrc[:, ql, kc * 128:(kc + 1) * 128], ident[:]),
                               reads=[Bs_, Bid], writes=[PS[bk]])
                        op("dve", lambda e, kc=kc, pst=pst, Ydst=Ydst: e.tensor_copy(out=Ydst[:, kc, :], in_=pst[:, 0:512]), reads=[PS[bk]], writes=[Bd_])

                if stop_after == "F4":
                    dump("YDT", YDT[:], [128, 4, 512], BYDT)
                    dump("YMT", YMT[:], [128, 4, 512], BYMT)
                    fw.barrier()
                    return nc, dump_d
                for br in range(3):
                    if br == 0:
                        ysrc = lambda kc, G=G: ZS[:, kc, G * 512:(G + 1) * 512]
                        By = BZSn
                    elif br == 1:
                        ysrc = lambda kc: YDT[:, kc, :]
                        By = BYDT
                    else:
                        ysrc = lambda kc: YMT[:, kc, :]
                        By = BYMT
                    for half in range(2):
                        wg_, Bwg = next_w()
                        wb_, Bwb = next_w()
                        for fl in range(4):
                            ft = half * 4 + fl
                            bb_, bg_ = gbank(), gbank()
                            for kc in range(4):
                                op("pe", lambda e, kc=kc, fl=fl, bb_=bb_, ysrc=ysrc, wb_=wb_: e.matmul(
                                    ps[bb_][:], wb_[:, kc, fl * 128:(fl + 1) * 128], ysrc(kc), start=(kc == 0), stop=(kc == 3)),
                                   reads=[Bwb, By], writes=[PS[bb_]])
                            for kc in range(8):
                                op("pe", lambda e, kc=kc, fl=fl, bg_=bg_, wg_=wg_: e.matmul(
                                    ps[bg_][:], wg_[:, kc, fl * 128:(fl + 1) * 128], xq[:, kc, :], start=(kc == 0), stop=(kc == 7)),
                                   reads=[Bwg, Bxq], writes=[PS[bg_]])
                            s_ = ft % 2
                            op("act", lambda e, bg_=bg_, s_=s_: e.activation(out=sgf[s_][:], in_=ps[bg_][:], func=AF.Sigmoid), reads=[PS[bg_]], writes=[Bsgf[s_]])
                            if br == 0:
                                op("dve", lambda e, bb_=bb_, s_=s_, ft=ft: e.tensor_tensor(out=ACC[:, ft, :], in0=sgf[s_][:], in1=ps[bb_][:], op=ALU.mult),
                                   reads=[PS[bb_], Bsgf[s_]], writes=[BACC])
                            else:
                                op("dve", lambda e, bb_=bb_, s_=s_: e.tensor_tensor(out=sgf[s_][:], in0=sgf[s_][:], in1=ps[bb_][:], op=ALU.mult),
                                   reads=[PS[bb_], Bsgf[s_]], writes=[Bsgf[s_]])
                                if br == 1:
                                    op("dve", lambda e, s_=s_, ft=ft: e.tensor_tensor(out=ACC[:, ft, :], in0=ACC[:, ft, :], in1=sgf[s_][:], op=ALU.add),
                                       reads=[Bsgf[s_], BACC], writes=[BACC])
                                else:
                                    op("dve", lambda e, s_=s_, ft=ft: e.tensor_tensor(out=MT[:, ft, :], in0=ACC[:, ft, :], in1=sgf[s_][:], op=ALU.add),
                                       reads=[Bsgf[s_], BACC], writes=[BMT])

                if stop_after == "F5":
                    dump("MT", MT[:], [128, 8, 512], BMT)
                    fw.barrier()
                    return nc, dump_d
                if G + 1 < NT:
                    dma("sp", dsxq, xq[:], wsrc(xTb[G + 1], 8), reads=[Bx[G + 1]], writes=[Bxq])
                wo = [next_w() for j in range(2)]
                HH = [sgf_all[:].rearrange("p a n -> p (a n)"), YMT[:].rearrange("p h n -> p (h n)").bitcast(F32)]
                HHB = [[Bsgf[0], Bsgf[1]], [BYMT]]
                OBK = (0, 1, 2, 3, 4, 5, 6, 7)

                def ln_head(ql):
                    r0 = G * 512 + ql * 128
                    Hc, HB = HH[ql % 2], HHB[ql % 2]
                    dma("sp", dsxr, xres[:], I["x"][r0:r0 + 128, :], writes=[Bxres])
                    for half in range(2):
                        bk = OBK[2 * ql + half]
                        wo_, Bwo = wo[half]
                        for kc in range(8):
                            op("pe", lambda e, kc=kc, ql=ql, bk=bk, wo_=wo_: e.matmul(
                                ps[bk][:], MT[:, kc, ql * 128:(ql + 1) * 128], wo_[:, kc, :], start=(kc == 0), stop=(kc == 7)),
                               reads=[BMT, Bwo], writes=[PS[bk]])
                        hs = slice(half * 512, (half + 1) * 512)
                        op("dve", lambda e, bk=bk, hs=hs, Hc=Hc: e.scalar_tensor_tensor(out=Hc[:, hs], in0=xres[:, hs], scalar=ALPHA, in1=ps[bk][:], op0=ALU.mult, op1=ALU.add),
                           reads=[PS[bk], Bxres] + HB, writes=HB)
                    R_, W_ = HB + [Bst[ql % 2], BC], [Bst[ql % 2]] + HB
                    st_, mv_ = stats2[ql % 2], mv2[ql % 2]
                    for half in range(2):
                        op("dve", lambda e, half=half, Hc=Hc: e.bn_stats(out=st_[:, half, :], in_=Hc[:, half * 512:(half + 1) * 512]), reads=R_, writes=W_)
                    op("dve", lambda e: e.bn_aggr(out=mv_[:, 0:2], in_=st_[:].rearrange("p a b -> p (a b)")), reads=R_, writes=W_)
                    op("dve", lambda e: e.tensor_scalar(out=mv_[:, 2:3], in0=mv_[:, 1:2], scalar1=1e-5, scalar2=None, op0=ALU.add), reads=R_, writes=W_)

                def ln_tail(ql, G=G, HH=HH, HHB=HHB):
                    r0 = G * 512 + ql * 128
                    Hc, HB = HH[ql % 2], HHB[ql % 2]
                    R_, W_ = HB + [Bst[ql % 2], BC], [Bst[ql % 2]] + HB
                    mv_ = mv2[ql % 2]
                    op("act", lambda e: e.activation(out=mv_[:, 2:3], in_=mv_[:, 2:3], func=AF.Ln), reads=R_, writes=W_)
                    op("act", lambda e: e.activation(out=mv_[:, 3:4], in_=mv_[:, 2:3], func=AF.Exp, scale=-0.5), reads=R_, writes=W_)
                    op("dve", lambda e, Hc=Hc: e.tensor_scalar(out=Hc, in0=Hc, scalar1=mv_[:, 0:1], scalar2=mv_[:, 3:4], op0=ALU.subtract, op1=ALU.mult), reads=R_, writes=W_)
                    op("pool", lambda e, Hc=Hc: e.tensor_tensor(out=Hc, in0=Hc, in1=lng[:], op=ALU.mult), reads=HB + [BC], writes=HB)
                    op("pool", lambda e, Hc=Hc: e.tensor_tensor(out=Hc, in0=Hc, in1=lnb[:], op=ALU.add), reads=HB + [BC], writes=HB)
                    dma("pool", dsout[ql % 2], out_d[r0:r0 + 128, :], Hc, reads=HB)

                ln_head(0)
                ln_tail(0)
                ln_head(1)
                ln_tail(1)
                ln_head(2)
                ln_head(3)
                if G + 1 < NT:
                    emit_qproj()
                    pend_ln.extend([lambda f=ln_tail: f(2), lambda f=ln_tail: f(3)])
                else:
                    ln_tail(2)
                    ln_tail(3)
                if stop_after == "F6":
                    fw.barrier()
                    return nc, dump_d
            fw.barrier()
        fw.barrier()
    return nc, dump_d


_NC_CACHE = {}


def kernel(**inputs):
    inp = {k: np.asarray(v) for k, v in inputs.items()}
    shared = _shared_layouts(inp)
    if "nc" not in _NC_CACHE:
        _NC_CACHE["nc"] = build()[0]
    nc = _NC_CACHE["nc"]
    in_maps = []
    for b in range(8):
        m = dict(shared)
        m["x"] = np.ascontiguousarray(inp["x"][b])
        m["xT"] = np.ascontiguousarray(inp["x"][b].T)
        m["memT"] = np.ascontiguousarray(inp["mem"][b].T)
        in_maps.append(m)
    res = run_bass_kernel_spmd(nc, in_maps, core_ids=list(range(8)))
    return np.stack([np.asarray(r["out"]) for r in res.results]).astype(np.float32)
```

```python
import math
from contextlib import ExitStack
import numpy as np
import concourse.bass as bass
import concourse.mybir as mybir
from concourse.bass_utils import run_bass_kernel_spmd

F32 = mybir.dt.float32
BF16 = mybir.dt.bfloat16
I32 = mybir.dt.int32
AF = mybir.ActivationFunctionType
ALU = mybir.AluOpType
AX = mybir.AxisListType

S = 4096
D = 1024
NT = 8
L = 16
NB = S // L
ALPHA = 2.0 ** 0.25
LAMBDA_INIT = 0.8 - 0.6 * math.exp(0.0)
PI = float(np.pi)


class Buf:
    __slots__ = ("name", "w", "r")

    def __init__(self, name):
        self.name = name
        self.w = None
        self.r = {}


class Eng:
    def __init__(self, name, eng, sem):
        self.name, self.eng, self.sem = name, eng, sem
        self.cnt = 0
        self.seen = {}


class FW:
    def __init__(self, nc, stack):
        self.nc, self.stack = nc, stack
        self.sems, self.engs, self.cnts = {}, {}, {}
        for name, eng in (("pe", nc.tensor), ("act", nc.scalar), ("dve", nc.vector),
                          ("pool", nc.gpsimd), ("sp", nc.sync)):
            sem = stack.enter_context(nc.semaphore("sem_" + name))
            self.sems[name] = sem
            self.engs[name] = Eng(name, eng, sem)
            self.cnts[name] = 0

    def dsem(self, name):
        self.sems[name] = self.stack.enter_context(self.nc.semaphore("d_" + name))
        self.cnts[name] = 0
        return name

    def _wait(self, e, key, val):
        if key == "pe" and e.name == "pe":
            return
        if e.seen.get(key, 0) >= val:
            return
        e.seen[key] = val
        e.eng.wait_ge(self.sems[key], val)

    def _deps(self, e, reads, writes):
        for b in reads:
            if b.w is not None:
                self._wait(e, *b.w)
        for b in writes:
            if b.w is not None:
                self._wait(e, *b.w)
            for k, v in b.r.items():
                self._wait(e, k, v)

    def _mark(self, tag, reads, writes):
        for b in reads:
            if b.r.get(tag[0], 0) < tag[1]:
                b.r[tag[0]] = tag[1]
        for b in writes:
            b.w = tag
            b.r = {}

    def op(self, en, fn, reads=(), writes=()):
        e = self.engs[en]
        self._deps(e, reads, writes)
        ins = fn(e.eng)
        self.cnts[en] += 1
        ins.then_inc(e.sem, 1)
        self._mark((en, self.cnts[en]), reads, writes)

    def dma(self, q, dsem, out, in_, reads=(), writes=()):
        e = self.engs[q]
        self._deps(e, reads, writes)
        ins = e.eng.dma_start(out=out, in_=in_)
        self.cnts[dsem] += 16
        ins.then_inc(self.sems[dsem], 16)
        self._mark((dsem, self.cnts[dsem]), reads, writes)

    def barrier(self, only=None, skip=()):
        for e in self.engs.values():
            if only is not None and e.name not in only:
                continue
            for k, v in self.cnts.items():
                if v > 0 and not k.startswith("cv_") and k not in skip:
                    self._wait(e, k, v)


def _t5_bucket_np(rel):
    ret = np.where(rel > 0, 16, 0)
    n = np.abs(rel)
    v = (np.log(np.maximum(n, 1).astype(np.float32) / np.float32(8)) / np.float32(math.log(16))
         * np.float32(8))
    large = np.minimum(8 + v.astype(np.int32), 15)
    return ret + np.where(n < 8, n, large)


def _shared_layouts(inp):
    f = np.float32
    o = {}
    lam_re, lam_im, log_dt = inp["lam_re"][0], inp["lam_im"][0], inp["log_dt"][0]
    b_re, b_im, c_re, c_im = inp["b_re"][0], inp["b_im"][0], inp["c_re"][0], inp["c_im"][0]
    def layA(v):
        return np.ascontiguousarray(v.reshape(16, 2, 64).transpose(1, 2, 0).reshape(128, 16)).astype(f)
    o["lamA"] = np.stack([layA(lam_re), layA(lam_im),
                          layA(np.broadcast_to(log_dt[:, None], (32, 64)))], axis=1)
    cbd = np.zeros((2, 2, 64, 16, 2, 16), f)
    bbd = np.zeros((2, 2, 64, 16, 2, 16), f)
    for ri, (cc, bb) in enumerate(((c_re, b_re), (c_im, b_im))):
        c4 = cc.reshape(16, 2, 16, 64)
        b4 = bb.reshape(16, 2, 64, 16)
        for g2 in range(2):
            cbd[ri, g2, :, :, g2, :] = c4[:, g2].transpose(2, 0, 1)
            bbd[ri, g2, :, :, g2, :] = b4[:, g2].transpose(1, 0, 2)
    o["cbdA"] = np.ascontiguousarray(cbd.reshape(2, 128, 16, 32).transpose(1, 0, 2, 3))
    o["bbdA"] = np.ascontiguousarray(bbd.reshape(2, 128, 16, 32).transpose(1, 0, 2, 3))
    def layB(v):
        a = v.reshape(4, 4, 2, 64).transpose(1, 0, 2, 3)
        a = np.broadcast_to(a[:, None], (4, 32, 4, 2, 64))
        return np.ascontiguousarray(a.reshape(128, 4, 128)).astype(f)
    o["lamB"] = np.stack([layB(lam_re), layB(lam_im),
                          layB(np.broadcast_to(log_dt[:, None], (32, 64)))], axis=1)
    btb = np.zeros((2, 4, 2, 16, 4, 2, 64), f)
    for ri, bb in enumerate((b_re, b_im)):
        b5 = bb.reshape(4, 4, 2, 64, 16)
        for g2 in range(2):
            btb[ri, :, g2, :, :, g2, :] = b5[:, :, g2].transpose(1, 3, 0, 2)
    o["btB"] = np.ascontiguousarray(btb.reshape(2, 128, 4, 128).transpose(1, 0, 2, 3))
    dsk = np.zeros((128, 4, 128), f)
    ds = inp["d_skip"][0].reshape(4, 128)
    for t in range(4):
        dsk[np.arange(128), t, np.arange(128)] = ds[t]
    o["dskD"] = dsk
    kl = np.arange(128)[:, None]
    qr = np.arange(256)[None, :]
    bidx = _t5_bucket_np(kl - qr)
    rb = inp["rel_bias"]
    o["tbias"] = np.ascontiguousarray(rb[bidx].transpose(0, 2, 1)).astype(f)
    maskc = np.where((kl >= 64) & (qr < 64), -30000.0, 0.0).astype(f)
    o["maskc"] = np.ascontiguousarray(np.broadcast_to(maskc[:, None, :], (128, 4, 256)))
    o["farb"] = np.ascontiguousarray(np.broadcast_to(rb[15][None, :], (128, 4))).astype(f)
    o["lamv"] = np.ascontiguousarray(np.broadcast_to(
        np.stack([inp["lambda_q1"][0], inp["lambda_k1"][0], inp["lambda_q2"][0], inp["lambda_k2"][0]])[None],
        (128, 4, 64))).astype(f)
    o["subw"] = np.ascontiguousarray(np.broadcast_to(inp["subln_w"][0][None], (128, 128))).astype(f)
    o["lng"] = np.ascontiguousarray(np.broadcast_to(inp["ln_g"][0][None], (128, D))).astype(f)
    o["lnb"] = np.ascontiguousarray(np.broadcast_to(inp["ln_b"][0][None], (128, D))).astype(f)
    o["ident"] = np.eye(128, dtype=f)
    o["w_in"] = np.ascontiguousarray(inp["w_in"][0])
    o["w_glu"] = np.ascontiguousarray(inp["w_glu"][0])
    o["w_mem_kv"] = np.ascontiguousarray(inp["w_mem_kv"][0])
    o["w_br"] = np.ascontiguousarray(np.stack([inp["w_br_ssm"][0], inp["w_br_diff"][0], inp["w_br_mem"][0]]))
    o["w_out"] = np.ascontiguousarray(inp["w_out"][0])
    return o


IN_SHAPES = {
    "xT": [D, S], "x": [S, D], "memT": [D, 256],
    "lamA": [128, 3, 16], "cbdA": [128, 2, 16, 32], "bbdA": [128, 2, 16, 32],
    "lamB": [128, 3, 4, 128], "btB": [128, 2, 4, 128], "dskD": [128, 4, 128],
    "tbias": [128, 4, 256], "maskc": [128, 4, 256], "farb": [128, 4], "lamv": [128, 4, 64],
    "subw": [128, 128], "lng": [128, D], "lnb": [128, D], "ident": [128, 128],
    "w_in": [D, 7168], "w_glu": [512, 1024], "w_mem_kv": [D, 1024], "w_br": [3, 512, 1024],
    "w_out": [D, D],
}


def build(stop_after=None, dumps=()):
    nc = bass.Bass("TRN2", target_bir_lowering=False)
    I = {k: nc.dram_tensor(k, v, F32, kind="ExternalInput").ap() for k, v in IN_SHAPES.items()}
    out_d = nc.dram_tensor("out", [S, D], F32, kind="ExternalOutput").ap()
    dump_d = {}
    xTb = nc.dram_tensor("xTb", [NT, D, 512], BF16).ap()
    wInb = nc.dram_tensor("wInb", [14, D, 512], BF16).ap()
    wGb = nc.dram_tensor("wGb", [2, 512, 512], BF16).ap()
    wMb = nc.dram_tensor("wMb", [2, D, 512], BF16).ap()
    wBb = nc.dram_tensor("wBb", [3, 2, 512, 512], BF16).ap()
    wOb = nc.dram_tensor("wOb", [2, D, 512], BF16).ap()
    memTb = nc.dram_tensor("memTb", [D, 256], BF16).ap()

    with ExitStack() as top:
        fw = FW(nc, top)
        op, dma = fw.op, fw.dma

        def sb(st, name, shape, dt):
            return st.enter_context(nc.sbuf_tensor("s_" + name, list(shape), dt))

        def dump(name, ap_sb, shape, buf):
            d = nc.dram_tensor("dump_" + name, list(shape), ap_sb.dtype, kind="ExternalOutput").ap()
            dump_d[name] = d
            dma("sp", fw.dsem("dump_" + name), d, ap_sb, reads=[buf])

        psA = top.enter_context(nc.psum_tensor("psA", [128, 1024], F32))
        psB = top.enter_context(nc.psum_tensor("psB", [128, 1024], F32))
        ps = [psA[:, 0:512], psA[:, 512:1024], psB[:, 0:512], psB[:, 512:1024]] + \
             [top.enter_context(nc.psum_tensor("ps%d" % i, [128, 512], F32)) for i in range(4, 8)]
        PS = [Buf("ps%d" % i) for i in range(8)]

        Bx = [Buf("xTb%d" % t) for t in range(NT)]
        Bw = [Buf("wInb%d" % j) for j in range(14)]
        Bg = [Buf("wGb%d" % j) for j in range(2)]
        Bm = [Buf("wMb%d" % j) for j in range(2)]
        Bb = [[Buf("wBb%d%d" % (b, j)) for j in range(2)] for b in range(3)]
        Bo = [Buf("wOb%d" % j) for j in range(2)]
        Bmem = Buf("memTb")

        def conv(dst, src, buf):
            dma("pool", fw.dsem("cv_" + buf.name), dst, src, writes=[buf])

        conv(wInb[0], I["w_in"][:, 0:512], Bw[0])
        conv(wInb[1], I["w_in"][:, 512:1024], Bw[1])
        for t in range(NT):
            conv(xTb[t], I["xT"][:, t * 512:(t + 1) * 512], Bx[t])
        for j in range(2):
            conv(wGb[j], I["w_glu"][:, j * 512:(j + 1) * 512], Bg[j])
        for j in (3, 4):
            conv(wInb[j], I["w_in"][:, j * 512:(j + 1) * 512], Bw[j])
        for j in range(2):
            conv(wMb[j], I["w_mem_kv"][:, j * 512:(j + 1) * 512], Bm[j])
        conv(memTb, I["memT"], Bmem)
        for j in (2, 5, 6, 7, 8, 9, 10, 11, 12, 13):
            conv(wInb[j], I["w_in"][:, j * 512:(j + 1) * 512], Bw[j])
        for b in range(3):
            for j in range(2):
                conv(wBb[b, j], I["w_br"][b, :, j * 512:(j + 1) * 512], Bb[b][j])
        for j in range(2):
            conv(wOb[j], I["w_out"][:, j * 512:(j + 1) * 512], Bo[j])

        def wsrc(piece, nkc):
            return piece.rearrange("(kc p) n -> p kc n", p=128)

        ident_f = sb(top, "ident_f", [128, 128], F32)
        ident = sb(top, "ident", [128, 128], BF16)
        Bid = Buf("ident")
        dma("sp", fw.dsem("c_id"), ident_f[:], I["ident"], writes=[Bid])
        op("dve", lambda e: e.tensor_copy(out=ident[:], in_=ident_f[:]), reads=[Bid], writes=[Bid])

        ZS = sb(top, "ZS", [128, 4, S], BF16)
        BZSn = Buf("ZSn")

        with ExitStack() as phS:
            UT = sb(phS, "UT", [128, 4, S], BF16)
            BUTp = [[Buf("UT%d_%d" % (i, j)) for j in range(8)] for i in range(4)]
            BUTall = [b for bl in BUTp for b in bl]
            ZSp = sb(phS, "ZSp", [128, 4, S], BF16)
            BZSp = Buf("ZSp")

            with ExitStack() as ph2:
                def cplx_setup(st, pref, lam_ap, shape, Bsrc, keep):
                    T = {}
                    for nm in ("dt", "th", "mg", "cs", "sn", "t1", "t2", "den"):
                        T[nm] = sb(st, pref + nm, shape, F32)
                    T.update(keep)
                    ti = sb(st, pref + "ti", shape, I32)
                    Bt = keep["buf"]
                    R, W = [Bsrc, Bt], [Bt]
                    lre, lim, ldt = lam_ap(0), lam_ap(1), lam_ap(2)
                    op("act", lambda e: e.activation(out=T["dt"][:], in_=ldt, func=AF.Exp), reads=R, writes=W)
                    op("dve", lambda e: e.tensor_tensor(out=T["th"][:], in0=lim, in1=T["dt"][:], op=ALU.mult), reads=R, writes=W)
                    op("dve", lambda e: e.tensor_tensor(out=T["t0"][:], in0=lre, in1=T["dt"][:], op=ALU.mult), reads=R, writes=W)
                    op("act", lambda e: e.activation(out=T["mg"][:], in_=T["t0"][:], func=AF.Exp), reads=R, writes=W)

                    def sin_of(dst, shift):
                        op("dve", lambda e: e.tensor_scalar(out=T["t0"][:], in0=T["th"][:], scalar1=shift, scalar2=None, op0=ALU.add), reads=R, writes=W)
                        op("dve", lambda e: e.tensor_scalar(out=T["t1"][:], in0=T["t0"][:], scalar1=1.0 / (2 * PI), scalar2=None, op0=ALU.mult), reads=R, writes=W)
                        op("dve", lambda e: e.tensor_copy(out=ti[:], in_=T["t1"][:]), reads=R, writes=W)
                        op("dve", lambda e: e.tensor_copy(out=T["t1"][:], in_=ti[:]), reads=R, writes=W)
                        op("dve", lambda e: e.scalar_tensor_tensor(out=T["t2"][:], in0=T["t1"][:], scalar=-2 * PI, in1=T["t0"][:], op0=ALU.mult, op1=ALU.add), reads=R, writes=W)
                        op("dve", lambda e: e.tensor_scalar(out=T["t1"][:], in0=T["t2"][:], scalar1=PI, scalar2=-2 * PI, op0=ALU.is_gt, op1=ALU.mult), reads=R, writes=W)
                        op("dve", lambda e: e.tensor_tensor(out=T["t2"][:], in0=T["t2"][:], in1=T["t1"][:], op=ALU.add), reads=R, writes=W)
                        op("dve", lambda e: e.tensor_scalar(out=T["t1"][:], in0=T["t2"][:], scalar1=-PI, scalar2=2 * PI, op0=ALU.is_lt, op1=ALU.mult), reads=R, writes=W)
                        op("dve", lambda e: e.tensor_tensor(out=T["t2"][:], in0=T["t2"][:], in1=T["t1"][:], op=ALU.add), reads=R, writes=W)
                        op("act", lambda e: e.activation(out=dst[:], in_=T["t2"][:], func=AF.Sin), reads=R, writes=W)
                    sin_of(T["sn"], 0.0)
                    sin_of(T["cs"], PI / 2)
                    op("dve", lambda e: e.tensor_tensor(out=T["lbr"][:], in0=T["mg"][:], in1=T["cs"][:], op=ALU.mult), reads=R, writes=W)
                    op("dve", lambda e: e.tensor_tensor(out=T["lbi"][:], in0=T["mg"][:], in1=T["sn"][:], op=ALU.mult), reads=R, writes=W)
                    op("dve", lambda e: e.tensor_tensor(out=T["den"][:], in0=lre, in1=lre, op=ALU.mult), reads=R, writes=W)
                    op("dve", lambda e: e.tensor_tensor(out=T["t0"][:], in0=lim, in1=lim, op=ALU.mult), reads=R, writes=W)
                    op("dve", lambda e: e.tensor_tensor(out=T["den"][:], in0=T["den"][:], in1=T["t0"][:], op=ALU.add), reads=R, writes=W)
                    op("dve", lambda e: e.reciprocal(out=T["den"][:], in_=T["den"][:]), reads=R, writes=W)
                    op("dve", lambda e: e.tensor_scalar(out=T["t0"][:], in0=T["lbr"][:], scalar1=-1.0, scalar2=None, op0=ALU.add), reads=R, writes=W)
                    op("dve", lambda e: e.tensor_tensor(out=T["t1"][:], in0=T["t0"][:], in1=lre, op=ALU.mult), reads=R, writes=W)
                    op("dve", lambda e: e.tensor_tensor(out=T["t2"][:], in0=T["lbi"][:], in1=lim, op=ALU.mult), reads=R, writes=W)
                    op("dve", lambda e: e.tensor_tensor(out=T["t1"][:], in0=T["t1"][:], in1=T["t2"][:], op=ALU.add), reads=R, writes=W)
                    op("dve", lambda e: e.tensor_tensor(out=T["cfr"][:], in0=T["t1"][:], in1=T["den"][:], op=ALU.mult), reads=R, writes=W)
                    op("dve", lambda e: e.tensor_tensor(out=T["t1"][:], in0=T["lbi"][:], in1=lre, op=ALU.mult), reads=R, writes=W)
                    op("dve", lambda e: e.tensor_tensor(out=T["t2"][:], in0=T["t0"][:], in1=lim, op=ALU.mult), reads=R, writes=W)
                    op("dve", lambda e: e.tensor_tensor(out=T["t1"][:], in0=T["t1"][:], in1=T["t2"][:], op=ALU.subtract), reads=R, writes=W)
                    op("dve", lambda e: e.tensor_tensor(out=T["cfi"][:], in0=T["t1"][:], in1=T["den"][:], op=ALU.mult), reads=R, writes=W)
                    return T, Bt

                def cmul(out_r, out_i, ar, ai, br, bi, t0, R, W, en="dve"):
                    op(en, lambda e: e.tensor_tensor(out=out_r, in0=ar, in1=br, op=ALU.mult), reads=R, writes=W)
                    op(en, lambda e: e.tensor_tensor(out=t0, in0=ai, in1=bi, op=ALU.mult), reads=R, writes=W)
                    op(en, lambda e: e.tensor_tensor(out=out_r, in0=out_r, in1=t0, op=ALU.subtract), reads=R, writes=W)
                    op(en, lambda e: e.tensor_tensor(out=out_i, in0=ar, in1=bi, op=ALU.mult), reads=R, writes=W)
                    op(en, lambda e: e.tensor_tensor(out=t0, in0=ai, in1=br, op=ALU.mult), reads=R, writes=W)
                    op(en, lambda e: e.tensor_tensor(out=out_i, in0=out_i, in1=t0, op=ALU.add), reads=R, writes=W)

                lamA = sb(ph2, "lamA", [128, 3, 16], F32)
                cbdA = sb(ph2, "cbdA", [128, 2, 16, 32], F32)
                keepA = {nm: sb(ph2, "A_" + nm, [128, 16], F32) for nm in ("lbr", "lbi", "cfr", "cfi", "t0")}
                keepA["buf"] = Buf("A_tmp")
                PW = sb(ph2, "PW", [128, 17, 2, 16], F32)
                AD = sb(ph2, "AD", [128, 8, 3, 16], F32)
                PWn = sb(ph2, "PWn", [128, 17, 16], F32)
                BbA = sb(ph2, "BbA", [128, 2, 16, 32], BF16)
                dskD = sb(ph2, "dskD", [128, 4, 128], F32)
                XB0 = sb(ph2, "XB0", [128, 2, 512], F32)
                LB1 = sb(ph2, "LB1", [128, 2, 512], F32)
                LB2 = sb(ph2, "LB2", [128, 2, 512], F32)
                keepB = {nm: sb(ph2, "B_" + nm, [128, 512], F32) for nm in ("cfr", "cfi", "t0")}
                keepB["lbr"], keepB["lbi"] = LB1[:, 0, :], LB1[:, 1, :]
                keepB["buf"] = Buf("B_tmp")
                FA2 = [sb(ph2, "FA%d" % i, [128, 17, 2, 4, 32], BF16) for i in range(2)]
                BFA2 = [Buf("FA%d" % i) for i in range(2)]
                tP8 = [sb(ph2, "tP%d" % i, [128, 4, 32], F32) for i in range(8)]
                BtP8 = [Buf("tP%d" % i) for i in range(8)]
                BA, BB = Buf("layA"), Buf("layB")
                dsA, dsB = fw.dsem("layA"), fw.dsem("layB")
                dma("sp", dsA, lamA[:], I["lamA"], writes=[BA])
                dma("sp", dsA, cbdA[:], I["cbdA"], writes=[BA])
                dma("sp", dsB, dskD[:], I["dskD"], writes=[BB])

                def bc(ap2, n=16):
                    return ap2.unsqueeze(2).to_broadcast([128, n, 32])

                def setup_FA(T_):
                    par = T_ % 2
                    FA, BFA = FA2[par], BFA2[par]
                    Ps = slice(4 * T_, 4 * T_ + 4)
                    cr, ci = cbdA[:, 0, Ps], cbdA[:, 1, Ps]
                    for a in range(17):
                        tt = tP8[4 * (a % 2):4 * (a % 2) + 4]
                        tb = BtP8[4 * (a % 2):4 * (a % 2) + 4]
                        pr, pi_, npi = bc(PW[:, a, 0, Ps], 4), bc(PW[:, a, 1, Ps], 4), bc(PWn[:, a, Ps], 4)
                        R0 = [BA, BtA]
                        op("pool", lambda e: e.tensor_tensor(out=tt[0][:], in0=cr, in1=pr, op=ALU.mult), reads=R0, writes=[tb[0]])
                        op("pool", lambda e: e.tensor_tensor(out=tt[1][:], in0=ci, in1=pi_, op=ALU.mult), reads=R0, writes=[tb[1]])
                        op("pool", lambda e: e.tensor_tensor(out=tt[2][:], in0=cr, in1=npi, op=ALU.mult), reads=R0, writes=[tb[2]])
                        op("pool", lambda e: e.tensor_tensor(out=tt[3][:], in0=ci, in1=pr, op=ALU.mult), reads=R0, writes=[tb[3]])
                        op("pool", lambda e, a=a: e.tensor_tensor(out=FA[:, a, 0], in0=tt[0][:], in1=tt[1][:], op=ALU.subtract),
                           reads=[tb[0], tb[1], BFA], writes=[BFA])
                        op("pool", lambda e, a=a: e.tensor_tensor(out=FA[:, a, 1], in0=tt[2][:], in1=tt[3][:], op=ALU.subtract),
                           reads=[tb[2], tb[3], BFA], writes=[BFA])

                ph1 = ExitStack()
                wS = sb(ph1, "wS", [128, 8, 1024], BF16)
                BwS = Buf("wS")
                ds_w = fw.dsem("wS")
                for j in range(2):
                    dma("sp", ds_w, wS[:, :, j * 512:(j + 1) * 512], wsrc(wInb[j], 8), reads=[Bw[j]], writes=[BwS])
                xth = [sb(ph1, "xtS%d" % i, [128, 4, 512], BF16) for i in range(2)]
                Bxth = [Buf("xtS%d" % i) for i in range(2)]
                dsxh = [fw.dsem("xtS%d" % i) for i in range(2)]

                def load_xh(t, hf):
                    dma("sp", dsxh[hf], xth[hf][:], wsrc(xTb[t], 8)[:, 4 * hf:4 * hf + 4, :], reads=[Bx[t]], writes=[Bxth[hf]])
                load_xh(0, 0)
                load_xh(0, 1)

                with ExitStack() as tmpS:
                    bbdA = sb(tmpS, "bbdA", [128, 2, 16, 32], F32)
                    lamB = sb(tmpS, "lamB", [128, 3, 512], F32)
                    btB = sb(tmpS, "btB", [128, 2, 512], F32)
                    dma("sp", dsA, bbdA[:], I["bbdA"], writes=[BA])
                    dma("sp", dsB, lamB[:], I["lamB"].rearrange("p a t n -> p a (t n)"), writes=[BB])
                    dma("sp", dsB, btB[:], I["btB"].rearrange("p a t n -> p a (t n)"), writes=[BB])
                    TA, BtA = cplx_setup(tmpS, "A_", lambda i: lamA[:, i, :], [128, 16], BA, keepA)
                    TA = keepA
                    RA, WA_ = [BA, BtA], [BtA]
                    op("dve", lambda e: e.memset(PW[:, 0, 0, :], 1.0), reads=RA, writes=WA_)
                    op("dve", lambda e: e.memset(PW[:, 0, 1, :], 0.0), reads=RA, writes=WA_)
                    for a in range(1, 17):
                        cmul(PW[:, a, 0, :], PW[:, a, 1, :], PW[:, a - 1, 0, :], PW[:, a - 1, 1, :],
                             TA["lbr"][:], TA["lbi"][:], TA["t0"][:], RA, WA_)
                    op("dve", lambda e: e.tensor_copy(out=AD[:, 0, 0:2, :], in_=PW[:, 16, :, :]), reads=RA, writes=WA_)
                    op("dve", lambda e: e.tensor_scalar(out=PWn[:], in0=PW[:, :, 1, :], scalar1=-1.0, scalar2=None, op0=ALU.mult), reads=RA, writes=WA_)
                    for k in range(1, 8):
                        cmul(AD[:, k, 0, :], AD[:, k, 1, :], AD[:, k - 1, 0, :], AD[:, k - 1, 1, :],
                             AD[:, k - 1, 0, :], AD[:, k - 1, 1, :], TA["t0"][:], RA, WA_)
                    op("dve", lambda e: e.tensor_scalar(out=AD[:, :, 2, :], in0=AD[:, :, 1, :], scalar1=-1.0, scalar2=None, op0=ALU.mult), reads=RA, writes=WA_)
                    setup_FA(0)
                    setup_FA(1)
                    TBt, BtB_ = cplx_setup(tmpS, "B_", lambda i: lamB[:, i, :], [128, 512], BB, keepB)
                    TB = keepB
                    RB, WB_ = [BB, BtB_], [BtB_]
                    cmul(XB0[:, 0], XB0[:, 1], btB[:, 0], btB[:, 1], TB["cfr"][:], TB["cfi"][:], TB["t0"][:], RB, WB_)
                    op("dve", lambda e: e.tensor_scalar(out=LB2[:, 0, :], in0=TB["lbi"][:], scalar1=-1.0, scalar2=None, op0=ALU.mult), reads=RB, writes=WB_)
                    op("dve", lambda e: e.tensor_copy(out=LB2[:, 1, :], in_=TB["lbr"][:]), reads=RB, writes=WB_)
                    tA = [TBt[nm][:].rearrange("p (a b) -> p a b", a=16) for nm in ("t1", "t2", "den")]
                    RAB, WAB = [BA, BtA, BtB_], [BtA, BtB_]
                    cmul(tA[0], tA[1], bbdA[:, 0], bbdA[:, 1], bc(TA["cfr"][:]), bc(TA["cfi"][:]), tA[2], RAB, WAB)
                    op("dve", lambda e: e.tensor_copy(out=BbA[:, 0], in_=tA[0]), reads=RAB, writes=WAB)
                    op("dve", lambda e: e.tensor_copy(out=BbA[:, 1], in_=tA[1]), reads=RAB, writes=WAB)

                    for t in range(NT):
                        for hf in range(2):
                            for m in range(8):
                                for k4 in range(4):
                                    kc = 4 * hf + k4
                                    op("pe", lambda e, m=m, kc=kc, k4=k4, hf=hf: e.matmul(
                                        ps[m][:], wS[:, kc, m * 128:(m + 1) * 128], xth[hf][:, k4, :],
                                        start=(kc == 0), stop=(kc == 7)),
                                       reads=[BwS, Bxth[hf]], writes=[PS[m]])
                            if t + 1 < NT:
                                load_xh(t + 1, hf)
                        for m in range(8):
                            if m < 4:
                                op("act", lambda e, m=m: e.activation(func=AF.Copy,
                                    out=UT[:, m, :].rearrange("p (i c) -> p i c", i=L)[:, :, 32 * t:32 * t + 32],
                                    in_=ps[m][:].rearrange("p (c i) -> p i c", i=L)),
                                   reads=[PS[m]], writes=BUTp[m])
                            else:
                                op("act", lambda e, m=m: e.activation(
                                    out=ZSp[:, m - 4, :].rearrange("p (i c) -> p i c", i=L)[:, :, 32 * t:32 * t + 32],
                                    in_=ps[m][:].rearrange("p (c i) -> p i c", i=L), func=AF.Silu),
                                   reads=[PS[m]], writes=[BZSp])
                    fw.barrier(skip=("pool",))
                ph1.close()
                fw.barrier(skip=("pool",))
                if "UT" in dumps:
                    dump("UT", UT[:], [128, 4, S], BUTp[3][0])
                    dump("ZS", ZSp[:], [128, 4, S], BZSp)
                EB = sb(ph2, "EB", [128, 16, 2, 128], BF16)
                KB2 = [sb(ph2, "KB%d" % i, [128, 16, 128], BF16) for i in range(2)]
                Wa = sb(ph2, "Wa", [128, 4, 2, NB], F32)
                Wb = sb(ph2, "Wb", [128, 4, 2, NB], F32)
                SP2 = [sb(ph2, "SP%d" % i, [128, 4, 2, NB], BF16) for i in range(2)]
                BEBa = [Buf("EB%d" % a) for a in range(16)]
                BKB2 = [Buf("KB%d" % i) for i in range(2)]
                BWam = [Buf("Wa%d" % m) for m in range(4)]
                BWbm = [Buf("Wb%d" % m) for m in range(4)]
                BSP2 = [Buf("SP%d" % i) for i in range(2)]
                KF0 = sb(ph2, "KF0", [128, 128], F32)
                BKF = Buf("KF0")
                XB = [sb(ph2, "XBp%d" % i, [128, 2, 128], F32) for i in range(2)]
                BXBf = [Buf("XBp%d" % i) for i in range(2)]
                XBt = sb(ph2, "XBt", [128, 2, 128], F32)
                BXBt = Buf("XBt")
                gel = [sb(ph2, "gel%d" % i, [128, 512], F32) for i in range(2)]
                Bgel = Buf("gel")
                def eb_ops(T_):
                    cs = slice(T_ * 128, (T_ + 1) * 128)
                    R0 = [BB, BtB_]
                    l1, l2 = LB1[:, :, cs], LB2[:, :, cs]
                    th = []
                    for a in range(16):
                        if a == 0:
                            cur, bc_ = XB0[:, :, cs], BtB_
                        else:
                            cur, bc_ = XB[a % 2][:], BXBf[a % 2]
                        th.append(lambda a=a, cur=cur, bc_=bc_: op("dve", lambda e: e.tensor_copy(out=EB[:, a, :, :], in_=cur), reads=R0 + [bc_, BEBa[a]], writes=[BEBa[a]]))
                        if a < 15:
                            nxt, bn_ = XB[(a + 1) % 2], BXBf[(a + 1) % 2]
                            cr_b = cur[:, 0:1, :].to_broadcast([128, 2, 128])
                            ci_b = cur[:, 1:2, :].to_broadcast([128, 2, 128])
                            th.append(lambda nxt=nxt, cr_b=cr_b, bc_=bc_, bn_=bn_: op("dve", lambda e: e.tensor_tensor(out=nxt[:], in0=cr_b, in1=l1, op=ALU.mult), reads=R0 + [bc_], writes=[bn_]))
                            th.append(lambda ci_b=ci_b, bc_=bc_: op("dve", lambda e: e.tensor_tensor(out=XBt[:], in0=ci_b, in1=l2, op=ALU.mult), reads=R0 + [bc_], writes=[BXBt]))
                            th.append(lambda nxt=nxt, bn_=bn_: op("dve", lambda e: e.tensor_tensor(out=nxt[:], in0=nxt[:], in1=XBt[:], op=ALU.add), reads=[bn_, BXBt], writes=[bn_]))
                    return th

                def emit_K(T_):
                    par = T_ % 2
                    FA, BFA, KB, BKB = FA2[par], BFA2[par], KB2[par], BKB2[par]
                    bk = 6 + (T_ % 2)
                    pk = ps[bk][:].rearrange("p (l n) -> p l n", l=16)
                    for lag in range(16):
                        for ri in range(2):
                            for m in range(4):
                                op("pe", lambda e, lag=lag, m=m, ri=ri: e.matmul(
                                    pk[32 * m:32 * m + 32, lag, :], BbA[:, ri, 4 * T_ + m, :], FA[:, lag, ri, m, :],
                                    start=(ri == 0), stop=(ri == 1), tile_position=(0, 32 * m), skip_group_check=True),
                                   reads=[BtA, BFA], writes=[PS[bk]])
                    op("dve", lambda e: e.memset(KB[:], 0.0), reads=[], writes=[BKB])
                    op("dve", lambda e: e.tensor_copy(out=KF0[:], in_=dskD[:, T_, :]), reads=[BB, BKF], writes=[BKF])
                    for m in range(4):
                        sl = slice(32 * m, 32 * m + 32)
                        op("dve", lambda e, sl=sl: e.tensor_copy(out=KB[sl, 1:16, sl], in_=pk[sl, 1:16, :]), reads=[PS[bk]], writes=[BKB])
                        op("dve", lambda e, sl=sl: e.tensor_tensor(out=KF0[sl, sl], in0=KF0[sl, sl], in1=pk[sl, 0, :], op=ALU.add), reads=[PS[bk], BKF], writes=[BKF])
                    op("dve", lambda e: e.tensor_copy(out=KB[:, 0, :], in_=KF0[:]), reads=[BKF, BKB], writes=[BKB])

                def emit_L1(T_):
                    UTv = UT[:, T_, :].rearrange("p (i c) -> p i c", i=L)
                    pws = [ps[m][:].rearrange("p (r n) -> p r n", r=2) for m in range(4)]
                    for ri in range(2):
                        for j in range(L):
                            for m in range(4):
                                rows = slice(32 * m, 32 * m + 32)
                                op("pe", lambda e, m=m, ri=ri, j=j, rows=rows: e.matmul(
                                    pws[m][:, ri, :], EB[rows, 15 - j, ri, :], UTv[rows, j, :],
                                    start=(j == 0), stop=(j == L - 1), tile_position=(32 * m, 0), skip_group_check=True),
                                   reads=[BEBa[15 - j]] + BUTp[T_], writes=[PS[m]])
                    for m in range(4):
                        op("act", lambda e, m=m: e.activation(out=Wa[:, m], in_=pws[m], func=AF.Copy), reads=[PS[m]], writes=[BWam[m]])

                def dbl_ops(T_):
                    par = T_ % 2
                    src, dst, Bs, Bd = Wa, Wb, BWam, BWbm
                    SP, BSP = SP2[par], BSP2[par]
                    th = []
                    for k in range(8):
                        d = 1 << k

                        def mk(kind, m, d=d, k=k, src=src, dst=dst, Bs=Bs, Bd=Bd):
                            P_ = 4 * T_ + m
                            aR, aI, nI = AD[:, k, 0, P_:P_ + 1], AD[:, k, 1, P_:P_ + 1], AD[:, k, 2, P_:P_ + 1]
                            R_, W_ = [Bs[m], Bd[m], BtA], [Bd[m]]
                            if kind == 0:
                                return lambda: op("dve", lambda e: e.scalar_tensor_tensor(
                                    out=dst[:, m, 0, d:], in0=src[:, m, 0, :NB - d], scalar=aR, in1=src[:, m, 0, d:], op0=ALU.mult, op1=ALU.add), reads=R_, writes=W_)
                            if kind == 1:
                                return lambda: op("dve", lambda e: e.scalar_tensor_tensor(
                                    out=dst[:, m, 1, d:], in0=src[:, m, 0, :NB - d], scalar=aI, in1=src[:, m, 1, d:], op0=ALU.mult, op1=ALU.add), reads=R_, writes=W_)
                            if kind == 2:
                                return lambda: op("dve", lambda e: e.tensor_copy(out=dst[:, m, :, :d], in_=src[:, m, :, :d]), reads=R_, writes=W_)
                            if kind == 3:
                                return lambda: op("dve", lambda e: e.scalar_tensor_tensor(
                                    out=dst[:, m, 0, d:], in0=src[:, m, 1, :NB - d], scalar=nI, in1=dst[:, m, 0, d:], op0=ALU.mult, op1=ALU.add), reads=R_, writes=W_)
                            return lambda: op("dve", lambda e: e.scalar_tensor_tensor(
                                out=dst[:, m, 1, d:], in0=src[:, m, 1, :NB - d], scalar=aR, in1=dst[:, m, 1, d:], op0=ALU.mult, op1=ALU.add), reads=R_, writes=W_)
                        th.append(lambda d=d, src=src, dst=dst, Bs=Bs, Bd=Bd: op("dve", lambda e: e.tensor_copy(
                            out=dst[:, :, :, :d], in_=src[:, :, :, :d]), reads=list(Bs) + list(Bd), writes=list(Bd)))
                        for kind in (0, 1, 3, 4):
                            for m in range(4):
                                th.append(mk(kind, m))
                        src, dst, Bs, Bd = dst, src, Bd, Bs

                    def fin(src=src, Bs=Bs):
                        op("dve", lambda e: e.memset(SP[:, :, :, 0:1], 0.0), reads=[], writes=[BSP])
                        op("dve", lambda e: e.tensor_copy(out=SP[:, :, :, 1:], in_=src[:, :, :, :NB - 1]), reads=list(Bs), writes=[BSP])
                    th.append(fin)
                    return th

                def emit_L03(T_, inter):
                    par = T_ % 2
                    FA, BFA, KB, BKB, SP, BSP = FA2[par], BFA2[par], KB2[par], BKB2[par], SP2[par], BSP2[par]
                    UTv = UT[:, T_, :].rearrange("p (i c) -> p i c", i=L)
                    nint = (len(inter) + 7) // 8
                    for n_, ip in enumerate(range(7, -1, -1)):
                        bk = 4 + (ip % 4)
                        py = ps[bk][:].rearrange("p (i n) -> p i n", i=2)
                        i0 = 2 * ip
                        for lag in range(i0 + 2):
                            ist = i0 if lag <= i0 else i0 + 1
                            ni = i0 + 2 - ist
                            op("pe", lambda e, lag=lag, ist=ist, ni=ni, py=py: e.matmul(
                                py[:, ist - i0:ist - i0 + ni, :], KB[:, lag, :], UTv[:, ist - lag:ist - lag + ni, :],
                                start=(lag == 0), stop=False, skip_group_check=True),
                               reads=[BKB] + BUTp[T_][:ip + 1], writes=[PS[bk]])
                        for il in range(2):
                            for ri in range(2):
                                for m in range(4):
                                    last = (m == 3 and il == 1 and ri == 1)
                                    op("pe", lambda e, m=m, il=il, ri=ri, py=py, last=last: e.matmul(
                                        py[32 * m:32 * m + 32, il, :], FA[:, i0 + il + 1, ri, m, :], SP[:, m, ri, :],
                                        start=False, stop=last, tile_position=(0, 32 * m), skip_group_check=True),
                                       reads=[BFA, BSP], writes=[PS[bk]])
                        g0, g1 = gel[0][:], gel[1][:]
                        pyf = ps[bk][:]
                        op("act", lambda e, pyf=pyf: e.activation(out=g0, in_=pyf, func=AF.Square), reads=[PS[bk], Bgel], writes=[Bgel])
                        op("dve", lambda e: e.tensor_scalar(out=g0, in0=g0, scalar1=0.044715, scalar2=1.0, op0=ALU.mult, op1=ALU.add), reads=[Bgel], writes=[Bgel])
                        op("dve", lambda e, pyf=pyf: e.tensor_tensor(out=g0, in0=g0, in1=pyf, op=ALU.mult), reads=[Bgel, PS[bk]], writes=[Bgel])
                        op("act", lambda e: e.activation(out=g1, in_=g0, func=AF.Sigmoid, scale=1.5957691216057308), reads=[Bgel], writes=[Bgel])
                        op("dve", lambda e, py=py, i0=i0: e.tensor_tensor(
                            out=UTv[:, i0:i0 + 2, :], in0=gel[1][:].rearrange("p (i n) -> p i n", i=2), in1=py, op=ALU.mult),
                           reads=[Bgel, PS[bk]], writes=[BUTp[T_][ip]])
                        for f_ in inter[n_ * nint:(n_ + 1) * nint]:
                            f_()

                def run(th):
                    for f_ in th:
                        f_()

                def mix(a_, b_):
                    out_, ia, ib = [], 0, 0
                    while ia < len(a_) or ib < len(b_):
                        if ib >= len(b_) or (ia < len(a_) and ia * len(b_) <= ib * len(a_)):
                            out_.append(a_[ia]); ia += 1
                        else:
                            out_.append(b_[ib]); ib += 1
                    return out_

                run(eb_ops(0))
                emit_K(0)
                emit_L1(0)
                run(mix(dbl_ops(0), eb_ops(1)))
                emit_K(1)
                emit_L1(1)
                emit_L03(0, mix(dbl_ops(1), eb_ops(2)))
                setup_FA(2)
                emit_K(2)
                emit_L1(2)
                emit_L03(1, mix(dbl_ops(2), eb_ops(3)))
                setup_FA(3)
                emit_K(3)
                emit_L1(3)
                emit_L03(2, dbl_ops(3))
                emit_L03(3, [])
                if "YG" in dumps:
                    dump("YG", UT[:], [128, 4, S], BUTp[3][0])
                fw.barrier()
                ph2.close()
                wG = sb(ph2, "wG", [128, 4, 1024], BF16)
                BwG = Buf("wG")
                dsG = fw.dsem("wG")
                for j in range(2):
                    dma("sp", dsG, wG[:, :, j * 512:(j + 1) * 512], wsrc(wGb[j], 4), reads=[Bg[j]], writes=[BwG])
                sg = [sb(ph2, "sg%d" % i, [128, 512], F32) for i in range(2)]
                Bsg = [Buf("sg%d" % i) for i in range(2)]
                cnt = 0
                for t in range(NT):
                    cols = slice(t * 512, (t + 1) * 512)
                    for f in range(4):
                        bv, bg_ = (2 * cnt) % 8, (2 * cnt + 1) % 8
                        s_ = cnt % 2
                        cnt += 1
                        for kc in range(4):
                            op("pe", lambda e, f=f, kc=kc, bv=bv: e.matmul(
                                ps[bv][:], wG[:, kc, f * 128:(f + 1) * 128], UT[:, kc, cols], start=(kc == 0), stop=(kc == 3)),
                               reads=[BwG] + BUTall, writes=[PS[bv]])
                        for kc in range(4):
                            op("pe", lambda e, f=f, kc=kc, bg_=bg_: e.matmul(
                                ps[bg_][:], wG[:, kc, 512 + f * 128:512 + (f + 1) * 128], UT[:, kc, cols], start=(kc == 0), stop=(kc == 3)),
                               reads=[BwG] + BUTall, writes=[PS[bg_]])
                        op("act", lambda e, bg_=bg_, s_=s_: e.activation(out=sg[s_][:], in_=ps[bg_][:], func=AF.Sigmoid),
                           reads=[PS[bg_]], writes=[Bsg[s_]])
                        op("dve", lambda e, bv=bv, s_=s_: e.tensor_tensor(out=sg[s_][:], in0=sg[s_][:], in1=ps[bv][:], op=ALU.mult),
                           reads=[PS[bv], Bsg[s_]], writes=[Bsg[s_]])
                        op("dve", lambda e, f=f, s_=s_, t=t: e.tensor_tensor(
                            out=ZS[:, f, :].rearrange("p (c i) -> p i c", i=L)[:, 2 * t:2 * t + 2, :],
                            in0=sg[s_][:].rearrange("p (i c) -> p i c", i=2), in1=ZSp[:, f, cols].rearrange("p (i c) -> p i c", i=2), op=ALU.mult),
                           reads=[Bsg[s_], BZSp, BZSn], writes=[BZSn])
            fw.barrier()
        if "YS" in dumps:
            dump("YS", ZS[:], [128, 4, S], BZSn)
        if stop_after == "S":
            fw.barrier()
            return nc, dump_d

        KT = sb(top, "KT", [128, 4, S], BF16)
        VA = sb(top, "VA", [128, 32, 4, 130], BF16)
        MKT = sb(top, "MKT", [128, 4, 256], BF16)
        MVA = sb(top, "MVA", [128, 2, 4, 130], BF16)
        BKT, BVA, BMK = Buf("KT"), Buf("VA"), Buf("MK")
        op("dve", lambda e: e.memset(VA[:, :, :, 128:130], 1.0), reads=[], writes=[BVA])
        op("dve", lambda e: e.memset(MVA[:, :, :, 128:130], 1.0), reads=[], writes=[BMK])
        with ExitStack() as phK:
            wKV = sb(phK, "wKV", [128, 8, 1024], BF16)
            wM = sb(phK, "wM", [128, 8, 1024], BF16)
            mT = sb(phK, "mT", [128, 8, 256], BF16)
            BwKV, BwM, BmT = Buf("wKV"), Buf("wM"), Buf("mT")
            dsk_, dsm_, dst_ = fw.dsem("wKV"), fw.dsem("wM"), fw.dsem("mT")
            for j in range(2):
                dma("sp", dsk_, wKV[:, :, j * 512:(j + 1) * 512], wsrc(wInb[3 + j], 8), reads=[Bw[3 + j]], writes=[BwKV])
            for j in range(2):
                dma("sp", dsm_, wM[:, :, j * 512:(j + 1) * 512], wsrc(wMb[j], 8), reads=[Bm[j]], writes=[BwM])
            dma("sp", dst_, mT[:], wsrc(memTb, 8), reads=[Bmem], writes=[BmT])
            xt = [sb(phK, "xtK%d" % i, [128, 8, 512], BF16) for i in range(2)]
            Bxt = [Buf("xtK%d" % i) for i in range(2)]
            dsx = [fw.dsem("xtK%d" % i) for i in range(2)]

            def load_xk(t):
                dma("sp", dsx[t % 2], xt[t % 2][:], wsrc(xTb[t], 8), reads=[Bx[t]], writes=[Bxt[t % 2]])
            load_xk(0)
            cnt = 0
            for h in range(4):
                bk = cnt % 8
                cnt += 1
                for kc in range(8):
                    op("pe", lambda e, h=h, kc=kc, bk=bk: e.matmul(ps[bk][:, 0:256], wM[:, kc, h * 128:(h + 1) * 128], mT[:, kc, :],
                                                                   start=(kc == 0), stop=(kc == 7)), reads=[BwM, BmT], writes=[PS[bk]])
                op("act", lambda e, h=h, bk=bk: e.activation(out=MKT[:, h, :], in_=ps[bk][:, 0:256], func=AF.Copy), reads=[PS[bk]], writes=[BMK])
            for mb in range(2):
                bk = cnt % 8
                cnt += 1
                for kc in range(8):
                    op("pe", lambda e, mb=mb, kc=kc, bk=bk: e.matmul(ps[bk][:], mT[:, kc, mb * 128:(mb + 1) * 128], wM[:, kc, 512:1024],
                                                                     start=(kc == 0), stop=(kc == 7)), reads=[BwM, BmT], writes=[PS[bk]])
                op("act", lambda e, mb=mb, bk=bk: e.activation(out=MVA[:, mb, :, 0:128], in_=ps[bk][:].rearrange("p (h e) -> p h e", h=4), func=AF.Copy),
                   reads=[PS[bk]], writes=[BMK])
            for t in range(NT):
                if t + 1 < NT:
                    load_xk(t + 1)
                x_t, Bx_t = xt[t % 2], Bxt[t % 2]
                cols = slice(t * 512, (t + 1) * 512)
                for h in range(4):
                    bk = cnt % 8
                    cnt += 1
                    for kc in range(8):
                        op("pe", lambda e, h=h, kc=kc, bk=bk: e.matmul(ps[bk][:], wKV[:, kc, h * 128:(h + 1) * 128], x_t[:, kc, :],
                                                                       start=(kc == 0), stop=(kc == 7)), reads=[BwKV, Bx_t], writes=[PS[bk]])
                    op("act", lambda e, h=h, bk=bk: e.activation(out=KT[:, h, cols], in_=ps[bk][:], func=AF.Copy), reads=[PS[bk]], writes=[BKT])
                for tb in range(4):
                    bk = cnt % 8
                    cnt += 1
                    for kc in range(8):
                        op("pe", lambda e, tb=tb, kc=kc, bk=bk: e.matmul(ps[bk][:], x_t[:, kc, tb * 128:(tb + 1) * 128], wKV[:, kc, 512:1024],
                                                                         start=(kc == 0), stop=(kc == 7)), reads=[BwKV, Bx_t], writes=[PS[bk]])
                    op("dve", lambda e, tb=tb, bk=bk, t=t: e.tensor_copy(out=VA[:, 4 * t + tb, :, 0:128], in_=ps[bk][:].rearrange("p (h e) -> p h e", h=4)),
                       reads=[PS[bk]], writes=[BVA])
            fw.barrier()
        if "KT" in dumps:
            dump("KT", KT[:], [128, 4, S], BKT)
            dump("VA", VA[:], [128, 32, 4, 130], BVA)
            dump("MKT", MKT[:], [128, 4, 256], BMK)
            dump("MVA", MVA[:], [128, 2, 4, 130], BMK)
        if stop_after == "KV":
            fw.barrier()
            return nc, dump_d

        with ExitStack() as phF:
            farb = sb(phF, "farb", [128, 4], F32)
            lamv = sb(phF, "lamv", [128, 4, 64], F32)
            subw = sb(phF, "subw", [128, 128], F32)
            lng = sb(phF, "lng", [128, D], F32)
            lnb = sb(phF, "lnb", [128, D], F32)
            NBh = sb(phF, "NBh", [128, 4, 256], BF16)
            NBl = sb(phF, "NBl", [128, 4, 256], BF16)
            lsc = sb(phF, "lsc", [128, 8], F32)
            BC = Buf("constF")
            dsc = fw.dsem("constF")
            tmpF = ExitStack()
            tb_f = sb(tmpF, "tb_f", [128, 4, 256], F32)
            mk_f = sb(tmpF, "mk_f", [128, 4, 256], F32)
            for dst_, nm in ((tb_f, "tbias"), (mk_f, "maskc"), (farb, "farb"), (lamv, "lamv"), (subw, "subw"), (lng, "lng"), (lnb, "lnb")):
                dma("sp", dsc, dst_[:], I[nm], writes=[BC])
            RC, WC = [BC], [BC]
            for h in range(4):
                op("dve", lambda e, h=h: e.tensor_scalar(out=tb_f[:, h, :], in0=tb_f[:, h, :], scalar1=farb[:, h:h + 1], scalar2=None, op0=ALU.subtract), reads=RC, writes=WC)
            op("dve", lambda e: e.tensor_tensor(out=tb_f[:], in0=tb_f[:], in1=mk_f[:], op=ALU.add), reads=RC, writes=WC)
            op("dve", lambda e: e.tensor_copy(out=NBh[:], in_=tb_f[:]), reads=RC, writes=WC)
            op("dve", lambda e: e.tensor_copy(out=mk_f[:], in_=NBh[:]), reads=RC, writes=WC)
            op("dve", lambda e: e.tensor_tensor(out=mk_f[:], in0=tb_f[:], in1=mk_f[:], op=ALU.subtract), reads=RC, writes=WC)
            op("dve", lambda e: e.tensor_copy(out=NBl[:], in_=mk_f[:]), reads=RC, writes=WC)
            fw.barrier()
            tmpF.close()
            op("dve", lambda e: e.tensor_tensor(out=lamv[:, 0, :], in0=lamv[:, 0, :], in1=lamv[:, 1, :], op=ALU.mult), reads=RC, writes=WC)
            op("dve", lambda e: e.tensor_tensor(out=lamv[:, 2, :], in0=lamv[:, 2, :], in1=lamv[:, 3, :], op=ALU.mult), reads=RC, writes=WC)
            op("dve", lambda e: e.reduce_sum(out=lsc[:, 1:2], in_=lamv[:, 0, :], axis=AX.X), reads=RC, writes=WC)
            op("dve", lambda e: e.reduce_sum(out=lsc[:, 2:3], in_=lamv[:, 2, :], axis=AX.X), reads=RC, writes=WC)
            op("act", lambda e: e.activation(out=lsc[:, 3:5], in_=lsc[:, 1:3], func=AF.Exp), reads=RC, writes=WC)
            op("dve", lambda e: e.tensor_tensor(out=lsc[:, 0:1], in0=lsc[:, 4:5], in1=lsc[:, 3:4], op=ALU.subtract), reads=RC, writes=WC)
            op("dve", lambda e: e.tensor_scalar(out=lsc[:, 0:1], in0=lsc[:, 0:1], scalar1=-LAMBDA_INIT, scalar2=None, op0=ALU.add), reads=RC, writes=WC)
            op("dve", lambda e: e.tensor_scalar(out=subw[:], in0=subw[:], scalar1=1.0 - LAMBDA_INIT, scalar2=None, op0=ALU.mult), reads=RC, writes=WC)

            xq = sb(phF, "xq", [128, 8, 512], BF16)
            Bxq = Buf("xq")
            dsxq = fw.dsem("xq")
            NW = 4
            wt = [sb(phF, "wt%d" % i, [128, 8, 512], BF16) for i in range(NW)]
            Bwt = [Buf("wt%d" % i) for i in range(NW)]
            dswt = [fw.dsem("wt%d" % i) for i in range(NW)]
            wctr = [0]

            gseq = [(wInb[2], 8, Bw[2]), (wInb[5], 8, Bw[5]), (wInb[6], 8, Bw[6]), (wInb[7], 8, Bw[7])]
            for br_ in range(3):
                for half_ in range(2):
                    gseq.append((wInb[8 + 2 * br_ + half_], 8, Bw[8 + 2 * br_ + half_]))
                    gseq.append((wBb[br_, half_], 4, Bb[br_][half_]))
            gseq += [(wOb[0], 8, Bo[0]), (wOb[1], 8, Bo[1])]
            wseq = gseq * NT
            wstate = {"issued": 0, "consumed": 0}

            def _issue_w(k):
                piece, nkc, bsrc = wseq[k]
                i = k % NW
                dma("sp", dswt[i], wt[i][:, 0:nkc, :], wsrc(piece, nkc), reads=[bsrc], writes=[Bwt[i]])

            def next_w():
                k = wstate["consumed"]
                while wstate["issued"] < min(len(wseq), k + 3):
                    _issue_w(wstate["issued"])
                    wstate["issued"] += 1
                wstate["consumed"] += 1
                return wt[k % NW], Bwt[k % NW]

            QT = sb(phF, "QT", [128, 4, 512], BF16)
            MQT = QT
            scr = sb(phF, "scr", [128, 4096], F32)
            Bscr = Buf("scr")
            ZD = scr[:, 0:2048].rearrange("p (q f) -> p q f", q=4)
            ZM = ZD
            BQT = Buf("QT")
            BMQ, BZD, BZM = BQT, Bscr, Bscr
            NP = 4
            PT_all = sb(phF, "PT", [128, NP, 512], BF16)
            PT = [PT_all[:, i, :] for i in range(NP)]
            BPT = [Buf("PT%d" % i) for i in range(NP)]
            pctr = [0]
            YD = scr[:, 2048:3072].bitcast(BF16).rearrange("p (q f) -> p q f", q=4)
            YM = scr[:, 3072:4096].bitcast(BF16).rearrange("p (q f) -> p q f", q=4)
            YDT = sb(phF, "YDT", [128, 4, 512], BF16)
            YMT = sb(phF, "YMT", [128, 4, 512], BF16)
            BYD, BYM, BYDT, BYMT = Bscr, Bscr, Buf("YDT"), Buf("YMT")
            xres = YDT[:].rearrange("p h n -> p (h n)").bitcast(F32)
            Bxres = BYDT
            sm = sb(phF, "sm", [128, 32], F32)
            Bsm = Buf("sm")
            od = [None, sb(phF, "od1", [128, 128], F32)]
            Bod = Buf("od")
            od4 = sb(phF, "od4", [128, 4, 128], F32)
            OS = sb(phF, "OS", [128, 2, 520], F32)
            BOS = Buf("OS")
            ACC = scr[:].rearrange("p (q f) -> p q f", q=8)
            BACC = Bscr
            MT = sb(phF, "MT", [128, 8, 512], BF16)
            BMT = Buf("MT")
            sgf_all = sb(phF, "sgf", [128, 2, 512], F32)
            sgf = [sgf_all[:, 0, :], sgf_all[:, 1, :]]
            Bsgf = [Buf("sgf%d" % i) for i in range(2)]

            dsxr = fw.dsem("xres")
            dsout = [fw.dsem("out0"), fw.dsem("out1")]
            stats2 = [sb(phF, "stats%d" % i, [128, 2, 6], F32) for i in range(2)]
            mv2 = [sb(phF, "mv%d" % i, [128, 4], F32) for i in range(2)]
            Bst = [Buf("stats%d" % i) for i in range(2)]

            pctr_bank = [0]

            def gbank():
                b = (0, 1, 2, 3)[pctr_bank[0] % 4]
                pctr_bank[0] += 1
                return b

            OB = [[4, 5], [6, 7]]

            def oreg(c, ql):
                if ql < 3:
                    return ps[OB[c][0]][:, ql * 130:ql * 130 + 130], OB[c][0]
                return ps[OB[c][1]][:, 0:130], OB[c][1]

            def emit_qproj():
                wq, Bwq = next_w()
                for h in range(4):
                    bk = gbank()
                    for kc in range(8):
                        op("pe", lambda e, h=h, kc=kc, bk=bk: e.matmul(ps[bk][:], wq[:, kc, h * 128:(h + 1) * 128], xq[:, kc, :],
                                                                       start=(kc == 0), stop=(kc == 7)), reads=[Bwq, Bxq], writes=[PS[bk]])
                    op("act", lambda e, h=h, bk=bk: e.activation(out=QT[:, h, :], in_=ps[bk][:], func=AF.Copy, scale=0.125), reads=[PS[bk]], writes=[BQT])
                wz, Bwz = next_w()
                for ql in range(4):
                    bk = gbank()
                    for kc in range(8):
                        op("pe", lambda e, ql=ql, kc=kc, bk=bk: e.matmul(ps[bk][:], xq[:, kc, ql * 128:(ql + 1) * 128], wz[:, kc, :],
                                                                         start=(kc == 0), stop=(kc == 7)), reads=[Bwz, Bxq], writes=[PS[bk]])
                    op("act", lambda e, ql=ql, bk=bk: e.activation(out=ZD[:, ql, :], in_=ps[bk][:], func=AF.Silu), reads=[PS[bk]], writes=[BZD])

            pend_ln = []
            for G in range(NT):
                if G == 0:
                    dma("sp", dsxq, xq[:], wsrc(xTb[G], 8), reads=[Bx[G]], writes=[Bxq])
                if G == 0:
                    emit_qproj()
                if stop_after == "F1":
                    dump("QT", QT[:], [128, 4, 512], BQT)
                    dump("ZD", scr[:, 0:2048], [128, 2048], Bscr)
                    fw.barrier()
                    return nc, dump_d
                nkb = 4 * G + 4
                steps = [(h, kb) for h in range(4) for kb in range(nkb)]
                SBANKS = ((0, 1), (2, 3))

                def emit_score(i):
                    h, kb = steps[i]
                    pair = i % 2
                    ql0 = max(0, kb - 4 * G)
                    cs_ = slice(ql0 * 128, 512)
                    nq = [ql for ql in range(4) if 4 * G + ql in (kb, kb + 1)]
                    for c in range(2):
                        rows = slice(64 * c, 64 * c + 64)
                        bk = SBANKS[pair][c]
                        op("pe", lambda e, h=h, kb=kb, bk=bk, cs_=cs_, rows=rows, nq=nq: e.matmul(
                            ps[bk][:, cs_], KT[rows, h, kb * 128:(kb + 1) * 128], QT[rows, h, cs_],
                            start=True, stop=(len(nq) == 0)), reads=[BKT, BQT], writes=[PS[bk]])
                    if nq:
                        qa = nq[0] * 128
                        qrel0 = (4 * G + nq[0] - kb) * 128
                        w_ = 128 * len(nq)
                        for c in range(2):
                            bk = SBANKS[pair][c]
                            for hl, NBx in enumerate((NBh, NBl)):
                                op("pe", lambda e, h=h, bk=bk, qa=qa, qrel0=qrel0, w_=w_, NBx=NBx, hl=hl: e.matmul(
                                    ps[bk][:, qa:qa + w_], ident[:], NBx[:, h, qrel0:qrel0 + w_],
                                    start=False, stop=(hl == 1)), reads=[Bid, BC], writes=[PS[bk]])
                    src2 = (psA if pair == 0 else psB)[:].rearrange("p (c n) -> p c n", c=2)[:, :, cs_]
                    dst2 = PT_all[:, 2 * pair:2 * pair + 2, cs_]
                    b0_, b1_ = SBANKS[pair]
                    op("act", lambda e, h=h, src2=src2, dst2=dst2: e.activation(
                        out=dst2, in_=src2, func=AF.Exp, bias=farb[:, h:h + 1], scale=1.0),
                       reads=[PS[b0_], PS[b1_], BC], writes=[BPT[2 * pair], BPT[2 * pair + 1]])

                def emit_pv(i):
                    h, kb = steps[i]
                    pair = i % 2
                    ql0 = max(0, kb - 4 * G)
                    for c in range(2):
                        pi = 2 * pair + c
                        for ql in range(ql0, 4):
                            oap, obk = oreg(c, ql)
                            st_ = (kb == 0 and ql in (0, 3))
                            last = (kb == 4 * G + ql)
                            op("pe", lambda e, h=h, kb=kb, ql=ql, pi=pi, oap=oap, st_=st_, last=last: e.matmul(
                                oap, PT[pi][:, ql * 128:(ql + 1) * 128], VA[:, kb, h, 0:130], start=st_, stop=last, skip_group_check=True),
                               reads=[BPT[pi], BVA], writes=[PS[obk]])
                    if kb == nkb - 1:
                        for pb in list(pend_b):
                            emit_norm_b(pb[0])
                            pend_b.remove(pb)
                        emit_norm_a(h)
                        pend_b.append([h, 10])

                def emit_norm_a(h):
                    for c in range(2):
                        op("dve", lambda e, c=c: e.tensor_copy(out=OS[:, c, 0:390], in_=ps[OB[c][0]][:, 0:390]), reads=[PS[OB[c][0]]], writes=[BOS])
                        op("dve", lambda e, c=c: e.tensor_copy(out=OS[:, c, 390:520], in_=ps[OB[c][1]][:, 0:130]), reads=[PS[OB[c][1]]], writes=[BOS])
                    R_, W_ = [BOS, Bsm, Bod, BC, BZD], [Bsm, Bod]
                    dens = OS[:].rearrange("p c (q e) -> p c q e", q=4)[:, :, :, 128]
                    op("dve", lambda e: e.reciprocal(out=sm[:, 0:8].rearrange("p (c q) -> p c q", c=2), in_=dens), reads=R_, writes=W_)
                    op("dve", lambda e: e.tensor_scalar(out=sm[:, 4:8], in0=sm[:, 4:8], scalar1=lsc[:, 0:1], scalar2=None, op0=ALU.mult), reads=R_, writes=W_)
                    for ql in range(4):
                        o0 = OS[:, 0, ql * 130:ql * 130 + 128]
                        o1 = OS[:, 1, ql * 130:ql * 130 + 128]
                        op("dve", lambda e, o0=o0, ql=ql: e.tensor_scalar(out=od4[:, ql, :], in0=o0, scalar1=sm[:, ql:ql + 1], scalar2=None, op0=ALU.mult), reads=R_, writes=W_)
                        op("dve", lambda e, o1=o1, ql=ql: e.scalar_tensor_tensor(out=od4[:, ql, :], in0=o1, scalar=sm[:, 4 + ql:5 + ql], in1=od4[:, ql, :], op0=ALU.mult, op1=ALU.add), reads=R_, writes=W_)
                        op("dve", lambda e, ql=ql: e.scalar_tensor_tensor(out=od[1][:], in0=od4[:, ql, :], scalar=1.0, in1=od4[:, ql, :], op0=ALU.mult, op1=ALU.mult,
                                                                         accum_out=sm[:, 8 + ql:9 + ql]), reads=R_, writes=W_)
                    op("dve", lambda e: e.tensor_scalar(out=sm[:, 12:16], in0=sm[:, 8:12], scalar1=1.0 / 128.0, scalar2=1e-5, op0=ALU.mult, op1=ALU.add), reads=R_, writes=W_)

                def emit_norm_b(h):
                    R_, W_ = [BOS, Bsm, Bod, BC, BZD], [Bsm, Bod]
                    op("act", lambda e: e.activation(out=sm[:, 16:20], in_=sm[:, 12:16], func=AF.Ln), reads=R_, writes=W_)
                    op("act", lambda e: e.activation(out=sm[:, 20:24], in_=sm[:, 16:20], func=AF.Exp, scale=-0.5), reads=R_, writes=W_)
                    for ql in range(4):
                        op("dve", lambda e, ql=ql: e.scalar_tensor_tensor(out=od4[:, ql, :], in0=od4[:, ql, :], scalar=sm[:, 20 + ql:21 + ql], in1=subw[:], op0=ALU.mult, op1=ALU.mult), reads=R_, writes=W_)
                    op("dve", lambda e, h=h: e.tensor_tensor(out=YD[:, :, h * 128:(h + 1) * 128], in0=od4[:], in1=ZD[:, :, h * 128:(h + 1) * 128], op=ALU.mult),
                       reads=R_ + [BYD], writes=[BYD, Bod])

                pend_b = []
                emit_score(0)
                for i in range(len(steps)):
                    if i + 1 < len(steps):
                        emit_score(i + 1)
                    emit_pv(i)
                    if pend_ln and (i == 10 or i == len(steps) - 1):
                        for f_ in pend_ln:
                            f_()
                        del pend_ln[:]
                    for pb in list(pend_b):
                        pb[1] -= 1
                        if pb[1] <= 0 or i == len(steps) - 1:
                            emit_norm_b(pb[0])
                            pend_b.remove(pb)

                if stop_after == "F2":
                    dump("YD", scr[:, 2048:3072], [128, 1024], Bscr)
                    dump("QT", QT[:], [128, 4, 512], BQT)
                    dump("ZD", scr[:, 0:2048], [128, 2048], Bscr)
                    op("dve", lambda e: e.tensor_copy(out=sgf[0][:], in_=ps[4][:]), reads=[PS[4]], writes=[Bsgf[0]])
                    op("dve", lambda e: e.tensor_copy(out=sgf[1][:], in_=ps[6][:]), reads=[PS[6]], writes=[Bsgf[1]])
                    dump("O0", sgf[0][:], [128, 512], Bsgf[0])
                    dump("O1", sgf[1][:], [128, 512], Bsgf[1])
                    dump("PT", PT[0], [128, 512], BPT[0])
                    dump("sm", sm[:], [128, 32], Bsm)
                    dump("lsc", lsc[:], [128, 8], BC)
                    fw.barrier()
                    return nc, dump_d
                wmq, Bwmq = next_w()
                for h in range(4):
                    bk = gbank()
                    for kc in range(8):
                        op("pe", lambda e, h=h, kc=kc, bk=bk: e.matmul(ps[bk][:], wmq[:, kc, h * 128:(h + 1) * 128], xq[:, kc, :],
                                                                       start=(kc == 0), stop=(kc == 7)), reads=[Bwmq, Bxq], writes=[PS[bk]])
                    op("act", lambda e, h=h, bk=bk: e.activation(out=MQT[:, h, :], in_=ps[bk][:], func=AF.Copy, scale=128.0 ** -0.5), reads=[PS[bk]], writes=[BMQ])
                wzm, Bwzm = next_w()
                for ql in range(4):
                    bk = gbank()
                    for kc in range(8):
                        op("pe", lambda e, ql=ql, kc=kc, bk=bk: e.matmul(ps[bk][:], xq[:, kc, ql * 128:(ql + 1) * 128], wzm[:, kc, :],
                                                                         start=(kc == 0), stop=(kc == 7)), reads=[Bwzm, Bxq], writes=[PS[bk]])
                    op("act", lambda e, ql=ql, bk=bk: e.activation(out=ZM[:, ql, :], in_=ps[bk][:], func=AF.Silu), reads=[PS[bk]], writes=[BZM])

                for h in range(4):
                    for mb in range(2):
                        bk = gbank()
                        op("pe", lambda e, h=h, mb=mb, bk=bk: e.matmul(ps[bk][:], MKT[:, h, mb * 128:(mb + 1) * 128], MQT[:, h, :], start=True, stop=True),
                           reads=[BMK, BMQ], writes=[PS[bk]])
                        pi = pctr[0] % NP
                        pctr[0] += 1
                        op("act", lambda e, bk=bk, pi=pi: e.activation(out=PT[pi][:], in_=ps[bk][:], func=AF.Exp), reads=[PS[bk]], writes=[BPT[pi]])
                        for ql in range(4):
                            oap, obk = oreg(0, ql)
                            op("pe", lambda e, h=h, mb=mb, ql=ql, pi=pi, oap=oap: e.matmul(
                                oap, PT[pi][:, ql * 128:(ql + 1) * 128], MVA[:, mb, h, 0:130], start=(mb == 0 and ql in (0, 3)), stop=(mb == 1), skip_group_check=True),
                               reads=[BPT[pi], BMK], writes=[PS[obk]])
                    for ql in range(4):
                        o0, b0 = oreg(0, ql)
                        R_, W_ = [PS[b0], Bsm, BZM], [Bsm]
                        op("dve", lambda e, o0=o0: e.reciprocal(out=sm[:, 8:9], in_=o0[:, 128:129]), reads=R_, writes=W_)
                        op("dve", lambda e, o0=o0, ql=ql, h=h: e.scalar_tensor_tensor(
                            out=YM[:, ql, h * 128:(h + 1) * 128], in0=o0[:, 0:128], scalar=sm[:, 8:9], in1=ZM[:, ql, h * 128:(h + 1) * 128], op0=ALU.mult, op1=ALU.mult),
                           reads=R_ + [BYM], writes=[BYM, Bsm])

                if stop_after == "F3":
                    dump("YM", scr[:, 3072:4096], [128, 1024], Bscr)
                    fw.barrier()
                    return nc, dump_d
                for (Ysrc, Bs_, Ydst, Bd_) in ((YD, BYD, YDT, BYDT), (YM, BYM, YMT, BYMT)):
                    for kc in range(4):
                        bk = gbank()
                        pst = ps[bk][:].bitcast(BF16)
                        for ql in range(4):
                            op("pe", lambda e, kc=kc, ql=ql, pst=pst, Ysrc=Ysrc: e.transpose(
                                pst[:, ql * 128:(ql + 1) * 128], Ysrc[:, ql, kc * 128:(kc + 1) * 128], ident[:]),
                               reads=[Bs_, Bid], writes=[PS[bk]])
                        op("dve", lambda e, kc=kc, pst=pst, Ydst=Ydst: e.tensor_copy(out=Ydst[:, kc, :], in_=pst[:, 0:512]), reads=[PS[bk]], writes=[Bd_])

                if stop_after == "F4":
                    dump("YDT", YDT[:], [128, 4, 512], BYDT)
                    dump("YMT", YMT[:], [128, 4, 512], BYMT)
                    fw.barrier()
                    return nc, dump_d
                for br in range(3):
                    if br == 0:
                        ysrc = lambda kc, G=G: ZS[:, kc, G * 512:(G + 1) * 512]
                        By = BZSn
                    elif br == 1:
                        ysrc = lambda kc: YDT[:, kc, :]
                        By = BYDT
                    else:
                        ysrc = lambda kc: YMT[:, kc, :]
                        By = BYMT
                    for half in range(2):
                        wg_, Bwg = next_w()
                        wb_, Bwb = next_w()
                        for fl in range(4):
                            ft = half * 4 + fl
                            bb_, bg_ = gbank(), gbank()
                            for kc in range(4):
                                op("pe", lambda e, kc=kc, fl=fl, bb_=bb_, ysrc=ysrc, wb_=wb_: e.matmul(
                                    ps[bb_][:], wb_[:, kc, fl * 128:(fl + 1) * 128], ysrc(kc), start=(kc == 0), stop=(kc == 3)),
                                   reads=[Bwb, By], writes=[PS[bb_]])
                            for kc in range(8):
                                op("pe", lambda e, kc=kc, fl=fl, bg_=bg_, wg_=wg_: e.matmul(
                                    ps[bg_][:], wg_[:, kc, fl * 128:(fl + 1) * 128], xq[:, kc, :], start=(kc == 0), stop=(kc == 7)),
                                   reads=[Bwg, Bxq], writes=[PS[bg_]])
                            s_ = ft % 2
                            op("act", lambda e, bg_=bg_, s_=s_: e.activation(out=sgf[s_][:], in_=ps[bg_][:], func=AF.Sigmoid), reads=[PS[bg_]], writes=[Bsgf[s_]])
                            if br == 0:
                                op("dve", lambda e, bb_=bb_, s_=s_, ft=ft: e.tensor_tensor(out=ACC[:, ft, :], in0=sgf[s_][:], in1=ps[bb_][:], op=ALU.mult),
                                   reads=[PS[bb_], Bsgf[s_]], writes=[BACC])
                            else:
                                op("dve", lambda e, bb_=bb_, s_=s_: e.tensor_tensor(out=sgf[s_][:], in0=sgf[s_][:], in1=ps[bb_][:], op=ALU.mult),
                                   reads=[PS[bb_], Bsgf[s_]], writes=[Bsgf[s_]])
                                if br == 1:
                                    op("dve", lambda e, s_=s_, ft=ft: e.tensor_tensor(out=ACC[:, ft, :], in0=ACC[:, ft, :], in1=sgf[s_][:], op=ALU.add),
                                       reads=[Bsgf[s_], BACC], writes=[BACC])
                                else:
                                    op("dve", lambda e, s_=s_, ft=ft: e.tensor_tensor(out=MT[:, ft, :], in0=ACC[:, ft, :], in1=sgf[s_][:], op=ALU.add),
                                       reads=[Bsgf[s_], BACC], writes=[BMT])

                if stop_after == "F5":
                    dump("MT", MT[:], [128, 8, 512], BMT)
                    fw.barrier()
                    return nc, dump_d
                if G + 1 < NT:
                    dma("sp", dsxq, xq[:], wsrc(xTb[G + 1], 8), reads=[Bx[G + 1]], writes=[Bxq])
                wo = [next_w() for j in range(2)]
                HH = [sgf_all[:].rearrange("p a n -> p (a n)"), YMT[:].rearrange("p h n -> p (h n)").bitcast(F32)]
                HHB = [[Bsgf[0], Bsgf[1]], [BYMT]]
                OBK = (0, 1, 2, 3, 4, 5, 6, 7)

                def ln_head(ql):
                    r0 = G * 512 + ql * 128
                    Hc, HB = HH[ql % 2], HHB[ql % 2]
                    dma("sp", dsxr, xres[:], I["x"][r0:r0 + 128, :], writes=[Bxres])
                    for half in range(2):
                        bk = OBK[2 * ql + half]
                        wo_, Bwo = wo[half]
                        for kc in range(8):
                            op("pe", lambda e, kc=kc, ql=ql, bk=bk, wo_=wo_: e.matmul(
                                ps[bk][:], MT[:, kc, ql * 128:(ql + 1) * 128], wo_[:, kc, :], start=(kc == 0), stop=(kc == 7)),
                               reads=[BMT, Bwo], writes=[PS[bk]])
                        hs = slice(half * 512, (half + 1) * 512)
                        op("dve", lambda e, bk=bk, hs=hs, Hc=Hc: e.scalar_tensor_tensor(out=Hc[:, hs], in0=xres[:, hs], scalar=ALPHA, in1=ps[bk][:], op0=ALU.mult, op1=ALU.add),
                           reads=[PS[bk], Bxres] + HB, writes=HB)
                    R_, W_ = HB + [Bst[ql % 2], BC], [Bst[ql % 2]] + HB
                    st_, mv_ = stats2[ql % 2], mv2[ql % 2]
                    for half in range(2):
                        op("dve", lambda e, half=half, Hc=Hc: e.bn_stats(out=st_[:, half, :], in_=Hc[:, half * 512:(half + 1) * 512]), reads=R_, writes=W_)
                    op("dve", lambda e: e.bn_aggr(out=mv_[:, 0:2], in_=st_[:].rearrange("p a b -> p (a b)")), reads=R_, writes=W_)
                    op("dve", lambda e: e.tensor_scalar(out=mv_[:, 2:3], in0=mv_[:, 1:2], scalar1=1e-5, scalar2=None, op0=ALU.add), reads=R_, writes=W_)

                def ln_tail(ql, G=G, HH=HH, HHB=HHB):
                    r0 = G * 512 + ql * 128
                    Hc, HB = HH[ql % 2], HHB[ql % 2]
                    R_, W_ = HB + [Bst[ql % 2], BC], [Bst[ql % 2]] + HB
                    mv_ = mv2[ql % 2]
                    op("act", lambda e: e.activation(out=mv_[:, 2:3], in_=mv_[:, 2:3], func=AF.Ln), reads=R_, writes=W_)
                    op("act", lambda e: e.activation(out=mv_[:, 3:4], in_=mv_[:, 2:3], func=AF.Exp, scale=-0.5), reads=R_, writes=W_)
                    op("dve", lambda e, Hc=Hc: e.tensor_scalar(out=Hc, in0=Hc, scalar1=mv_[:, 0:1], scalar2=mv_[:, 3:4], op0=ALU.subtract, op1=ALU.mult), reads=R_, writes=W_)
                    op("pool", lambda e, Hc=Hc: e.tensor_tensor(out=Hc, in0=Hc, in1=lng[:], op=ALU.mult), reads=HB + [BC], writes=HB)
                    op("pool", lambda e, Hc=Hc: e.tensor_tensor(out=Hc, in0=Hc, in1=lnb[:], op=ALU.add), reads=HB + [BC], writes=HB)
                    dma("pool", dsout[ql % 2], out_d[r0:r0 + 128, :], Hc, reads=HB)

                ln_head(0)
                ln_tail(0)
                ln_head(1)
                ln_tail(1)
                ln_head(2)
                ln_head(3)
                if G + 1 < NT:
                    emit_qproj()
                    pend_ln.extend([lambda f=ln_tail: f(2), lambda f=ln_tail: f(3)])
                else:
                    ln_tail(2)
                    ln_tail(3)
                if stop_after == "F6":
                    fw.barrier()
                    return nc, dump_d
            fw.barrier()
        fw.barrier()
    return nc, dump_d


_NC_CACHE = {}


def kernel(**inputs):
    inp = {k: np.asarray(v) for k, v in inputs.items()}
    shared = _shared_layouts(inp)
    if "nc" not in _NC_CACHE:
        _NC_CACHE["nc"] = build()[0]
    nc = _NC_CACHE["nc"]
    in_maps = []
    for b in range(8):
        m = dict(shared)
        m["x"] = np.ascontiguousarray(inp["x"][b])
        m["xT"] = np.ascontiguousarray(inp["x"][b].T)
        m["memT"] = np.ascontiguousarray(inp["mem"][b].T)
        in_maps.append(m)
    res = run_bass_kernel_spmd(nc, in_maps, core_ids=list(range(8)))
    return np.stack([np.asarray(r["out"]) for r in res.results]).astype(np.float32)
```

```python
import math
from contextlib import ExitStack
import numpy as np
import concourse.bass as bass
import concourse.mybir as mybir
from concourse.bass_utils import run_bass_kernel_spmd

F32 = mybir.dt.float32
BF16 = mybir.dt.bfloat16
I32 = mybir.dt.int32
AF = mybir.ActivationFunctionType
ALU = mybir.AluOpType
AX = mybir.AxisListType

S = 4096
D = 1024
NT = 8
L = 16
NB = S // L
ALPHA = 2.0 ** 0.25
LAMBDA_INIT = 0.8 - 0.6 * math.exp(0.0)
PI = float(np.pi)


class Buf:
    __slots__ = ("name", "w", "r")

    def __init__(self, name):
        self.name = name
        self.w = None
        self.r = {}


class Eng:
    def __init__(self, name, eng, sem):
        self.name, self.eng, self.sem = name, eng, sem
        self.cnt = 0
        self.seen = {}


class FW:
    def __init__(self, nc, stack):
        self.nc, self.stack = nc, stack
        self.sems, self.engs, self.cnts = {}, {}, {}
        for name, eng in (("pe", nc.tensor), ("act", nc.scalar), ("dve", nc.vector),
                          ("pool", nc.gpsimd), ("sp", nc.sync)):
            sem = stack.enter_context(nc.semaphore("sem_" + name))
            self.sems[name] = sem
            self.engs[name] = Eng(name, eng, sem)
            self.cnts[name] = 0

    def dsem(self, name):
        self.sems[name] = self.stack.enter_context(self.nc.semaphore("d_" + name))
        self.cnts[name] = 0
        return name

    def _wait(self, e, key, val):
        if key == "pe" and e.name == "pe":
            return
        if e.seen.get(key, 0) >= val:
            return
        e.seen[key] = val
        e.eng.wait_ge(self.sems[key], val)

    def _deps(self, e, reads, writes):
        for b in reads:
            if b.w is not None:
                self._wait(e, *b.w)
        for b in writes:
            if b.w is not None:
                self._wait(e, *b.w)
            for k, v in b.r.items():
                self._wait(e, k, v)

    def _mark(self, tag, reads, writes):
        for b in reads:
            if b.r.get(tag[0], 0) < tag[1]:
                b.r[tag[0]] = tag[1]
        for b in writes:
            b.w = tag
            b.r = {}

    def op(self, en, fn, reads=(), writes=()):
        e = self.engs[en]
        self._deps(e, reads, writes)
        ins = fn(e.eng)
        self.cnts[en] += 1
        ins.then_inc(e.sem, 1)
        self._mark((en, self.cnts[en]), reads, writes)

    def dma(self, q, dsem, out, in_, reads=(), writes=()):
        e = self.engs[q]
        self._deps(e, reads, writes)
        ins = e.eng.dma_start(out=out, in_=in_)
        self.cnts[dsem] += 16
        ins.then_inc(self.sems[dsem], 16)
        self._mark((dsem, self.cnts[dsem]), reads, writes)

    def barrier(self, only=None, skip=()):
        for e in self.engs.values():
            if only is not None and e.name not in only:
                continue
            for k, v in self.cnts.items():
                if v > 0 and not k.startswith("cv_") and k not in skip:
                    self._wait(e, k, v)


def _t5_bucket_np(rel):
    ret = np.where(rel > 0, 16, 0)
    n = np.abs(rel)
    v = (np.log(np.maximum(n, 1).astype(np.float32) / np.float32(8)) / np.float32(math.log(16))
         * np.float32(8))
    large = np.minimum(8 + v.astype(np.int32), 15)
    return ret + np.where(n < 8, n, large)


def _shared_layouts(inp):
    f = np.float32
    o = {}
    lam_re, lam_im, log_dt = inp["lam_re"][0], inp["lam_im"][0], inp["log_dt"][0]
    b_re, b_im, c_re, c_im = inp["b_re"][0], inp["b_im"][0], inp["c_re"][0], inp["c_im"][0]
    def layA(v):
        return np.ascontiguousarray(v.reshape(16, 2, 64).transpose(1, 2, 0).reshape(128, 16)).astype(f)
    o["lamA"] = np.stack([layA(lam_re), layA(lam_im),
                          layA(np.broadcast_to(log_dt[:, None], (32, 64)))], axis=1)
    cbd = np.zeros((2, 2, 64, 16, 2, 16), f)
    bbd = np.zeros((2, 2, 64, 16, 2, 16), f)
    for ri, (cc, bb) in enumerate(((c_re, b_re), (c_im, b_im))):
        c4 = cc.reshape(16, 2, 16, 64)
        b4 = bb.reshape(16, 2, 64, 16)
        for g2 in range(2):
            cbd[ri, g2, :, :, g2, :] = c4[:, g2].transpose(2, 0, 1)
            bbd[ri, g2, :, :, g2, :] = b4[:, g2].transpose(1, 0, 2)
    o["cbdA"] = np.ascontiguousarray(cbd.reshape(2, 128, 16, 32).transpose(1, 0, 2, 3))
    o["bbdA"] = np.ascontiguousarray(bbd.reshape(2, 128, 16, 32).transpose(1, 0, 2, 3))
    def layB(v):
        a = v.reshape(4, 4, 2, 64).transpose(1, 0, 2, 3)
        a = np.broadcast_to(a[:, None], (4, 32, 4, 2, 64))
        return np.ascontiguousarray(a.reshape(128, 4, 128)).astype(f)
    o["lamB"] = np.stack([layB(lam_re), layB(lam_im),
                          layB(np.broadcast_to(log_dt[:, None], (32, 64)))], axis=1)
    btb = np.zeros((2, 4, 2, 16, 4, 2, 64), f)
    for ri, bb in enumerate((b_re, b_im)):
        b5 = bb.reshape(4, 4, 2, 64, 16)
        for g2 in range(2):
            btb[ri, :, g2, :, :, g2, :] = b5[:, :, g2].transpose(1, 3, 0, 2)
    o["btB"] = np.ascontiguousarray(btb.reshape(2, 128, 4, 128).transpose(1, 0, 2, 3))
    dsk = np.zeros((128, 4, 128), f)
    ds = inp["d_skip"][0].reshape(4, 128)
    for t in range(4):
        dsk[np.arange(128), t, np.arange(128)] = ds[t]
    o["dskD"] = dsk
    kl = np.arange(128)[:, None]
    qr = np.arange(256)[None, :]
    bidx = _t5_bucket_np(kl - qr)
    rb = inp["rel_bias"]
    o["tbias"] = np.ascontiguousarray(rb[bidx].transpose(0, 2, 1)).astype(f)
    maskc = np.where((kl >= 64) & (qr < 64), -30000.0, 0.0).astype(f)
    o["maskc"] = np.ascontiguousarray(np.broadcast_to(maskc[:, None, :], (128, 4, 256)))
    o["farb"] = np.ascontiguousarray(np.broadcast_to(rb[15][None, :], (128, 4))).astype(f)
    o["lamv"] = np.ascontiguousarray(np.broadcast_to(
        np.stack([inp["lambda_q1"][0], inp["lambda_k1"][0], inp["lambda_q2"][0], inp["lambda_k2"][0]])[None],
        (128, 4, 64))).astype(f)
    o["subw"] = np.ascontiguousarray(np.broadcast_to(inp["subln_w"][0][None], (128, 128))).astype(f)
    o["lng"] = np.ascontiguousarray(np.broadcast_to(inp["ln_g"][0][None], (128, D))).astype(f)
    o["lnb"] = np.ascontiguousarray(np.broadcast_to(inp["ln_b"][0][None], (128, D))).astype(f)
    o["ident"] = np.eye(128, dtype=f)
    o["w_in"] = np.ascontiguousarray(inp["w_in"][0])
    o["w_glu"] = np.ascontiguousarray(inp["w_glu"][0])
    o["w_mem_kv"] = np.ascontiguousarray(inp["w_mem_kv"][0])
    o["w_br"] = np.ascontiguousarray(np.stack([inp["w_br_ssm"][0], inp["w_br_diff"][0], inp["w_br_mem"][0]]))
    o["w_out"] = np.ascontiguousarray(inp["w_out"][0])
    return o


IN_SHAPES = {
    "xT": [D, S], "x": [S, D], "memT": [D, 256],
    "lamA": [128, 3, 16], "cbdA": [128, 2, 16, 32], "bbdA": [128, 2, 16, 32],
    "lamB": [128, 3, 4, 128], "btB": [128, 2, 4, 128], "dskD": [128, 4, 128],
    "tbias": [128, 4, 256], "maskc": [128, 4, 256], "farb": [128, 4], "lamv": [128, 4, 64],
    "subw": [128, 128], "lng": [128, D], "lnb": [128, D], "ident": [128, 128],
    "w_in": [D, 7168], "w_glu": [512, 1024], "w_mem_kv": [D, 1024], "w_br": [3, 512, 1024],
    "w_out": [D, D],
}


def build(stop_after=None, dumps=()):
    nc = bass.Bass("TRN2", target_bir_lowering=False)
    I = {k: nc.dram_tensor(k, v, F32, kind="ExternalInput").ap() for k, v in IN_SHAPES.items()}
    out_d = nc.dram_tensor("out", [S, D], F32, kind="ExternalOutput").ap()
    dump_d = {}
    xTb = nc.dram_tensor("xTb", [NT, D, 512], BF16).ap()
    wInb = nc.dram_tensor("wInb", [14, D, 512], BF16).ap()
    wGb = nc.dram_tensor("wGb", [2, 512, 512], BF16).ap()
    wMb = nc.dram_tensor("wMb", [2, D, 512], BF16).ap()
    wBb = nc.dram_tensor("wBb", [3, 2, 512, 512], BF16).ap()
    wOb = nc.dram_tensor("wOb", [2, D, 512], BF16).ap()
    memTb = nc.dram_tensor("memTb", [D, 256], BF16).ap()

    with ExitStack() as top:
        fw = FW(nc, top)
        op, dma = fw.op, fw.dma

        def sb(st, name, shape, dt):
            return st.enter_context(nc.sbuf_tensor("s_" + name, list(shape), dt))

        def dump(name, ap_sb, shape, buf):
            d = nc.dram_tensor("dump_" + name, list(shape), ap_sb.dtype, kind="ExternalOutput").ap()
            dump_d[name] = d
            dma("sp", fw.dsem("dump_" + name), d, ap_sb, reads=[buf])

        psA = top.enter_context(nc.psum_tensor("psA", [128, 1024], F32))
        psB = top.enter_context(nc.psum_tensor("psB", [128, 1024], F32))
        ps = [psA[:, 0:512], psA[:, 512:1024], psB[:, 0:512], psB[:, 512:1024]] + \
             [top.enter_context(nc.psum_tensor("ps%d" % i, [128, 512], F32)) for i in range(4, 8)]
        PS = [Buf("ps%d" % i) for i in range(8)]

        Bx = [Buf("xTb%d" % t) for t in range(NT)]
        Bw = [Buf("wInb%d" % j) for j in range(14)]
        Bg = [Buf("wGb%d" % j) for j in range(2)]
        Bm = [Buf("wMb%d" % j) for j in range(2)]
        Bb = [[Buf("wBb%d%d" % (b, j)) for j in range(2)] for b in range(3)]
        Bo = [Buf("wOb%d" % j) for j in range(2)]
        Bmem = Buf("memTb")

        def conv(dst, src, buf):
            dma("pool", fw.dsem("cv_" + buf.name), dst, src, writes=[buf])

        conv(wInb[0], I["w_in"][:, 0:512], Bw[0])
        conv(wInb[1], I["w_in"][:, 512:1024], Bw[1])
        for t in range(NT):
            conv(xTb[t], I["xT"][:, t * 512:(t + 1) * 512], Bx[t])
        for j in range(2):
            conv(wGb[j], I["w_glu"][:, j * 512:(j + 1) * 512], Bg[j])
        for j in (3, 4):
            conv(wInb[j], I["w_in"][:, j * 512:(j + 1) * 512], Bw[j])
        for j in range(2):
            conv(wMb[j], I["w_mem_kv"][:, j * 512:(j + 1) * 512], Bm[j])
        conv(memTb, I["memT"], Bmem)
        for j in (2, 5, 6, 7, 8, 9, 10, 11, 12, 13):
            conv(wInb[j], I["w_in"][:, j * 512:(j + 1) * 512], Bw[j])
        for b in range(3):
            for j in range(2):
                conv(wBb[b, j], I["w_br"][b, :, j * 512:(j + 1) * 512], Bb[b][j])
        for j in range(2):
            conv(wOb[j], I["w_out"][:, j * 512:(j + 1) * 512], Bo[j])

        def wsrc(piece, nkc):
            return piece.rearrange("(kc p) n -> p kc n", p=128)

        ident_f = sb(top, "ident_f", [128, 128], F32)
        ident = sb(top, "ident", [128, 128], BF16)
        Bid = Buf("ident")
        dma("sp", fw.dsem("c_id"), ident_f[:], I["ident"], writes=[Bid])
        op("dve", lambda e: e.tensor_copy(out=ident[:], in_=ident_f[:]), reads=[Bid], writes=[Bid])

        ZS = sb(top, "ZS", [128, 4, S], BF16)
        BZSn = Buf("ZSn")

        with ExitStack() as phS:
            UT = sb(phS, "UT", [128, 4, S], BF16)
            BUTp = [[Buf("UT%d_%d" % (i, j)) for j in range(8)] for i in range(4)]
            BUTall = [b for bl in BUTp for b in bl]
            ZSp = sb(phS, "ZSp", [128, 4, S], BF16)
            BZSp = Buf("ZSp")

            with ExitStack() as ph2:
                def cplx_setup(st, pref, lam_ap, shape, Bsrc, keep):
                    T = {}
                    for nm in ("dt", "th", "mg", "cs", "sn", "t1", "t2", "den"):
                        T[nm] = sb(st, pref + nm, shape, F32)
                    T.update(keep)
                    ti = sb(st, pref + "ti", shape, I32)
                    Bt = keep["buf"]
                    R, W = [Bsrc, Bt], [Bt]
                    lre, lim, ldt = lam_ap(0), lam_ap(1), lam_ap(2)
                    op("act", lambda e: e.activation(out=T["dt"][:], in_=ldt, func=AF.Exp), reads=R, writes=W)
                    op("dve", lambda e: e.tensor_tensor(out=T["th"][:], in0=lim, in1=T["dt"][:], op=ALU.mult), reads=R, writes=W)
                    op("dve", lambda e: e.tensor_tensor(out=T["t0"][:], in0=lre, in1=T["dt"][:], op=ALU.mult), reads=R, writes=W)
                    op("act", lambda e: e.activation(out=T["mg"][:], in_=T["t0"][:], func=AF.Exp), reads=R, writes=W)

                    def sin_of(dst, shift):
                        op("dve", lambda e: e.tensor_scalar(out=T["t0"][:], in0=T["th"][:], scalar1=shift, scalar2=None, op0=ALU.add), reads=R, writes=W)
                        op("dve", lambda e: e.tensor_scalar(out=T["t1"][:], in0=T["t0"][:], scalar1=1.0 / (2 * PI), scalar2=None, op0=ALU.mult), reads=R, writes=W)
                        op("dve", lambda e: e.tensor_copy(out=ti[:], in_=T["t1"][:]), reads=R, writes=W)
                        op("dve", lambda e: e.tensor_copy(out=T["t1"][:], in_=ti[:]), reads=R, writes=W)
                        op("dve", lambda e: e.scalar_tensor_tensor(out=T["t2"][:], in0=T["t1"][:], scalar=-2 * PI, in1=T["t0"][:], op0=ALU.mult, op1=ALU.add), reads=R, writes=W)
                        op("dve", lambda e: e.tensor_scalar(out=T["t1"][:], in0=T["t2"][:], scalar1=PI, scalar2=-2 * PI, op0=ALU.is_gt, op1=ALU.mult), reads=R, writes=W)
                        op("dve", lambda e: e.tensor_tensor(out=T["t2"][:], in0=T["t2"][:], in1=T["t1"][:], op=ALU.add), reads=R, writes=W)
                        op("dve", lambda e: e.tensor_scalar(out=T["t1"][:], in0=T["t2"][:], scalar1=-PI, scalar2=2 * PI, op0=ALU.is_lt, op1=ALU.mult), reads=R, writes=W)
                        op("dve", lambda e: e.tensor_tensor(out=T["t2"][:], in0=T["t2"][:], in1=T["t1"][:], op=ALU.add), reads=R, writes=W)
                        op("act", lambda e: e.activation(out=dst[:], in_=T["t2"][:], func=AF.Sin), reads=R, writes=W)
                    sin_of(T["sn"], 0.0)
                    sin_of(T["cs"], PI / 2)
                    op("dve", lambda e: e.tensor_tensor(out=T["lbr"][:], in0=T["mg"][:], in1=T["cs"][:], op=ALU.mult), reads=R, writes=W)
                    op("dve", lambda e: e.tensor_tensor(out=T["lbi"][:], in0=T["mg"][:], in1=T["sn"][:], op=ALU.mult), reads=R, writes=W)
                    op("dve", lambda e: e.tensor_tensor(out=T["den"][:], in0=lre, in1=lre, op=ALU.mult), reads=R, writes=W)
                    op("dve", lambda e: e.tensor_tensor(out=T["t0"][:], in0=lim, in1=lim, op=ALU.mult), reads=R, writes=W)
                    op("dve", lambda e: e.tensor_tensor(out=T["den"][:], in0=T["den"][:], in1=T["t0"][:], op=ALU.add), reads=R, writes=W)
                    op("dve", lambda e: e.reciprocal(out=T["den"][:], in_=T["den"][:]), reads=R, writes=W)
                    op("dve", lambda e: e.tensor_scalar(out=T["t0"][:], in0=T["lbr"][:], scalar1=-1.0, scalar2=None, op0=ALU.add), reads=R, writes=W)
                    op("dve", lambda e: e.tensor_tensor(out=T["t1"][:], in0=T["t0"][:], in1=lre, op=ALU.mult), reads=R, writes=W)
                    op("dve", lambda e: e.tensor_tensor(out=T["t2"][:], in0=T["lbi"][:], in1=lim, op=ALU.mult), reads=R, writes=W)
                    op("dve", lambda e: e.tensor_tensor(out=T["t1"][:], in0=T["t1"][:], in1=T["t2"][:], op=ALU.add), reads=R, writes=W)
                    op("dve", lambda e: e.tensor_tensor(out=T["cfr"][:], in0=T["t1"][:], in1=T["den"][:], op=ALU.mult), reads=R, writes=W)
                    op("dve", lambda e: e.tensor_tensor(out=T["t1"][:], in0=T["lbi"][:], in1=lre, op=ALU.mult), reads=R, writes=W)
                    op("dve", lambda e: e.tensor_tensor(out=T["t2"][:], in0=T["t0"][:], in1=lim, op=ALU.mult), reads=R, writes=W)
                    op("dve", lambda e: e.tensor_tensor(out=T["t1"][:], in0=T["t1"][:], in1=T["t2"][:], op=ALU.subtract), reads=R, writes=W)
                    op("dve", lambda e: e.tensor_tensor(out=T["cfi"][:], in0=T["t1"][:], in1=T["den"][:], op=ALU.mult), reads=R, writes=W)
                    return T, Bt

                def cmul(out_r, out_i, ar, ai, br, bi, t0, R, W, en="dve"):
                    op(en, lambda e: e.tensor_tensor(out=out_r, in0=ar, in1=br, op=ALU.mult), reads=R, writes=W)
                    op(en, lambda e: e.tensor_tensor(out=t0, in0=ai, in1=bi, op=ALU.mult), reads=R, writes=W)
                    op(en, lambda e: e.tensor_tensor(out=out_r, in0=out_r, in1=t0, op=ALU.subtract), reads=R, writes=W)
                    op(en, lambda e: e.tensor_tensor(out=out_i, in0=ar, in1=bi, op=ALU.mult), reads=R, writes=W)
                    op(en, lambda e: e.tensor_tensor(out=t0, in0=ai, in1=br, op=ALU.mult), reads=R, writes=W)
                    op(en, lambda e: e.tensor_tensor(out=out_i, in0=out_i, in1=t0, op=ALU.add), reads=R, writes=W)

                lamA = sb(ph2, "lamA", [128, 3, 16], F32)
                cbdA = sb(ph2, "cbdA", [128, 2, 16, 32], F32)
                keepA = {nm: sb(ph2, "A_" + nm, [128, 16], F32) for nm in ("lbr", "lbi", "cfr", "cfi", "t0")}
                keepA["buf"] = Buf("A_tmp")
                PW = sb(ph2, "PW", [128, 17, 2, 16], F32)
                AD = sb(ph2, "AD", [128, 8, 3, 16], F32)
                PWn = sb(ph2, "PWn", [128, 17, 16], F32)
                BbA = sb(ph2, "BbA", [128, 2, 16, 32], BF16)
                dskD = sb(ph2, "dskD", [128, 4, 128], F32)
                XB0 = sb(ph2, "XB0", [128, 2, 512], F32)
                LB1 = sb(ph2, "LB1", [128, 2, 512], F32)
                LB2 = sb(ph2, "LB2", [128, 2, 512], F32)
                keepB = {nm: sb(ph2, "B_" + nm, [128, 512], F32) for nm in ("cfr", "cfi", "t0")}
                keepB["lbr"], keepB["lbi"] = LB1[:, 0, :], LB1[:, 1, :]
                keepB["buf"] = Buf("B_tmp")
                FA2 = [sb(ph2, "FA%d" % i, [128, 17, 2, 4, 32], BF16) for i in range(2)]
                BFA2 = [Buf("FA%d" % i) for i in range(2)]
                tP8 = [sb(ph2, "tP%d" % i, [128, 4, 32], F32) for i in range(8)]
                BtP8 = [Buf("tP%d" % i) for i in range(8)]
                BA, BB = Buf("layA"), Buf("layB")
                dsA, dsB = fw.dsem("layA"), fw.dsem("layB")
                dma("sp", dsA, lamA[:], I["lamA"], writes=[BA])
                dma("sp", dsA, cbdA[:], I["cbdA"], writes=[BA])
                dma("sp", dsB, dskD[:], I["dskD"], writes=[BB])

                def bc(ap2, n=16):
                    return ap2.unsqueeze(2).to_broadcast([128, n, 32])

                def setup_FA(T_):
                    par = T_ % 2
                    FA, BFA = FA2[par], BFA2[par]
                    Ps = slice(4 * T_, 4 * T_ + 4)
                    cr, ci = cbdA[:, 0, Ps], cbdA[:, 1, Ps]
                    for a in range(17):
                        tt = tP8[4 * (a % 2):4 * (a % 2) + 4]
                        tb = BtP8[4 * (a % 2):4 * (a % 2) + 4]
                        pr, pi_, npi = bc(PW[:, a, 0, Ps], 4), bc(PW[:, a, 1, Ps], 4), bc(PWn[:, a, Ps], 4)
                        R0 = [BA, BtA]
                        op("pool", lambda e: e.tensor_tensor(out=tt[0][:], in0=cr, in1=pr, op=ALU.mult), reads=R0, writes=[tb[0]])
                        op("pool", lambda e: e.tensor_tensor(out=tt[1][:], in0=ci, in1=pi_, op=ALU.mult), reads=R0, writes=[tb[1]])
                        op("pool", lambda e: e.tensor_tensor(out=tt[2][:], in0=cr, in1=npi, op=ALU.mult), reads=R0, writes=[tb[2]])
                        op("pool", lambda e: e.tensor_tensor(out=tt[3][:], in0=ci, in1=pr, op=ALU.mult), reads=R0, writes=[tb[3]])
                        op("pool", lambda e, a=a: e.tensor_tensor(out=FA[:, a, 0], in0=tt[0][:], in1=tt[1][:], op=ALU.subtract),
                           reads=[tb[0], tb[1], BFA], writes=[BFA])
                        op("pool", lambda e, a=a: e.tensor_tensor(out=FA[:, a, 1], in0=tt[2][:], in1=tt[3][:], op=ALU.subtract),
                           reads=[tb[2], tb[3], BFA], writes=[BFA])

                ph1 = ExitStack()
                wS = sb(ph1, "wS", [128, 8, 1024], BF16)
                BwS = Buf("wS")
                ds_w = fw.dsem("wS")
                for j in range(2):
                    dma("sp", ds_w, wS[:, :, j * 512:(j + 1) * 512], wsrc(wInb[j], 8), reads=[Bw[j]], writes=[BwS])
                xth = [sb(ph1, "xtS%d" % i, [128, 4, 512], BF16) for i in range(2)]
                Bxth = [Buf("xtS%d" % i) for i in range(2)]
                dsxh = [fw.dsem("xtS%d" % i) for i in range(2)]

                def load_xh(t, hf):
                    dma("sp", dsxh[hf], xth[hf][:], wsrc(xTb[t], 8)[:, 4 * hf:4 * hf + 4, :], reads=[Bx[t]], writes=[Bxth[hf]])
                load_xh(0, 0)
                load_xh(0, 1)

                with ExitStack() as tmpS:
                    bbdA = sb(tmpS, "bbdA", [128, 2, 16, 32], F32)
                    lamB = sb(tmpS, "lamB", [128, 3, 512], F32)
                    btB = sb(tmpS, "btB", [128, 2, 512], F32)
                    dma("sp", dsA, bbdA[:], I["bbdA"], writes=[BA])
                    dma("sp", dsB, lamB[:], I["lamB"].rearrange("p a t n -> p a (t n)"), writes=[BB])
                    dma("sp", dsB, btB[:], I["btB"].rearrange("p a t n -> p a (t n)"), writes=[BB])
                    TA, BtA = cplx_setup(tmpS, "A_", lambda i: lamA[:, i, :], [128, 16], BA, keepA)
                    TA = keepA
                    RA, WA_ = [BA, BtA], [BtA]
                    op("dve", lambda e: e.memset(PW[:, 0, 0, :], 1.0), reads=RA, writes=WA_)
                    op("dve", lambda e: e.memset(PW[:, 0, 1, :], 0.0), reads=RA, writes=WA_)
                    for a in range(1, 17):
                        cmul(PW[:, a, 0, :], PW[:, a, 1, :], PW[:, a - 1, 0, :], PW[:, a - 1, 1, :],
                             TA["lbr"][:], TA["lbi"][:], TA["t0"][:], RA, WA_)
                    op("dve", lambda e: e.tensor_copy(out=AD[:, 0, 0:2, :], in_=PW[:, 16, :, :]), reads=RA, writes=WA_)
                    op("dve", lambda e: e.tensor_scalar(out=PWn[:], in0=PW[:, :, 1, :], scalar1=-1.0, scalar2=None, op0=ALU.mult), reads=RA, writes=WA_)
                    for k in range(1, 8):
                        cmul(AD[:, k, 0, :], AD[:, k, 1, :], AD[:, k - 1, 0, :], AD[:, k - 1, 1, :],
                             AD[:, k - 1, 0, :], AD[:, k - 1, 1, :], TA["t0"][:], RA, WA_)
                    op("dve", lambda e: e.tensor_scalar(out=AD[:, :, 2, :], in0=AD[:, :, 1, :], scalar1=-1.0, scalar2=None, op0=ALU.mult), reads=RA, writes=WA_)
                    setup_FA(0)
                    setup_FA(1)
                    Bst_ = {}

                    def emit_setup_B():
                        TBt, BtB_ = cplx_setup(tmpS, "B_", lambda i: lamB[:, i, :], [128, 512], BB, keepB)
                        TB = keepB
                        RB, WB_ = [BB, BtB_], [BtB_]
                        cmul(XB0[:, 0], XB0[:, 1], btB[:, 0], btB[:, 1], TB["cfr"][:], TB["cfi"][:], TB["t0"][:], RB, WB_)
                        op("dve", lambda e: e.tensor_scalar(out=LB2[:, 0, :], in0=TB["lbi"][:], scalar1=-1.0, scalar2=None, op0=ALU.mult), reads=RB, writes=WB_)
                        op("dve", lambda e: e.tensor_copy(out=LB2[:, 1, :], in_=TB["lbr"][:]), reads=RB, writes=WB_)
                        tA = [TBt[nm][:].rearrange("p (a b) -> p a b", a=16) for nm in ("t1", "t2", "den")]
                        RAB, WAB = [BA, BtA, BtB_], [BtA, BtB_]
                        cmul(tA[0], tA[1], bbdA[:, 0], bbdA[:, 1], bc(TA["cfr"][:]), bc(TA["cfi"][:]), tA[2], RAB, WAB)
                        op("dve", lambda e: e.tensor_copy(out=BbA[:, 0], in_=tA[0]), reads=RAB, writes=WAB)
                        op("dve", lambda e: e.tensor_copy(out=BbA[:, 1], in_=tA[1]), reads=RAB, writes=WAB)
                        Bst_["BtB_"] = BtB_

                    for t in range(NT):
                        for hf in range(2):
                            for m in range(8):
                                for k4 in range(4):
                                    kc = 4 * hf + k4
                                    op("pe", lambda e, m=m, kc=kc, k4=k4, hf=hf: e.matmul(
                                        ps[m][:], wS[:, kc, m * 128:(m + 1) * 128], xth[hf][:, k4, :],
                                        start=(kc == 0), stop=(kc == 7)),
                                       reads=[BwS, Bxth[hf]], writes=[PS[m]])
                            if t + 1 < NT:
                                load_xh(t + 1, hf)
                        if t == 1:
                            emit_setup_B()
                        for m in range(8):
                            if m < 4:
                                op("act", lambda e, m=m: e.activation(func=AF.Copy,
                                    out=UT[:, m, :].rearrange("p (i c) -> p i c", i=L)[:, :, 32 * t:32 * t + 32],
                                    in_=ps[m][:].rearrange("p (c i) -> p i c", i=L)),
                                   reads=[PS[m]], writes=BUTp[m])
                            else:
                                op("act", lambda e, m=m: e.activation(
                                    out=ZSp[:, m - 4, :].rearrange("p (i c) -> p i c", i=L)[:, :, 32 * t:32 * t + 32],
                                    in_=ps[m][:].rearrange("p (c i) -> p i c", i=L), func=AF.Silu),
                                   reads=[PS[m]], writes=[BZSp])
                    fw.barrier(skip=("pool",))
                BtB_ = Bst_["BtB_"]
                ph1.close()
                fw.barrier(skip=("pool",))
                if "UT" in dumps:
                    dump("UT", UT[:], [128, 4, S], BUTp[3][0])
                    dump("ZS", ZSp[:], [128, 4, S], BZSp)
                EB = sb(ph2, "EB", [128, 16, 2, 128], BF16)
                KB2 = [sb(ph2, "KB%d" % i, [128, 16, 128], BF16) for i in range(2)]
                Wa = sb(ph2, "Wa", [128, 4, 2, NB], F32)
                Wb = sb(ph2, "Wb", [128, 4, 2, NB], F32)
                SP2 = [sb(ph2, "SP%d" % i, [128, 4, 2, NB], BF16) for i in range(2)]
                BEBa = [Buf("EB%d" % a) for a in range(16)]
                BKB2 = [Buf("KB%d" % i) for i in range(2)]
                BWam = [Buf("Wa%d" % m) for m in range(4)]
                BWbm = [Buf("Wb%d" % m) for m in range(4)]
                BSP2 = [Buf("SP%d" % i) for i in range(2)]
                KF0 = sb(ph2, "KF0", [128, 128], F32)
                BKF = Buf("KF0")
                XB = [sb(ph2, "XBp%d" % i, [128, 2, 128], F32) for i in range(2)]
                BXBf = [Buf("XBp%d" % i) for i in range(2)]
                XBt = sb(ph2, "XBt", [128, 2, 128], F32)
                BXBt = Buf("XBt")
                gel = [sb(ph2, "gel%d" % i, [128, 512], F32) for i in range(2)]
                Bgel = Buf("gel")
                def eb_ops(T_):
                    cs = slice(T_ * 128, (T_ + 1) * 128)
                    R0 = [BB, BtB_]
                    l1, l2 = LB1[:, :, cs], LB2[:, :, cs]
                    th = []
                    for a in range(16):
                        if a == 0:
                            cur, bc_ = XB0[:, :, cs], BtB_
                        else:
                            cur, bc_ = XB[a % 2][:], BXBf[a % 2]
                        th.append(lambda a=a, cur=cur, bc_=bc_: op("dve", lambda e: e.tensor_copy(out=EB[:, a, :, :], in_=cur), reads=R0 + [bc_, BEBa[a]], writes=[BEBa[a]]))
                        if a < 15:
                            nxt, bn_ = XB[(a + 1) % 2], BXBf[(a + 1) % 2]
                            cr_b = cur[:, 0:1, :].to_broadcast([128, 2, 128])
                            ci_b = cur[:, 1:2, :].to_broadcast([128, 2, 128])
                            th.append(lambda nxt=nxt, cr_b=cr_b, bc_=bc_, bn_=bn_: op("dve", lambda e: e.tensor_tensor(out=nxt[:], in0=cr_b, in1=l1, op=ALU.mult), reads=R0 + [bc_], writes=[bn_]))
                            th.append(lambda ci_b=ci_b, bc_=bc_: op("dve", lambda e: e.tensor_tensor(out=XBt[:], in0=ci_b, in1=l2, op=ALU.mult), reads=R0 + [bc_], writes=[BXBt]))
                            th.append(lambda nxt=nxt, bn_=bn_: op("dve", lambda e: e.tensor_tensor(out=nxt[:], in0=nxt[:], in1=XBt[:], op=ALU.add), reads=[bn_, BXBt], writes=[bn_]))
                    return th

                def emit_K(T_):
                    par = T_ % 2
                    FA, BFA, KB, BKB = FA2[par], BFA2[par], KB2[par], BKB2[par]
                    bk = 6 + (T_ % 2)
                    pk = ps[bk][:].rearrange("p (l n) -> p l n", l=16)
                    for lag in range(16):
                        for ri in range(2):
                            for m in range(4):
                                op("pe", lambda e, lag=lag, m=m, ri=ri: e.matmul(
                                    pk[32 * m:32 * m + 32, lag, :], BbA[:, ri, 4 * T_ + m, :], FA[:, lag, ri, m, :],
                                    start=(ri == 0), stop=(ri == 1), tile_position=(0, 32 * m), skip_group_check=True),
                                   reads=[BtA, BFA], writes=[PS[bk]])
                    op("dve", lambda e: e.memset(KB[:], 0.0), reads=[], writes=[BKB])
                    op("dve", lambda e: e.tensor_copy(out=KF0[:], in_=dskD[:, T_, :]), reads=[BB, BKF], writes=[BKF])
                    for m in range(4):
                        sl = slice(32 * m, 32 * m + 32)
                        op("dve", lambda e, sl=sl: e.tensor_copy(out=KB[sl, 1:16, sl], in_=pk[sl, 1:16, :]), reads=[PS[bk]], writes=[BKB])
                        op("dve", lambda e, sl=sl: e.tensor_tensor(out=KF0[sl, sl], in0=KF0[sl, sl], in1=pk[sl, 0, :], op=ALU.add), reads=[PS[bk], BKF], writes=[BKF])
                    op("dve", lambda e: e.tensor_copy(out=KB[:, 0, :], in_=KF0[:]), reads=[BKF, BKB], writes=[BKB])

                def emit_L1(T_):
                    UTv = UT[:, T_, :].rearrange("p (i c) -> p i c", i=L)
                    pws = [ps[m][:].rearrange("p (r n) -> p r n", r=2) for m in range(4)]
                    for ri in range(2):
                        for j in range(L):
                            for m in range(4):
                                rows = slice(32 * m, 32 * m + 32)
                                op("pe", lambda e, m=m, ri=ri, j=j, rows=rows: e.matmul(
                                    pws[m][:, ri, :], EB[rows, 15 - j, ri, :], UTv[rows, j, :],
                                    start=(j == 0), stop=(j == L - 1), tile_position=(32 * m, 0), skip_group_check=True),
                                   reads=[BEBa[15 - j]] + BUTp[T_], writes=[PS[m]])
                    for m in range(4):
                        op("act", lambda e, m=m: e.activation(out=Wa[:, m], in_=pws[m], func=AF.Copy), reads=[PS[m]], writes=[BWam[m]])

                def dbl_ops(T_):
                    par = T_ % 2
                    src, dst, Bs, Bd = Wa, Wb, BWam, BWbm
                    SP, BSP = SP2[par], BSP2[par]
                    th = []
                    for k in range(8):
                        d = 1 << k

                        def mk(kind, m, d=d, k=k, src=src, dst=dst, Bs=Bs, Bd=Bd):
                            P_ = 4 * T_ + m
                            aR, aI, nI = AD[:, k, 0, P_:P_ + 1], AD[:, k, 1, P_:P_ + 1], AD[:, k, 2, P_:P_ + 1]
                            R_, W_ = [Bs[m], Bd[m], BtA], [Bd[m]]
                            if kind == 0:
                                return lambda: op("dve", lambda e: e.scalar_tensor_tensor(
                                    out=dst[:, m, 0, d:], in0=src[:, m, 0, :NB - d], scalar=aR, in1=src[:, m, 0, d:], op0=ALU.mult, op1=ALU.add), reads=R_, writes=W_)
                            if kind == 1:
                                return lambda: op("dve", lambda e: e.scalar_tensor_tensor(
                                    out=dst[:, m, 1, d:], in0=src[:, m, 0, :NB - d], scalar=aI, in1=src[:, m, 1, d:], op0=ALU.mult, op1=ALU.add), reads=R_, writes=W_)
                            if kind == 2:
                                return lambda: op("dve", lambda e: e.tensor_copy(out=dst[:, m, :, :d], in_=src[:, m, :, :d]), reads=R_, writes=W_)
                            if kind == 3:
                                return lambda: op("dve", lambda e: e.scalar_tensor_tensor(
                                    out=dst[:, m, 0, d:], in0=src[:, m, 1, :NB - d], scalar=nI, in1=dst[:, m, 0, d:], op0=ALU.mult, op1=ALU.add), reads=R_, writes=W_)
                            return lambda: op("dve", lambda e: e.scalar_tensor_tensor(
                                out=dst[:, m, 1, d:], in0=src[:, m, 1, :NB - d], scalar=aR, in1=dst[:, m, 1, d:], op0=ALU.mult, op1=ALU.add), reads=R_, writes=W_)
                        th.append(lambda d=d, src=src, dst=dst, Bs=Bs, Bd=Bd: op("dve", lambda e: e.tensor_copy(
                            out=dst[:, :, :, :d], in_=src[:, :, :, :d]), reads=list(Bs) + list(Bd), writes=list(Bd)))
                        for kind in (0, 1, 3, 4):
                            for m in range(4):
                                th.append(mk(kind, m))
                        src, dst, Bs, Bd = dst, src, Bd, Bs

                    def fin(src=src, Bs=Bs):
                        op("dve", lambda e: e.memset(SP[:, :, :, 0:1], 0.0), reads=[], writes=[BSP])
                        op("dve", lambda e: e.tensor_copy(out=SP[:, :, :, 1:], in_=src[:, :, :, :NB - 1]), reads=list(Bs), writes=[BSP])
                    th.append(fin)
                    return th

                def emit_L03(T_, inter):
                    par = T_ % 2
                    FA, BFA, KB, BKB, SP, BSP = FA2[par], BFA2[par], KB2[par], BKB2[par], SP2[par], BSP2[par]
                    UTv = UT[:, T_, :].rearrange("p (i c) -> p i c", i=L)
                    nint = (len(inter) + 7) // 8
                    for n_, ip in enumerate(range(7, -1, -1)):
                        bk = 4 + (ip % 4)
                        py = ps[bk][:].rearrange("p (i n) -> p i n", i=2)
                        i0 = 2 * ip
                        for lag in range(i0 + 2):
                            ist = i0 if lag <= i0 else i0 + 1
                            ni = i0 + 2 - ist
                            op("pe", lambda e, lag=lag, ist=ist, ni=ni, py=py: e.matmul(
                                py[:, ist - i0:ist - i0 + ni, :], KB[:, lag, :], UTv[:, ist - lag:ist - lag + ni, :],
                                start=(lag == 0), stop=False, skip_group_check=True),
                               reads=[BKB] + BUTp[T_][:ip + 1], writes=[PS[bk]])
                        for il in range(2):
                            for ri in range(2):
                                for m in range(4):
                                    last = (m == 3 and il == 1 and ri == 1)
                                    op("pe", lambda e, m=m, il=il, ri=ri, py=py, last=last: e.matmul(
                                        py[32 * m:32 * m + 32, il, :], FA[:, i0 + il + 1, ri, m, :], SP[:, m, ri, :],
                                        start=False, stop=last, tile_position=(0, 32 * m), skip_group_check=True),
                                       reads=[BFA, BSP], writes=[PS[bk]])
                        g0, g1 = gel[0][:], gel[1][:]
                        pyf = ps[bk][:]
                        op("act", lambda e, pyf=pyf: e.activation(out=g0, in_=pyf, func=AF.Square), reads=[PS[bk], Bgel], writes=[Bgel])
                        op("dve", lambda e: e.tensor_scalar(out=g0, in0=g0, scalar1=0.044715, scalar2=1.0, op0=ALU.mult, op1=ALU.add), reads=[Bgel], writes=[Bgel])
                        op("dve", lambda e, pyf=pyf: e.tensor_tensor(out=g0, in0=g0, in1=pyf, op=ALU.mult), reads=[Bgel, PS[bk]], writes=[Bgel])
                        op("act", lambda e: e.activation(out=g1, in_=g0, func=AF.Sigmoid, scale=1.5957691216057308), reads=[Bgel], writes=[Bgel])
                        op("dve", lambda e, py=py, i0=i0: e.tensor_tensor(
                            out=UTv[:, i0:i0 + 2, :], in0=gel[1][:].rearrange("p (i n) -> p i n", i=2), in1=py, op=ALU.mult),
                           reads=[Bgel, PS[bk]], writes=[BUTp[T_][ip]])
                        for f_ in inter[n_ * nint:(n_ + 1) * nint]:
                            f_()

                def run(th):
                    for f_ in th:
                        f_()

                def mix(a_, b_):
                    out_, ia, ib = [], 0, 0
                    while ia < len(a_) or ib < len(b_):
                        if ib >= len(b_) or (ia < len(a_) and ia * len(b_) <= ib * len(a_)):
                            out_.append(a_[ia]); ia += 1
                        else:
                            out_.append(b_[ib]); ib += 1
                    return out_

                run(eb_ops(0))
                emit_K(0)
                emit_L1(0)
                run(mix(dbl_ops(0), eb_ops(1)))
                emit_K(1)
                emit_L1(1)
                emit_L03(0, mix(dbl_ops(1), eb_ops(2)))
                setup_FA(2)
                emit_K(2)
                emit_L1(2)
                emit_L03(1, mix(dbl_ops(2), eb_ops(3)))
                setup_FA(3)
                emit_K(3)
                emit_L1(3)
                emit_L03(2, dbl_ops(3))
                emit_L03(3, [])
                if "YG" in dumps:
                    dump("YG", UT[:], [128, 4, S], BUTp[3][0])
                fw.barrier()
                ph2.close()
                wG = sb(ph2, "wG", [128, 4, 1024], BF16)
                BwG = Buf("wG")
                dsG = fw.dsem("wG")
                for j in range(2):
                    dma("sp", dsG, wG[:, :, j * 512:(j + 1) * 512], wsrc(wGb[j], 4), reads=[Bg[j]], writes=[BwG])
                sg = [sb(ph2, "sg%d" % i, [128, 512], F32) for i in range(2)]
                Bsg = [Buf("sg%d" % i) for i in range(2)]
                cnt = 0
                for t in range(NT):
                    cols = slice(t * 512, (t + 1) * 512)
                    for f in range(4):
                        bv, bg_ = (2 * cnt) % 8, (2 * cnt + 1) % 8
                        s_ = cnt % 2
                        cnt += 1
                        for kc in range(4):
                            op("pe", lambda e, f=f, kc=kc, bv=bv: e.matmul(
                                ps[bv][:], wG[:, kc, f * 128:(f + 1) * 128], UT[:, kc, cols], start=(kc == 0), stop=(kc == 3)),
                               reads=[BwG] + BUTall, writes=[PS[bv]])
                        for kc in range(4):
                            op("pe", lambda e, f=f, kc=kc, bg_=bg_: e.matmul(
                                ps[bg_][:], wG[:, kc, 512 + f * 128:512 + (f + 1) * 128], UT[:, kc, cols], start=(kc == 0), stop=(kc == 3)),
                               reads=[BwG] + BUTall, writes=[PS[bg_]])
                        op("act", lambda e, bg_=bg_, s_=s_: e.activation(out=sg[s_][:], in_=ps[bg_][:], func=AF.Sigmoid),
                           reads=[PS[bg_]], writes=[Bsg[s_]])
                        op("dve", lambda e, bv=bv, s_=s_: e.tensor_tensor(out=sg[s_][:], in0=sg[s_][:], in1=ps[bv][:], op=ALU.mult),
                           reads=[PS[bv], Bsg[s_]], writes=[Bsg[s_]])
                        op("dve", lambda e, f=f, s_=s_, t=t: e.tensor_tensor(
                            out=ZS[:, f, :].rearrange("p (c i) -> p i c", i=L)[:, 2 * t:2 * t + 2, :],
                            in0=sg[s_][:].rearrange("p (i c) -> p i c", i=2), in1=ZSp[:, f, cols].rearrange("p (i c) -> p i c", i=2), op=ALU.mult),
                           reads=[Bsg[s_], BZSp, BZSn], writes=[BZSn])
            fw.barrier()
        if "YS" in dumps:
            dump("YS", ZS[:], [128, 4, S], BZSn)
        if stop_after == "S":
            fw.barrier()
            return nc, dump_d

        KT = sb(top, "KT", [128, 4, S], BF16)
        VA = sb(top, "VA", [128, 32, 4, 130], BF16)
        MKT = sb(top, "MKT", [128, 4, 256], BF16)
        MVA = sb(top, "MVA", [128, 2, 4, 130], BF16)
        BKT, BVA, BMK = Buf("KT"), Buf("VA"), Buf("MK")
        op("dve", lambda e: e.memset(VA[:, :, :, 128:130], 1.0), reads=[], writes=[BVA])
        op("dve", lambda e: e.memset(MVA[:, :, :, 128:130], 1.0), reads=[], writes=[BMK])
        with ExitStack() as phK:
            wKV = sb(phK, "wKV", [128, 8, 1024], BF16)
            wM = sb(phK, "wM", [128, 8, 1024], BF16)
            mT = sb(phK, "mT", [128, 8, 256], BF16)
            BwKV, BwM, BmT = Buf("wKV"), Buf("wM"), Buf("mT")
            dsk_, dsm_, dst_ = fw.dsem("wKV"), fw.dsem("wM"), fw.dsem("mT")
            for j in range(2):
                dma("sp", dsk_, wKV[:, :, j * 512:(j + 1) * 512], wsrc(wInb[3 + j], 8), reads=[Bw[3 + j]], writes=[BwKV])
            for j in range(2):
                dma("sp", dsm_, wM[:, :, j * 512:(j + 1) * 512], wsrc(wMb[j], 8), reads=[Bm[j]], writes=[BwM])
            dma("sp", dst_, mT[:], wsrc(memTb, 8), reads=[Bmem], writes=[BmT])
            xt = [sb(phK, "xtK%d" % i, [128, 8, 512], BF16) for i in range(2)]
            Bxt = [Buf("xtK%d" % i) for i in range(2)]
            dsx = [fw.dsem("xtK%d" % i) for i in range(2)]

            def load_xk(t):
                dma("sp", dsx[t % 2], xt[t % 2][:], wsrc(xTb[t], 8), reads=[Bx[t]], writes=[Bxt[t % 2]])
            load_xk(0)
            cnt = 0
            for h in range(4):
                bk = cnt % 8
                cnt += 1
                for kc in range(8):
                    op("pe", lambda e, h=h, kc=kc, bk=bk: e.matmul(ps[bk][:, 0:256], wM[:, kc, h * 128:(h + 1) * 128], mT[:, kc, :],
                                                                   start=(kc == 0), stop=(kc == 7)), reads=[BwM, BmT], writes=[PS[bk]])
                op("act", lambda e, h=h, bk=bk: e.activation(out=MKT[:, h, :], in_=ps[bk][:, 0:256], func=AF.Copy), reads=[PS[bk]], writes=[BMK])
            for mb in range(2):
                bk = cnt % 8
                cnt += 1
                for kc in range(8):
                    op("pe", lambda e, mb=mb, kc=kc, bk=bk: e.matmul(ps[bk][:], mT[:, kc, mb * 128:(mb + 1) * 128], wM[:, kc, 512:1024],
                                                                     start=(kc == 0), stop=(kc == 7)), reads=[BwM, BmT], writes=[PS[bk]])
                op("act", lambda e, mb=mb, bk=bk: e.activation(out=MVA[:, mb, :, 0:128], in_=ps[bk][:].rearrange("p (h e) -> p h e", h=4), func=AF.Copy),
                   reads=[PS[bk]], writes=[BMK])
            for t in range(NT):
                if t + 1 < NT:
                    load_xk(t + 1)
                x_t, Bx_t = xt[t % 2], Bxt[t % 2]
                cols = slice(t * 512, (t + 1) * 512)
                for h in range(4):
                    bk = cnt % 8
                    cnt += 1
                    for kc in range(8):
                        op("pe", lambda e, h=h, kc=kc, bk=bk: e.matmul(ps[bk][:], wKV[:, kc, h * 128:(h + 1) * 128], x_t[:, kc, :],
                                                                       start=(kc == 0), stop=(kc == 7)), reads=[BwKV, Bx_t], writes=[PS[bk]])
                    op("act", lambda e, h=h, bk=bk: e.activation(out=KT[:, h, cols], in_=ps[bk][:], func=AF.Copy), reads=[PS[bk]], writes=[BKT])
                for tb in range(4):
                    bk = cnt % 8
                    cnt += 1
                    for kc in range(8):
                        op("pe", lambda e, tb=tb, kc=kc, bk=bk: e.matmul(ps[bk][:], x_t[:, kc, tb * 128:(tb + 1) * 128], wKV[:, kc, 512:1024],
                                                                         start=(kc == 0), stop=(kc == 7)), reads=[BwKV, Bx_t], writes=[PS[bk]])
                    op("dve", lambda e, tb=tb, bk=bk, t=t: e.tensor_copy(out=VA[:, 4 * t + tb, :, 0:128], in_=ps[bk][:].rearrange("p (h e) -> p h e", h=4)),
                       reads=[PS[bk]], writes=[BVA])
            fw.barrier()
        if "KT" in dumps:
            dump("KT", KT[:], [128, 4, S], BKT)
            dump("VA", VA[:], [128, 32, 4, 130], BVA)
            dump("MKT", MKT[:], [128, 4, 256], BMK)
            dump("MVA", MVA[:], [128, 2, 4, 130], BMK)
        if stop_after == "KV":
            fw.barrier()
            return nc, dump_d

        with ExitStack() as phF:
            farb = sb(phF, "farb", [128, 4], F32)
            lamv = sb(phF, "lamv", [128, 4, 64], F32)
            subw = sb(phF, "subw", [128, 128], F32)
            lng = sb(phF, "lng", [128, D], F32)
            lnb = sb(phF, "lnb", [128, D], F32)
            NBh = sb(phF, "NBh", [128, 4, 256], BF16)
            NBl = sb(phF, "NBl", [128, 4, 256], BF16)
            lsc = sb(phF, "lsc", [128, 8], F32)
            BC = Buf("constF")
            dsc = fw.dsem("constF")
            tmpF = ExitStack()
            tb_f = sb(tmpF, "tb_f", [128, 4, 256], F32)
            mk_f = sb(tmpF, "mk_f", [128, 4, 256], F32)
            for dst_, nm in ((tb_f, "tbias"), (mk_f, "maskc"), (farb, "farb"), (lamv, "lamv"), (subw, "subw"), (lng, "lng"), (lnb, "lnb")):
                dma("sp", dsc, dst_[:], I[nm], writes=[BC])
            RC, WC = [BC], [BC]
            for h in range(4):
                op("dve", lambda e, h=h: e.tensor_scalar(out=tb_f[:, h, :], in0=tb_f[:, h, :], scalar1=farb[:, h:h + 1], scalar2=None, op0=ALU.subtract), reads=RC, writes=WC)
            op("dve", lambda e: e.tensor_tensor(out=tb_f[:], in0=tb_f[:], in1=mk_f[:], op=ALU.add), reads=RC, writes=WC)
            op("dve", lambda e: e.tensor_copy(out=NBh[:], in_=tb_f[:]), reads=RC, writes=WC)
            op("dve", lambda e: e.tensor_copy(out=mk_f[:], in_=NBh[:]), reads=RC, writes=WC)
            op("dve", lambda e: e.tensor_tensor(out=mk_f[:], in0=tb_f[:], in1=mk_f[:], op=ALU.subtract), reads=RC, writes=WC)
            op("dve", lambda e: e.tensor_copy(out=NBl[:], in_=mk_f[:]), reads=RC, writes=WC)
            fw.barrier()
            tmpF.close()
            op("dve", lambda e: e.tensor_tensor(out=lamv[:, 0, :], in0=lamv[:, 0, :], in1=lamv[:, 1, :], op=ALU.mult), reads=RC, writes=WC)
            op("dve", lambda e: e.tensor_tensor(out=lamv[:, 2, :], in0=lamv[:, 2, :], in1=lamv[:, 3, :], op=ALU.mult), reads=RC, writes=WC)
            op("dve", lambda e: e.reduce_sum(out=lsc[:, 1:2], in_=lamv[:, 0, :], axis=AX.X), reads=RC, writes=WC)
            op("dve", lambda e: e.reduce_sum(out=lsc[:, 2:3], in_=lamv[:, 2, :], axis=AX.X), reads=RC, writes=WC)
            op("act", lambda e: e.activation(out=lsc[:, 3:5], in_=lsc[:, 1:3], func=AF.Exp), reads=RC, writes=WC)
            op("dve", lambda e: e.tensor_tensor(out=lsc[:, 0:1], in0=lsc[:, 4:5], in1=lsc[:, 3:4], op=ALU.subtract), reads=RC, writes=WC)
            op("dve", lambda e: e.tensor_scalar(out=lsc[:, 0:1], in0=lsc[:, 0:1], scalar1=-LAMBDA_INIT, scalar2=None, op0=ALU.add), reads=RC, writes=WC)
            op("dve", lambda e: e.tensor_scalar(out=subw[:], in0=subw[:], scalar1=1.0 - LAMBDA_INIT, scalar2=None, op0=ALU.mult), reads=RC, writes=WC)

            xq = sb(phF, "xq", [128, 8, 512], BF16)
            Bxq = Buf("xq")
            dsxq = fw.dsem("xq")
            NW = 4
            wt = [sb(phF, "wt%d" % i, [128, 8, 512], BF16) for i in range(NW)]
            Bwt = [Buf("wt%d" % i) for i in range(NW)]
            dswt = [fw.dsem("wt%d" % i) for i in range(NW)]
            wctr = [0]

            gseq = [(wInb[2], 8, Bw[2]), (wInb[5], 8, Bw[5]), (wInb[6], 8, Bw[6]), (wInb[7], 8, Bw[7])]
            for br_ in range(3):
                for half_ in range(2):
                    gseq.append((wInb[8 + 2 * br_ + half_], 8, Bw[8 + 2 * br_ + half_]))
                    gseq.append((wBb[br_, half_], 4, Bb[br_][half_]))
            gseq += [(wOb[0], 8, Bo[0]), (wOb[1], 8, Bo[1])]
            wseq = gseq * NT
            wstate = {"issued": 0, "consumed": 0}

            def _issue_w(k):
                piece, nkc, bsrc = wseq[k]
                i = k % NW
                dma("sp", dswt[i], wt[i][:, 0:nkc, :], wsrc(piece, nkc), reads=[bsrc], writes=[Bwt[i]])

            def next_w():
                k = wstate["consumed"]
                while wstate["issued"] < min(len(wseq), k + 3):
                    _issue_w(wstate["issued"])
                    wstate["issued"] += 1
                wstate["consumed"] += 1
                return wt[k % NW], Bwt[k % NW]

            QT = sb(phF, "QT", [128, 4, 512], BF16)
            MQT = QT
            scr = sb(phF, "scr", [128, 4096], F32)
            Bscr = Buf("scr")
            ZD = scr[:, 0:2048].rearrange("p (q f) -> p q f", q=4)
            ZM = ZD
            BQT = Buf("QT")
            BMQ, BZD, BZM = BQT, Bscr, Bscr
            NP = 4
            PT_all = sb(phF, "PT", [128, NP, 512], BF16)
            PT = [PT_all[:, i, :] for i in range(NP)]
            BPT = [Buf("PT%d" % i) for i in range(NP)]
            pctr = [0]
            YD = scr[:, 2048:3072].bitcast(BF16).rearrange("p (q f) -> p q f", q=4)
            YM = scr[:, 3072:4096].bitcast(BF16).rearrange("p (q f) -> p q f", q=4)
            YDT = sb(phF, "YDT", [128, 4, 512], BF16)
            YMT = sb(phF, "YMT", [128, 4, 512], BF16)
            BYD, BYM, BYDT, BYMT = Bscr, Bscr, Buf("YDT"), Buf("YMT")
            xres = YDT[:].rearrange("p h n -> p (h n)").bitcast(F32)
            Bxres = BYDT
            sm = sb(phF, "sm", [128, 32], F32)
            Bsm = Buf("sm")
            od = [None, sb(phF, "od1", [128, 128], F32)]
            Bod = Buf("od")
            od4 = sb(phF, "od4", [128, 4, 128], F32)
            OS = sb(phF, "OS", [128, 2, 520], F32)
            BOS = Buf("OS")
            ACC = scr[:].rearrange("p (q f) -> p q f", q=8)
            BACC = Bscr
            MT = sb(phF, "MT", [128, 8, 512], BF16)
            BMT = Buf("MT")
            sgf_all = sb(phF, "sgf", [128, 2, 512], F32)
            sgf = [sgf_all[:, 0, :], sgf_all[:, 1, :]]
            Bsgf = [Buf("sgf%d" % i) for i in range(2)]

            dsxr = fw.dsem("xres")
            dsout = [fw.dsem("out0"), fw.dsem("out1")]
            stats2 = [sb(phF, "stats%d" % i, [128, 2, 6], F32) for i in range(2)]
            mv2 = [sb(phF, "mv%d" % i, [128, 4], F32) for i in range(2)]
            Bst = [Buf("stats%d" % i) for i in range(2)]

            pctr_bank = [0]

            def gbank():
                b = (0, 1, 2, 3)[pctr_bank[0] % 4]
                pctr_bank[0] += 1
                return b

            OB = [[4, 5], [6, 7]]

            def oreg(c, ql):
                if ql < 3:
                    return ps[OB[c][0]][:, ql * 130:ql * 130 + 130], OB[c][0]
                return ps[OB[c][1]][:, 0:130], OB[c][1]

            def emit_qproj():
                wq, Bwq = next_w()
                for h in range(4):
                    bk = gbank()
                    for kc in range(8):
                        op("pe", lambda e, h=h, kc=kc, bk=bk: e.matmul(ps[bk][:], wq[:, kc, h * 128:(h + 1) * 128], xq[:, kc, :],
                                                                       start=(kc == 0), stop=(kc == 7)), reads=[Bwq, Bxq], writes=[PS[bk]])
                    op("act", lambda e, h=h, bk=bk: e.activation(out=QT[:, h, :], in_=ps[bk][:], func=AF.Copy, scale=0.125), reads=[PS[bk]], writes=[BQT])
                wz, Bwz = next_w()
                for ql in range(4):
                    bk = gbank()
                    for kc in range(8):
                        op("pe", lambda e, ql=ql, kc=kc, bk=bk: e.matmul(ps[bk][:], xq[:, kc, ql * 128:(ql + 1) * 128], wz[:, kc, :],
                                                                         start=(kc == 0), stop=(kc == 7)), reads=[Bwz, Bxq], writes=[PS[bk]])
                    op("act", lambda e, ql=ql, bk=bk: e.activation(out=ZD[:, ql, :], in_=ps[bk][:], func=AF.Silu), reads=[PS[bk]], writes=[BZD])

            pend_ln = []
            for G in range(NT):
                if G == 0:
                    dma("sp", dsxq, xq[:], wsrc(xTb[G], 8), reads=[Bx[G]], writes=[Bxq])
                if G == 0:
                    emit_qproj()
                if stop_after == "F1":
                    dump("QT", QT[:], [128, 4, 512], BQT)
                    dump("ZD", scr[:, 0:2048], [128, 2048], Bscr)
                    fw.barrier()
                    return nc, dump_d
                nkb = 4 * G + 4
                steps = [(h, kb) for h in range(4) for kb in range(nkb)]
                SBANKS = ((0, 1), (2, 3))

                def emit_score(i):
                    h, kb = steps[i]
                    pair = i % 2
                    ql0 = max(0, kb - 4 * G)
                    cs_ = slice(ql0 * 128, 512)
                    nq = [ql for ql in range(4) if 4 * G + ql in (kb, kb + 1)]
                    for c in range(2):
                        rows = slice(64 * c, 64 * c + 64)
                        bk = SBANKS[pair][c]
                        op("pe", lambda e, h=h, kb=kb, bk=bk, cs_=cs_, rows=rows, nq=nq: e.matmul(
                            ps[bk][:, cs_], KT[rows, h, kb * 128:(kb + 1) * 128], QT[rows, h, cs_],
                            start=True, stop=(len(nq) == 0)), reads=[BKT, BQT], writes=[PS[bk]])
                    if nq:
                        qa = nq[0] * 128
                        qrel0 = (4 * G + nq[0] - kb) * 128
                        w_ = 128 * len(nq)
                        for c in range(2):
                            bk = SBANKS[pair][c]
                            for hl, NBx in enumerate((NBh, NBl)):
                                op("pe", lambda e, h=h, bk=bk, qa=qa, qrel0=qrel0, w_=w_, NBx=NBx, hl=hl: e.matmul(
                                    ps[bk][:, qa:qa + w_], ident[:], NBx[:, h, qrel0:qrel0 + w_],
                                    start=False, stop=(hl == 1)), reads=[Bid, BC], writes=[PS[bk]])
                    src2 = (psA if pair == 0 else psB)[:].rearrange("p (c n) -> p c n", c=2)[:, :, cs_]
                    dst2 = PT_all[:, 2 * pair:2 * pair + 2, cs_]
                    b0_, b1_ = SBANKS[pair]
                    op("act", lambda e, h=h, src2=src2, dst2=dst2: e.activation(
                        out=dst2, in_=src2, func=AF.Exp, bias=farb[:, h:h + 1], scale=1.0),
                       reads=[PS[b0_], PS[b1_], BC], writes=[BPT[2 * pair], BPT[2 * pair + 1]])

                def emit_pv(i):
                    h, kb = steps[i]
                    pair = i % 2
                    ql0 = max(0, kb - 4 * G)
                    for c in range(2):
                        pi = 2 * pair + c
                        for ql in range(ql0, 4):
                            oap, obk = oreg(c, ql)
                            st_ = (kb == 0 and ql in (0, 3))
                            last = (kb == 4 * G + ql)
                            op("pe", lambda e, h=h, kb=kb, ql=ql, pi=pi, oap=oap, st_=st_, last=last: e.matmul(
                                oap, PT[pi][:, ql * 128:(ql + 1) * 128], VA[:, kb, h, 0:130], start=st_, stop=last, skip_group_check=True),
                               reads=[BPT[pi], BVA], writes=[PS[obk]])
                    if kb == nkb - 1:
                        for pb in list(pend_b):
                            emit_norm_b(pb[0])
                            pend_b.remove(pb)
                        emit_norm_a(h)
                        pend_b.append([h, 10])

                def emit_norm_a(h):
                    for c in range(2):
                        op("dve", lambda e, c=c: e.tensor_copy(out=OS[:, c, 0:390], in_=ps[OB[c][0]][:, 0:390]), reads=[PS[OB[c][0]]], writes=[BOS])
                        op("dve", lambda e, c=c: e.tensor_copy(out=OS[:, c, 390:520], in_=ps[OB[c][1]][:, 0:130]), reads=[PS[OB[c][1]]], writes=[BOS])
                    R_, W_ = [BOS, Bsm, Bod, BC, BZD], [Bsm, Bod]
                    dens = OS[:].rearrange("p c (q e) -> p c q e", q=4)[:, :, :, 128]
                    op("dve", lambda e: e.reciprocal(out=sm[:, 0:8].rearrange("p (c q) -> p c q", c=2), in_=dens), reads=R_, writes=W_)
                    op("dve", lambda e: e.tensor_scalar(out=sm[:, 4:8], in0=sm[:, 4:8], scalar1=lsc[:, 0:1], scalar2=None, op0=ALU.mult), reads=R_, writes=W_)
                    for ql in range(4):
                        o0 = OS[:, 0, ql * 130:ql * 130 + 128]
                        o1 = OS[:, 1, ql * 130:ql * 130 + 128]
                        op("dve", lambda e, o0=o0, ql=ql: e.tensor_scalar(out=od4[:, ql, :], in0=o0, scalar1=sm[:, ql:ql + 1], scalar2=None, op0=ALU.mult), reads=R_, writes=W_)
                        op("dve", lambda e, o1=o1, ql=ql: e.scalar_tensor_tensor(out=od4[:, ql, :], in0=o1, scalar=sm[:, 4 + ql:5 + ql], in1=od4[:, ql, :], op0=ALU.mult, op1=ALU.add), reads=R_, writes=W_)
                        op("dve", lambda e, ql=ql: e.scalar_tensor_tensor(out=od[1][:], in0=od4[:, ql, :], scalar=1.0, in1=od4[:, ql, :], op0=ALU.mult, op1=ALU.mult,
                                                                         accum_out=sm[:, 8 + ql:9 + ql]), reads=R_, writes=W_)
                    op("dve", lambda e: e.tensor_scalar(out=sm[:, 12:16], in0=sm[:, 8:12], scalar1=1.0 / 128.0, scalar2=1e-5, op0=ALU.mult, op1=ALU.add), reads=R_, writes=W_)

                def emit_norm_b(h):
                    R_, W_ = [BOS, Bsm, Bod, BC, BZD], [Bsm, Bod]
                    op("act", lambda e: e.activation(out=sm[:, 16:20], in_=sm[:, 12:16], func=AF.Ln), reads=R_, writes=W_)
                    op("act", lambda e: e.activation(out=sm[:, 20:24], in_=sm[:, 16:20], func=AF.Exp, scale=-0.5), reads=R_, writes=W_)
                    for ql in range(4):
                        op("dve", lambda e, ql=ql: e.scalar_tensor_tensor(out=od4[:, ql, :], in0=od4[:, ql, :], scalar=sm[:, 20 + ql:21 + ql], in1=subw[:], op0=ALU.mult, op1=ALU.mult), reads=R_, writes=W_)
                    op("dve", lambda e, h=h: e.tensor_tensor(out=YD[:, :, h * 128:(h + 1) * 128], in0=od4[:], in1=ZD[:, :, h * 128:(h + 1) * 128], op=ALU.mult),
                       reads=R_ + [BYD], writes=[BYD, Bod])

                pend_b = []
                emit_score(0)
                for i in range(len(steps)):
                    if i + 1 < len(steps):
                        emit_score(i + 1)
                    emit_pv(i)
                    if pend_ln and (i == 10 or i == len(steps) - 1):
                        for f_ in pend_ln:
                            f_()
                        del pend_ln[:]
                    for pb in list(pend_b):
                        pb[1] -= 1
                        if pb[1] <= 0 or i == len(steps) - 1:
                            emit_norm_b(pb[0])
                            pend_b.remove(pb)

                if stop_after == "F2":
                    dump("YD", scr[:, 2048:3072], [128, 1024], Bscr)
                    dump("QT", QT[:], [128, 4, 512], BQT)
                    dump("ZD", scr[:, 0:2048], [128, 2048], Bscr)
                    op("dve", lambda e: e.tensor_copy(out=sgf[0][:], in_=ps[4][:]), reads=[PS[4]], writes=[Bsgf[0]])
                    op("dve", lambda e: e.tensor_copy(out=sgf[1][:], in_=ps[6][:]), reads=[PS[6]], writes=[Bsgf[1]])
                    dump("O0", sgf[0][:], [128, 512], Bsgf[0])
                    dump("O1", sgf[1][:], [128, 512], Bsgf[1])
                    dump("PT", PT[0], [128, 512], BPT[0])
                    dump("sm", sm[:], [128, 32], Bsm)
                    dump("lsc", lsc[:], [128, 8], BC)
                    fw.barrier()
                    return nc, dump_d
                wmq, Bwmq = next_w()
                for h in range(4):
                    bk = gbank()
                    for kc in range(8):
                        op("pe", lambda e, h=h, kc=kc, bk=bk: e.matmul(ps[bk][:], wmq[:, kc, h * 128:(h + 1) * 128], xq[:, kc, :],
                                                                       start=(kc == 0), stop=(kc == 7)), reads=[Bwmq, Bxq], writes=[PS[bk]])
                    op("act", lambda e, h=h, bk=bk: e.activation(out=MQT[:, h, :], in_=ps[bk][:], func=AF.Copy, scale=128.0 ** -0.5), reads=[PS[bk]], writes=[BMQ])
                wzm, Bwzm = next_w()
                for ql in range(4):
                    bk = gbank()
                    for kc in range(8):
                        op("pe", lambda e, ql=ql, kc=kc, bk=bk: e.matmul(ps[bk][:], xq[:, kc, ql * 128:(ql + 1) * 128], wzm[:, kc, :],
                                                                         start=(kc == 0), stop=(kc == 7)), reads=[Bwzm, Bxq], writes=[PS[bk]])
                    op("act", lambda e, ql=ql, bk=bk: e.activation(out=ZM[:, ql, :], in_=ps[bk][:], func=AF.Silu), reads=[PS[bk]], writes=[BZM])

                for h in range(4):
                    for mb in range(2):
                        bk = gbank()
                        op("pe", lambda e, h=h, mb=mb, bk=bk: e.matmul(ps[bk][:], MKT[:, h, mb * 128:(mb + 1) * 128], MQT[:, h, :], start=True, stop=True),
                           reads=[BMK, BMQ], writes=[PS[bk]])
                        pi = pctr[0] % NP
                        pctr[0] += 1
                        op("act", lambda e, bk=bk, pi=pi: e.activation(out=PT[pi][:], in_=ps[bk][:], func=AF.Exp), reads=[PS[bk]], writes=[BPT[pi]])
                        for ql in range(4):
                            oap, obk = oreg(0, ql)
                            op("pe", lambda e, h=h, mb=mb, ql=ql, pi=pi, oap=oap: e.matmul(
                                oap, PT[pi][:, ql * 128:(ql + 1) * 128], MVA[:, mb, h, 0:130], start=(mb == 0 and ql in (0, 3)), stop=(mb == 1), skip_group_check=True),
                               reads=[BPT[pi], BMK], writes=[PS[obk]])
                    for ql in range(4):
                        o0, b0 = oreg(0, ql)
                        R_, W_ = [PS[b0], Bsm, BZM], [Bsm]
                        op("dve", lambda e, o0=o0: e.reciprocal(out=sm[:, 8:9], in_=o0[:, 128:129]), reads=R_, writes=W_)
                        op("dve", lambda e, o0=o0, ql=ql, h=h: e.scalar_tensor_tensor(
                            out=YM[:, ql, h * 128:(h + 1) * 128], in0=o0[:, 0:128], scalar=sm[:, 8:9], in1=ZM[:, ql, h * 128:(h + 1) * 128], op0=ALU.mult, op1=ALU.mult),
                           reads=R_ + [BYM], writes=[BYM, Bsm])

                if stop_after == "F3":
                    dump("YM", scr[:, 3072:4096], [128, 1024], Bscr)
                    fw.barrier()
                    return nc, dump_d
                for (Ysrc, Bs_, Ydst, Bd_) in ((YD, BYD, YDT, BYDT), (YM, BYM, YMT, BYMT)):
                    for kc in range(4):
                        bk = gbank()
                        pst = ps[bk][:].bitcast(BF16)
                        for ql in range(4):
                            op("pe", lambda e, kc=kc, ql=ql, pst=pst, Ysrc=Ysrc: e.transpose(
                                pst[:, ql * 128:(ql + 1) * 128], Ysrc[:, ql, kc * 128:(kc + 1) * 128], ident[:]),
                               reads=[Bs_, Bid], writes=[PS[bk]])
                        op("dve", lambda e, kc=kc, pst=pst, Ydst=Ydst: e.tensor_copy(out=Ydst[:, kc, :], in_=pst[:, 0:512]), reads=[PS[bk]], writes=[Bd_])

                if stop_after == "F4":
                    dump("YDT", YDT[:], [128, 4, 512], BYDT)
                    dump("YMT", YMT[:], [128, 4, 512], BYMT)
                    fw.barrier()
                    return nc, dump_d
                for br in range(3):
                    if br == 0:
                        ysrc = lambda kc, G=G: ZS[:, kc, G * 512:(G + 1) * 512]
                        By = BZSn
                    elif br == 1:
                        ysrc = lambda kc: YDT[:, kc, :]
                        By = BYDT
                    else:
                        ysrc = lambda kc: YMT[:, kc, :]
                        By = BYMT
                    for half in range(2):
                        wg_, Bwg = next_w()
                        wb_, Bwb = next_w()
                        for fl in range(4):
                            ft = half * 4 + fl
                            bb_, bg_ = gbank(), gbank()
                            for kc in range(4):
                                op("pe", lambda e, kc=kc, fl=fl, bb_=bb_, ysrc=ysrc, wb_=wb_: e.matmul(
                                    ps[bb_][:], wb_[:, kc, fl * 128:(fl + 1) * 128], ysrc(kc), start=(kc == 0), stop=(kc == 3)),
                                   reads=[Bwb, By], writes=[PS[bb_]])
                            for kc in range(8):
                                op("pe", lambda e, kc=kc, fl=fl, bg_=bg_, wg_=wg_: e.matmul(
                                    ps[bg_][:], wg_[:, kc, fl * 128:(fl + 1) * 128], xq[:, kc, :], start=(kc == 0), stop=(kc == 7)),
                                   reads=[Bwg, Bxq], writes=[PS[bg_]])
                            s_ = ft % 2
                            op("act", lambda e, bg_=bg_, s_=s_: e.activation(out=sgf[s_][:], in_=ps[bg_][:], func=AF.Sigmoid), reads=[PS[bg_]], writes=[Bsgf[s_]])
                            if br == 0:
                                op("dve", lambda e, bb_=bb_, s_=s_, ft=ft: e.tensor_tensor(out=ACC[:, ft, :], in0=sgf[s_][:], in1=ps[bb_][:], op=ALU.mult),
                                   reads=[PS[bb_], Bsgf[s_]], writes=[BACC])
                            else:
                                op("dve", lambda e, bb_=bb_, s_=s_: e.tensor_tensor(out=sgf[s_][:], in0=sgf[s_][:], in1=ps[bb_][:], op=ALU.mult),
                                   reads=[PS[bb_], Bsgf[s_]], writes=[Bsgf[s_]])
                                if br == 1:
                                    op("dve", lambda e, s_=s_, ft=ft: e.tensor_tensor(out=ACC[:, ft, :], in0=ACC[:, ft, :], in1=sgf[s_][:], op=ALU.add),
                                       reads=[Bsgf[s_], BACC], writes=[BACC])
                                else:
                                    op("dve", lambda e, s_=s_, ft=ft: e.tensor_tensor(out=MT[:, ft, :], in0=ACC[:, ft, :], in1=sgf[s_][:], op=ALU.add),
                                       reads=[Bsgf[s_], BACC], writes=[BMT])

                if stop_after == "F5":
                    dump("MT", MT[:], [128, 8, 512], BMT)
                    fw.barrier()
                    return nc, dump_d
                if G + 1 < NT:
                    dma("sp", dsxq, xq[:], wsrc(xTb[G + 1], 8), reads=[Bx[G + 1]], writes=[Bxq])
                wo = [next_w() for j in range(2)]
                HH = [sgf_all[:].rearrange("p a n -> p (a n)"), YMT[:].rearrange("p h n -> p (h n)").bitcast(F32)]
                HHB = [[Bsgf[0], Bsgf[1]], [BYMT]]
                OBK = (0, 1, 2, 3, 4, 5, 6, 7)

                def ln_head(ql):
                    r0 = G * 512 + ql * 128
                    Hc, HB = HH[ql % 2], HHB[ql % 2]
                    dma("sp", dsxr, xres[:], I["x"][r0:r0 + 128, :], writes=[Bxres])
                    for half in range(2):
                        bk = OBK[2 * ql + half]
                        wo_, Bwo = wo[half]
                        for kc in range(8):
                            op("pe", lambda e, kc=kc, ql=ql, bk=bk, wo_=wo_: e.matmul(
                                ps[bk][:], MT[:, kc, ql * 128:(ql + 1) * 128], wo_[:, kc, :], start=(kc == 0), stop=(kc == 7)),
                               reads=[BMT, Bwo], writes=[PS[bk]])
                        hs = slice(half * 512, (half + 1) * 512)
                        op("dve", lambda e, bk=bk, hs=hs, Hc=Hc: e.scalar_tensor_tensor(out=Hc[:, hs], in0=xres[:, hs], scalar=ALPHA, in1=ps[bk][:], op0=ALU.mult, op1=ALU.add),
                           reads=[PS[bk], Bxres] + HB, writes=HB)
                    R_, W_ = HB + [Bst[ql % 2], BC], [Bst[ql % 2]] + HB
                    st_, mv_ = stats2[ql % 2], mv2[ql % 2]
                    for half in range(2):
                        op("dve", lambda e, half=half, Hc=Hc: e.bn_stats(out=st_[:, half, :], in_=Hc[:, half * 512:(half + 1) * 512]), reads=R_, writes=W_)
                    op("dve", lambda e: e.bn_aggr(out=mv_[:, 0:2], in_=st_[:].rearrange("p a b -> p (a b)")), reads=R_, writes=W_)
                    op("dve", lambda e: e.tensor_scalar(out=mv_[:, 2:3], in0=mv_[:, 1:2], scalar1=1e-5, scalar2=None, op0=ALU.add), reads=R_, writes=W_)

                def ln_tail(ql, G=G, HH=HH, HHB=HHB):
                    r0 = G * 512 + ql * 128
                    Hc, HB = HH[ql % 2], HHB[ql % 2]
                    R_, W_ = HB + [Bst[ql % 2], BC], [Bst[ql % 2]] + HB
                    mv_ = mv2[ql % 2]
                    op("act", lambda e: e.activation(out=mv_[:, 2:3], in_=mv_[:, 2:3], func=AF.Ln), reads=R_, writes=W_)
                    op("act", lambda e: e.activation(out=mv_[:, 3:4], in_=mv_[:, 2:3], func=AF.Exp, scale=-0.5), reads=R_, writes=W_)
                    op("dve", lambda e, Hc=Hc: e.tensor_scalar(out=Hc, in0=Hc, scalar1=mv_[:, 0:1], scalar2=mv_[:, 3:4], op0=ALU.subtract, op1=ALU.mult), reads=R_, writes=W_)
                    op("pool", lambda e, Hc=Hc: e.tensor_tensor(out=Hc, in0=Hc, in1=lng[:], op=ALU.mult), reads=HB + [BC], writes=HB)
                    op("pool", lambda e, Hc=Hc: e.tensor_tensor(out=Hc, in0=Hc, in1=lnb[:], op=ALU.add), reads=HB + [BC], writes=HB)
                    dma("pool", dsout[ql % 2], out_d[r0:r0 + 128, :], Hc, reads=HB)

                ln_head(0)
                ln_tail(0)
                ln_head(1)
                ln_tail(1)
                ln_head(2)
                ln_head(3)
                if G + 1 < NT:
                    emit_qproj()
                    pend_ln.extend([lambda f=ln_tail: f(2), lambda f=ln_tail: f(3)])
                else:
                    ln_tail(2)
                    ln_tail(3)
                if stop_after == "F6":
                    fw.barrier()
                    return nc, dump_d
            fw.barrier()
        fw.barrier()
    return nc, dump_d


_NC_CACHE = {}


def kernel(**inputs):
    inp = {k: np.asarray(v) for k, v in inputs.items()}
    shared = _shared_layouts(inp)
    if "nc" not in _NC_CACHE:
        _NC_CACHE["nc"] = build()[0]
    nc = _NC_CACHE["nc"]
    in_maps = []
    for b in range(8):
        m = dict(shared)
        m["x"] = np.ascontiguousarray(inp["x"][b])
        m["xT"] = np.ascontiguousarray(inp["x"][b].T)
        m["memT"] = np.ascontiguousarray(inp["mem"][b].T)
        in_maps.append(m)
    res = run_bass_kernel_spmd(nc, in_maps, core_ids=list(range(8)))
    return np.stack([np.asarray(r["out"]) for r in res.results]).astype(np.float32)
```

```python
import math
from contextlib import ExitStack
import numpy as np
import concourse.bass as bass
import concourse.mybir as mybir
from concourse.bass_utils import run_bass_kernel_spmd

F32 = mybir.dt.float32
BF16 = mybir.dt.bfloat16
I32 = mybir.dt.int32
AF = mybir.ActivationFunctionType
ALU = mybir.AluOpType
AX = mybir.AxisListType

S = 4096
D = 1024
NT = 8
L = 16
NB = S // L
ALPHA = 2.0 ** 0.25
LAMBDA_INIT = 0.8 - 0.6 * math.exp(0.0)
PI = float(np.pi)


class Buf:
    __slots__ = ("name", "w", "r")

    def __init__(self, name):
        self.name = name
        self.w = None
        self.r = {}


class Eng:
    def __init__(self, name, eng, sem):
        self.name, self.eng, self.sem = name, eng, sem
        self.cnt = 0
        self.seen = {}


class FW:
    def __init__(self, nc, stack):
        self.nc, self.stack = nc, stack
        self.sems, self.engs, self.cnts = {}, {}, {}
        for name, eng in (("pe", nc.tensor), ("act", nc.scalar), ("dve", nc.vector),
                          ("pool", nc.gpsimd), ("sp", nc.sync)):
            sem = stack.enter_context(nc.semaphore("sem_" + name))
            self.sems[name] = sem
            self.engs[name] = Eng(name, eng, sem)
            self.cnts[name] = 0

    def dsem(self, name):
        self.sems[name] = self.stack.enter_context(self.nc.semaphore("d_" + name))
        self.cnts[name] = 0
        return name

    def _wait(self, e, key, val):
        if key == "pe" and e.name == "pe":
            return
        if e.seen.get(key, 0) >= val:
            return
        e.seen[key] = val
        e.eng.wait_ge(self.sems[key], val)

    def _deps(self, e, reads, writes):
        for b in reads:
            if b.w is not None:
                self._wait(e, *b.w)
        for b in writes:
            if b.w is not None:
                self._wait(e, *b.w)
            for k, v in b.r.items():
                self._wait(e, k, v)

    def _mark(self, tag, reads, writes):
        for b in reads:
            if b.r.get(tag[0], 0) < tag[1]:
                b.r[tag[0]] = tag[1]
        for b in writes:
            b.w = tag
            b.r = {}

    def op(self, en, fn, reads=(), writes=()):
        e = self.engs[en]
        self._deps(e, reads, writes)
        ins = fn(e.eng)
        self.cnts[en] += 1
        ins.then_inc(e.sem, 1)
        self._mark((en, self.cnts[en]), reads, writes)

    def dma(self, q, dsem, out, in_, reads=(), writes=()):
        e = self.engs[q]
        self._deps(e, reads, writes)
        ins = e.eng.dma_start(out=out, in_=in_)
        self.cnts[dsem] += 16
        ins.then_inc(self.sems[dsem], 16)
        self._mark((dsem, self.cnts[dsem]), reads, writes)

    def barrier(self, only=None, skip=()):
        for e in self.engs.values():
            if only is not None and e.name not in only:
                continue
            for k, v in self.cnts.items():
                if v > 0 and not k.startswith("cv_") and k not in skip:
                    self._wait(e, k, v)


def _t5_bucket_np(rel):
    ret = np.where(rel > 0, 16, 0)
    n = np.abs(rel)
    v = (np.log(np.maximum(n, 1).astype(np.float32) / np.float32(8)) / np.float32(math.log(16))
         * np.float32(8))
    large = np.minimum(8 + v.astype(np.int32), 15)
    return ret + np.where(n < 8, n, large)


def _shared_layouts(inp):
    f = np.float32
    o = {}
    lam_re, lam_im, log_dt = inp["lam_re"][0], inp["lam_im"][0], inp["log_dt"][0]
    b_re, b_im, c_re, c_im = inp["b_re"][0], inp["b_im"][0], inp["c_re"][0], inp["c_im"][0]
    def layA(v):
        return np.ascontiguousarray(v.reshape(16, 2, 64).transpose(1, 2, 0).reshape(128, 16)).astype(f)
    o["lamA"] = np.stack([layA(lam_re), layA(lam_im),
                          layA(np.broadcast_to(log_dt[:, None], (32, 64)))], axis=1)
    cbd = np.zeros((2, 2, 64, 16, 2, 16), f)
    bbd = np.zeros((2, 2, 64, 16, 2, 16), f)
    for ri, (cc, bb) in enumerate(((c_re, b_re), (c_im, b_im))):
        c4 = cc.reshape(16, 2, 16, 64)
        b4 = bb.reshape(16, 2, 64, 16)
        for g2 in range(2):
            cbd[ri, g2, :, :, g2, :] = c4[:, g2].transpose(2, 0, 1)
            bbd[ri, g2, :, :, g2, :] = b4[:, g2].transpose(1, 0, 2)
    o["cbdA"] = np.ascontiguousarray(cbd.reshape(2, 128, 16, 32).transpose(1, 0, 2, 3))
    o["bbdA"] = np.ascontiguousarray(bbd.reshape(2, 128, 16, 32).transpose(1, 0, 2, 3))
    def layB(v):
        a = v.reshape(4, 4, 2, 64).transpose(1, 0, 2, 3)
        a = np.broadcast_to(a[:, None], (4, 32, 4, 2, 64))
        return np.ascontiguousarray(a.reshape(128, 4, 128)).astype(f)
    o["lamB"] = np.stack([layB(lam_re), layB(lam_im),
                          layB(np.broadcast_to(log_dt[:, None], (32, 64)))], axis=1)
    btb = np.zeros((2, 4, 2, 16, 4, 2, 64), f)
    for ri, bb in enumerate((b_re, b_im)):
        b5 = bb.reshape(4, 4, 2, 64, 16)
        for g2 in range(2):
            btb[ri, :, g2, :, :, g2, :] = b5[:, :, g2].transpose(1, 3, 0, 2)
    o["btB"] = np.ascontiguousarray(btb.reshape(2, 128, 4, 128).transpose(1, 0, 2, 3))
    dsk = np.zeros((128, 4, 128), f)
    ds = inp["d_skip"][0].reshape(4, 128)
    for t in range(4):
        dsk[np.arange(128), t, np.arange(128)] = ds[t]
    o["dskD"] = dsk
    kl = np.arange(128)[:, None]
    qr = np.arange(256)[None, :]
    bidx = _t5_bucket_np(kl - qr)
    rb = inp["rel_bias"]
    o["tbias"] = np.ascontiguousarray(rb[bidx].transpose(0, 2, 1)).astype(f)
    maskc = np.where((kl >= 64) & (qr < 64), -30000.0, 0.0).astype(f)
    o["maskc"] = np.ascontiguousarray(np.broadcast_to(maskc[:, None, :], (128, 4, 256)))
    o["farb"] = np.ascontiguousarray(np.broadcast_to(rb[15][None, :], (128, 4))).astype(f)
    o["lamv"] = np.ascontiguousarray(np.broadcast_to(
        np.stack([inp["lambda_q1"][0], inp["lambda_k1"][0], inp["lambda_q2"][0], inp["lambda_k2"][0]])[None],
        (128, 4, 64))).astype(f)
    o["subw"] = np.ascontiguousarray(np.broadcast_to(inp["subln_w"][0][None], (128, 128))).astype(f)
    o["lng"] = np.ascontiguousarray(np.broadcast_to(inp["ln_g"][0][None], (128, D))).astype(f)
    o["lnb"] = np.ascontiguousarray(np.broadcast_to(inp["ln_b"][0][None], (128, D))).astype(f)
    o["ident"] = np.eye(128, dtype=f)
    o["w_in"] = np.ascontiguousarray(inp["w_in"][0])
    o["w_glu"] = np.ascontiguousarray(inp["w_glu"][0])
    o["w_mem_kv"] = np.ascontiguousarray(inp["w_mem_kv"][0])
    o["w_br"] = np.ascontiguousarray(np.stack([inp["w_br_ssm"][0], inp["w_br_diff"][0], inp["w_br_mem"][0]]))
    o["w_out"] = np.ascontiguousarray(inp["w_out"][0])
    return o


IN_SHAPES = {
    "xT": [D, S], "x": [S, D], "memT": [D, 256],
    "lamA": [128, 3, 16], "cbdA": [128, 2, 16, 32], "bbdA": [128, 2, 16, 32],
    "lamB": [128, 3, 4, 128], "btB": [128, 2, 4, 128], "dskD": [128, 4, 128],
    "tbias": [128, 4, 256], "maskc": [128, 4, 256], "farb": [128, 4], "lamv": [128, 4, 64],
    "subw": [128, 128], "lng": [128, D], "lnb": [128, D], "ident": [128, 128],
    "w_in": [D, 7168], "w_glu": [512, 1024], "w_mem_kv": [D, 1024], "w_br": [3, 512, 1024],
    "w_out": [D, D],
}


def build(stop_after=None, dumps=()):
    nc = bass.Bass("TRN2", target_bir_lowering=False)
    I = {k: nc.dram_tensor(k, v, F32, kind="ExternalInput").ap() for k, v in IN_SHAPES.items()}
    out_d = nc.dram_tensor("out", [S, D], F32, kind="ExternalOutput").ap()
    dump_d = {}
    xTb = nc.dram_tensor("xTb", [NT, D, 512], BF16).ap()
    wInb = nc.dram_tensor("wInb", [14, D, 512], BF16).ap()
    wGb = nc.dram_tensor("wGb", [2, 512, 512], BF16).ap()
    wMb = nc.dram_tensor("wMb", [2, D, 512], BF16).ap()
    wBb = nc.dram_tensor("wBb", [3, 2, 512, 512], BF16).ap()
    wOb = nc.dram_tensor("wOb", [2, D, 512], BF16).ap()
    memTb = nc.dram_tensor("memTb", [D, 256], BF16).ap()

    with ExitStack() as top:
        fw = FW(nc, top)
        op, dma = fw.op, fw.dma

        def sb(st, name, shape, dt):
            return st.enter_context(nc.sbuf_tensor("s_" + name, list(shape), dt))

        def dump(name, ap_sb, shape, buf):
            d = nc.dram_tensor("dump_" + name, list(shape), ap_sb.dtype, kind="ExternalOutput").ap()
            dump_d[name] = d
            dma("sp", fw.dsem("dump_" + name), d, ap_sb, reads=[buf])

        psA = top.enter_context(nc.psum_tensor("psA", [128, 1024], F32))
        psB = top.enter_context(nc.psum_tensor("psB", [128, 1024], F32))
        ps = [psA[:, 0:512], psA[:, 512:1024], psB[:, 0:512], psB[:, 512:1024]] + \
             [top.enter_context(nc.psum_tensor("ps%d" % i, [128, 512], F32)) for i in range(4, 8)]
        PS = [Buf("ps%d" % i) for i in range(8)]

        Bx = [Buf("xTb%d" % t) for t in range(NT)]
        Bw = [Buf("wInb%d" % j) for j in range(14)]
        Bg = [Buf("wGb%d" % j) for j in range(2)]
        Bm = [Buf("wMb%d" % j) for j in range(2)]
        Bb = [[Buf("wBb%d%d" % (b, j)) for j in range(2)] for b in range(3)]
        Bo = [Buf("wOb%d" % j) for j in range(2)]
        Bmem = Buf("memTb")

        def conv(dst, src, buf):
            dma("pool", fw.dsem("cv_" + buf.name), dst, src, writes=[buf])

        conv(wInb[0], I["w_in"][:, 0:512], Bw[0])
        conv(wInb[1], I["w_in"][:, 512:1024], Bw[1])
        for t in range(NT):
            conv(xTb[t], I["xT"][:, t * 512:(t + 1) * 512], Bx[t])
        for j in range(2):
            conv(wGb[j], I["w_glu"][:, j * 512:(j + 1) * 512], Bg[j])
        for j in (3, 4):
            conv(wInb[j], I["w_in"][:, j * 512:(j + 1) * 512], Bw[j])
        for j in range(2):
            conv(wMb[j], I["w_mem_kv"][:, j * 512:(j + 1) * 512], Bm[j])
        conv(memTb, I["memT"], Bmem)
        for j in (2, 5, 6, 7, 8, 9, 10, 11, 12, 13):
            conv(wInb[j], I["w_in"][:, j * 512:(j + 1) * 512], Bw[j])
        for b in range(3):
            for j in range(2):
                conv(wBb[b, j], I["w_br"][b, :, j * 512:(j + 1) * 512], Bb[b][j])
        for j in range(2):
            conv(wOb[j], I["w_out"][:, j * 512:(j + 1) * 512], Bo[j])

        def wsrc(piece, nkc):
            return piece.rearrange("(kc p) n -> p kc n", p=128)

        ident_f = sb(top, "ident_f", [128, 128], F32)
        ident = sb(top, "ident", [128, 128], BF16)
        Bid = Buf("ident")
        dma("sp", fw.dsem("c_id"), ident_f[:], I["ident"], writes=[Bid])
        op("dve", lambda e: e.tensor_copy(out=ident[:], in_=ident_f[:]), reads=[Bid], writes=[Bid])

        ZS = sb(top, "ZS", [128, 4, S], BF16)
        BZSn = Buf("ZSn")

        with ExitStack() as phS:
            UT = sb(phS, "UT", [128, 4, S], BF16)
            BUTp = [[Buf("UT%d_%d" % (i, j)) for j in range(8)] for i in range(4)]
            BUTall = [b for bl in BUTp for b in bl]
            ZSp = sb(phS, "ZSp", [128, 4, S], BF16)
            BZSp = Buf("ZSp")

            with ExitStack() as ph2:
                def cplx_setup(st, pref, lam_ap, shape, Bsrc, keep):
                    T = {}
                    for nm in ("dt", "th", "mg", "cs", "sn", "t1", "t2", "den"):
                        T[nm] = sb(st, pref + nm, shape, F32)
                    T.update(keep)
                    ti = sb(st, pref + "ti", shape, I32)
                    Bt = keep["buf"]
                    R, W = [Bsrc, Bt], [Bt]
                    lre, lim, ldt = lam_ap(0), lam_ap(1), lam_ap(2)
                    op("act", lambda e: e.activation(out=T["dt"][:], in_=ldt, func=AF.Exp), reads=R, writes=W)
                    op("dve", lambda e: e.tensor_tensor(out=T["th"][:], in0=lim, in1=T["dt"][:], op=ALU.mult), reads=R, writes=W)
                    op("dve", lambda e: e.tensor_tensor(out=T["t0"][:], in0=lre, in1=T["dt"][:], op=ALU.mult), reads=R, writes=W)
                    op("act", lambda e: e.activation(out=T["mg"][:], in_=T["t0"][:], func=AF.Exp), reads=R, writes=W)

                    def sin_of(dst, shift):
                        op("dve", lambda e: e.tensor_scalar(out=T["t0"][:], in0=T["th"][:], scalar1=shift, scalar2=None, op0=ALU.add), reads=R, writes=W)
                        op("dve", lambda e: e.tensor_scalar(out=T["t1"][:], in0=T["t0"][:], scalar1=1.0 / (2 * PI), scalar2=None, op0=ALU.mult), reads=R, writes=W)
                        op("dve", lambda e: e.tensor_copy(out=ti[:], in_=T["t1"][:]), reads=R, writes=W)
                        op("dve", lambda e: e.tensor_copy(out=T["t1"][:], in_=ti[:]), reads=R, writes=W)
                        op("dve", lambda e: e.scalar_tensor_tensor(out=T["t2"][:], in0=T["t1"][:], scalar=-2 * PI, in1=T["t0"][:], op0=ALU.mult, op1=ALU.add), reads=R, writes=W)
                        op("dve", lambda e: e.tensor_scalar(out=T["t1"][:], in0=T["t2"][:], scalar1=PI, scalar2=-2 * PI, op0=ALU.is_gt, op1=ALU.mult), reads=R, writes=W)
                        op("dve", lambda e: e.tensor_tensor(out=T["t2"][:], in0=T["t2"][:], in1=T["t1"][:], op=ALU.add), reads=R, writes=W)
                        op("dve", lambda e: e.tensor_scalar(out=T["t1"][:], in0=T["t2"][:], scalar1=-PI, scalar2=2 * PI, op0=ALU.is_lt, op1=ALU.mult), reads=R, writes=W)
                        op("dve", lambda e: e.tensor_tensor(out=T["t2"][:], in0=T["t2"][:], in1=T["t1"][:], op=ALU.add), reads=R, writes=W)
                        op("act", lambda e: e.activation(out=dst[:], in_=T["t2"][:], func=AF.Sin), reads=R, writes=W)
                    sin_of(T["sn"], 0.0)
                    sin_of(T["cs"], PI / 2)
                    op("dve", lambda e: e.tensor_tensor(out=T["lbr"][:], in0=T["mg"][:], in1=T["cs"][:], op=ALU.mult), reads=R, writes=W)
                    op("dve", lambda e: e.tensor_tensor(out=T["lbi"][:], in0=T["mg"][:], in1=T["sn"][:], op=ALU.mult), reads=R, writes=W)
                    op("dve", lambda e: e.tensor_tensor(out=T["den"][:], in0=lre, in1=lre, op=ALU.mult), reads=R, writes=W)
                    op("dve", lambda e: e.tensor_tensor(out=T["t0"][:], in0=lim, in1=lim, op=ALU.mult), reads=R, writes=W)
                    op("dve", lambda e: e.tensor_tensor(out=T["den"][:], in0=T["den"][:], in1=T["t0"][:], op=ALU.add), reads=R, writes=W)
                    op("dve", lambda e: e.reciprocal(out=T["den"][:], in_=T["den"][:]), reads=R, writes=W)
                    op("dve", lambda e: e.tensor_scalar(out=T["t0"][:], in0=T["lbr"][:], scalar1=-1.0, scalar2=None, op0=ALU.add), reads=R, writes=W)
                    op("dve", lambda e: e.tensor_tensor(out=T["t1"][:], in0=T["t0"][:], in1=lre, op=ALU.mult), reads=R, writes=W)
                    op("dve", lambda e: e.tensor_tensor(out=T["t2"][:], in0=T["lbi"][:], in1=lim, op=ALU.mult), reads=R, writes=W)
                    op("dve", lambda e: e.tensor_tensor(out=T["t1"][:], in0=T["t1"][:], in1=T["t2"][:], op=ALU.add), reads=R, writes=W)
                    op("dve", lambda e: e.tensor_tensor(out=T["cfr"][:], in0=T["t1"][:], in1=T["den"][:], op=ALU.mult), reads=R, writes=W)
                    op("dve", lambda e: e.tensor_tensor(out=T["t1"][:], in0=T["lbi"][:], in1=lre, op=ALU.mult), reads=R, writes=W)
                    op("dve", lambda e: e.tensor_tensor(out=T["t2"][:], in0=T["t0"][:], in1=lim, op=ALU.mult), reads=R, writes=W)
                    op("dve", lambda e: e.tensor_tensor(out=T["t1"][:], in0=T["t1"][:], in1=T["t2"][:], op=ALU.subtract), reads=R, writes=W)
                    op("dve", lambda e: e.tensor_tensor(out=T["cfi"][:], in0=T["t1"][:], in1=T["den"][:], op=ALU.mult), reads=R, writes=W)
                    return T, Bt

                def cmul(out_r, out_i, ar, ai, br, bi, t0, R, W, en="dve"):
                    op(en, lambda e: e.tensor_tensor(out=out_r, in0=ar, in1=br, op=ALU.mult), reads=R, writes=W)
                    op(en, lambda e: e.tensor_tensor(out=t0, in0=ai, in1=bi, op=ALU.mult), reads=R, writes=W)
                    op(en, lambda e: e.tensor_tensor(out=out_r, in0=out_r, in1=t0, op=ALU.subtract), reads=R, writes=W)
                    op(en, lambda e: e.tensor_tensor(out=out_i, in0=ar, in1=bi, op=ALU.mult), reads=R, writes=W)
                    op(en, lambda e: e.tensor_tensor(out=t0, in0=ai, in1=br, op=ALU.mult), reads=R, writes=W)
                    op(en, lambda e: e.tensor_tensor(out=out_i, in0=out_i, in1=t0, op=ALU.add), reads=R, writes=W)

                lamA = sb(ph2, "lamA", [128, 3, 16], F32)
                cbdA = sb(ph2, "cbdA", [128, 2, 16, 32], F32)
                keepA = {nm: sb(ph2, "A_" + nm, [128, 16], F32) for nm in ("lbr", "lbi", "cfr", "cfi", "t0")}
                keepA["buf"] = Buf("A_tmp")
                PW = sb(ph2, "PW", [128, 17, 2, 16], F32)
                AD = sb(ph2, "AD", [128, 8, 3, 16], F32)
                PWn = sb(ph2, "PWn", [128, 17, 16], F32)
                BbA = sb(ph2, "BbA", [128, 2, 16, 32], BF16)
                dskD = sb(ph2, "dskD", [128, 4, 128], F32)
                XB0 = sb(ph2, "XB0", [128, 2, 512], F32)
                LB1 = sb(ph2, "LB1", [128, 2, 512], F32)
                LB2 = sb(ph2, "LB2", [128, 2, 512], F32)
                keepB = {nm: sb(ph2, "B_" + nm, [128, 512], F32) for nm in ("cfr", "cfi", "t0")}
                keepB["lbr"], keepB["lbi"] = LB1[:, 0, :], LB1[:, 1, :]
                keepB["buf"] = Buf("B_tmp")
                FA2 = [sb(ph2, "FA%d" % i, [128, 17, 2, 4, 32], BF16) for i in range(2)]
                BFA2 = [Buf("FA%d" % i) for i in range(2)]
                tP8 = [sb(ph2, "tP%d" % i, [128, 4, 32], F32) for i in range(8)]
                BtP8 = [Buf("tP%d" % i) for i in range(8)]
                BA, BB = Buf("layA"), Buf("layB")
                dsA, dsB = fw.dsem("layA"), fw.dsem("layB")
                dma("sp", dsA, lamA[:], I["lamA"], writes=[BA])
                dma("sp", dsA, cbdA[:], I["cbdA"], writes=[BA])
                dma("sp", dsB, dskD[:], I["dskD"], writes=[BB])

                def bc(ap2, n=16):
                    return ap2.unsqueeze(2).to_broadcast([128, n, 32])

                def setup_FA(T_):
                    par = T_ % 2
                    FA, BFA = FA2[par], BFA2[par]
                    Ps = slice(4 * T_, 4 * T_ + 4)
                    cr, ci = cbdA[:, 0, Ps], cbdA[:, 1, Ps]
                    for a in range(17):
                        tt = tP8[4 * (a % 2):4 * (a % 2) + 4]
                        tb = BtP8[4 * (a % 2):4 * (a % 2) + 4]
                        pr, pi_, npi = bc(PW[:, a, 0, Ps], 4), bc(PW[:, a, 1, Ps], 4), bc(PWn[:, a, Ps], 4)
                        R0 = [BA, BtA]
                        op("pool", lambda e: e.tensor_tensor(out=tt[0][:], in0=cr, in1=pr, op=ALU.mult), reads=R0, writes=[tb[0]])
                        op("pool", lambda e: e.tensor_tensor(out=tt[1][:], in0=ci, in1=pi_, op=ALU.mult), reads=R0, writes=[tb[1]])
                        op("pool", lambda e: e.tensor_tensor(out=tt[2][:], in0=cr, in1=npi, op=ALU.mult), reads=R0, writes=[tb[2]])
                        op("pool", lambda e: e.tensor_tensor(out=tt[3][:], in0=ci, in1=pr, op=ALU.mult), reads=R0, writes=[tb[3]])
                        op("pool", lambda e, a=a: e.tensor_tensor(out=FA[:, a, 0], in0=tt[0][:], in1=tt[1][:], op=ALU.subtract),
                           reads=[tb[0], tb[1], BFA], writes=[BFA])
                        op("pool", lambda e, a=a: e.tensor_tensor(out=FA[:, a, 1], in0=tt[2][:], in1=tt[3][:], op=ALU.subtract),
                           reads=[tb[2], tb[3], BFA], writes=[BFA])

                ph1 = ExitStack()
                wS = sb(ph1, "wS", [128, 8, 1024], BF16)
                BwS = Buf("wS")
                ds_w = fw.dsem("wS")
                for j in range(2):
                    dma("sp", ds_w, wS[:, :, j * 512:(j + 1) * 512], wsrc(wInb[j], 8), reads=[Bw[j]], writes=[BwS])
                xth = [sb(ph1, "xtS%d" % i, [128, 4, 512], BF16) for i in range(2)]
                Bxth = [Buf("xtS%d" % i) for i in range(2)]
                dsxh = [fw.dsem("xtS%d" % i) for i in range(2)]

                def load_xh(t, hf):
                    dma("sp", dsxh[hf], xth[hf][:], wsrc(xTb[t], 8)[:, 4 * hf:4 * hf + 4, :], reads=[Bx[t]], writes=[Bxth[hf]])
                load_xh(0, 0)
                load_xh(0, 1)

                with ExitStack() as tmpS:
                    bbdA = sb(tmpS, "bbdA", [128, 2, 16, 32], F32)
                    lamB = sb(tmpS, "lamB", [128, 3, 512], F32)
                    btB = sb(tmpS, "btB", [128, 2, 512], F32)
                    dma("sp", dsA, bbdA[:], I["bbdA"], writes=[BA])
                    dma("sp", dsB, lamB[:], I["lamB"].rearrange("p a t n -> p a (t n)"), writes=[BB])
                    dma("sp", dsB, btB[:], I["btB"].rearrange("p a t n -> p a (t n)"), writes=[BB])
                    TA, BtA = cplx_setup(tmpS, "A_", lambda i: lamA[:, i, :], [128, 16], BA, keepA)
                    TA = keepA
                    RA, WA_ = [BA, BtA], [BtA]
                    op("dve", lambda e: e.memset(PW[:, 0, 0, :], 1.0), reads=RA, writes=WA_)
                    op("dve", lambda e: e.memset(PW[:, 0, 1, :], 0.0), reads=RA, writes=WA_)
                    for a in range(1, 17):
                        cmul(PW[:, a, 0, :], PW[:, a, 1, :], PW[:, a - 1, 0, :], PW[:, a - 1, 1, :],
                             TA["lbr"][:], TA["lbi"][:], TA["t0"][:], RA, WA_)
                    op("dve", lambda e: e.tensor_copy(out=AD[:, 0, 0:2, :], in_=PW[:, 16, :, :]), reads=RA, writes=WA_)
                    op("dve", lambda e: e.tensor_scalar(out=PWn[:], in0=PW[:, :, 1, :], scalar1=-1.0, scalar2=None, op0=ALU.mult), reads=RA, writes=WA_)
                    for k in range(1, 8):
                        cmul(AD[:, k, 0, :], AD[:, k, 1, :], AD[:, k - 1, 0, :], AD[:, k - 1, 1, :],
                             AD[:, k - 1, 0, :], AD[:, k - 1, 1, :], TA["t0"][:], RA, WA_)
                    op("dve", lambda e: e.tensor_scalar(out=AD[:, :, 2, :], in0=AD[:, :, 1, :], scalar1=-1.0, scalar2=None, op0=ALU.mult), reads=RA, writes=WA_)
                    setup_FA(0)
                    setup_FA(1)
                    Bst_ = {}

                    def emit_setup_B():
                        TBt, BtB_ = cplx_setup(tmpS, "B_", lambda i: lamB[:, i, :], [128, 512], BB, keepB)
                        TB = keepB
                        RB, WB_ = [BB, BtB_], [BtB_]
                        cmul(XB0[:, 0], XB0[:, 1], btB[:, 0], btB[:, 1], TB["cfr"][:], TB["cfi"][:], TB["t0"][:], RB, WB_)
                        op("dve", lambda e: e.tensor_scalar(out=LB2[:, 0, :], in0=TB["lbi"][:], scalar1=-1.0, scalar2=None, op0=ALU.mult), reads=RB, writes=WB_)
                        op("dve", lambda e: e.tensor_copy(out=LB2[:, 1, :], in_=TB["lbr"][:]), reads=RB, writes=WB_)
                        tA = [TBt[nm][:].rearrange("p (a b) -> p a b", a=16) for nm in ("t1", "t2", "den")]
                        RAB, WAB = [BA, BtA, BtB_], [BtA, BtB_]
                        cmul(tA[0], tA[1], bbdA[:, 0], bbdA[:, 1], bc(TA["cfr"][:]), bc(TA["cfi"][:]), tA[2], RAB, WAB)
                        op("dve", lambda e: e.tensor_copy(out=BbA[:, 0], in_=tA[0]), reads=RAB, writes=WAB)
                        op("dve", lambda e: e.tensor_copy(out=BbA[:, 1], in_=tA[1]), reads=RAB, writes=WAB)
                        Bst_["BtB_"] = BtB_

                    for t in range(NT):
                        for hf in range(2):
                            for m in range(8):
                                for k4 in range(4):
                                    kc = 4 * hf + k4
                                    op("pe", lambda e, m=m, kc=kc, k4=k4, hf=hf: e.matmul(
                                        ps[m][:], wS[:, kc, m * 128:(m + 1) * 128], xth[hf][:, k4, :],
                                        start=(kc == 0), stop=(kc == 7)),
                                       reads=[BwS, Bxth[hf]], writes=[PS[m]])
                            if t + 1 < NT:
                                load_xh(t + 1, hf)
                        if t == 1:
                            emit_setup_B()
                        for m in range(8):
                            if m < 4:
                                op("act", lambda e, m=m: e.activation(func=AF.Copy,
                                    out=UT[:, m, :].rearrange("p (i c) -> p i c", i=L)[:, :, 32 * t:32 * t + 32],
                                    in_=ps[m][:].rearrange("p (c i) -> p i c", i=L)),
                                   reads=[PS[m]], writes=BUTp[m])
                            else:
                                op("act", lambda e, m=m: e.activation(
                                    out=ZSp[:, m - 4, :].rearrange("p (i c) -> p i c", i=L)[:, :, 32 * t:32 * t + 32],
                                    in_=ps[m][:].rearrange("p (c i) -> p i c", i=L), func=AF.Silu),
                                   reads=[PS[m]], writes=[BZSp])
                    fw.barrier(skip=("pool",))
                BtB_ = Bst_["BtB_"]
                ph1.close()
                fw.barrier(skip=("pool",))
                if "UT" in dumps:
                    dump("UT", UT[:], [128, 4, S], BUTp[3][0])
                    dump("ZS", ZSp[:], [128, 4, S], BZSp)
                EB = sb(ph2, "EB", [128, 16, 2, 128], BF16)
                KB2 = [sb(ph2, "KB%d" % i, [128, 16, 128], BF16) for i in range(2)]
                Wa = sb(ph2, "Wa", [128, 4, 2, NB], F32)
                Wb = sb(ph2, "Wb", [128, 4, 2, NB], F32)
                SP2 = [sb(ph2, "SP%d" % i, [128, 4, 2, NB], BF16) for i in range(2)]
                BEBa = [Buf("EB%d" % a) for a in range(16)]
                BKB2 = [Buf("KB%d" % i) for i in range(2)]
                BWam = [Buf("Wa%d" % m) for m in range(4)]
                BWbm = [Buf("Wb%d" % m) for m in range(4)]
                BSP2 = [Buf("SP%d" % i) for i in range(2)]
                KF0 = sb(ph2, "KF0", [128, 128], F32)
                BKF = Buf("KF0")
                XB = [sb(ph2, "XBp%d" % i, [128, 2, 128], F32) for i in range(2)]
                BXBf = [Buf("XBp%d" % i) for i in range(2)]
                XBt = sb(ph2, "XBt", [128, 2, 128], F32)
                BXBt = Buf("XBt")
                gelG = [sb(ph2, "gelG%d" % i, [128, 512], F32) for i in range(2)]
                gelX = [sb(ph2, "gelX%d" % i, [128, 512], F32) for i in range(2)]
                BgelG = [Buf("gelG%d" % i) for i in range(2)]
                BgelX = [Buf("gelX%d" % i) for i in range(2)]
                def eb_ops(T_):
                    cs = slice(T_ * 128, (T_ + 1) * 128)
                    R0 = [BB, BtB_]
                    l1, l2 = LB1[:, :, cs], LB2[:, :, cs]
                    th = []
                    for a in range(16):
                        if a == 0:
                            cur, bc_ = XB0[:, :, cs], BtB_
                        else:
                            cur, bc_ = XB[a % 2][:], BXBf[a % 2]
                        th.append(lambda a=a, cur=cur, bc_=bc_: op("dve", lambda e: e.tensor_copy(out=EB[:, a, :, :], in_=cur), reads=R0 + [bc_, BEBa[a]], writes=[BEBa[a]]))
                        if a < 15:
                            nxt, bn_ = XB[(a + 1) % 2], BXBf[(a + 1) % 2]
                            cr_b = cur[:, 0:1, :].to_broadcast([128, 2, 128])
                            ci_b = cur[:, 1:2, :].to_broadcast([128, 2, 128])
                            th.append(lambda nxt=nxt, cr_b=cr_b, bc_=bc_, bn_=bn_: op("dve", lambda e: e.tensor_tensor(out=nxt[:], in0=cr_b, in1=l1, op=ALU.mult), reads=R0 + [bc_], writes=[bn_]))
                            th.append(lambda ci_b=ci_b, bc_=bc_: op("dve", lambda e: e.tensor_tensor(out=XBt[:], in0=ci_b, in1=l2, op=ALU.mult), reads=R0 + [bc_], writes=[BXBt]))
                            th.append(lambda nxt=nxt, bn_=bn_: op("dve", lambda e: e.tensor_tensor(out=nxt[:], in0=nxt[:], in1=XBt[:], op=ALU.add), reads=[bn_, BXBt], writes=[bn_]))
                    return th

                def emit_K(T_):
                    par = T_ % 2
                    FA, BFA, KB, BKB = FA2[par], BFA2[par], KB2[par], BKB2[par]
                    bk = 6 + (T_ % 2)
                    pk = ps[bk][:].rearrange("p (l n) -> p l n", l=16)
                    for lag in range(16):
                        for ri in range(2):
                            for m in range(4):
                                op("pe", lambda e, lag=lag, m=m, ri=ri: e.matmul(
                                    pk[32 * m:32 * m + 32, lag, :], BbA[:, ri, 4 * T_ + m, :], FA[:, lag, ri, m, :],
                                    start=(ri == 0), stop=(ri == 1), tile_position=(0, 32 * m), skip_group_check=True),
                                   reads=[BtA, BFA], writes=[PS[bk]])
                    op("dve", lambda e: e.memset(KB[:], 0.0), reads=[], writes=[BKB])
                    op("dve", lambda e: e.tensor_copy(out=KF0[:], in_=dskD[:, T_, :]), reads=[BB, BKF], writes=[BKF])
                    for m in range(4):
                        sl = slice(32 * m, 32 * m + 32)
                        op("dve", lambda e, sl=sl: e.tensor_copy(out=KB[sl, 1:16, sl], in_=pk[sl, 1:16, :]), reads=[PS[bk]], writes=[BKB])
                        op("dve", lambda e, sl=sl: e.tensor_tensor(out=KF0[sl, sl], in0=KF0[sl, sl], in1=pk[sl, 0, :], op=ALU.add), reads=[PS[bk], BKF], writes=[BKF])
                    op("dve", lambda e: e.tensor_copy(out=KB[:, 0, :], in_=KF0[:]), reads=[BKF, BKB], writes=[BKB])

                def emit_L1(T_):
                    UTv = UT[:, T_, :].rearrange("p (i c) -> p i c", i=L)
                    pws = [ps[m][:].rearrange("p (r n) -> p r n", r=2) for m in range(4)]
                    for ri in range(2):
                        for j in range(L):
                            for m in range(4):
                                rows = slice(32 * m, 32 * m + 32)
                                op("pe", lambda e, m=m, ri=ri, j=j, rows=rows: e.matmul(
                                    pws[m][:, ri, :], EB[rows, 15 - j, ri, :], UTv[rows, j, :],
                                    start=(j == 0), stop=(j == L - 1), tile_position=(32 * m, 0), skip_group_check=True),
                                   reads=[BEBa[15 - j]] + BUTp[T_], writes=[PS[m]])
                    for m in range(4):
                        op("act", lambda e, m=m: e.activation(out=Wa[:, m], in_=pws[m], func=AF.Copy), reads=[PS[m]], writes=[BWam[m]])

                def dbl_ops(T_):
                    par = T_ % 2
                    src, dst, Bs, Bd = Wa, Wb, BWam, BWbm
                    SP, BSP = SP2[par], BSP2[par]
                    th = []
                    for k in range(8):
                        d = 1 << k

                        def mk(kind, m, d=d, k=k, src=src, dst=dst, Bs=Bs, Bd=Bd):
                            P_ = 4 * T_ + m
                            aR, aI, nI = AD[:, k, 0, P_:P_ + 1], AD[:, k, 1, P_:P_ + 1], AD[:, k, 2, P_:P_ + 1]
                            R_, W_ = [Bs[m], Bd[m], BtA], [Bd[m]]
                            if kind == 0:
                                return lambda: op("dve", lambda e: e.scalar_tensor_tensor(
                                    out=dst[:, m, 0, d:], in0=src[:, m, 0, :NB - d], scalar=aR, in1=src[:, m, 0, d:], op0=ALU.mult, op1=ALU.add), reads=R_, writes=W_)
                            if kind == 1:
                                return lambda: op("dve", lambda e: e.scalar_tensor_tensor(
                                    out=dst[:, m, 1, d:], in0=src[:, m, 0, :NB - d], scalar=aI, in1=src[:, m, 1, d:], op0=ALU.mult, op1=ALU.add), reads=R_, writes=W_)
                            if kind == 2:
                                return lambda: op("dve", lambda e: e.tensor_copy(out=dst[:, m, :, :d], in_=src[:, m, :, :d]), reads=R_, writes=W_)
                            if kind == 3:
                                return lambda: op("dve", lambda e: e.scalar_tensor_tensor(
                                    out=dst[:, m, 0, d:], in0=src[:, m, 1, :NB - d], scalar=nI, in1=dst[:, m, 0, d:], op0=ALU.mult, op1=ALU.add), reads=R_, writes=W_)
                            return lambda: op("dve", lambda e: e.scalar_tensor_tensor(
                                out=dst[:, m, 1, d:], in0=src[:, m, 1, :NB - d], scalar=aR, in1=dst[:, m, 1, d:], op0=ALU.mult, op1=ALU.add), reads=R_, writes=W_)
                        th.append(lambda d=d, src=src, dst=dst, Bs=Bs, Bd=Bd: op("dve", lambda e: e.tensor_copy(
                            out=dst[:, :, :, :d], in_=src[:, :, :, :d]), reads=list(Bs) + list(Bd), writes=list(Bd)))
                        for kind in (0, 1, 3, 4):
                            for m in range(4):
                                th.append(mk(kind, m))
                        src, dst, Bs, Bd = dst, src, Bd, Bs

                    def fin(src=src, Bs=Bs):
                        op("dve", lambda e: e.memset(SP[:, :, :, 0:1], 0.0), reads=[], writes=[BSP])
                        op("dve", lambda e: e.tensor_copy(out=SP[:, :, :, 1:], in_=src[:, :, :, :NB - 1]), reads=list(Bs), writes=[BSP])
                    th.append(fin)
                    return th

                def emit_L03(T_, inter):
                    par = T_ % 2
                    FA, BFA, KB, BKB, SP, BSP = FA2[par], BFA2[par], KB2[par], BKB2[par], SP2[par], BSP2[par]
                    UTv = UT[:, T_, :].rearrange("p (i c) -> p i c", i=L)
                    nint = (len(inter) + 7) // 8
                    for n_, ip in enumerate(range(7, -1, -1)):
                        bk = 4 + (ip % 4)
                        py = ps[bk][:].rearrange("p (i n) -> p i n", i=2)
                        i0 = 2 * ip
                        for lag in range(i0 + 2):
                            ist = i0 if lag <= i0 else i0 + 1
                            ni = i0 + 2 - ist
                            op("pe", lambda e, lag=lag, ist=ist, ni=ni, py=py: e.matmul(
                                py[:, ist - i0:ist - i0 + ni, :], KB[:, lag, :], UTv[:, ist - lag:ist - lag + ni, :],
                                start=(lag == 0), stop=False, skip_group_check=True),
                               reads=[BKB] + BUTp[T_][:ip + 1], writes=[PS[bk]])
                        for il in range(2):
                            for ri in range(2):
                                for m in range(4):
                                    last = (m == 3 and il == 1 and ri == 1)
                                    op("pe", lambda e, m=m, il=il, ri=ri, py=py, last=last: e.matmul(
                                        py[32 * m:32 * m + 32, il, :], FA[:, i0 + il + 1, ri, m, :], SP[:, m, ri, :],
                                        start=False, stop=last, tile_position=(0, 32 * m), skip_group_check=True),
                                       reads=[BFA, BSP], writes=[PS[bk]])
                        pp = n_ % 2
                        gG, gX, BgG, BgX = gelG[pp][:], gelX[pp][:], BgelG[pp], BgelX[pp]
                        pyf = ps[bk][:]
                        op("act", lambda e, pyf=pyf, gX=gX: e.activation(out=gX, in_=pyf, func=AF.Copy), reads=[PS[bk], BgX], writes=[BgX])
                        op("act", lambda e, pyf=pyf, gG=gG: e.activation(out=gG, in_=pyf, func=AF.Square, scale=0.2114592159259085),
                           reads=[PS[bk], BgG], writes=[BgG])
                        op("dve", lambda e, gG=gG, gX=gX: e.scalar_tensor_tensor(out=gG, in0=gG, scalar=1.0, in1=gX, op0=ALU.add, op1=ALU.mult),
                           reads=[BgG, BgX], writes=[BgG])
                        op("act", lambda e, gG=gG: e.activation(out=gG, in_=gG, func=AF.Sigmoid, scale=1.5957691216057308), reads=[BgG], writes=[BgG])
                        op("dve", lambda e, i0=i0, gG=gG, gX=gX: e.tensor_tensor(
                            out=UTv[:, i0:i0 + 2, :], in0=gG.rearrange("p (i n) -> p i n", i=2), in1=gX.rearrange("p (i n) -> p i n", i=2), op=ALU.mult),
                           reads=[BgG, BgX], writes=[BUTp[T_][ip]])
                        for f_ in inter[n_ * nint:(n_ + 1) * nint]:
                            f_()

                def run(th):
                    for f_ in th:
                        f_()

                def mix(a_, b_):
                    out_, ia, ib = [], 0, 0
                    while ia < len(a_) or ib < len(b_):
                        if ib >= len(b_) or (ia < len(a_) and ia * len(b_) <= ib * len(a_)):
                            out_.append(a_[ia]); ia += 1
                        else:
                            out_.append(b_[ib]); ib += 1
                    return out_

                run(eb_ops(0))
                emit_K(0)
                emit_L1(0)
                run(mix(dbl_ops(0), eb_ops(1)))
                emit_K(1)
                emit_L1(1)
                emit_L03(0, mix(dbl_ops(1), eb_ops(2)))
                setup_FA(2)
                emit_K(2)
                emit_L1(2)
                emit_L03(1, mix(dbl_ops(2), eb_ops(3)))
                setup_FA(3)
                emit_K(3)
                emit_L1(3)
                emit_L03(2, dbl_ops(3))
                emit_L03(3, [])
                if "YG" in dumps:
                    dump("YG", UT[:], [128, 4, S], BUTp[3][0])
                fw.barrier()
                ph2.close()
                wG = sb(ph2, "wG", [128, 4, 1024], BF16)
                BwG = Buf("wG")
                dsG = fw.dsem("wG")
                for j in range(2):
                    dma("sp", dsG, wG[:, :, j * 512:(j + 1) * 512], wsrc(wGb[j], 4), reads=[Bg[j]], writes=[BwG])
                sg = [sb(ph2, "sg%d" % i, [128, 512], F32) for i in range(2)]
                Bsg = [Buf("sg%d" % i) for i in range(2)]
                cnt = 0
                for t in range(NT):
                    cols = slice(t * 512, (t + 1) * 512)
                    for f in range(4):
                        bv, bg_ = (2 * cnt) % 8, (2 * cnt + 1) % 8
                        s_ = cnt % 2
                        cnt += 1
                        for kc in range(4):
                            op("pe", lambda e, f=f, kc=kc, bv=bv: e.matmul(
                                ps[bv][:], wG[:, kc, f * 128:(f + 1) * 128], UT[:, kc, cols], start=(kc == 0), stop=(kc == 3)),
                               reads=[BwG] + BUTall, writes=[PS[bv]])
                        for kc in range(4):
                            op("pe", lambda e, f=f, kc=kc, bg_=bg_: e.matmul(
                                ps[bg_][:], wG[:, kc, 512 + f * 128:512 + (f + 1) * 128], UT[:, kc, cols], start=(kc == 0), stop=(kc == 3)),
                               reads=[BwG] + BUTall, writes=[PS[bg_]])
                        op("act", lambda e, bg_=bg_, s_=s_: e.activation(out=sg[s_][:], in_=ps[bg_][:], func=AF.Sigmoid),
                           reads=[PS[bg_]], writes=[Bsg[s_]])
                        op("dve", lambda e, bv=bv, s_=s_: e.tensor_tensor(out=sg[s_][:], in0=sg[s_][:], in1=ps[bv][:], op=ALU.mult),
                           reads=[PS[bv], Bsg[s_]], writes=[Bsg[s_]])
                        op("dve", lambda e, f=f, s_=s_, t=t: e.tensor_tensor(
                            out=ZS[:, f, :].rearrange("p (c i) -> p i c", i=L)[:, 2 * t:2 * t + 2, :],
                            in0=sg[s_][:].rearrange("p (i c) -> p i c", i=2), in1=ZSp[:, f, cols].rearrange("p (i c) -> p i c", i=2), op=ALU.mult),
                           reads=[Bsg[s_], BZSp, BZSn], writes=[BZSn])
            fw.barrier()
        if "YS" in dumps:
            dump("YS", ZS[:], [128, 4, S], BZSn)
        if stop_after == "S":
            fw.barrier()
            return nc, dump_d

        KT = sb(top, "KT", [128, 4, S], BF16)
        VA = sb(top, "VA", [128, 32, 4, 130], BF16)
        MKT = sb(top, "MKT", [128, 4, 256], BF16)
        MVA = sb(top, "MVA", [128, 2, 4, 130], BF16)
        BKT, BVA, BMK = Buf("KT"), Buf("VA"), Buf("MK")
        op("dve", lambda e: e.memset(VA[:, :, :, 128:130], 1.0), reads=[], writes=[BVA])
        op("dve", lambda e: e.memset(MVA[:, :, :, 128:130], 1.0), reads=[], writes=[BMK])
        with ExitStack() as phK:
            wKV = sb(phK, "wKV", [128, 8, 1024], BF16)
            wM = sb(phK, "wM", [128, 8, 1024], BF16)
            mT = sb(phK, "mT", [128, 8, 256], BF16)
            BwKV, BwM, BmT = Buf("wKV"), Buf("wM"), Buf("mT")
            dsk_, dsm_, dst_ = fw.dsem("wKV"), fw.dsem("wM"), fw.dsem("mT")
            for j in range(2):
                dma("sp", dsk_, wKV[:, :, j * 512:(j + 1) * 512], wsrc(wInb[3 + j], 8), reads=[Bw[3 + j]], writes=[BwKV])
            for j in range(2):
                dma("sp", dsm_, wM[:, :, j * 512:(j + 1) * 512], wsrc(wMb[j], 8), reads=[Bm[j]], writes=[BwM])
            dma("sp", dst_, mT[:], wsrc(memTb, 8), reads=[Bmem], writes=[BmT])
            xt = [sb(phK, "xtK%d" % i, [128, 8, 512], BF16) for i in range(2)]
            Bxt = [Buf("xtK%d" % i) for i in range(2)]
            dsx = [fw.dsem("xtK%d" % i) for i in range(2)]

            def load_xk(t):
                dma("sp", dsx[t % 2], xt[t % 2][:], wsrc(xTb[t], 8), reads=[Bx[t]], writes=[Bxt[t % 2]])
            load_xk(0)
            cnt = 0
            for h in range(4):
                bk = cnt % 8
                cnt += 1
                for kc in range(8):
                    op("pe", lambda e, h=h, kc=kc, bk=bk: e.matmul(ps[bk][:, 0:256], wM[:, kc, h * 128:(h + 1) * 128], mT[:, kc, :],
                                                                   start=(kc == 0), stop=(kc == 7)), reads=[BwM, BmT], writes=[PS[bk]])
                op("act", lambda e, h=h, bk=bk: e.activation(out=MKT[:, h, :], in_=ps[bk][:, 0:256], func=AF.Copy), reads=[PS[bk]], writes=[BMK])
            for mb in range(2):
                bk = cnt % 8
                cnt += 1
                for kc in range(8):
                    op("pe", lambda e, mb=mb, kc=kc, bk=bk: e.matmul(ps[bk][:], mT[:, kc, mb * 128:(mb + 1) * 128], wM[:, kc, 512:1024],
                                                                     start=(kc == 0), stop=(kc == 7)), reads=[BwM, BmT], writes=[PS[bk]])
                op("act", lambda e, mb=mb, bk=bk: e.activation(out=MVA[:, mb, :, 0:128], in_=ps[bk][:].rearrange("p (h e) -> p h e", h=4), func=AF.Copy),
                   reads=[PS[bk]], writes=[BMK])
            for t in range(NT):
                if t + 1 < NT:
                    load_xk(t + 1)
                x_t, Bx_t = xt[t % 2], Bxt[t % 2]
                cols = slice(t * 512, (t + 1) * 512)
                for h in range(4):
                    bk = cnt % 8
                    cnt += 1
                    for kc in range(8):
                        op("pe", lambda e, h=h, kc=kc, bk=bk: e.matmul(ps[bk][:], wKV[:, kc, h * 128:(h + 1) * 128], x_t[:, kc, :],
                                                                       start=(kc == 0), stop=(kc == 7)), reads=[BwKV, Bx_t], writes=[PS[bk]])
                    op("act", lambda e, h=h, bk=bk: e.activation(out=KT[:, h, cols], in_=ps[bk][:], func=AF.Copy), reads=[PS[bk]], writes=[BKT])
                for tb in range(4):
                    bk = cnt % 8
                    cnt += 1
                    for kc in range(8):
                        op("pe", lambda e, tb=tb, kc=kc, bk=bk: e.matmul(ps[bk][:], x_t[:, kc, tb * 128:(tb + 1) * 128], wKV[:, kc, 512:1024],
                                                                         start=(kc == 0), stop=(kc == 7)), reads=[BwKV, Bx_t], writes=[PS[bk]])
                    op("dve", lambda e, tb=tb, bk=bk, t=t: e.tensor_copy(out=VA[:, 4 * t + tb, :, 0:128], in_=ps[bk][:].rearrange("p (h e) -> p h e", h=4)),
                       reads=[PS[bk]], writes=[BVA])
            fw.barrier()
        if "KT" in dumps:
            dump("KT", KT[:], [128, 4, S], BKT)
            dump("VA", VA[:], [128, 32, 4, 130], BVA)
            dump("MKT", MKT[:], [128, 4, 256], BMK)
            dump("MVA", MVA[:], [128, 2, 4, 130], BMK)
        if stop_after == "KV":
            fw.barrier()
            return nc, dump_d

        with ExitStack() as phF:
            farb = sb(phF, "farb", [128, 4], F32)
            lamv = sb(phF, "lamv", [128, 4, 64], F32)
            subw = sb(phF, "subw", [128, 128], F32)
            lng = sb(phF, "lng", [128, D], F32)
            lnb = sb(phF, "lnb", [128, D], F32)
            NBh = sb(phF, "NBh", [128, 4, 256], BF16)
            NBl = sb(phF, "NBl", [128, 4, 256], BF16)
            lsc = sb(phF, "lsc", [128, 8], F32)
            BC = Buf("constF")
            dsc = fw.dsem("constF")
            tmpF = ExitStack()
            tb_f = sb(tmpF, "tb_f", [128, 4, 256], F32)
            mk_f = sb(tmpF, "mk_f", [128, 4, 256], F32)
            for dst_, nm in ((tb_f, "tbias"), (mk_f, "maskc"), (farb, "farb"), (lamv, "lamv"), (subw, "subw"), (lng, "lng"), (lnb, "lnb")):
                dma("sp", dsc, dst_[:], I[nm], writes=[BC])
            RC, WC = [BC], [BC]
            for h in range(4):
                op("dve", lambda e, h=h: e.tensor_scalar(out=tb_f[:, h, :], in0=tb_f[:, h, :], scalar1=farb[:, h:h + 1], scalar2=None, op0=ALU.subtract), reads=RC, writes=WC)
            op("dve", lambda e: e.tensor_tensor(out=tb_f[:], in0=tb_f[:], in1=mk_f[:], op=ALU.add), reads=RC, writes=WC)
            op("dve", lambda e: e.tensor_copy(out=NBh[:], in_=tb_f[:]), reads=RC, writes=WC)
            op("dve", lambda e: e.tensor_copy(out=mk_f[:], in_=NBh[:]), reads=RC, writes=WC)
            op("dve", lambda e: e.tensor_tensor(out=mk_f[:], in0=tb_f[:], in1=mk_f[:], op=ALU.subtract), reads=RC, writes=WC)
            op("dve", lambda e: e.tensor_copy(out=NBl[:], in_=mk_f[:]), reads=RC, writes=WC)
            fw.barrier()
            tmpF.close()
            op("dve", lambda e: e.tensor_tensor(out=lamv[:, 0, :], in0=lamv[:, 0, :], in1=lamv[:, 1, :], op=ALU.mult), reads=RC, writes=WC)
            op("dve", lambda e: e.tensor_tensor(out=lamv[:, 2, :], in0=lamv[:, 2, :], in1=lamv[:, 3, :], op=ALU.mult), reads=RC, writes=WC)
            op("dve", lambda e: e.reduce_sum(out=lsc[:, 1:2], in_=lamv[:, 0, :], axis=AX.X), reads=RC, writes=WC)
            op("dve", lambda e: e.reduce_sum(out=lsc[:, 2:3], in_=lamv[:, 2, :], axis=AX.X), reads=RC, writes=WC)
            op("act", lambda e: e.activation(out=lsc[:, 3:5], in_=lsc[:, 1:3], func=AF.Exp), reads=RC, writes=WC)
            op("dve", lambda e: e.tensor_tensor(out=lsc[:, 0:1], in0=lsc[:, 4:5], in1=lsc[:, 3:4], op=ALU.subtract), reads=RC, writes=WC)
            op("dve", lambda e: e.tensor_scalar(out=lsc[:, 0:1], in0=lsc[:, 0:1], scalar1=-LAMBDA_INIT, scalar2=None, op0=ALU.add), reads=RC, writes=WC)
            op("dve", lambda e: e.tensor_scalar(out=subw[:], in0=subw[:], scalar1=1.0 - LAMBDA_INIT, scalar2=None, op0=ALU.mult), reads=RC, writes=WC)

            xq = sb(phF, "xq", [128, 8, 512], BF16)
            Bxq = Buf("xq")
            dsxq = fw.dsem("xq")
            NW = 4
            wt = [sb(phF, "wt%d" % i, [128, 8, 512], BF16) for i in range(NW)]
            Bwt = [Buf("wt%d" % i) for i in range(NW)]
            dswt = [fw.dsem("wt%d" % i) for i in range(NW)]
            wctr = [0]

            gseq = [(wInb[2], 8, Bw[2]), (wInb[5], 8, Bw[5]), (wInb[6], 8, Bw[6]), (wInb[7], 8, Bw[7])]
            for br_ in range(3):
                for half_ in range(2):
                    gseq.append((wInb[8 + 2 * br_ + half_], 8, Bw[8 + 2 * br_ + half_]))
                    gseq.append((wBb[br_, half_], 4, Bb[br_][half_]))
            gseq += [(wOb[0], 8, Bo[0]), (wOb[1], 8, Bo[1])]
            wseq = gseq * NT
            wstate = {"issued": 0, "consumed": 0}

            def _issue_w(k):
                piece, nkc, bsrc = wseq[k]
                i = k % NW
                dma("sp", dswt[i], wt[i][:, 0:nkc, :], wsrc(piece, nkc), reads=[bsrc], writes=[Bwt[i]])

            def next_w():
                k = wstate["consumed"]
                while wstate["issued"] < min(len(wseq), k + 3):
                    _issue_w(wstate["issued"])
                    wstate["issued"] += 1
                wstate["consumed"] += 1
                return wt[k % NW], Bwt[k % NW]

            QT = sb(phF, "QT", [128, 4, 512], BF16)
            MQT = QT
            scr = sb(phF, "scr", [128, 4096], F32)
            Bscr = Buf("scr")
            ZD = scr[:, 0:2048].rearrange("p (q f) -> p q f", q=4)
            ZM = ZD
            BQT = Buf("QT")
            BMQ, BZD, BZM = BQT, Bscr, Bscr
            NP = 4
            PT_all = sb(phF, "PT", [128, NP, 512], BF16)
            PT = [PT_all[:, i, :] for i in range(NP)]
            BPT = [Buf("PT%d" % i) for i in range(NP)]
            pctr = [0]
            YD = scr[:, 2048:3072].bitcast(BF16).rearrange("p (q f) -> p q f", q=4)
            YM = scr[:, 3072:4096].bitcast(BF16).rearrange("p (q f) -> p q f", q=4)
            YDT = sb(phF, "YDT", [128, 4, 512], BF16)
            YMT = sb(phF, "YMT", [128, 4, 512], BF16)
            BYD, BYM, BYDT, BYMT = Bscr, Bscr, Buf("YDT"), Buf("YMT")
            xres = YDT[:].rearrange("p h n -> p (h n)").bitcast(F32)
            Bxres = BYDT
            sm = sb(phF, "sm", [128, 32], F32)
            Bsm = Buf("sm")
            od = [None, sb(phF, "od1", [128, 128], F32)]
            Bod = Buf("od")
            od4 = sb(phF, "od4", [128, 4, 128], F32)
            OS = sb(phF, "OS", [128, 2, 520], F32)
            BOS = Buf("OS")
            ACC = scr[:].rearrange("p (q f) -> p q f", q=8)
            BACC = Bscr
            MT = sb(phF, "MT", [128, 8, 512], BF16)
            BMT = Buf("MT")
            sgf_all = sb(phF, "sgf", [128, 2, 512], F32)
            sgf = [sgf_all[:, 0, :], sgf_all[:, 1, :]]
            Bsgf = [Buf("sgf%d" % i) for i in range(2)]

            dsxr = fw.dsem("xres")
            dsout = [fw.dsem("out0"), fw.dsem("out1")]
            stats2 = [sb(phF, "stats%d" % i, [128, 2, 6], F32) for i in range(2)]
            mv2 = [sb(phF, "mv%d" % i, [128, 4], F32) for i in range(2)]
            Bst = [Buf("stats%d" % i) for i in range(2)]

            pctr_bank = [0]

            def gbank():
                b = (0, 1, 2, 3)[pctr_bank[0] % 4]
                pctr_bank[0] += 1
                return b

            OB = [[4, 5], [6, 7]]

            def oreg(c, ql):
                if ql < 3:
                    return ps[OB[c][0]][:, ql * 130:ql * 130 + 130], OB[c][0]
                return ps[OB[c][1]][:, 0:130], OB[c][1]

            def emit_qproj():
                wq, Bwq = next_w()
                for h in range(4):
                    bk = gbank()
                    for kc in range(8):
                        op("pe", lambda e, h=h, kc=kc, bk=bk: e.matmul(ps[bk][:], wq[:, kc, h * 128:(h + 1) * 128], xq[:, kc, :],
                                                                       start=(kc == 0), stop=(kc == 7)), reads=[Bwq, Bxq], writes=[PS[bk]])
                    op("act", lambda e, h=h, bk=bk: e.activation(out=QT[:, h, :], in_=ps[bk][:], func=AF.Copy, scale=0.125), reads=[PS[bk]], writes=[BQT])
                wz, Bwz = next_w()
                for ql in range(4):
                    bk = gbank()
                    for kc in range(8):
                        op("pe", lambda e, ql=ql, kc=kc, bk=bk: e.matmul(ps[bk][:], xq[:, kc, ql * 128:(ql + 1) * 128], wz[:, kc, :],
                                                                         start=(kc == 0), stop=(kc == 7)), reads=[Bwz, Bxq], writes=[PS[bk]])
                    op("act", lambda e, ql=ql, bk=bk: e.activation(out=ZD[:, ql, :], in_=ps[bk][:], func=AF.Silu), reads=[PS[bk]], writes=[BZD])

            pend_ln = []
            for G in range(NT):
                if G == 0:
                    dma("sp", dsxq, xq[:], wsrc(xTb[G], 8), reads=[Bx[G]], writes=[Bxq])
                if G == 0:
                    emit_qproj()
                if stop_after == "F1":
                    dump("QT", QT[:], [128, 4, 512], BQT)
                    dump("ZD", scr[:, 0:2048], [128, 2048], Bscr)
                    fw.barrier()
                    return nc, dump_d
                nkb = 4 * G + 4
                steps = [(h, kb) for h in range(4) for kb in range(nkb)]
                SBANKS = ((0, 1), (2, 3))

                def emit_score(i):
                    h, kb = steps[i]
                    pair = i % 2
                    ql0 = max(0, kb - 4 * G)
                    cs_ = slice(ql0 * 128, 512)
                    nq = [ql for ql in range(4) if 4 * G + ql in (kb, kb + 1)]
                    for c in range(2):
                        rows = slice(64 * c, 64 * c + 64)
                        bk = SBANKS[pair][c]
                        op("pe", lambda e, h=h, kb=kb, bk=bk, cs_=cs_, rows=rows, nq=nq: e.matmul(
                            ps[bk][:, cs_], KT[rows, h, kb * 128:(kb + 1) * 128], QT[rows, h, cs_],
                            start=True, stop=(len(nq) == 0)), reads=[BKT, BQT], writes=[PS[bk]])
                    if nq:
                        qa = nq[0] * 128
                        qrel0 = (4 * G + nq[0] - kb) * 128
                        w_ = 128 * len(nq)
                        for c in range(2):
                            bk = SBANKS[pair][c]
                            for hl, NBx in enumerate((NBh, NBl)):
                                op("pe", lambda e, h=h, bk=bk, qa=qa, qrel0=qrel0, w_=w_, NBx=NBx, hl=hl: e.matmul(
                                    ps[bk][:, qa:qa + w_], ident[:], NBx[:, h, qrel0:qrel0 + w_],
                                    start=False, stop=(hl == 1)), reads=[Bid, BC], writes=[PS[bk]])
                    src2 = (psA if pair == 0 else psB)[:].rearrange("p (c n) -> p c n", c=2)[:, :, cs_]
                    dst2 = PT_all[:, 2 * pair:2 * pair + 2, cs_]
                    b0_, b1_ = SBANKS[pair]
                    op("act", lambda e, h=h, src2=src2, dst2=dst2: e.activation(
                        out=dst2, in_=src2, func=AF.Exp, bias=farb[:, h:h + 1], scale=1.0),
                       reads=[PS[b0_], PS[b1_], BC], writes=[BPT[2 * pair], BPT[2 * pair + 1]])

                def emit_pv(i):
                    h, kb = steps[i]
                    pair = i % 2
                    ql0 = max(0, kb - 4 * G)
                    for c in range(2):
                        pi = 2 * pair + c
                        for ql in range(ql0, 4):
                            oap, obk = oreg(c, ql)
                            st_ = (kb == 0 and ql in (0, 3))
                            last = (kb == 4 * G + ql)
                            op("pe", lambda e, h=h, kb=kb, ql=ql, pi=pi, oap=oap, st_=st_, last=last: e.matmul(
                                oap, PT[pi][:, ql * 128:(ql + 1) * 128], VA[:, kb, h, 0:130], start=st_, stop=last, skip_group_check=True),
                               reads=[BPT[pi], BVA], writes=[PS[obk]])
                    if kb == nkb - 1:
                        for pb in list(pend_b):
                            emit_norm_b(pb[0])
                            pend_b.remove(pb)
                        emit_norm_a(h)
                        pend_b.append([h, 10])

                def emit_norm_a(h):
                    for c in range(2):
                        op("dve", lambda e, c=c: e.tensor_copy(out=OS[:, c, 0:390], in_=ps[OB[c][0]][:, 0:390]), reads=[PS[OB[c][0]]], writes=[BOS])
                        op("dve", lambda e, c=c: e.tensor_copy(out=OS[:, c, 390:520], in_=ps[OB[c][1]][:, 0:130]), reads=[PS[OB[c][1]]], writes=[BOS])
                    R_, W_ = [BOS, Bsm, Bod, BC, BZD], [Bsm, Bod]
                    dens = OS[:].rearrange("p c (q e) -> p c q e", q=4)[:, :, :, 128]
                    op("dve", lambda e: e.reciprocal(out=sm[:, 0:8].rearrange("p (c q) -> p c q", c=2), in_=dens), reads=R_, writes=W_)
                    op("dve", lambda e: e.tensor_scalar(out=sm[:, 4:8], in0=sm[:, 4:8], scalar1=lsc[:, 0:1], scalar2=None, op0=ALU.mult), reads=R_, writes=W_)
                    for ql in range(4):
                        o0 = OS[:, 0, ql * 130:ql * 130 + 128]
                        o1 = OS[:, 1, ql * 130:ql * 130 + 128]
                        op("dve", lambda e, o0=o0, ql=ql: e.tensor_scalar(out=od4[:, ql, :], in0=o0, scalar1=sm[:, ql:ql + 1], scalar2=None, op0=ALU.mult), reads=R_, writes=W_)
                        op("dve", lambda e, o1=o1, ql=ql: e.scalar_tensor_tensor(out=od4[:, ql, :], in0=o1, scalar=sm[:, 4 + ql:5 + ql], in1=od4[:, ql, :], op0=ALU.mult, op1=ALU.add), reads=R_, writes=W_)
                        op("dve", lambda e, ql=ql: e.scalar_tensor_tensor(out=od[1][:], in0=od4[:, ql, :], scalar=1.0, in1=od4[:, ql, :], op0=ALU.mult, op1=ALU.mult,
                                                                         accum_out=sm[:, 8 + ql:9 + ql]), reads=R_, writes=W_)
                    op("dve", lambda e: e.tensor_scalar(out=sm[:, 12:16], in0=sm[:, 8:12], scalar1=1.0 / 128.0, scalar2=1e-5, op0=ALU.mult, op1=ALU.add), reads=R_, writes=W_)

                def emit_norm_b(h):
                    R_, W_ = [BOS, Bsm, Bod, BC, BZD], [Bsm, Bod]
                    op("act", lambda e: e.activation(out=sm[:, 16:20], in_=sm[:, 12:16], func=AF.Ln), reads=R_, writes=W_)
                    op("act", lambda e: e.activation(out=sm[:, 20:24], in_=sm[:, 16:20], func=AF.Exp, scale=-0.5), reads=R_, writes=W_)
                    for ql in range(4):
                        op("dve", lambda e, ql=ql: e.scalar_tensor_tensor(out=od4[:, ql, :], in0=od4[:, ql, :], scalar=sm[:, 20 + ql:21 + ql], in1=subw[:], op0=ALU.mult, op1=ALU.mult), reads=R_, writes=W_)
                    op("dve", lambda e, h=h: e.tensor_tensor(out=YD[:, :, h * 128:(h + 1) * 128], in0=od4[:], in1=ZD[:, :, h * 128:(h + 1) * 128], op=ALU.mult),
                       reads=R_ + [BYD], writes=[BYD, Bod])

                pend_b = []
                emit_score(0)
                for i in range(len(steps)):
                    if i + 1 < len(steps):
                        emit_score(i + 1)
                    emit_pv(i)
                    if pend_ln and (i == 10 or i == len(steps) - 1):
                        for f_ in pend_ln:
                            f_()
                        del pend_ln[:]
                    for pb in list(pend_b):
                        pb[1] -= 1
                        if pb[1] <= 0 or i == len(steps) - 1:
                            emit_norm_b(pb[0])
                            pend_b.remove(pb)

                if stop_after == "F2":
                    dump("YD", scr[:, 2048:3072], [128, 1024], Bscr)
                    dump("QT", QT[:], [128, 4, 512], BQT)
                    dump("ZD", scr[:, 0:2048], [128, 2048], Bscr)
                    op("dve", lambda e: e.tensor_copy(out=sgf[0][:], in_=ps[4][:]), reads=[PS[4]], writes=[Bsgf[0]])
                    op("dve", lambda e: e.tensor_copy(out=sgf[1][:], in_=ps[6][:]), reads=[PS[6]], writes=[Bsgf[1]])
                    dump("O0", sgf[0][:], [128, 512], Bsgf[0])
                    dump("O1", sgf[1][:], [128, 512], Bsgf[1])
                    dump("PT", PT[0], [128, 512], BPT[0])
                    dump("sm", sm[:], [128, 32], Bsm)
                    dump("lsc", lsc[:], [128, 8], BC)
                    fw.barrier()
                    return nc, dump_d
                wmq, Bwmq = next_w()
                for h in range(4):
                    bk = gbank()
                    for kc in range(8):
                        op("pe", lambda e, h=h, kc=kc, bk=bk: e.matmul(ps[bk][:], wmq[:, kc, h * 128:(h + 1) * 128], xq[:, kc, :],
                                                                       start=(kc == 0), stop=(kc == 7)), reads=[Bwmq, Bxq], writes=[PS[bk]])
                    op("act", lambda e, h=h, bk=bk: e.activation(out=MQT[:, h, :], in_=ps[bk][:], func=AF.Copy, scale=128.0 ** -0.5), reads=[PS[bk]], writes=[BMQ])
                wzm, Bwzm = next_w()
                for ql in range(4):
                    bk = gbank()
                    for kc in range(8):
                        op("pe", lambda e, ql=ql, kc=kc, bk=bk: e.matmul(ps[bk][:], xq[:, kc, ql * 128:(ql + 1) * 128], wzm[:, kc, :],
                                                                         start=(kc == 0), stop=(kc == 7)), reads=[Bwzm, Bxq], writes=[PS[bk]])
                    op("act", lambda e, ql=ql, bk=bk: e.activation(out=ZM[:, ql, :], in_=ps[bk][:], func=AF.Silu), reads=[PS[bk]], writes=[BZM])

                for h in range(4):
                    for mb in range(2):
                        bk = gbank()
                        op("pe", lambda e, h=h, mb=mb, bk=bk: e.matmul(ps[bk][:], MKT[:, h, mb * 128:(mb + 1) * 128], MQT[:, h, :], start=True, stop=True),
                           reads=[BMK, BMQ], writes=[PS[bk]])
                        pi = pctr[0] % NP
                        pctr[0] += 1
                        op("act", lambda e, bk=bk, pi=pi: e.activation(out=PT[pi][:], in_=ps[bk][:], func=AF.Exp), reads=[PS[bk]], writes=[BPT[pi]])
                        for ql in range(4):
                            oap, obk = oreg(0, ql)
                            op("pe", lambda e, h=h, mb=mb, ql=ql, pi=pi, oap=oap: e.matmul(
                                oap, PT[pi][:, ql * 128:(ql + 1) * 128], MVA[:, mb, h, 0:130], start=(mb == 0 and ql in (0, 3)), stop=(mb == 1), skip_group_check=True),
                               reads=[BPT[pi], BMK], writes=[PS[obk]])
                    for ql in range(4):
                        o0, b0 = oreg(0, ql)
                        R_, W_ = [PS[b0], Bsm, BZM], [Bsm]
                        op("dve", lambda e, o0=o0: e.reciprocal(out=sm[:, 8:9], in_=o0[:, 128:129]), reads=R_, writes=W_)
                        op("dve", lambda e, o0=o0, ql=ql, h=h: e.scalar_tensor_tensor(
                            out=YM[:, ql, h * 128:(h + 1) * 128], in0=o0[:, 0:128], scalar=sm[:, 8:9], in1=ZM[:, ql, h * 128:(h + 1) * 128], op0=ALU.mult, op1=ALU.mult),
                           reads=R_ + [BYM], writes=[BYM, Bsm])

                if stop_after == "F3":
                    dump("YM", scr[:, 3072:4096], [128, 1024], Bscr)
                    fw.barrier()
                    return nc, dump_d
                for (Ysrc, Bs_, Ydst, Bd_) in ((YD, BYD, YDT, BYDT), (YM, BYM, YMT, BYMT)):
                    for kc in range(4):
                        bk = gbank()
                        pst = ps[bk][:].bitcast(BF16)
                        for ql in range(4):
                            op("pe", lambda e, kc=kc, ql=ql, pst=pst, Ysrc=Ysrc: e.transpose(
                                pst[:, ql * 128:(ql + 1) * 128], Ysrc[:, ql, kc * 128:(kc + 1) * 128], ident[:]),
                               reads=[Bs_, Bid], writes=[PS[bk]])
                        op("dve", lambda e, kc=kc, pst=pst, Ydst=Ydst: e.tensor_copy(out=Ydst[:, kc, :], in_=pst[:, 0:512]), reads=[PS[bk]], writes=[Bd_])

                if stop_after == "F4":
                    dump("YDT", YDT[:], [128, 4, 512], BYDT)
                    dump("YMT", YMT[:], [128, 4, 512], BYMT)
                    fw.barrier()
                    return nc, dump_d
                for br in range(3):
                    if br == 0:
                        ysrc = lambda kc, G=G: ZS[:, kc, G * 512:(G + 1) * 512]
                        By = BZSn
                    elif br == 1:
                        ysrc = lambda kc: YDT[:, kc, :]
                        By = BYDT
                    else:
                        ysrc = lambda kc: YMT[:, kc, :]
                        By = BYMT
                    for half in range(2):
                        wg_, Bwg = next_w()
                        wb_, Bwb = next_w()
                        for fl in range(4):
                            ft = half * 4 + fl
                            bb_, bg_ = gbank(), gbank()
                            for kc in range(4):
                                op("pe", lambda e, kc=kc, fl=fl, bb_=bb_, ysrc=ysrc, wb_=wb_: e.matmul(
                                    ps[bb_][:], wb_[:, kc, fl * 128:(fl + 1) * 128], ysrc(kc), start=(kc == 0), stop=(kc == 3)),
                                   reads=[Bwb, By], writes=[PS[bb_]])
                            for kc in range(8):
                                op("pe", lambda e, kc=kc, fl=fl, bg_=bg_, wg_=wg_: e.matmul(
                                    ps[bg_][:], wg_[:, kc, fl * 128:(fl + 1) * 128], xq[:, kc, :], start=(kc == 0), stop=(kc == 7)),
                                   reads=[Bwg, Bxq], writes=[PS[bg_]])
                            s_ = ft % 2
                            op("act", lambda e, bg_=bg_, s_=s_: e.activation(out=sgf[s_][:], in_=ps[bg_][:], func=AF.Sigmoid), reads=[PS[bg_]], writes=[Bsgf[s_]])
                            if br == 0:
                                op("dve", lambda e, bb_=bb_, s_=s_, ft=ft: e.tensor_tensor(out=ACC[:, ft, :], in0=sgf[s_][:], in1=ps[bb_][:], op=ALU.mult),
                                   reads=[PS[bb_], Bsgf[s_]], writes=[BACC])
                            else:
                                op("dve", lambda e, bb_=bb_, s_=s_: e.tensor_tensor(out=sgf[s_][:], in0=sgf[s_][:], in1=ps[bb_][:], op=ALU.mult),
                                   reads=[PS[bb_], Bsgf[s_]], writes=[Bsgf[s_]])
                                if br == 1:
                                    op("dve", lambda e, s_=s_, ft=ft: e.tensor_tensor(out=ACC[:, ft, :], in0=ACC[:, ft, :], in1=sgf[s_][:], op=ALU.add),
                                       reads=[Bsgf[s_], BACC], writes=[BACC])
                                else:
                                    op("dve", lambda e, s_=s_, ft=ft: e.tensor_tensor(out=MT[:, ft, :], in0=ACC[:, ft, :], in1=sgf[s_][:], op=ALU.add),
                                       reads=[Bsgf[s_], BACC], writes=[BMT])

                if stop_after == "F5":
                    dump("MT", MT[:], [128, 8, 512], BMT)
                    fw.barrier()
                    return nc, dump_d
                if G + 1 < NT:
                    dma("sp", dsxq, xq[:], wsrc(xTb[G + 1], 8), reads=[Bx[G + 1]], writes=[Bxq])
                wo = [next_w() for j in range(2)]
                HH = [sgf_all[:].rearrange("p a n -> p (a n)"), YMT[:].rearrange("p h n -> p (h n)").bitcast(F32)]
                HHB = [[Bsgf[0], Bsgf[1]], [BYMT]]
                OBK = (0, 1, 2, 3, 4, 5, 6, 7)

                def ln_head(ql):
                    r0 = G * 512 + ql * 128
                    Hc, HB = HH[ql % 2], HHB[ql % 2]
                    dma("sp", dsxr, xres[:], I["x"][r0:r0 + 128, :], writes=[Bxres])
                    for half in range(2):
                        bk = OBK[2 * ql + half]
                        wo_, Bwo = wo[half]
                        for kc in range(8):
                            op("pe", lambda e, kc=kc, ql=ql, bk=bk, wo_=wo_: e.matmul(
                                ps[bk][:], MT[:, kc, ql * 128:(ql + 1) * 128], wo_[:, kc, :], start=(kc == 0), stop=(kc == 7)),
                               reads=[BMT, Bwo], writes=[PS[bk]])
                        hs = slice(half * 512, (half + 1) * 512)
                        op("dve", lambda e, bk=bk, hs=hs, Hc=Hc: e.scalar_tensor_tensor(out=Hc[:, hs], in0=xres[:, hs], scalar=ALPHA, in1=ps[bk][:], op0=ALU.mult, op1=ALU.add),
                           reads=[PS[bk], Bxres] + HB, writes=HB)
                    R_, W_ = HB + [Bst[ql % 2], BC], [Bst[ql % 2]] + HB
                    st_, mv_ = stats2[ql % 2], mv2[ql % 2]
                    for half in range(2):
                        op("dve", lambda e, half=half, Hc=Hc: e.bn_stats(out=st_[:, half, :], in_=Hc[:, half * 512:(half + 1) * 512]), reads=R_, writes=W_)
                    op("dve", lambda e: e.bn_aggr(out=mv_[:, 0:2], in_=st_[:].rearrange("p a b -> p (a b)")), reads=R_, writes=W_)
                    op("dve", lambda e: e.tensor_scalar(out=mv_[:, 2:3], in0=mv_[:, 1:2], scalar1=1e-5, scalar2=None, op0=ALU.add), reads=R_, writes=W_)

                def ln_tail(ql, G=G, HH=HH, HHB=HHB):
                    r0 = G * 512 + ql * 128
                    Hc, HB = HH[ql % 2], HHB[ql % 2]
                    R_, W_ = HB + [Bst[ql % 2], BC], [Bst[ql % 2]] + HB
                    mv_ = mv2[ql % 2]
                    op("act", lambda e: e.activation(out=mv_[:, 2:3], in_=mv_[:, 2:3], func=AF.Ln), reads=R_, writes=W_)
                    op("act", lambda e: e.activation(out=mv_[:, 3:4], in_=mv_[:, 2:3], func=AF.Exp, scale=-0.5), reads=R_, writes=W_)
                    op("dve", lambda e, Hc=Hc: e.tensor_scalar(out=Hc, in0=Hc, scalar1=mv_[:, 0:1], scalar2=mv_[:, 3:4], op0=ALU.subtract, op1=ALU.mult), reads=R_, writes=W_)
                    op("pool", lambda e, Hc=Hc: e.tensor_tensor(out=Hc, in0=Hc, in1=lng[:], op=ALU.mult), reads=HB + [BC], writes=HB)
                    op("pool", lambda e, Hc=Hc: e.tensor_tensor(out=Hc, in0=Hc, in1=lnb[:], op=ALU.add), reads=HB + [BC], writes=HB)
                    dma("pool", dsout[ql % 2], out_d[r0:r0 + 128, :], Hc, reads=HB)

                ln_head(0)
                ln_tail(0)
                ln_head(1)
                ln_tail(1)
                ln_head(2)
                ln_head(3)
                if G + 1 < NT:
                    emit_qproj()
                    pend_ln.extend([lambda f=ln_tail: f(2), lambda f=ln_tail: f(3)])
                else:
                    ln_tail(2)
                    ln_tail(3)
                if stop_after == "F6":
                    fw.barrier()
                    return nc, dump_d
            fw.barrier()
        fw.barrier()
    return nc, dump_d


_NC_CACHE = {}


def kernel(**inputs):
    inp = {k: np.asarray(v) for k, v in inputs.items()}
    shared = _shared_layouts(inp)
    if "nc" not in _NC_CACHE:
        _NC_CACHE["nc"] = build()[0]
    nc = _NC_CACHE["nc"]
    in_maps = []
    for b in range(8):
        m = dict(shared)
        m["x"] = np.ascontiguousarray(inp["x"][b])
        m["xT"] = np.ascontiguousarray(inp["x"][b].T)
        m["memT"] = np.ascontiguousarray(inp["mem"][b].T)
        in_maps.append(m)
    res = run_bass_kernel_spmd(nc, in_maps, core_ids=list(range(8)))
    return np.stack([np.asarray(r["out"]) for r in res.results]).astype(np.float32)
```
